# Optimizing a Trainium2 kernel written in Bass

```python
import math
import jax
import jax.numpy as jnp
from jax import lax
import numpy as np

D_MODEL = 2048
BATCH = 8
SEQ = 4096
DEPTH = 4

HEAD_DIM = 128
REL_HEADS = 8
A_HEADS = REL_HEADS
A_KV_HEADS = 2
A_WINDOW = 128
B_HEADS = 8
C_HEADS = REL_HEADS
C_KV_GROUPS = 2
CMP_BLOCK = 32
CMP_STRIDE = 16
CMP_HIDDEN = 256
SEL_BLOCK = 64
N_SELECT = 8
C_WINDOW = 512
D_HEADS = 8
D_CONV = 4
GDN_CHUNK = 64
NUM_BUCKETS = 32
MAX_DISTANCE = 128
D_FF = 11 * D_MODEL // 4
FFN_CONV = 3
Q_BLOCK = 128
EPS = 1e-6
NEG_INF = -1e30
FORCE_SCORE = 1e9
SCALE = HEAD_DIM ** -0.5
EVEN_WIDTH = (A_HEADS + B_HEADS) * HEAD_DIM
ODD_WIDTH = (C_HEADS + D_HEADS) * HEAD_DIM
EVEN_SPLITS = (A_HEADS * HEAD_DIM, A_KV_HEADS * HEAD_DIM, A_KV_HEADS * HEAD_DIM,
               B_HEADS * HEAD_DIM, B_HEADS * HEAD_DIM, B_HEADS * HEAD_DIM, B_HEADS)
ODD_SPLITS = ((C_HEADS * HEAD_DIM,) + (C_KV_GROUPS * HEAD_DIM,) * 6 + (3 * C_HEADS,)
              + (D_HEADS * HEAD_DIM,) * 3 + (D_HEADS, D_HEADS, D_HEADS * HEAD_DIM))
EVEN_COLS = sum(EVEN_SPLITS)
ODD_COLS = sum(ODD_SPLITS)

kernel_name = 'hybrid_swa_fox_nsa_gdn_convffn_trunk'


def rmsnorm(x, g):
    xf = x.astype(jnp.float32)
    y = xf * lax.rsqrt(jnp.mean(xf * xf, axis=-1, keepdims=True) + EPS)
    return (y * g.astype(jnp.float32)).astype(x.dtype)


def l2norm(x):
    return x * lax.rsqrt(jnp.sum(x * x, axis=-1, keepdims=True) + EPS)


def split_cols(x, sizes):
    cuts = [int(c) for c in np.cumsum(sizes)[:-1]]
    return jnp.split(x, cuts, axis=-1)


def causal_dwconv(x, w):
    width, t = w.shape[0], x.shape[1]
    xp = jnp.pad(x, ((0, 0), (width - 1, 0), (0, 0)))
    return sum(xp[:, j:j + t] * w[j] for j in range(width))


def masked_softmax(logits, mask):
    p = jax.nn.softmax(jnp.where(mask, logits, NEG_INF), axis=-1)
    return jnp.where(mask, p, 0.0)


def t5_bucket(dist):
    max_exact = NUM_BUCKETS // 2
    n = jnp.maximum(dist, 0)
    log_ratio = jnp.log(jnp.maximum(n, 1).astype(jnp.float32) / max_exact) / math.log(MAX_DISTANCE / max_exact)
    large = jnp.minimum(max_exact + (log_ratio * (NUM_BUCKETS - max_exact)).astype(jnp.int32), NUM_BUCKETS - 1)
    return jnp.where(n < max_exact, n, large)


def banded_blocks(x, window):
    bsz, t = x.shape[:2]
    nb, nprev = t // Q_BLOCK, window // Q_BLOCK
    xp = jnp.pad(x, ((0, 0), (window, 0), (0, 0), (0, 0))).reshape(bsz, nb + nprev, Q_BLOCK, *x.shape[2:])
    return jnp.concatenate([xp[:, s:s + nb] for s in range(nprev + 1)], axis=2)


def banded_mask_bias(rel_bias, window, nb):
    span = window + Q_BLOCK
    kl = jnp.arange(span)
    dist = jnp.arange(Q_BLOCK)[:, None] + window - kl[None, :]
    kpos = jnp.arange(nb)[:, None, None] * Q_BLOCK - window + kl
    mask = (dist >= 0) & (dist < window) & (kpos >= 0)
    bias = rel_bias[t5_bucket(dist)].transpose(2, 0, 1)
    return mask, bias


def swa_sink_attention(q, k, v, sinks, rel_bias):
    bsz, t, _ = q.shape
    nb = t // Q_BLOCK
    g, r = A_KV_HEADS, A_HEADS // A_KV_HEADS
    qb = q.reshape(bsz, nb, Q_BLOCK, g, r, HEAD_DIM)
    kb = banded_blocks(k.reshape(bsz, t, g, HEAD_DIM), A_WINDOW)
    vb = banded_blocks(v.reshape(bsz, t, g, HEAD_DIM), A_WINDOW)
    mask, bias = banded_mask_bias(rel_bias, A_WINDOW, nb)
    logits = (jnp.einsum('bnqgrd,bnkgd->bgrnqk', qb, kb).astype(jnp.float32) * SCALE
              + bias.reshape(g, r, 1, Q_BLOCK, -1).astype(jnp.float32))
    logits = jnp.where(mask, logits, NEG_INF)
    sink = sinks.astype(jnp.float32).reshape(g, r, 1, 1, 1)
    m = jnp.maximum(logits.max(axis=-1, keepdims=True), sink)
    e = jnp.where(mask, jnp.exp(logits - m), 0.0)
    p = e / (e.sum(axis=-1, keepdims=True) + jnp.exp(sink - m))
    o = jnp.einsum('bgrnqk,bnkgd->bnqgrd', p.astype(v.dtype), vb)
    return o.reshape(bsz, t, A_HEADS * HEAD_DIM)


def forgetting_attention(q, k, v, f_logit):
    bsz, t, _ = q.shape
    nb = t // Q_BLOCK
    heads = lambda a: a.reshape(bsz, t, B_HEADS, HEAD_DIM).transpose(0, 2, 1, 3)
    q, k, v = heads(q), heads(k), heads(v)
    c = jnp.cumsum(jax.nn.log_sigmoid(f_logit.astype(jnp.float32)), axis=1).transpose(0, 2, 1)
    qb = jnp.moveaxis(q.reshape(bsz, B_HEADS, nb, Q_BLOCK, HEAD_DIM), 2, 0)
    cb = jnp.moveaxis(c.reshape(bsz, B_HEADS, nb, Q_BLOCK), 2, 0)
    kpos = jnp.arange(t)

    def block(args):
        i, qi, ci = args
        qpos = i * Q_BLOCK + jnp.arange(Q_BLOCK)
        logits = (jnp.einsum('bhqd,bhkd->bhqk', qi, k).astype(jnp.float32) * SCALE
                  + ci[..., None] - c[:, :, None, :])
        p = masked_softmax(logits, kpos[None, :] <= qpos[:, None])
        return jnp.einsum('bhqk,bhkd->bhqd', p.astype(v.dtype), v)

    o = lax.map(block, (jnp.arange(nb), qb, cb))
    return o.transpose(1, 0, 3, 2, 4).reshape(bsz, t, B_HEADS * HEAD_DIM)


def compress_tokens(x, pe, w1, w2):
    bsz, t, g, d = x.shape
    ratio = CMP_BLOCK // CMP_STRIDE
    n_cmp = t // CMP_STRIDE - ratio + 1
    chunks = x.reshape(bsz, t // CMP_STRIDE, CMP_STRIDE, g, d)
    blocks = jnp.concatenate([chunks[:, m:m + n_cmp] for m in range(ratio)], axis=2) + pe[:, None, :]
    flat = blocks.transpose(0, 1, 3, 2, 4).reshape(bsz, n_cmp, g, CMP_BLOCK * d)
    return jax.nn.gelu(flat @ w1) @ w2


def nsa_attention(q, k_cmp, v_cmp, k_sel, v_sel, k_win, v_win, gate_logits, cmp_pos, cmp_w1, cmp_w2, rel_bias):
    bsz, t, _ = q.shape
    g, r = C_KV_GROUPS, C_HEADS // C_KV_GROUPS
    nb = t // Q_BLOCK
    kv = lambda a: a.reshape(bsz, t, g, HEAD_DIM)
    kc = compress_tokens(kv(k_cmp), cmp_pos[0], cmp_w1[0], cmp_w2[0])
    vc = compress_tokens(kv(v_cmp), cmp_pos[1], cmp_w1[1], cmp_w2[1])
    cmp_end = jnp.arange(kc.shape[1]) * CMP_STRIDE + CMP_BLOCK - 1
    n_sb = t // SEL_BLOCK
    n_sel = min(N_SELECT, n_sb)
    ks = kv(k_sel).reshape(bsz, n_sb, SEL_BLOCK, g, HEAD_DIM).transpose(0, 3, 1, 2, 4)
    vs = kv(v_sel).reshape(bsz, n_sb, SEL_BLOCK, g, HEAD_DIM).transpose(0, 3, 1, 2, 4)
    kw = banded_blocks(kv(k_win), C_WINDOW)
    vw = banded_blocks(kv(v_win), C_WINDOW)
    mask_w, bias_w = banded_mask_bias(rel_bias, C_WINDOW, nb)
    bias_w = bias_w.reshape(g, r, Q_BLOCK, -1).astype(jnp.float32)
    tab = rel_bias.T.reshape(g, r, NUM_BUCKETS).astype(jnp.float32)
    gates = jax.nn.sigmoid(gate_logits.astype(jnp.float32)).reshape(bsz, nb, Q_BLOCK, 3, g, r)
    qb = q.reshape(bsz, nb, Q_BLOCK, g, r, HEAD_DIM)
    b_ar = jnp.arange(bsz)[:, None, None, None]
    g_ar = jnp.arange(g)[None, :, None, None]
    ratio = CMP_BLOCK // CMP_STRIDE
    n_chunk = t // CMP_STRIDE
    sel_ids = jnp.arange(n_sb)

    def block(args):
        i, qi, kwi, vwi, mwi, gi = args
        qpos = i * Q_BLOCK + jnp.arange(Q_BLOCK)
        lc = jnp.einsum('bqgrd,bcgd->bgrqc', qi, kc).astype(jnp.float32) * SCALE
        pc = masked_softmax(lc, cmp_end[None, :] <= qpos[:, None])
        o_cmp = jnp.einsum('bgrqc,bcgd->bqgrd', pc.astype(vc.dtype), vc)
        imp = jnp.pad(pc.sum(axis=2), ((0, 0), (0, 0), (0, 0), (ratio - 1, ratio - 1)))
        chunk = sum(imp[..., m:m + n_chunk] for m in range(ratio))
        blk = chunk.reshape(bsz, g, Q_BLOCK, n_sb, SEL_BLOCK // CMP_STRIDE).sum(-1)
        cur = (qpos // SEL_BLOCK)[:, None]
        forced = (sel_ids == 0) | (sel_ids == cur) | (sel_ids == cur - 1)
        score = jnp.where(forced, FORCE_SCORE, jnp.where(sel_ids > cur, -FORCE_SCORE, blk))
        _, idx = lax.top_k(score, n_sel)
        k_g = ks[b_ar, g_ar, idx].reshape(bsz, g, Q_BLOCK, n_sel * SEL_BLOCK, HEAD_DIM)
        v_g = vs[b_ar, g_ar, idx].reshape(bsz, g, Q_BLOCK, n_sel * SEL_BLOCK, HEAD_DIM)
        kpos = (idx[..., None] * SEL_BLOCK + jnp.arange(SEL_BLOCK)).reshape(bsz, g, Q_BLOCK, -1)
        dist = qpos[:, None] - kpos
        bias_s = jnp.moveaxis(tab[g_ar, :, t5_bucket(dist)], -1, 2)
        ls = jnp.einsum('bqgrd,bgqkd->bgrqk', qi, k_g).astype(jnp.float32) * SCALE + bias_s
        ps = masked_softmax(ls, (dist >= 0)[:, :, None])
        o_sel = jnp.einsum('bgrqk,bgqkd->bqgrd', ps.astype(v_g.dtype), v_g)
        lw = jnp.einsum('bqgrd,bkgd->bgrqk', qi, kwi).astype(jnp.float32) * SCALE + bias_w
        pw = masked_softmax(lw, mwi)
        o_win = jnp.einsum('bgrqk,bkgd->bqgrd', pw.astype(vwi.dtype), vwi)
        out = (gi[:, :, 0, :, :, None] * o_cmp + gi[:, :, 1, :, :, None] * o_sel
               + gi[:, :, 2, :, :, None] * o_win)
        return out.astype(q.dtype)

    mv = lambda a: jnp.moveaxis(a, 1, 0)
    o = lax.map(block, (jnp.arange(nb), mv(qb), mv(kw), mv(vw), mask_w, mv(gates)))
    return jnp.moveaxis(o, 0, 1).reshape(bsz, t, C_HEADS * HEAD_DIM)


def gated_deltanet(q, k, v, beta_logit, a, z, conv_w, a_log, dt_bias, norm_g):
    dtype = q.dtype
    bsz, t, _ = q.shape
    h, d, c = D_HEADS, HEAD_DIM, GDN_CHUNK
    n = t // c
    f32 = jnp.float32
    qkv = jax.nn.silu(causal_dwconv(jnp.concatenate([q, k, v], axis=-1), conv_w)).astype(f32)
    q, k, v = jnp.split(qkv, 3, axis=-1)
    chunked = lambda x: x.reshape(bsz, n, c, h, d).transpose(0, 3, 1, 2, 4)
    q = l2norm(chunked(q)) * SCALE
    k = l2norm(chunked(k))
    v = chunked(v)
    beta = jax.nn.sigmoid(beta_logit.astype(f32)).reshape(bsz, n, c, h).transpose(0, 3, 1, 2)
    g = -jnp.exp(a_log.astype(f32)) * jax.nn.softplus(a.astype(f32) + dt_bias.astype(f32))
    gam = jnp.cumsum(g.reshape(bsz, n, c, h).transpose(0, 3, 1, 2), axis=-1)
    causal = jnp.tril(jnp.ones((c, c), bool))
    strict = jnp.tril(jnp.ones((c, c), bool), -1)
    diff = gam[..., :, None] - gam[..., None, :]
    decay = jnp.where(causal, jnp.exp(jnp.where(causal, diff, 0.0)), 0.0)
    k_beta = k * beta[..., None]
    m = jnp.eye(c, dtype=f32) + jnp.where(strict, jnp.einsum('bhnid,bhnjd->bhnij', k_beta, k) * decay, 0.0)
    u = lax.linalg.triangular_solve(m, v * beta[..., None], left_side=True, lower=True, unit_diagonal=True)
    w = lax.linalg.triangular_solve(m, k_beta * jnp.exp(gam)[..., None], left_side=True, lower=True, unit_diagonal=True)

    def step(state, inp):
        qc, kc, uc, wc, dc, gc = inp
        v_new = uc - jnp.einsum('bhid,bhde->bhie', wc, state)
        o = (jnp.einsum('bhid,bhde->bhie', qc * jnp.exp(gc)[..., None], state)
             + jnp.einsum('bhij,bhje->bhie', jnp.einsum('bhid,bhjd->bhij', qc, kc) * dc, v_new))
        g_last = gc[..., -1:]
        state = (state * jnp.exp(g_last)[..., None]
                 + jnp.einsum('bhjd,bhje->bhde', kc * jnp.exp(g_last - gc)[..., None], v_new))
        return state, o

    xs = tuple(jnp.moveaxis(arr, 2, 0) for arr in (q, k, u, w, decay, gam))
    _, o = lax.scan(step, jnp.zeros((bsz, h, d, d), f32), xs)
    o = o.transpose(1, 0, 3, 2, 4).reshape(bsz, t, h, d)
    o = rmsnorm(o, norm_g) * jax.nn.silu(z.astype(f32).reshape(bsz, t, h, d))
    return o.reshape(bsz, t, h * d).astype(dtype)


def even_mixer(h, w_in, b_forget, sinks, w_out, rel_bias):
    qa, ka, va, qb, kb, vb, f = split_cols(h @ w_in, EVEN_SPLITS)
    o_a = swa_sink_attention(qa, ka, va, sinks, rel_bias)
    o_b = forgetting_attention(qb, kb, vb, f + b_forget)
    return jnp.concatenate([o_a, o_b.astype(o_a.dtype)], axis=-1) @ w_out


def odd_mixer(h, w_in, cmp_pos, cmp_w1, cmp_w2, conv_w, a_log, dt_bias, gdn_norm, w_out, rel_bias):
    (qc, kcmp, vcmp, ksel, vsel, kwin, vwin, gates,
     qd, kd, vd, beta, a, z) = split_cols(h @ w_in, ODD_SPLITS)
    o_c = nsa_attention(qc, kcmp, vcmp, ksel, vsel, kwin, vwin, gates, cmp_pos, cmp_w1, cmp_w2, rel_bias)
    o_d = gated_deltanet(qd, kd, vd, beta, a, z, conv_w, a_log, dt_bias, gdn_norm)
    return jnp.concatenate([o_c, o_d.astype(o_c.dtype)], axis=-1) @ w_out


def conv_ffn(h, w_up, conv_w, conv_b, w_down):
    u, g = jnp.split(h @ w_up, 2, axis=-1)
    g = causal_dwconv(g, conv_w) + conv_b
    return (jax.nn.silu(g) * u) @ w_down


def setup_inputs(seed: int = 0) -> dict:
    key = jax.random.key(seed)
    ks = jax.random.split(key, 22)
    n_ev, n_od = (DEPTH + 1) // 2, DEPTH // 2
    f32 = jnp.float32

    def nrm(k, shape, scale):
        return scale * jax.random.normal(k, shape, f32)

    def gain(k, shape):
        return 1.0 + 0.02 * jax.random.normal(k, shape, f32)

    dt = jnp.exp(jax.random.uniform(ks[15], (n_od, D_HEADS), f32, math.log(1e-3), math.log(1e-1)))
    return {
        'x': nrm(ks[0], (BATCH, SEQ, D_MODEL), 1.0),
        'rel_bias': nrm(ks[1], (NUM_BUCKETS, REL_HEADS), 0.5),
        'norm_mix': gain(ks[2], (DEPTH, D_MODEL)),
        'norm_ffn': gain(ks[3], (DEPTH, D_MODEL)),
        'norm_final': gain(ks[4], (D_MODEL,)),
        'ev_w_in': nrm(ks[5], (n_ev, D_MODEL, EVEN_COLS), D_MODEL ** -0.5),
        'ev_b_forget': 2.0 + nrm(ks[6], (n_ev, B_HEADS), 0.5),
        'ev_sinks': nrm(ks[7], (n_ev, A_HEADS), 0.5),
        'ev_w_out': nrm(ks[8], (n_ev, EVEN_WIDTH, D_MODEL), EVEN_WIDTH ** -0.5),
        'od_w_in': nrm(ks[9], (n_od, D_MODEL, ODD_COLS), D_MODEL ** -0.5),
        'od_cmp_pos': nrm(ks[10], (n_od, 2, CMP_BLOCK, HEAD_DIM), 0.1),
        'od_cmp_w1': nrm(ks[11], (n_od, 2, CMP_BLOCK * HEAD_DIM, CMP_HIDDEN), (CMP_BLOCK * HEAD_DIM) ** -0.5),
        'od_cmp_w2': nrm(ks[12], (n_od, 2, CMP_HIDDEN, HEAD_DIM), CMP_HIDDEN ** -0.5),
        'od_conv_w': nrm(ks[13], (n_od, D_CONV, 3 * D_HEADS * HEAD_DIM), D_CONV ** -0.5),
        'od_a_log': jnp.log(jax.random.uniform(ks[14], (n_od, D_HEADS), f32, 1.0, 16.0)),
        'od_dt_bias': dt + jnp.log(-jnp.expm1(-dt)),
        'od_gdn_norm': gain(ks[16], (n_od, HEAD_DIM)),
        'od_w_out': nrm(ks[17], (n_od, ODD_WIDTH, D_MODEL), ODD_WIDTH ** -0.5),
        'ffn_w_up': nrm(ks[18], (DEPTH, D_MODEL, 2 * D_FF), D_MODEL ** -0.5),
        'ffn_conv_w': nrm(ks[19], (DEPTH, FFN_CONV, D_FF), FFN_CONV ** -0.5),
        'ffn_conv_b': nrm(ks[20], (DEPTH, D_FF), 0.02),
        'ffn_w_down': nrm(ks[21], (DEPTH, D_FF, D_MODEL), D_FF ** -0.5),
    }


def reference(x, rel_bias, norm_mix, norm_ffn, norm_final,
              ev_w_in, ev_b_forget, ev_sinks, ev_w_out,
              od_w_in, od_cmp_pos, od_cmp_w1, od_cmp_w2, od_conv_w, od_a_log, od_dt_bias, od_gdn_norm, od_w_out,
              ffn_w_up, ffn_conv_w, ffn_conv_b, ffn_w_down):
    h = x
    for layer in range(DEPTH):
        j = layer // 2
        hn = rmsnorm(h, norm_mix[layer])
        if layer % 2 == 0:
            mix = even_mixer(hn, ev_w_in[j], ev_b_forget[j], ev_sinks[j], ev_w_out[j], rel_bias)
        else:
            mix = odd_mixer(hn, od_w_in[j], od_cmp_pos[j], od_cmp_w1[j], od_cmp_w2[j], od_conv_w[j],
                            od_a_log[j], od_dt_bias[j], od_gdn_norm[j], od_w_out[j], rel_bias)
        h = h + mix.astype(h.dtype)
        h = h + conv_ffn(rmsnorm(h, norm_ffn[layer]), ffn_w_up[layer], ffn_conv_w[layer],
                         ffn_conv_b[layer], ffn_w_down[layer]).astype(h.dtype)
    return rmsnorm(h, norm_final)
```

```python
import math
import os
import numpy as np
from contextlib import ExitStack
import concourse.bass as bass
import concourse.mybir as mybir
from concourse.bass_utils import run_bass_kernel_spmd

F32 = mybir.dt.float32
BF16 = mybir.dt.bfloat16
AF = mybir.ActivationFunctionType
ALU = mybir.AluOpType
AX = mybir.AxisListType

D = 2048
T = 4096
KC = D // 128
NT = T // 128
TG = 512
NG = T // TG
DEPTH = 4
HD = 128
D_FF = 5632
FC = D_FF // 128
EVEN_COLS = 4616
ODD_COLS = 6696
SCALE = HD ** -0.5
EPS = 1e-6
NEG = -30000.0
N_CORES = 8


class Buf:
    __slots__ = ("name", "w", "r", "excl")

    def __init__(self, name="", excl=False):
        self.name = name
        self.w = {}
        self.r = {}
        self.excl = excl


class Chan:
    __slots__ = ("sem", "cnt", "key")

    def __init__(self, sem, key):
        self.sem = sem
        self.cnt = 0
        self.key = key


class Sched:
    ENG = ("pe", "act", "dve", "pool", "sp")

    def __init__(self, nc, stack):
        self.nc = nc
        self.stack = stack
        self.cnt = {}
        self.semobj = {}
        for e in self.ENG:
            self.semobj[("e", e)] = stack.enter_context(nc.semaphore("s_" + e))
            self.cnt[e] = 0
        self.known = {e: {} for e in self.ENG}
        self.prog = {e: [] for e in self.ENG}
        self.chans = []

    def chan(self):
        key = ("c", len(self.chans))
        sem = self.stack.enter_context(self.nc.semaphore("c%d" % key[1]))
        self.semobj[key] = sem
        c = Chan(sem, key)
        self.chans.append(c)
        return c

    def _collect(self, e, reads, writes, extra=()):
        need = {}
        for b in reads:
            for k, v in b.w.items():
                if need.get(k, 0) < v:
                    need[k] = v
            if b.excl:
                for k, v in b.r.items():
                    if need.get(k, 0) < v:
                        need[k] = v
        for b in writes:
            for k, v in b.w.items():
                if need.get(k, 0) < v:
                    need[k] = v
            for k, v in b.r.items():
                if need.get(k, 0) < v:
                    need[k] = v
        for k, v in extra:
            if need.get(k, 0) < v:
                need[k] = v
        if e == "pe":
            need.pop(("e", "pe"), None)
        waits = []
        kn = self.known[e]
        for k, v in need.items():
            if kn.get(k, 0) >= v:
                continue
            kn[k] = v
            waits.append((k, v))
        return waits

    @staticmethod
    def _commit(ev, reads, writes):
        k, v = ev
        for b in reads:
            if b.r.get(k, 0) < v:
                b.r[k] = v
        for b in writes:
            if b.w.get(k, 0) < v:
                b.w[k] = v

    def op(self, e, fn, reads=(), writes=(), inc=True):
        waits = self._collect(e, reads, writes)
        ev = (("e", e), self.cnt[e] + 1)
        if inc:
            self.cnt[e] += 1
        self.prog[e].append((waits, fn, (("e", e), 1) if inc else None))
        self._commit(ev, reads, writes)

    def dma(self, q, out, in_, chan, reads=(), writes=()):
        self.dma_group(q, [(out, in_)], chan, reads, writes)

    def dma_group(self, q, pairs, chan, reads=(), writes=()):
        extra = [(chan.key, chan.cnt)] if chan.cnt > 0 else []
        waits = self._collect(q, reads, writes, extra)
        for (o, i) in pairs:
            chan.cnt += 16
            fn = lambda eng, o=o, i=i: eng.dma_start(out=o, in_=i)
            self.prog[q].append((waits, fn, (chan.key, 16)))
            waits = []
        self._commit((chan.key, chan.cnt), reads, writes)

    def barrier(self):
        evs = [(("e", o), self.cnt[o]) for o in self.ENG if self.cnt[o] > 0]
        evs += [(c.key, c.cnt) for c in self.chans if c.cnt > 0]
        for e in self.ENG:
            waits = self._collect(e, (), (), evs)
            if e == "pe":
                pass
            if waits:
                self.prog[e].append((waits, None, None))

    def emit(self):
        nc = self.nc
        with nc.Block() as block:
            def run(e):
                def body(eng):
                    for waits, fn, inc in self.prog[e]:
                        for k, v in waits:
                            eng.wait_ge(self.semobj[k], v)
                        if fn is None:
                            continue
                        ins = fn(eng)
                        if inc is not None:
                            ins.then_inc(self.semobj[inc[0]], inc[1])
                return body
            block.tensor(run("pe"))
            block.scalar(run("act"))
            block.vector(run("dve"))
            block.gpsimd(run("pool"))
            block.sync(run("sp"))


class Tl:
    __slots__ = ("t", "b", "c")

    def __init__(self, t, b, c=None):
        self.t = t
        self.b = b
        self.c = c


def _t5_bucket(dist):
    n = np.maximum(dist, 0)
    lr = np.log(np.maximum(n, 1).astype(np.float32) / np.float32(16)) / np.float32(math.log(128 / 16))
    large = np.minimum(16 + (lr * np.float32(16)).astype(np.int32), 31)
    return np.where(n < 16, n, large)


def host_consts():
    k = np.arange(128)[:, None]
    q = np.arange(128)[None, :]
    oh = np.zeros((2, 32, 128, 128), np.float32)
    for o in range(2):
        dist = q - k + 128 * o
        bk = _t5_bucket(dist)
        for b in range(32):
            oh[o, b] = ((bk == b) & (dist >= 0)).astype(np.float32)
    c = {}
    c["c_oh"] = oh.transpose(2, 0, 1, 3).reshape(128, 64 * 128).copy()
    c["c_ident"] = np.eye(128, dtype=np.float32)
    c["c_causal01"] = (q >= k).astype(np.float32)
    c["c_causalneg"] = np.where(q >= k, 0.0, NEG).astype(np.float32)
    c["c_anticausalneg"] = np.where(q < k, 0.0, NEG).astype(np.float32)
    rm = np.ones((8, T), np.float32)
    rm[:, ::64] = 0.0
    c["c_rmask"] = rm
    ii = np.arange(64)[:, None]
    jj = np.arange(64)[None, :]
    c["c_gdn_pmask"] = np.where(ii > jj, 0.0, -NEG).astype(np.float32)
    c["c_gdn_nmask"] = np.where(jj >= ii, 0.0, NEG).astype(np.float32)
    cl = np.arange(128)[:, None, None]
    ti = np.arange(32)[None, :, None]
    cm = np.zeros((128, 64, 128), np.float32)
    qq = np.arange(128)[None, :]
    for i in range(32):
        for ct in range(2):
            cc = ct * 128 + np.arange(128)[:, None]
            cm[:, i * 2 + ct, :] = ((cc <= 254) & (16 * cc + 31 <= 128 * i + qq)).astype(np.float32)
    c["c_cmask"] = cm.reshape(128, 64 * 128)
    wi = np.zeros((256, 64), np.float32)
    for cidx in range(255):
        for j in range(64):
            wi[cidx, j] = sum(1 for m in range(4 * j, 4 * j + 4) if m == cidx or m == cidx + 1)
    c["c_wimp"] = wi.reshape(2, 128, 64).transpose(1, 0, 2).reshape(128, 128).copy()
    keep = np.zeros((128, 32, 64), np.float32)
    add = np.zeros((128, 32, 64), np.float32)
    jv = np.arange(64)[None, :]
    for i in range(32):
        cur = (2 * i + (np.arange(128) >= 64).astype(np.int64))[:, None]
        forced = (jv == 0) | (jv == cur) | (jv == cur - 1)
        fut = (jv > cur) & ~forced
        keep[:, i, :] = (~forced & ~fut).astype(np.float32)
        add[:, i, :] = np.where(forced, 1e9, np.where(fut, -1e9, 0.0))
    c["c_selkeep"] = keep.reshape(128, 32 * 64)
    c["c_seladd"] = add.reshape(128, 32 * 64)
    es = np.zeros((64, 32, 128), np.float32)
    for jb in range(32):
        es[2 * jb, jb, 0:64] = 1.0
        es[2 * jb + 1, jb, 64:128] = 1.0
    c["c_esel"] = es.reshape(64, 32 * 128)
    return c


class Builder:
    def __init__(self, layers=None, debug=False):
        self.layers = list(range(DEPTH)) if layers is None else list(layers)
        self.debug = debug
        self.nc = bass.Bass("TRN2", target_bir_lowering=False)
        self.uid = 0
        self.free_chans = []
        import os
        self.skip = set(os.environ.get("KSKIP", "").split(","))
        self.feed = set(os.environ.get("KFEED", "").split(","))
        self.gdn_heads = int(os.environ.get("KGDNH", "8"))
        self.gdn_stage = float(os.environ.get("KGDNS", "99"))
        self.small = os.environ.get("KSMALL", "") == "1"

    def din(self, name, shape, dt=F32):
        if self.small and name in ("ev_w_in", "ev_w_out", "od_w_in", "od_w_out", "ffn_w_up", "ffn_w_down"):
            shape = [1, 1]
        return self.nc.dram_tensor(name, list(shape), dt, kind="ExternalInput").ap()

    def dscr(self, name, shape, dt):
        kind = "ExternalOutput" if (self.debug and name in self.debug) else "Internal"
        if name in self.feed:
            kind = "ExternalInput"
        return self.nc.dram_tensor(name, list(shape), dt, kind=kind).ap()

    def sb(self, ph, name, shape, dt, chan=False):
        self.uid += 1
        t = ph.enter_context(self.nc.sbuf_tensor("%s_%d" % (name, self.uid), list(shape), dt))
        c = None
        if chan is True:
            c = self.free_chans.pop() if self.free_chans else self.S.chan()
            ph.callback(self.free_chans.append, c)
        return Tl(t, Buf(name), c)

    def mm(self, out_ap, out_buf, pairs, reads):
        n = len(pairs)
        for i, (l, r) in enumerate(pairs):
            self.S.op("pe", lambda e, l=l, r=r, i=i: e.matmul(out_ap, l, r, start=(i == 0), stop=(i == n - 1)),
                      reads, [out_buf], inc=(i == n - 1))

    def build(self):
        nc = self.nc
        self.x = self.din("x", [T, D])
        self.rel_bias = self.din("rel_bias", [32, 8])
        self.norm_mix = self.din("norm_mix", [DEPTH, D])
        self.norm_ffn = self.din("norm_ffn", [DEPTH, D])
        self.norm_final = self.din("norm_final", [D])
        self.ev_w_in = self.din("ev_w_in", [2, D, EVEN_COLS])
        self.ev_b_forget = self.din("ev_b_forget", [2, 8])
        self.ev_sinks = self.din("ev_sinks", [2, 8])
        self.ev_w_out = self.din("ev_w_out", [2, D, D])
        self.od_w_in = self.din("od_w_in", [2, D, ODD_COLS])
        self.od_cmp_pos = self.din("od_cmp_pos", [2, 2, 32, 128])
        self.od_cmp_w1 = self.din("od_cmp_w1", [2, 2, 4096, 256])
        self.od_cmp_w2 = self.din("od_cmp_w2", [2, 2, 256, 128])
        self.od_conv_w = self.din("od_conv_w", [2, 4, 3072])
        self.od_a_log = self.din("od_a_log", [2, 8])
        self.od_dt_bias = self.din("od_dt_bias", [2, 8])
        self.od_gdn_norm = self.din("od_gdn_norm", [2, 128])
        self.od_w_out = self.din("od_w_out", [2, D, D])
        self.ffn_w_up = self.din("ffn_w_up", [DEPTH, D, 2 * D_FF])
        self.ffn_conv_w = self.din("ffn_conv_w", [DEPTH, 3, D_FF])
        self.ffn_conv_b = self.din("ffn_conv_b", [DEPTH, D_FF])
        self.ffn_w_down = self.din("ffn_w_down", [DEPTH, D_FF, D])
        self.c_oh = self.din("c_oh", [128, 64 * 128])
        self.c_ident = self.din("c_ident", [128, 128])
        self.c_causal01 = self.din("c_causal01", [128, 128])
        self.c_causalneg = self.din("c_causalneg", [128, 128])
        self.c_anticausalneg = self.din("c_anticausalneg", [128, 128])
        self.c_rmask = self.din("c_rmask", [8, T])
        self.c_gdn_pmask = self.din("c_gdn_pmask", [64, 64])
        self.c_gdn_nmask = self.din("c_gdn_nmask", [64, 64])
        self.c_cmask = self.din("c_cmask", [128, 64 * 128])
        self.c_wimp = self.din("c_wimp", [128, 128])
        self.c_selkeep = self.din("c_selkeep", [128, 32 * 64])
        self.c_seladd = self.din("c_seladd", [128, 32 * 64])
        self.c_esel = self.din("c_esel", [64, 32 * 128])
        self.y = nc.dram_tensor("y", [T, D], F32, kind="ExternalOutput").ap()
        self.xr = self.dscr("xr", [T, D], F32)
        self.qaT = self.dscr("qaT", [8, 128, T], BF16)
        self.kaT = self.dscr("kaT", [2, 128, T], BF16)
        self.va = self.dscr("va", [T, 256], BF16)
        self.qbT = self.dscr("qbT", [8, 128, T], BF16)
        self.kbT = self.dscr("kbT", [8, 128, T], BF16)
        self.vb = self.dscr("vb", [T, 1024], BF16)
        self.fT = self.dscr("fT", [8, T], F32)
        self.csd = self.dscr("csd", [8, T], F32)
        self.oT = self.dscr("oT", [16, 128, T], BF16)
        self.qcT = self.dscr("qcT", [8, 128, T], BF16)
        self.kcmpT = self.dscr("kcmpT", [2, 128, T], BF16)
        self.vcmpT = self.dscr("vcmpT", [2, 128, T], BF16)
        self.kselT = self.dscr("kselT", [2, 128, T], BF16)
        self.kwinT = self.dscr("kwinT", [2, 128, T], BF16)
        self.vsel = self.dscr("vsel", [T, 256], BF16)
        self.vwin = self.dscr("vwin", [T, 256], BF16)
        self.gates = self.dscr("gates", [T, 24], F32)
        self.qdT = self.dscr("qdT", [8, 128, T], F32)
        self.kdT = self.dscr("kdT", [8, 128, T], F32)
        self.vdT = self.dscr("vdT", [8, 128, T], F32)
        self.baT = self.dscr("baT", [16, T], F32)
        self.zg = self.dscr("zg", [T, 1024], F32)

        with ExitStack() as st:
            self.S = S = Sched(nc, st)
            self.ps = [Tl(st.enter_context(nc.psum_tensor("ps%d" % i, [128, 512], F32)), Buf("ps%d" % i, excl=True)) for i in range(8)]
            self.setup_consts(st)
            xsrc = self.x
            for layer in self.layers:
                j = layer // 2
                if layer % 2 == 0:
                    self.phase_inproj_even(layer, j, xsrc)
                    S.barrier()
                    self.phase_swa(j)
                    S.barrier()
                    self.phase_fox(j)
                    S.barrier()
                    wout = self.ev_w_out[j]
                else:
                    if "inproj" not in self.skip:
                        self.phase_inproj_odd(layer, j, xsrc)
                        S.barrier()
                    if "nsa" not in self.skip:
                        self.phase_nsa(j)
                        S.barrier()
                    if "gdn" not in self.skip:
                        self.phase_gdn(j)
                        S.barrier()
                    wout = self.od_w_out[j]
                if "ffn" not in self.skip:
                    self.phase_outproj_ffn(layer, wout, xsrc)
                    S.barrier()
                xsrc = self.xr
            if "final" not in self.skip:
                self.phase_final(xsrc)
                S.barrier()
            S.emit()
        return nc

    def setup_consts(self, st):
        nc, S = self.nc, self.S
        self.identf = self.sb(st, "identf", [128, 128], F32, chan=True)
        self.identb = self.sb(st, "identb", [128, 128], BF16)
        self.causal01 = self.sb(st, "causal01", [128, 128], BF16)
        self.onesrow = self.sb(st, "onesrow", [1, 128], BF16)
        self.bias0 = self.sb(st, "bias0", [128, 8, 128], F32)
        self.bias1w = self.sb(st, "bias1w", [128, 8, 128], F32)
        self.nb0 = self.sb(st, "nb0", [128, 8, 128], F32)
        self.nb1 = self.sb(st, "nb1", [128, 8, 128], F32)
        self.acneg = self.sb(st, "acneg", [128, 128], F32, chan=True)
        S.dma("sp", self.acneg.t[:], self.c_anticausalneg[:, :], self.acneg.c, writes=[self.acneg.b])
        S.dma("sp", self.identf.t[:], self.c_ident[:, :], self.identf.c, writes=[self.identf.b])
        S.op("dve", lambda e: e.tensor_copy(self.identb.t[:], self.identf.t[:]), [self.identf.b], [self.identb.b])
        S.op("dve", lambda e: e.memset(self.onesrow.t[:], 1.0), [], [self.onesrow.b])
        with ExitStack() as ph:
            oh = self.sb(ph, "oh", [128, 64 * 128], F32, chan=True)
            rbb = self.sb(ph, "rbb", [128, 256], F32, chan=True)
            cz = self.sb(ph, "cz", [128, 128], F32, chan=True)
            cn = self.sb(ph, "cn", [128, 128], F32, chan=True)
            an = self.sb(ph, "an", [128, 128], F32, chan=True)
            S.dma("sp", oh.t[:], self.c_oh[:, :], oh.c, writes=[oh.b])
            S.dma("sp", rbb.t[:], self.rel_bias.rearrange("b h -> (b h)").partition_broadcast(128), rbb.c, writes=[rbb.b])
            S.dma("sp", cz.t[:], self.c_causal01[:, :], cz.c, writes=[cz.b])
            S.dma("sp", cn.t[:], self.c_causalneg[:, :], cn.c, writes=[cn.b])
            S.dma("sp", an.t[:], self.c_anticausalneg[:, :], an.c, writes=[an.b])
            S.op("dve", lambda e: e.tensor_copy(self.causal01.t[:], cz.t[:]), [cz.b], [self.causal01.b])
            for h in range(8):
                for o, (dst, base) in enumerate(((self.bias0, cn), (self.bias1w, an))):
                    for b in range(32):
                        src1 = base.t[:] if b == 0 else dst.t[:, h, :]
                        col = b * 8 + h
                        S.op("dve", lambda e, o=o, b=b, h=h, dst=dst, src1=src1, col=col: e.scalar_tensor_tensor(
                            out=dst.t[:, h, :], in0=oh.t[:, (o * 32 + b) * 128:(o * 32 + b + 1) * 128],
                            scalar=rbb.t[:, col:col + 1], in1=src1, op0=ALU.mult, op1=ALU.add),
                            [oh.b, rbb.b, base.b, dst.b], [dst.b])
            for h in range(8):
                c31 = rbb.t[:, 31 * 8 + h:31 * 8 + h + 1]
                S.op("dve", lambda e, h=h: e.tensor_tensor(self.nb1.t[:, h, :], self.bias1w.t[:, h, :], an.t[:], ALU.subtract), [self.bias1w.b, an.b], [self.nb1.b])
                S.op("dve", lambda e, h=h, c31=c31: e.tensor_scalar(self.nb1.t[:, h, :], self.nb1.t[:, h, :], c31, None, ALU.subtract), [self.nb1.b, rbb.b], [self.nb1.b])
                S.op("dve", lambda e, h=h, c31=c31: e.tensor_scalar(self.nb0.t[:, h, :], self.bias0.t[:, h, :], c31, None, ALU.subtract), [self.bias0.b, rbb.b], [self.nb0.b])
            S.barrier()

    def norm_group(self, g, xsrc, gain, xt, sq, xn, st_, hs, keep_x=None):
        S = self.S
        for tt in range(4):
            tile = g * 4 + tt
            x = xt[tile % len(xt)]
            S.dma("sp", x.t[:], xsrc[tile * 128:(tile + 1) * 128, :], x.c, writes=[x.b])
            ss, rs = st_[0], st_[1]
            S.op("act", lambda e, x=x: e.activation(sq.t[:], x.t[:], AF.Square, accum_out=ss.t[:]), [x.b], [sq.b, ss.b])
            S.op("dve", lambda e: e.tensor_scalar(rs.t[:], ss.t[:], 1.0 / D, EPS, ALU.mult, ALU.add), [ss.b], [rs.b])
            S.op("act", lambda e: e.activation(rs.t[:], rs.t[:], AF.Sqrt), [rs.b], [rs.b])
            S.op("dve", lambda e: e.reciprocal(rs.t[:], rs.t[:]), [rs.b], [rs.b])
            n = xn[tile % len(xn)]
            S.op("dve", lambda e, x=x, n=n: e.scalar_tensor_tensor(out=n.t[:], in0=x.t[:], scalar=rs.t[:, 0:1], in1=gain.t[:],
                                                                  op0=ALU.mult, op1=ALU.mult), [x.b, rs.b, gain.b], [n.b])
            for half in range(2):
                pb = self.ps[6 + half]
                pT = pb.t[:, :].bitcast(BF16)
                for kk in range(8):
                    k = half * 8 + kk
                    S.op("pe", lambda e, n=n, k=k, kk=kk, pT=pT: e.transpose(pT[:, kk * 128:(kk + 1) * 128], n.t[:, k * 128:(k + 1) * 128], self.identb.t[:]),
                         [n.b, self.identb.b], [pb.b], inc=(kk == 7))
                eng = "act" if half == 0 else "dve"
                dst = hs.t[:, half * 8:(half + 1) * 8, tt * 128:(tt + 1) * 128]
                src = pT.rearrange("p (k t) -> p k t", t=128)
                if eng == "act":
                    S.op("act", lambda e, dst=dst, src=src: e.activation(dst, src, AF.Copy), [pb.b], [hs.b])
                else:
                    S.op("dve", lambda e, dst=dst, src=src: e.tensor_copy(dst, src), [pb.b], [hs.b])

    def inproj(self, gain_src, W2d, blocks, xsrc):
        S = self.S
        W = W2d.rearrange("(k p) n -> p k n", p=128)
        with ExitStack() as ph:
            gain = self.sb(ph, "gain", [128, D], F32, chan=True)
            S.dma("sp", gain.t[:], gain_src.partition_broadcast(128), gain.c, writes=[gain.b])
            xt = [self.sb(ph, "xt", [128, D], F32, chan=True) for _ in range(2)]
            sq = self.sb(ph, "sq", [128, D], BF16)
            xn = [self.sb(ph, "xn", [128, D], BF16) for _ in range(2)]
            st_ = [self.sb(ph, "ss", [128, 1], F32), self.sb(ph, "rs", [128, 1], F32)]
            hT = [self.sb(ph, "hT", [128, KC, TG], BF16) for _ in range(2)]
            wt = [self.sb(ph, "wt", [128, KC, 512], BF16, chan=True) for _ in range(2)]
            ev = [self.sb(ph, "ev", [128, 512], BF16, chan=True) for _ in range(4)]
            evf = [self.sb(ph, "evf", [128, 512], F32, chan=True) for _ in range(4)]
            nblk = 0
            nev = 0
            for g in range(NG):
                hs = hT[g % 2]
                self.norm_group(g, xsrc, gain, xt, sq, xn, st_, hs)
                tok = slice(g * TG, (g + 1) * TG)
                for (c0, wd, segs) in blocks:
                    w = wt[nblk % 2]
                    nblk += 1
                    S.dma("pool", w.t[:, :, 0:wd], W[:, :, c0:c0 + wd], w.c, writes=[w.b])
                    for sg in segs:
                        if sg[0] == "f":
                            _, loc, dst, scl, dt = sg
                            pb = self.ps[nev % 4]
                            self.mm(pb.t[:, :], pb.b, [(w.t[:, k, loc:loc + 128], hs.t[:, k, :]) for k in range(KC)], [w.b, hs.b])
                            e_ = (ev if dt == BF16 else evf)[nev % 4]
                            if nev % 2 == 0:
                                S.op("act", lambda e, e_=e_, pb=pb, scl=scl: e.activation(e_.t[:], pb.t[:, :], AF.Copy, scale=scl), [pb.b], [e_.b])
                            else:
                                S.op("dve", lambda e, e_=e_, pb=pb, scl=scl: e.tensor_scalar(e_.t[:], pb.t[:, :], scl, None, ALU.mult), [pb.b], [e_.b])
                            S.dma("sp", dst[:, tok], e_.t[:], e_.c, reads=[e_.b])
                            nev += 1
                        elif sg[0] == "t":
                            _, loc, width, dst, dcol, dt = sg
                            for tt in range(4):
                                pb = self.ps[nev % 4]
                                self.mm(pb.t[:, 0:width], pb.b, [(hs.t[:, k, tt * 128:(tt + 1) * 128], w.t[:, k, loc:loc + width]) for k in range(KC)], [w.b, hs.b])
                                e_ = (ev if dt == BF16 else evf)[nev % 4]
                                if nev % 2 == 0:
                                    S.op("act", lambda e, e_=e_, pb=pb, width=width: e.activation(e_.t[:, 0:width], pb.t[:, 0:width], AF.Copy), [pb.b], [e_.b])
                                else:
                                    S.op("dve", lambda e, e_=e_, pb=pb, width=width: e.tensor_copy(e_.t[:, 0:width], pb.t[:, 0:width]), [pb.b], [e_.b])
                                r0 = g * TG + tt * 128
                                S.dma("sp", dst[r0:r0 + 128, dcol:dcol + width], e_.t[:, 0:width], e_.c, reads=[e_.b])
                                nev += 1
                        else:
                            _, loc, width, dst = sg
                            pb = self.ps[4 + nev % 2]
                            self.mm(pb.t[0:width, :], pb.b, [(w.t[:, k, loc:loc + width], hs.t[:, k, :]) for k in range(KC)], [w.b, hs.b])
                            e_ = evf[nev % 4]
                            S.op("dve", lambda e, pb=pb, e_=e_, width=width: e.tensor_copy(e_.t[0:width, :], pb.t[0:width, :]), [pb.b], [e_.b])
                            S.dma("sp", dst[:, tok], e_.t[0:width, :], e_.c, reads=[e_.b])
                            nev += 1

    def phase_inproj_even(self, layer, j, xsrc):
        blocks = []
        for b in range(2):
            blocks.append((b * 512, 512, [("f", cc * 128, self.qaT[b * 4 + cc], SCALE, BF16) for cc in range(4)]))
        blocks.append((1024, 512, [("f", 0, self.kaT[0], 1.0, BF16), ("f", 128, self.kaT[1], 1.0, BF16), ("t", 256, 256, self.va, 0, BF16)]))
        for b in range(2):
            blocks.append((1536 + b * 512, 512, [("f", cc * 128, self.qbT[b * 4 + cc], SCALE, BF16) for cc in range(4)]))
        for b in range(2):
            blocks.append((2560 + b * 512, 512, [("f", cc * 128, self.kbT[b * 4 + cc], 1.0, BF16) for cc in range(4)]))
        for b in range(2):
            blocks.append((3584 + b * 512, 512, [("t", 0, 512, self.vb, b * 512, BF16)]))
        blocks.append((4608, 8, [("ff", 0, 8, self.fT)]))
        self.inproj(self.norm_mix[layer, :], self.ev_w_in[j], blocks, xsrc)

    def phase_inproj_odd(self, layer, j, xsrc):
        blocks = []
        for b in range(2):
            blocks.append((b * 512, 512, [("f", cc * 128, self.qcT[b * 4 + cc], SCALE, BF16) for cc in range(4)]))
        blocks.append((1024, 512, [("f", 0, self.kcmpT[0], 1.0, BF16), ("f", 128, self.kcmpT[1], 1.0, BF16),
                                   ("f", 256, self.vcmpT[0], 1.0, BF16), ("f", 384, self.vcmpT[1], 1.0, BF16)]))
        blocks.append((1536, 512, [("f", 0, self.kselT[0], 1.0, BF16), ("f", 128, self.kselT[1], 1.0, BF16), ("t", 256, 256, self.vsel, 0, BF16)]))
        blocks.append((2048, 512, [("f", 0, self.kwinT[0], 1.0, BF16), ("f", 128, self.kwinT[1], 1.0, BF16), ("t", 256, 256, self.vwin, 0, BF16)]))
        blocks.append((2560, 24, [("t", 0, 24, self.gates, 0, F32)]))
        for i, dst in enumerate((self.qdT, self.kdT, self.vdT)):
            for b in range(2):
                blocks.append((2584 + i * 1024 + b * 512, 512, [("f", cc * 128, dst[b * 4 + cc], 1.0, F32) for cc in range(4)]))
        blocks.append((5656, 16, [("ff", 0, 16, self.baT)]))
        for b in range(2):
            blocks.append((5672 + b * 512, 512, [("t", 0, 512, self.zg, b * 512, F32)]))
        self.inproj(self.norm_mix[layer, :], self.od_w_in[j], blocks, xsrc)

    def phase_swa(self, j):
        S = self.S
        with ExitStack() as ph:
            esink = self.sb(ph, "esink", [128, 8], F32, chan=True)
            S.dma("sp", esink.t[:], self.ev_sinks[j, :].partition_broadcast(128), esink.c, writes=[esink.b])
            S.op("act", lambda e: e.activation(esink.t[:], esink.t[:], AF.Exp), [esink.b], [esink.b])
            kT = [self.sb(ph, "kT", [128, T], BF16, chan=True) for _ in range(2)]
            v1 = [self.sb(ph, "v1", [128, NT, 132], BF16, chan=True) for _ in range(2)]
            qT = [self.sb(ph, "qT", [128, T], BF16, chan=True) for _ in range(2)]
            sb_ = [self.sb(ph, "sb", [128, 128], F32) for _ in range(2)]
            pT = [self.sb(ph, "pT", [128, 128], BF16) for _ in range(4)]
            rd = [self.sb(ph, "rd", [128, 1], F32) for _ in range(2)]
            on = [self.sb(ph, "on", [128, 128], BF16) for _ in range(2)]
            ost = [self.sb(ph, "ost", [128, 512], BF16, chan=True) for _ in range(2)]
            for g in range(2):
                S.dma("sp", kT[g].t[:], self.kaT[g], kT[g].c, writes=[kT[g].b])
                S.op("pool", lambda e, g=g: e.memset(v1[g].t[:, :, 128:129], 1.0), [], [v1[g].b])
                S.dma("sp", v1[g].t[:, :, 0:128], self.va.rearrange("(n p) c -> p n c", p=128)[:, :, g * 128:(g + 1) * 128], v1[g].c, writes=[v1[g].b])
            it = 0
            for h in range(8):
                g = h // 4
                q = qT[h % 2]
                S.dma("sp", q.t[:], self.qaT[h], q.c, writes=[q.b])
                for i in range(NT):
                    acc = self.ps[4 + (i % 2)]
                    js = [jb for jb in (i - 1, i) if jb >= 0]
                    for jb in js:
                        sp = self.ps[it % 4]
                        self.mm(sp.t[:, 0:128], sp.b, [(kT[g].t[:, jb * 128:(jb + 1) * 128], q.t[:, i * 128:(i + 1) * 128])], [kT[g].b, q.b])
                        bias = self.bias0 if jb == i else self.bias1w
                        s_ = sb_[it % 2]
                        S.op("dve", lambda e, s_=s_, sp=sp, bias=bias, h=h: e.tensor_tensor(s_.t[:], sp.t[:, 0:128], bias.t[:, h, :], ALU.add),
                             [sp.b, bias.b], [s_.b])
                        p = pT[it % 4]
                        S.op("act", lambda e, p=p, s_=s_: e.activation(p.t[:], s_.t[:], AF.Exp), [s_.b], [p.b])
                        self.S.op("pe", lambda e, acc=acc, p=p, jb=jb, g=g, first=(jb == js[0]), last=(jb == i): e.matmul(
                            acc.t[:, 0:129], p.t[:], v1[g].t[:, jb, 0:129], start=first, stop=last), [p.b, v1[g].b], [acc.b], inc=(jb == i))
                        it += 1
                    r = rd[i % 2]
                    S.op("dve", lambda e, r=r, acc=acc, h=h: e.tensor_tensor(r.t[:], acc.t[:, 128:129], esink.t[:, h:h + 1], ALU.add), [acc.b, esink.b], [r.b])
                    S.op("dve", lambda e, r=r: e.reciprocal(r.t[:], r.t[:]), [r.b], [r.b])
                    o = on[i % 2]
                    S.op("dve", lambda e, o=o, acc=acc, r=r: e.tensor_scalar(o.t[:], acc.t[:, 0:128], r.t[:, 0:1], None, ALU.mult), [acc.b, r.b], [o.b])
                    self.out_transpose(o, ost, i, self.oT[h])

    def out_transpose(self, o, ost, i, dst):
        S = self.S
        pb = self.ps[7]
        pTr = pb.t[:, :].bitcast(BF16)
        stg = ost[(i // 4) % 2]
        S.op("pe", lambda e, o=o, pTr=pTr: e.transpose(pTr[:, 0:128], o.t[:], self.identb.t[:]), [o.b, self.identb.b], [pb.b])
        S.op("act", lambda e, stg=stg, pTr=pTr, i=i: e.activation(stg.t[:, (i % 4) * 128:(i % 4 + 1) * 128], pTr[:, 0:128], AF.Copy), [pb.b], [stg.b])
        if i % 4 == 3:
            c = i // 4
            S.dma("sp", dst[:, c * 512:(c + 1) * 512], stg.t[:], stg.c, reads=[stg.b])

    def phase_fox(self, j):
        S = self.S
        with ExitStack() as ph:
            fr = self.sb(ph, "fr", [8, T], F32, chan=True)
            cs = self.sb(ph, "cs2", [8, T], F32, chan=True)
            ones8 = self.sb(ph, "ones8", [8, T], F32)
            nb = self.sb(ph, "nb", [8, 1], F32, chan=True)
            ck = self.sb(ph, "ck", [128, NT, 8], F32)
            S.dma("sp", fr.t[:], self.fT[:, :], fr.c, writes=[fr.b])
            S.dma("sp", nb.t[:], self.ev_b_forget[j, :].rearrange("(h o) -> h o", o=1), nb.c, writes=[nb.b])
            S.op("dve", lambda e: e.tensor_scalar(nb.t[:], nb.t[:], -1.0, None, ALU.mult), [nb.b], [nb.b])
            S.op("pool", lambda e: e.memset(ones8.t[:], 1.0), [], [ones8.b])
            S.op("act", lambda e: e.activation(fr.t[:], fr.t[:], AF.Exp, bias=nb.t[:, 0:1], scale=-1.0), [fr.b, nb.b], [fr.b])
            S.op("act", lambda e: e.activation(fr.t[:], fr.t[:], AF.Ln, bias=1.0), [fr.b], [fr.b])
            S.op("dve", lambda e: e.tensor_tensor_scan(out=cs.t[:], data0=ones8.t[:], data1=fr.t[:], initial=0.0, op0=ALU.mult, op1=ALU.add),
                 [fr.b, ones8.b], [cs.b])
            S.dma("sp", self.csd[:, :], cs.t[:], cs.c, reads=[cs.b])
            pb = self.ps[7]
            for n in range(NT):
                S.op("pe", lambda e, n=n: e.transpose(pb.t[:, n * 8:(n + 1) * 8], cs.t[0:8, n * 128:(n + 1) * 128], self.identf.t[0:8, 0:8]),
                     [cs.b, self.identf.b], [pb.b], inc=(n == NT - 1))
            S.op("dve", lambda e: e.tensor_copy(ck.t[:], pb.t[:, 0:NT * 8].rearrange("p (n h) -> p n h", h=8)), [pb.b], [ck.b])
            S.barrier()
            kT = [self.sb(ph, "kT", [128, T], BF16, chan=True) for _ in range(2)]
            qT = [self.sb(ph, "qT", [128, T], BF16, chan=True) for _ in range(2)]
            v1 = [self.sb(ph, "v1", [128, NT, 132], BF16, chan=True) for _ in range(2)]
            crow = [self.sb(ph, "crow", [1, T], F32, chan=True) for _ in range(2)]
            ncq = [self.sb(ph, "ncq", [1, T], BF16) for _ in range(2)]
            pT = [self.sb(ph, "pT", [128, 512], BF16) for _ in range(3)]
            rd = [self.sb(ph, "rd", [128, 1], F32) for _ in range(2)]
            on = [self.sb(ph, "on", [128, 128], BF16) for _ in range(2)]
            ost = [self.sb(ph, "ost", [128, 512], BF16, chan=True) for _ in range(2)]
            for s in range(2):
                S.op("pool", lambda e, s=s: e.memset(v1[s].t[:, :, 128:129], 1.0), [], [v1[s].b])
            it = 0
            for h in range(8):
                s = h % 2
                k_, q_, v_, cr, nq = kT[s], qT[s], v1[s], crow[s], ncq[s]
                S.dma("sp", k_.t[:], self.kbT[h], k_.c, writes=[k_.b])
                S.dma("sp", q_.t[:], self.qbT[h], q_.c, writes=[q_.b])
                S.dma("sp", v_.t[:, :, 0:128], self.vb.rearrange("(n p) c -> p n c", p=128)[:, :, h * 128:(h + 1) * 128], v_.c, writes=[v_.b])
                S.dma("sp", cr.t[:], self.csd[h:h + 1, :], cr.c, writes=[cr.b])
                S.op("dve", lambda e, nq=nq, cr=cr: e.tensor_scalar(nq.t[:], cr.t[:], -1.0, None, ALU.mult), [cr.b], [nq.b])
                for c in range(NG):
                    accs = [self.ps[3 + qt] for qt in range(4)]
                    njb = 4 * c + 4
                    pend = None
                    for jb in range(njb + 1):
                        if jb < njb:
                            q0 = max(c * 512, jb * 128)
                            n = (c + 1) * 512 - q0
                            sp = self.ps[it % 3]
                            self.mm(sp.t[:, 0:n], sp.b, [(k_.t[:, jb * 128:(jb + 1) * 128], q_.t[:, q0:q0 + n]),
                                                         (self.onesrow.t[0:1, :], nq.t[0:1, q0:q0 + n])], [k_.b, q_.b, nq.b, self.onesrow.b])
                            p = pT[it % 3]
                            S.op("act", lambda e, p=p, sp=sp, n=n, jb=jb, h=h: e.activation(p.t[:, 0:n], sp.t[:, 0:n], AF.Exp, bias=ck.t[:, jb, h:h + 1]),
                                 [sp.b, ck.b], [p.b])
                            if jb * 128 >= c * 512:
                                S.op("dve", lambda e, p=p: e.tensor_tensor(p.t[:, 0:128], p.t[:, 0:128], self.causal01.t[:], ALU.mult),
                                     [p.b, self.causal01.b], [p.b])
                            it += 1
                            cur = (p, jb, q0, n)
                        else:
                            cur = None
                        if pend is not None:
                            p2, jb2, q02, n2 = pend
                            for qt in range(4):
                                tile = 4 * c + qt
                                if tile < jb2:
                                    continue
                                off = tile * 128 - q02
                                acc = accs[qt]
                                S.op("pe", lambda e, acc=acc, p2=p2, off=off, jb2=jb2, tile=tile, v_=v_: e.matmul(
                                    acc.t[:, 0:129], p2.t[:, off:off + 128], v_.t[:, jb2, 0:129], start=(jb2 == 0), stop=(jb2 == tile)),
                                    [p2.b, v_.b], [acc.b], inc=(jb2 == tile))
                        pend = cur
                    for qt in range(4):
                        i = 4 * c + qt
                        acc = accs[qt]
                        r = rd[i % 2]
                        S.op("dve", lambda e, r=r, acc=acc: e.reciprocal(r.t[:], acc.t[:, 128:129]), [acc.b], [r.b])
                        o = on[i % 2]
                        S.op("dve", lambda e, o=o, acc=acc, r=r: e.tensor_scalar(o.t[:], acc.t[:, 0:128], r.t[:, 0:1], None, ALU.mult), [acc.b, r.b], [o.b])
                        self.out_transpose(o, ost, i, self.oT[8 + h])

    def attend(self, acc, tiles, q_ap, q_buf, pT, sbs):
        S = self.S
        n = len(tiles)
        pend = None
        for idx in range(n + 1):
            cur = None
            if idx < n:
                kT_ap, k_buf, v_ap, v_buf, extra, bias = tiles[idx]
                sp = self.ps[self.nsp % 3]
                self.nsp += 1
                pairs = [(kT_ap, q_ap)]
                reads = [k_buf, q_buf]
                if extra is not None:
                    pairs.append((extra[0], extra[1]))
                    reads += list(extra[2])
                self.mm(sp.t[:, 0:128], sp.b, pairs, reads)
                p = pT[self.npt % len(pT)]
                self.npt += 1
                if bias is not None:
                    s_ = sbs[self.npt % len(sbs)]
                    S.op("dve", lambda e, s_=s_, sp=sp, bias=bias: e.tensor_tensor(s_.t[:], sp.t[:, 0:128], bias[0], ALU.add), [sp.b, bias[1]], [s_.b])
                    S.op("act", lambda e, p=p, s_=s_: e.activation(p.t[:], s_.t[:], AF.Exp), [s_.b], [p.b])
                else:
                    S.op("act", lambda e, p=p, sp=sp: e.activation(p.t[:], sp.t[:, 0:128], AF.Exp), [sp.b], [p.b])
                cur = (p, v_ap, v_buf, idx)
            if pend is not None:
                p2, v2, vb2, i2 = pend
                S.op("pe", lambda e, p2=p2, v2=v2, i2=i2: e.matmul(acc.t[:, 0:129], p2.t[:], v2, start=(i2 == 0), stop=(i2 == n - 1)),
                     [p2.b, vb2], [acc.b], inc=(i2 == n - 1))
            pend = cur

    def phase_nsa(self, j):
        S = self.S
        idf = self.identf
        self.nsp = 0
        self.npt = 0
        with ExitStack() as ph:
            gsig = self.sb(ph, "gsig", [128, NT, 24], F32, chan=True)
            S.dma("sp", gsig.t[:], self.gates.rearrange("(n p) c -> p n c", p=128), gsig.c, writes=[gsig.b])
            S.op("act", lambda e: e.activation(gsig.t[:], gsig.t[:], AF.Sigmoid), [gsig.b], [gsig.b])
            kcT = [self.sb(ph, "kcT", [128, 256], BF16) for _ in range(2)]
            vc1 = [self.sb(ph, "vc1", [128, 2, 132], BF16) for _ in range(2)]
            with ExitStack() as cp:
                w1 = self.sb(cp, "w1", [128, 32, 256], BF16, chan=True)
                w2 = self.sb(cp, "w2", [128, 2, 128], BF16, chan=True)
                per = self.sb(cp, "per", [32, 128], F32, chan=True)
                peT = self.sb(cp, "peT", [128, 32], BF16)
                hb = self.sb(cp, "hb", [128, 2], F32)
                xT = self.sb(cp, "xT", [128, T], BF16, chan=True)
                GT = [self.sb(cp, "GT", [128, 256], BF16) for _ in range(2)]
                xs = self.sb(cp, "xs", [128, 256], F32)
                x2 = self.sb(cp, "x2", [128, 256], F32)
                for g in range(2):
                    S.op("dve", lambda e, g=g: e.memset(kcT[g].t[:], 0.0), [], [kcT[g].b])
                    S.op("dve", lambda e, g=g: e.memset(vc1[g].t[:], 0.0), [], [vc1[g].b])
                    S.op("dve", lambda e, g=g: e.memset(vc1[g].t[:, :, 128:129], 1.0), [], [vc1[g].b])
                for hc in range(2):
                    S.op("dve", lambda e, hc=hc: e.memset(GT[hc].t[:], 0.0), [], [GT[hc].b])
                for kv in range(2):
                    S.dma("pool", w1.t[:], self.od_cmp_w1[j, kv].rearrange("(jj d) n -> d jj n", d=128), w1.c, writes=[w1.b])
                    S.dma("pool", w2.t[:], self.od_cmp_w2[j, kv].rearrange("(c p) n -> p c n", p=128), w2.c, writes=[w2.b])
                    S.dma("sp", per.t[:], self.od_cmp_pos[j, kv], per.c, writes=[per.b])
                    pb7 = self.ps[7]
                    S.op("pe", lambda e: e.transpose(pb7.t[:, 0:32], per.t[:], idf.t[0:32, 0:32]), [per.b, idf.b], [pb7.b])
                    S.op("dve", lambda e: e.tensor_copy(peT.t[:], pb7.t[:, 0:32]), [pb7.b], [peT.b])
                    for hc in range(2):
                        pbh = self.ps[6]
                        self.mm(pbh.t[:, hc:hc + 1], pbh.b, [(w1.t[:, jj, hc * 128:(hc + 1) * 128], peT.t[:, jj:jj + 1]) for jj in range(32)], [w1.b, peT.b])
                        S.op("dve", lambda e, hc=hc, pbh=pbh: e.tensor_copy(hb.t[:, hc:hc + 1], pbh.t[:, hc:hc + 1]), [pbh.b], [hb.b])
                    for g in range(2):
                        src = (self.kcmpT if kv == 0 else self.vcmpT)[g]
                        S.dma("sp", xT.t[:], src, xT.c, writes=[xT.b])
                        xv = xT.t[:].rearrange("p (n s) -> p n s", s=16)
                        for hc in range(2):
                            pbx = self.ps[hc]
                            self.mm(pbx.t[:, 0:255], pbx.b, [(w1.t[:, jj, hc * 128:(hc + 1) * 128], xv[:, jj // 16:jj // 16 + 255, jj % 16]) for jj in range(32)], [w1.b, xT.b])
                            S.op("dve", lambda e, hc=hc, pbx=pbx: e.tensor_scalar(xs.t[:, 0:255], pbx.t[:, 0:255], hb.t[:, hc:hc + 1], None, ALU.add), [pbx.b, hb.b], [xs.b])
                            S.op("dve", lambda e: e.tensor_tensor(x2.t[:, 0:255], xs.t[:, 0:255], xs.t[:, 0:255], ALU.mult), [xs.b], [x2.b])
                            S.op("dve", lambda e: e.tensor_scalar(x2.t[:, 0:255], x2.t[:, 0:255], 0.044715, 1.0, ALU.mult, ALU.add), [x2.b], [x2.b])
                            S.op("dve", lambda e: e.tensor_tensor(x2.t[:, 0:255], x2.t[:, 0:255], xs.t[:, 0:255], ALU.mult), [x2.b, xs.b], [x2.b])
                            S.op("act", lambda e: e.activation(x2.t[:, 0:255], x2.t[:, 0:255], AF.Tanh, scale=0.7978845608028654), [x2.b], [x2.b])
                            S.op("dve", lambda e: e.tensor_scalar(x2.t[:, 0:255], x2.t[:, 0:255], 0.5, 0.5, ALU.mult, ALU.add), [x2.b], [x2.b])
                            S.op("dve", lambda e, hc=hc: e.tensor_tensor(GT[hc].t[:, 0:255], x2.t[:, 0:255], xs.t[:, 0:255], ALU.mult), [x2.b, xs.b], [GT[hc].b])
                        if kv == 0:
                            pbk = self.ps[2]
                            self.mm(pbk.t[:, 0:255], pbk.b, [(w2.t[:, hc, :], GT[hc].t[:, 0:255]) for hc in range(2)], [w2.b, GT[0].b, GT[1].b])
                            S.op("act", lambda e, g=g, pbk=pbk: e.activation(kcT[g].t[:, 0:255], pbk.t[:, 0:255], AF.Copy), [pbk.b], [kcT[g].b])
                        else:
                            for ct in range(2):
                                rows = 128 if ct == 0 else 127
                                pbk = self.ps[2 + ct]
                                self.mm(pbk.t[0:rows, 0:128], pbk.b, [(GT[hc].t[:, ct * 128:ct * 128 + rows], w2.t[:, hc, :]) for hc in range(2)], [w2.b, GT[0].b, GT[1].b])
                                S.op("act", lambda e, g=g, ct=ct, rows=rows, pbk=pbk: e.activation(vc1[g].t[0:rows, ct, 0:128], pbk.t[0:rows, 0:128], AF.Copy), [pbk.b], [vc1[g].b])
                S.barrier()
            cmask = self.sb(ph, "cmask", [128, 64, 128], BF16, chan=True)
            wimp = self.sb(ph, "wimp", [128, 2, 64], BF16, chan=True)
            skeep = self.sb(ph, "skeep", [128, NT, 64], F32, chan=True)
            sadd = self.sb(ph, "sadd", [128, NT, 64], F32, chan=True)
            esel = self.sb(ph, "esel", [64, NT, 128], BF16, chan=True)
            S.dma("pool", cmask.t[:], self.c_cmask.rearrange("p (m q) -> p m q", q=128), cmask.c, writes=[cmask.b])
            S.dma("pool", wimp.t[:], self.c_wimp.rearrange("p (c j) -> p c j", j=64), wimp.c, writes=[wimp.b])
            S.dma("sp", skeep.t[:], self.c_selkeep.rearrange("p (n j) -> p n j", j=64), skeep.c, writes=[skeep.b])
            S.dma("sp", sadd.t[:], self.c_seladd.rearrange("p (n j) -> p n j", j=64), sadd.c, writes=[sadd.b])
            S.dma("pool", esel.t[:], self.c_esel.rearrange("p (n k) -> p n k", k=128), esel.c, writes=[esel.b])
            qT = [self.sb(ph, "qT", [128, T], BF16, chan=True) for _ in range(4)]
            ksT = self.sb(ph, "ksT", [128, T], BF16, chan=True)
            kwT = self.sb(ph, "kwT", [128, T], BF16, chan=True)
            vs1 = self.sb(ph, "vs1", [128, NT, 132], BF16, chan=True)
            vw1 = self.sb(ph, "vw1", [128, NT, 132], BF16, chan=True)
            S.op("dve", lambda e: e.memset(vs1.t[:, :, 128:129], 1.0), [], [vs1.b])
            S.op("dve", lambda e: e.memset(vw1.t[:, :, 128:129], 1.0), [], [vw1.b])
            pT = [self.sb(ph, "pT", [128, 128], BF16) for _ in range(6)]
            sbs = [self.sb(ph, "sbs", [128, 128], F32) for _ in range(3)]
            imp = self.sb(ph, "imp", [128, 64], F32)
            sc = self.sb(ph, "sc", [128, 64], F32)
            m8 = self.sb(ph, "m8", [128, 8], F32)
            nsel = self.sb(ph, "nsel", [128, 64], BF16)
            nsT = self.sb(ph, "nsT", [64, 128], BF16)
            oacc = [self.sb(ph, "oacc", [128, 128], F32) for _ in range(4)]
            onb = [self.sb(ph, "onb", [128, 128], BF16) for _ in range(2)]
            rd = [self.sb(ph, "rd", [128, 1], F32) for _ in range(4)]
            ostg = [[self.sb(ph, "ostg", [128, 512], BF16, chan=True) for _ in range(2)] for _ in range(4)]
            for g in range(2):
                for r in range(4):
                    S.dma("sp", qT[r].t[:], self.qcT[4 * g + r], qT[r].c, writes=[qT[r].b])
                S.dma("sp", ksT.t[:], self.kselT[g], ksT.c, writes=[ksT.b])
                S.dma("sp", kwT.t[:], self.kwinT[g], kwT.c, writes=[kwT.b])
                S.dma("sp", vs1.t[:, :, 0:128], self.vsel.rearrange("(n p) c -> p n c", p=128)[:, :, g * 128:(g + 1) * 128], vs1.c, writes=[vs1.b])
                S.dma("sp", vw1.t[:, :, 0:128], self.vwin.rearrange("(n p) c -> p n c", p=128)[:, :, g * 128:(g + 1) * 128], vw1.c, writes=[vw1.b])
                for i in range(NT):
                    qs = slice(i * 128, (i + 1) * 128)
                    cts = [0] if i < 16 else [0, 1]
                    for r in range(4):
                        h = 4 * g + r
                        accc, blk = self.ps[3], self.ps[4]
                        pcs = []
                        for ct in cts:
                            sp = self.ps[self.nsp % 3]
                            self.nsp += 1
                            self.mm(sp.t[:, 0:128], sp.b, [(kcT[g].t[:, ct * 128:(ct + 1) * 128], qT[r].t[:, qs])], [kcT[g].b, qT[r].b])
                            p = pT[self.npt % len(pT)]
                            self.npt += 1
                            S.op("act", lambda e, p=p, sp=sp: e.activation(p.t[:], sp.t[:, 0:128], AF.Exp), [sp.b], [p.b])
                            S.op("dve", lambda e, p=p, i=i, ct=ct: e.tensor_tensor(p.t[:], p.t[:], cmask.t[:, i * 2 + ct, :], ALU.mult), [p.b, cmask.b], [p.b])
                            pcs.append((p, ct))
                        self.mm(accc.t[:, 0:129], accc.b, [(p.t[:], vc1[g].t[:, ct, 0:129]) for p, ct in pcs], [vc1[g].b] + [p.b for p, _ in pcs])
                        self.mm(blk.t[:, 0:64], blk.b, [(p.t[:], wimp.t[:, ct, :]) for p, ct in pcs], [wimp.b] + [p.b for p, _ in pcs])
                        r_ = rd[r]
                        S.op("dve", lambda e, r_=r_, accc=accc: e.tensor_scalar(r_.t[:], accc.t[:, 128:129], 1e-30, None, ALU.max), [accc.b], [r_.b])
                        S.op("dve", lambda e, r_=r_: e.reciprocal(r_.t[:], r_.t[:]), [r_.b], [r_.b])
                        if r == 0:
                            S.op("dve", lambda e, r_=r_, blk=blk: e.tensor_scalar(imp.t[:], blk.t[:, 0:64], r_.t[:, 0:1], None, ALU.mult), [blk.b, r_.b], [imp.b])
                        else:
                            S.op("dve", lambda e, r_=r_, blk=blk: e.scalar_tensor_tensor(out=imp.t[:], in0=blk.t[:, 0:64], scalar=r_.t[:, 0:1], in1=imp.t[:], op0=ALU.mult, op1=ALU.add),
                                 [blk.b, r_.b, imp.b], [imp.b])
                        S.op("dve", lambda e, r_=r_, i=i, h=h: e.tensor_tensor(r_.t[:], r_.t[:], gsig.t[:, i, h:h + 1], ALU.mult), [r_.b, gsig.b], [r_.b])
                        S.op("dve", lambda e, r=r, r_=r_, accc=accc: e.tensor_scalar(oacc[r].t[:], accc.t[:, 0:128], r_.t[:, 0:1], None, ALU.mult), [accc.b, r_.b], [oacc[r].b])
                    S.op("dve", lambda e, i=i: e.tensor_tensor(sc.t[:], imp.t[:], skeep.t[:, i, :], ALU.mult), [imp.b, skeep.b], [sc.b])
                    S.op("dve", lambda e, i=i: e.tensor_tensor(sc.t[:], sc.t[:], sadd.t[:, i, :], ALU.add), [sc.b, sadd.b], [sc.b])
                    S.op("dve", lambda e: e.max(out=m8.t[:], in_=sc.t[:]), [sc.b], [m8.b])
                    S.op("dve", lambda e: e.tensor_scalar(sc.t[:], sc.t[:], m8.t[:, 7:8], None, ALU.is_ge), [sc.b, m8.b], [sc.b])
                    S.op("dve", lambda e: e.tensor_scalar(nsel.t[:], sc.t[:], -NEG, NEG, ALU.mult, ALU.add), [sc.b], [nsel.b])
                    pb7 = self.ps[7]
                    pTr = pb7.t[:, :].bitcast(BF16)
                    S.op("pe", lambda e, pTr=pTr: e.transpose(pTr[0:64, 0:128], nsel.t[:], self.identb.t[:]), [nsel.b, self.identb.b], [pb7.b])
                    S.op("act", lambda e, pTr=pTr: e.activation(nsT.t[:], pTr[0:64, 0:128], AF.Copy), [pb7.b], [nsT.b])
                    for r in range(4):
                        h = 4 * g + r
                        tiles = []
                        for jb in range(i + 1):
                            bias = None
                            if jb == i:
                                bias = (self.nb0.t[:, h, :], self.nb0.b)
                            elif jb == i - 1:
                                bias = (self.nb1.t[:, h, :], self.nb1.b)
                            tiles.append((ksT.t[:, jb * 128:(jb + 1) * 128], ksT.b, vs1.t[:, jb, 0:129], vs1.b,
                                          (esel.t[:, jb, :], nsT.t[:], (esel.b, nsT.b)), bias))
                        acc = self.ps[5]
                        self.attend(acc, tiles, qT[r].t[:, qs], qT[r].b, pT, sbs)
                        self.nsa_combine(acc, rd[r], gsig, i, 8 + h, oacc[r])
                        tiles = []
                        for jb in range(max(0, i - 4), i + 1):
                            bias = None
                            if jb == i:
                                bias = (self.nb0.t[:, h, :], self.nb0.b)
                            elif jb == i - 1:
                                bias = (self.nb1.t[:, h, :], self.nb1.b)
                            elif jb == i - 4:
                                bias = (self.acneg.t[:], self.acneg.b)
                            tiles.append((kwT.t[:, jb * 128:(jb + 1) * 128], kwT.b, vw1.t[:, jb, 0:129], vw1.b, None, bias))
                        acc = self.ps[6]
                        self.attend(acc, tiles, qT[r].t[:, qs], qT[r].b, pT, sbs)
                        self.nsa_combine(acc, rd[r], gsig, i, 16 + h, oacc[r])
                        o = onb[r % 2]
                        S.op("act", lambda e, o=o, r=r: e.activation(o.t[:], oacc[r].t[:], AF.Copy), [oacc[r].b], [o.b])
                        self.out_transpose(o, ostg[r], i, self.oT[h])

    def nsa_combine(self, acc, r_, gsig, i, gcol, oacc):
        S = self.S
        S.op("dve", lambda e: e.reciprocal(r_.t[:], acc.t[:, 128:129]), [acc.b], [r_.b])
        S.op("dve", lambda e: e.tensor_tensor(r_.t[:], r_.t[:], gsig.t[:, i, gcol:gcol + 1], ALU.mult), [r_.b, gsig.b], [r_.b])
        S.op("dve", lambda e: e.scalar_tensor_tensor(out=oacc.t[:], in0=acc.t[:, 0:128], scalar=r_.t[:, 0:1], in1=oacc.t[:], op0=ALU.mult, op1=ALU.add),
             [acc.b, r_.b, oacc.b], [oacc.b])

    def phase_gdn(self, j):
        S = self.S
        B = 4
        C = 64
        NCH = T // C
        idf = self.identf
        with ExitStack() as ph:
            psq = []
            for q in range(4):
                for b in range(6):
                    psq.append(Tl(self.ps[b].t[:, q * 128:(q + 1) * 128], self.ps[b].b))
            nq = [0]

            def slot():
                nq[0] += 1
                return psq[nq[0] % len(psq)]
            cwg = self.sb(ph, "cwg", [128, 4, 24], F32)
            gamc = self.sb(ph, "gamc", [C, NCH, 8], F32)
            betac = self.sb(ph, "betac", [C, NCH, 8], F32)
            egamc = self.sb(ph, "egamc", [C, NCH, 8], F32)
            nbetac = self.sb(ph, "nbetac", [C, NCH, 8], F32)
            begamc = self.sb(ph, "begamc", [C, NCH, 8], F32)
            dtb = self.sb(ph, "dtb", [8, 1], F32, chan=True)
            nega = self.sb(ph, "nega", [8, 1], F32, chan=True)
            prep = ExitStack()
            cwr = self.sb(prep, "cwr", [24, 4, 128], F32, chan=True)
            S.dma_group("sp", [(cwr.t[:, jj, :], self.od_conv_w[j, jj, :].rearrange("(c p) -> c p", p=128)) for jj in range(4)], cwr.c, writes=[cwr.b])
            pb7 = self.ps[7]
            for jj in range(4):
                S.op("pe", lambda e, jj=jj: e.transpose(pb7.t[:, jj * 24:(jj + 1) * 24], cwr.t[:, jj, :], idf.t[0:24, 0:24]), [cwr.b, idf.b], [pb7.b], inc=(jj == 3))
            S.op("dve", lambda e: e.tensor_copy(cwg.t[:], pb7.t[:, 0:96].rearrange("p (j c) -> p j c", c=24)), [pb7.b], [cwg.b])
            bb = self.sb(prep, "bb", [8, T], F32, chan=True)
            ba = self.sb(prep, "ba", [8, T], F32, chan=True)
            gam = self.sb(prep, "gam", [8, T], F32)
            rmask = self.sb(prep, "rmask", [8, T], F32, chan=True)
            S.dma("sp", bb.t[:], self.baT[0:8, :], bb.c, writes=[bb.b])
            S.dma("sp", ba.t[:], self.baT[8:16, :], ba.c, writes=[ba.b])
            S.dma("sp", rmask.t[:], self.c_rmask[:, :], rmask.c, writes=[rmask.b])
            S.dma("sp", dtb.t[:], self.od_dt_bias[j, :].rearrange("(h o) -> h o", o=1), dtb.c, writes=[dtb.b])
            S.dma("sp", nega.t[:], self.od_a_log[j, :].rearrange("(h o) -> h o", o=1), nega.c, writes=[nega.b])
            S.op("act", lambda e: e.activation(nega.t[:], nega.t[:], AF.Exp), [nega.b], [nega.b])
            S.op("dve", lambda e: e.tensor_scalar(nega.t[:], nega.t[:], -1.0, None, ALU.mult), [nega.b], [nega.b])
            S.op("act", lambda e: e.activation(ba.t[:], ba.t[:], AF.Exp, bias=dtb.t[:, 0:1]), [ba.b, dtb.b], [ba.b])
            S.op("act", lambda e: e.activation(ba.t[:], ba.t[:], AF.Ln, bias=1.0), [ba.b], [ba.b])
            S.op("dve", lambda e: e.tensor_scalar(ba.t[:], ba.t[:], nega.t[:, 0:1], None, ALU.mult), [ba.b, nega.b], [ba.b])
            S.op("dve", lambda e: e.tensor_tensor_scan(out=gam.t[:], data0=rmask.t[:], data1=ba.t[:], initial=0.0, op0=ALU.mult, op1=ALU.add),
                 [rmask.b, ba.b], [gam.b])
            S.op("act", lambda e: e.activation(bb.t[:], bb.t[:], AF.Sigmoid), [bb.b], [bb.b])
            for src, dst, pbk in ((gam, gamc, self.ps[6]), (bb, betac, self.ps[7])):
                for c in range(NCH):
                    S.op("pe", lambda e, c=c, src=src, pbk=pbk: e.transpose(pbk.t[0:C, c * 8:(c + 1) * 8], src.t[0:8, c * C:(c + 1) * C], idf.t[0:8, 0:8]),
                         [src.b, idf.b], [pbk.b], inc=(c == NCH - 1))
                S.op("dve", lambda e, dst=dst, pbk=pbk: e.tensor_copy(dst.t[:], pbk.t[0:C, :].rearrange("p (c h) -> p c h", h=8)), [pbk.b], [dst.b])
            S.barrier()
            prep.close()
            S.op("act", lambda e: e.activation(egamc.t[:], gamc.t[:], AF.Exp), [gamc.b], [egamc.b])
            S.op("dve", lambda e: e.tensor_scalar(nbetac.t[:], betac.t[:], -1.0, None, ALU.mult), [betac.b], [nbetac.b])
            S.op("dve", lambda e: e.tensor_tensor(begamc.t[:], betac.t[:], egamc.t[:], ALU.mult), [betac.b, egamc.b], [begamc.b])
            gnr = self.sb(ph, "gnr", [C, 128], F32, chan=True)
            S.dma("sp", gnr.t[:], self.od_gdn_norm[j, :].partition_broadcast(C), gnr.c, writes=[gnr.b])
            ones64 = self.sb(ph, "ones64", [C, 128], F32)
            onescol = self.sb(ph, "onescol", [128, 1], F32)
            S.op("dve", lambda e: e.memset(ones64.t[:], 1.0), [], [ones64.b])
            S.op("dve", lambda e: e.memset(onescol.t[:], 1.0), [], [onescol.b])
            pmask = self.sb(ph, "pmask", [C, C], F32, chan=True)
            nmask = self.sb(ph, "nmask", [C, C], F32, chan=True)
            S.dma("sp", pmask.t[:], self.c_gdn_pmask[:, :], pmask.c, writes=[pmask.b])
            S.dma("sp", nmask.t[:], self.c_gdn_nmask[:, :], nmask.c, writes=[nmask.b])
            if self.gdn_stage <= 0:
                return
            raw = self.sb(ph, "raw", [128, T + 3], F32, chan=True)
            S.op("dve", lambda e: e.memset(raw.t[:, 0:3], 0.0), [], [raw.b])
            qkv = [self.sb(ph, "qkv", [128, T], F32) for _ in range(3)]
            sqs = self.sb(ph, "sqs", [128, T], F32)
            rnc = self.sb(ph, "rnc", [C, NCH, 2], F32)
            zt = [self.sb(ph, "zt", [C, B, 128], F32, chan=True) for _ in range(2)]
            St = self.sb(ph, "St", [128, 128], F32)
            ost = [self.sb(ph, "ost", [128, 512], BF16, chan=True) for _ in range(2)]

            def mk(name, shape, n, dt=F32):
                return [self.sb(ph, name, shape, dt) for _ in range(n)]
            kn = mk("kn", [C, 128], B); qn = mk("qn", [C, 128], B); vt = mk("vt", [C, 128], B)
            knT = mk("knT", [128, C], B); qnT = mk("qnT", [128, C], 2 * B)
            dg = mk("dg", [C, C], B); t1 = mk("t1", [C, C], B); Dm = mk("Dm", [C, C], B); DT = mk("DT", [C, C], B)
            X = mk("X", [C, C], 2 * B); XT = mk("XT", [C, C], 2 * B); Y = mk("Y", [C, C], 2 * B)
            Vb = mk("Vb", [C, 128], B); Kb = mk("Kb", [C, 128], B)
            U = mk("U", [C, 128], 2 * B); WmT = mk("WmT", [128, C], 2 * B); MT = mk("MT", [C, C], 2 * B); Kd = mk("Kd", [C, 128], 2 * B)
            kdc = mk("kdc", [C, 1], 2 * B); egl = mk("egl", [128, 1], 2 * B)
            vnew = mk("vnew", [C, 128], 2); mvs = mk("mvs", [C, 128], 2); osb = mk("osb", [C, 128], 2)
            oss = mk("oss", [C, 1], 2); ors = mk("ors", [C, 1], 2); ojunk = mk("ojunk", [C, 128], 1)
            zs = mk("zs", [C, 128], 2); ofb = mk("ofb", [C, 128], 2, BF16)
            for h in range(self.gdn_heads):
                for ti, src in enumerate((self.qdT, self.kdT, self.vdT)):
                    S.dma("sp", raw.t[:, 3:T + 3], src[h], raw.c, writes=[raw.b])
                    dst = qkv[ti]
                    ci = ti * 8 + h
                    S.op("dve", lambda e, dst=dst, ci=ci: e.tensor_scalar(dst.t[:], raw.t[:, 0:T], cwg.t[:, 0, ci:ci + 1], None, ALU.mult), [raw.b, cwg.b], [dst.b])
                    for jj in range(1, 4):
                        S.op("dve", lambda e, dst=dst, ci=ci, jj=jj: e.scalar_tensor_tensor(out=dst.t[:], in0=raw.t[:, jj:T + jj], scalar=cwg.t[:, jj, ci:ci + 1], in1=dst.t[:],
                                                                                           op0=ALU.mult, op1=ALU.add), [raw.b, cwg.b, dst.b], [dst.b])
                    S.op("act", lambda e, dst=dst: e.activation(dst.t[:], dst.t[:], AF.Silu), [dst.b], [dst.b])
                if self.gdn_stage <= 1:
                    continue
                pss = self.ps[6]
                for ti in range(2):
                    S.op("act", lambda e, ti=ti: e.activation(sqs.t[:], qkv[ti].t[:], AF.Square), [qkv[ti].b], [sqs.b])
                    for c in range(NCH):
                        S.op("pe", lambda e, c=c, ti=ti: e.matmul(pss.t[0:C, c * 2 + ti:c * 2 + ti + 1], sqs.t[:, c * C:(c + 1) * C], onescol.t[:, 0:1], start=True, stop=True),
                             [sqs.b, onescol.b], [pss.b], inc=(c == NCH - 1))
                S.op("dve", lambda e: e.tensor_scalar(rnc.t[:], pss.t[0:C, 0:2 * NCH].rearrange("p (c t) -> p c t", t=2), EPS, None, ALU.add), [pss.b], [rnc.b])
                S.op("act", lambda e: e.activation(rnc.t[:], rnc.t[:], AF.Sqrt), [rnc.b], [rnc.b])
                S.op("dve", lambda e: e.reciprocal(rnc.t[:], rnc.t[:]), [rnc.b], [rnc.b])
                S.op("dve", lambda e: e.tensor_scalar(rnc.t[:, :, 0:1], rnc.t[:, :, 0:1], SCALE, None, ALU.mult), [rnc.b], [rnc.b])
                if self.gdn_stage <= 2:
                    continue
                S.op("dve", lambda e: e.memset(St.t[:], 0.0), [], [St.b])
                for bt in range(NCH // B):
                    if os.environ.get("KGDNBAR", "") == "1":
                        S.barrier()
                    par = bt % 2
                    z_ = zt[par]
                    t0 = bt * B * C
                    S.dma("sp", z_.t[:], self.zg[t0:t0 + B * C, h * 128:(h + 1) * 128].rearrange("(b p) e -> p b e", p=C), z_.c, writes=[z_.b])
                    cs = [bt * B + bi for bi in range(B)]
                    o2 = [par * B + bi for bi in range(B)]
                    for bi, c in enumerate(cs):
                        sl = slice(c * C, (c + 1) * C)
                        pq_, pk_, pv_ = slot(), slot(), slot()
                        for p_, src in ((pq_, qkv[0]), (pk_, qkv[1]), (pv_, qkv[2])):
                            S.op("pe", lambda e, p_=p_, src=src, sl=sl: e.transpose(p_.t[0:C, :], src.t[:, sl], idf.t[:]), [src.b, idf.b], [p_.b])
                        S.op("dve", lambda e, bi=bi, c=c, pq_=pq_: e.tensor_scalar(qn[bi].t[:], pq_.t[0:C, :], rnc.t[:, c, 0:1], None, ALU.mult), [pq_.b, rnc.b], [qn[bi].b])
                        S.op("dve", lambda e, bi=bi, c=c, pk_=pk_: e.tensor_scalar(kn[bi].t[:], pk_.t[0:C, :], rnc.t[:, c, 1:2], None, ALU.mult), [pk_.b, rnc.b], [kn[bi].b])
                        S.op("act", lambda e, bi=bi, pv_=pv_: e.activation(vt[bi].t[:], pv_.t[0:C, :], AF.Copy), [pv_.b], [vt[bi].b])
                    if self.gdn_stage <= 3:
                        continue
                    for bi, c in enumerate(cs):
                        o = o2[bi]
                        p1, p2, p3 = slot(), slot(), slot()
                        S.op("pe", lambda e, bi=bi, p1=p1: e.transpose(p1.t[:, 0:C], kn[bi].t[:], idf.t[0:C, 0:C]), [kn[bi].b, idf.b], [p1.b])
                        S.op("pe", lambda e, bi=bi, p2=p2: e.transpose(p2.t[:, 0:C], qn[bi].t[:], idf.t[0:C, 0:C]), [qn[bi].b, idf.b], [p2.b])
                        S.op("act", lambda e, bi=bi, p1=p1: e.activation(knT[bi].t[:], p1.t[:, 0:C], AF.Copy), [p1.b], [knT[bi].b])
                        S.op("dve", lambda e, o=o2[bi], p2=p2: e.tensor_copy(qnT[o].t[:], p2.t[:, 0:C]), [p2.b], [qnT[o].b])
                        S.op("dve", lambda e, h=h, bi=bi, c=c: e.tensor_scalar(dg[bi].t[:], idf.t[0:C, 0:C], gamc.t[:, c, h:h + 1], None, ALU.mult), [idf.b, gamc.b], [dg[bi].b])
                        S.op("pe", lambda e, bi=bi, p3=p3: e.matmul(p3.t[:, 0:C], ones64.t[:, :], dg[bi].t[:], start=True, stop=True), [ones64.b, dg[bi].b], [p3.b])
                        S.op("dve", lambda e, h=h, bi=bi, c=c, p3=p3: e.scalar_tensor_tensor(out=t1[bi].t[:], in0=p3.t[0:C, 0:C], scalar=gamc.t[:, c, h:h + 1], in1=pmask.t[:],
                                                                                       op0=ALU.subtract, op1=ALU.add), [p3.b, gamc.b, pmask.b], [t1[bi].b])
                        S.op("act", lambda e, bi=bi: e.activation(Dm[bi].t[:], t1[bi].t[:], AF.Exp, scale=-1.0), [t1[bi].b], [Dm[bi].b])
                        S.op("dve", lambda e, h=h, bi=bi, c=c, p3=p3: e.scalar_tensor_tensor(out=t1[bi].t[:], in0=p3.t[0:C, 0:C], scalar=gamc.t[:, c, h:h + 1], in1=nmask.t[:],
                                                                                       op0=ALU.subtract, op1=ALU.add), [p3.b, gamc.b, nmask.b, Dm[bi].b], [t1[bi].b])
                        S.op("act", lambda e, bi=bi: e.activation(DT[bi].t[:], t1[bi].t[:], AF.Exp), [t1[bi].b], [DT[bi].b])
                        S.op("act", lambda e, o=o, p3=p3: e.activation(egl[o].t[:], p3.t[:, C - 1:C], AF.Exp), [p3.b], [egl[o].b])
                        S.op("dve", lambda e, h=h, o=o2[bi], c=c, p3=p3: e.tensor_scalar(kdc[o].t[:], p3.t[0:C, C - 1:C], gamc.t[:, c, h:h + 1], None, ALU.subtract), [p3.b, gamc.b], [kdc[o].b])
                        S.op("act", lambda e, o=o2[bi]: e.activation(kdc[o].t[:], kdc[o].t[:], AF.Exp), [kdc[o].b], [kdc[o].b])
                    if self.gdn_stage <= 4:
                        continue
                    for bi, c in enumerate(cs):
                        o = o2[bi]
                        pg_, pm_ = slot(), slot()
                        S.op("pe", lambda e, bi=bi, pg_=pg_: e.matmul(pg_.t[0:C, 0:C], knT[bi].t[:], knT[bi].t[:], start=True, stop=True), [knT[bi].b], [pg_.b])
                        S.op("dve", lambda e, h=h, bi=bi, c=c, o=o, pg_=pg_: e.scalar_tensor_tensor(out=X[o].t[:], in0=pg_.t[0:C, 0:C], scalar=nbetac.t[:, c, h:h + 1], in1=Dm[bi].t[:],
                                                                                              op0=ALU.mult, op1=ALU.mult), [pg_.b, nbetac.b, Dm[bi].b], [X[o].b])
                        S.op("pe", lambda e, bi=bi, o=o, pm_=pm_: e.matmul(pm_.t[0:C, 0:C], knT[bi].t[:], qnT[o].t[:], start=True, stop=True), [knT[bi].b, qnT[o].b], [pm_.b])
                        S.op("dve", lambda e, bi=bi, o=o, pm_=pm_: e.tensor_tensor(MT[o].t[:], pm_.t[0:C, 0:C], DT[bi].t[:], ALU.mult), [pm_.b, DT[bi].b], [MT[o].b])
                    for bi, c in enumerate(cs):
                        o = o2[bi]
                        px = slot()
                        S.op("pe", lambda e, o=o, px=px: e.transpose(px.t[0:C, 0:C], X[o].t[:], idf.t[0:C, 0:C]), [X[o].b, idf.b], [px.b])
                        S.op("act", lambda e, o=o, px=px: e.activation(XT[o].t[:], px.t[0:C, 0:C], AF.Copy), [px.b], [XT[o].b])
                        S.op("dve", lambda e, o=o, px=px: e.tensor_tensor(Y[o].t[:], px.t[0:C, 0:C], idf.t[0:C, 0:C], ALU.add), [px.b, idf.b], [Y[o].b])
                    if self.gdn_stage <= 5:
                        continue
                    for s_ in range(5):
                        for bi, c in enumerate(cs):
                            o = o2[bi]
                            pa, pbq = slot(), slot()
                            S.op("pe", lambda e, o=o, pa=pa: e.matmul(pa.t[0:C, 0:C], XT[o].t[:], X[o].t[:], start=True, stop=True), [XT[o].b, X[o].b], [pa.b])
                            if s_ < 4:
                                S.op("pe", lambda e, o=o, pbq=pbq: e.matmul(pbq.t[0:C, 0:C], X[o].t[:], XT[o].t[:], start=True, stop=True), [XT[o].b, X[o].b], [pbq.b])
                            S.op("act", lambda e, o=o, pa=pa: e.activation(X[o].t[:], pa.t[0:C, 0:C], AF.Copy), [pa.b], [X[o].b])
                            if s_ < 4:
                                S.op("dve", lambda e, o=o, pbq=pbq: e.tensor_copy(XT[o].t[:], pbq.t[0:C, 0:C]), [pbq.b], [XT[o].b])
                        for bi, c in enumerate(cs):
                            o = o2[bi]
                            py = slot()
                            S.op("pe", lambda e, o=o, py=py: e.matmul(py.t[0:C, 0:C], X[o].t[:], Y[o].t[:], start=True, stop=True), [X[o].b, Y[o].b], [py.b])
                            S.op("dve", lambda e, o=o, py=py: e.tensor_tensor(Y[o].t[:], Y[o].t[:], py.t[0:C, 0:C], ALU.add), [py.b, Y[o].b], [Y[o].b])
                    if self.gdn_stage <= 6:
                        continue
                    for bi, c in enumerate(cs):
                        o = o2[bi]
                        S.op("dve", lambda e, h=h, bi=bi, c=c: e.tensor_scalar(Vb[bi].t[:], vt[bi].t[:], betac.t[:, c, h:h + 1], None, ALU.mult), [vt[bi].b, betac.b], [Vb[bi].b])
                        S.op("dve", lambda e, h=h, bi=bi, c=c: e.tensor_scalar(Kb[bi].t[:], kn[bi].t[:], begamc.t[:, c, h:h + 1], None, ALU.mult), [kn[bi].b, begamc.b], [Kb[bi].b])
                        S.op("dve", lambda e, bi=bi, o=o: e.tensor_scalar(Kd[o].t[:], kn[bi].t[:], kdc[o].t[:, 0:1], None, ALU.mult), [kn[bi].b, kdc[o].b], [Kd[o].b])
                        if self.gdn_stage <= 6.3:
                            continue
                        pu_, pw_ = slot(), slot()
                        S.op("pe", lambda e, bi=bi, o=o, pu_=pu_: e.matmul(pu_.t[0:C, :], Y[o].t[:], Vb[bi].t[:], start=True, stop=True), [Y[o].b, Vb[bi].b], [pu_.b])
                        S.op("act", lambda e, o=o, pu_=pu_: e.activation(U[o].t[:], pu_.t[0:C, :], AF.Copy), [pu_.b], [U[o].b])
                        if self.gdn_stage <= 6.5:
                            continue
                        S.op("pe", lambda e, bi=bi, o=o, pw_=pw_: e.matmul(pw_.t[:, 0:C], Kb[bi].t[:], Y[o].t[:], start=True, stop=True), [Y[o].b, Kb[bi].b], [pw_.b])
                        if self.gdn_stage <= 6.7:
                            continue
                        S.op("act", lambda e, o=o, pw_=pw_: e.activation(WmT[o].t[:], pw_.t[:, 0:C], AF.Copy), [pw_.b], [WmT[o].b])
                    if self.gdn_stage <= 7:
                        continue
                    for bi, c in enumerate(cs):
                        o = o2[bi]
                        k2 = c % 2
                        pws, pqs, pmv, psu = slot(), slot(), slot(), slot()
                        S.op("pe", lambda e, o=o, pws=pws: e.matmul(pws.t[0:C, :], WmT[o].t[:], St.t[:], start=True, stop=True), [WmT[o].b, St.b], [pws.b])
                        S.op("pe", lambda e, o=o, pqs=pqs: e.matmul(pqs.t[0:C, :], qnT[o].t[:], St.t[:], start=True, stop=True), [qnT[o].b, St.b], [pqs.b])
                        S.op("dve", lambda e, o=o, k2=k2, pws=pws: e.tensor_tensor(vnew[k2].t[:], U[o].t[:], pws.t[0:C, :], ALU.subtract), [U[o].b, pws.b], [vnew[k2].b])
                        S.op("pe", lambda e, o=o, k2=k2, pmv=pmv: e.matmul(pmv.t[0:C, :], MT[o].t[:], vnew[k2].t[:], start=True, stop=True), [MT[o].b, vnew[k2].b], [pmv.b])
                        S.op("pe", lambda e, o=o, k2=k2, psu=psu: e.matmul(psu.t[:, :], Kd[o].t[:], vnew[k2].t[:], start=True, stop=True), [Kd[o].b, vnew[k2].b], [psu.b])
                        S.op("dve", lambda e, o=o, psu=psu: e.scalar_tensor_tensor(out=St.t[:], in0=St.t[:], scalar=egl[o].t[:, 0:1], in1=psu.t[:, :], op0=ALU.mult, op1=ALU.add),
                             [St.b, egl[o].b, psu.b], [St.b])
                        S.op("act", lambda e, k2=k2, pmv=pmv: e.activation(mvs[k2].t[:], pmv.t[0:C, :], AF.Copy), [pmv.b], [mvs[k2].b])
                        S.op("dve", lambda e, h=h, k2=k2, c=c, pqs=pqs: e.scalar_tensor_tensor(out=osb[k2].t[:], in0=pqs.t[0:C, :], scalar=egamc.t[:, c, h:h + 1], in1=mvs[k2].t[:],
                                                                                        op0=ALU.mult, op1=ALU.add), [pqs.b, egamc.b, mvs[k2].b], [osb[k2].b])
                        S.op("act", lambda e, k2=k2: e.activation(ojunk[0].t[:], osb[k2].t[:], AF.Square, accum_out=oss[k2].t[:]), [osb[k2].b], [ojunk[0].b, oss[k2].b])
                        S.op("dve", lambda e, k2=k2: e.tensor_scalar(ors[k2].t[:], oss[k2].t[:], 1.0 / 128, EPS, ALU.mult, ALU.add), [oss[k2].b], [ors[k2].b])
                        S.op("act", lambda e, k2=k2: e.activation(ors[k2].t[:], ors[k2].t[:], AF.Sqrt), [ors[k2].b], [ors[k2].b])
                        S.op("dve", lambda e, k2=k2: e.reciprocal(ors[k2].t[:], ors[k2].t[:]), [ors[k2].b], [ors[k2].b])
                        S.op("dve", lambda e, k2=k2: e.scalar_tensor_tensor(out=osb[k2].t[:], in0=osb[k2].t[:], scalar=ors[k2].t[:, 0:1], in1=gnr.t[:], op0=ALU.mult, op1=ALU.mult),
                             [osb[k2].b, ors[k2].b, gnr.b], [osb[k2].b])
                        S.op("act", lambda e, k2=k2, z_=z_, bi=bi: e.activation(zs[k2].t[:], z_.t[:, bi, :], AF.Silu), [z_.b], [zs[k2].b])
                        dbgm = os.environ.get("KGDNDBG", "")
                        dsel = {"vn": vnew[k2], "u": U[o], "vb": Vb[bi], "kb": Kb[bi], "kd": Kd[o], "vt": vt[bi], "kn": kn[bi], "qn": qn[bi], "mv": mvs[k2]}.get(dbgm)
                        d64 = {"y": Y[o], "x": X[o], "mt": MT[o], "dm": Dm[bi], "dt": DT[bi]}.get(dbgm)
                        if d64 is not None:
                            S.op("dve", lambda e, k2=k2, d64=d64: e.tensor_copy(ofb[k2].t[:, 0:64], d64.t[:]), [osb[k2].b, zs[k2].b, d64.b], [ofb[k2].b])
                            S.op("dve", lambda e, k2=k2, d64=d64: e.tensor_copy(ofb[k2].t[:, 64:128], d64.t[:]), [osb[k2].b, zs[k2].b, d64.b], [ofb[k2].b])
                        elif dsel is not None:
                            S.op("dve", lambda e, k2=k2, dsel=dsel: e.tensor_copy(ofb[k2].t[:], dsel.t[:]), [osb[k2].b, zs[k2].b, dsel.b], [ofb[k2].b])
                        elif os.environ.get("KGDNDBG", "") == "z":
                            S.op("dve", lambda e, k2=k2: e.tensor_copy(ofb[k2].t[:], zs[k2].t[:]), [osb[k2].b, zs[k2].b], [ofb[k2].b])
                        elif os.environ.get("KGDNDBG", "") == "o":
                            S.op("dve", lambda e, k2=k2: e.tensor_copy(ofb[k2].t[:], osb[k2].t[:]), [osb[k2].b, zs[k2].b], [ofb[k2].b])
                        else:
                            S.op("dve", lambda e, k2=k2: e.tensor_tensor(ofb[k2].t[:], osb[k2].t[:], zs[k2].t[:], ALU.mult), [osb[k2].b, zs[k2].b], [ofb[k2].b])
                        pbt = self.ps[7]
                        pTr = pbt.t[:, :].bitcast(BF16)
                        stg = ost[(c // 8) % 2]
                        S.op("pe", lambda e, k2=k2, pTr=pTr: e.transpose(pTr[:, 0:C], ofb[k2].t[:], self.identb.t[0:C, 0:C]), [ofb[k2].b, self.identb.b], [pbt.b])
                        S.op("act", lambda e, stg=stg, pTr=pTr, c=c: e.activation(stg.t[:, (c % 8) * C:(c % 8 + 1) * C], pTr[:, 0:C], AF.Copy), [pbt.b], [stg.b])
                        if c % 8 == 7:
                            cc = c // 8
                            S.dma("sp", self.oT[8 + h][:, cc * 512:(cc + 1) * 512], stg.t[:], stg.c, reads=[stg.b])

    def phase_outproj_ffn(self, layer, wout, xsrc):
        S = self.S
        Wo = wout.rearrange("(k p) n -> p k n", p=128)
        Wu = self.ffn_w_up[layer].rearrange("(k p) n -> p k n", p=128)
        Wd = self.ffn_w_down[layer].rearrange("(c p) n -> p c n", p=128)
        oTv = self.oT.rearrange("f p t -> p f t")
        xrb = [Buf("xr%d" % i) for i in range(NT)]
        with ExitStack() as ph:
            gain = self.sb(ph, "gain", [128, D], F32, chan=True)
            S.dma("sp", gain.t[:], self.norm_ffn[layer, :].partition_broadcast(128), gain.c, writes=[gain.b])
            cwr = self.sb(ph, "cwr", [FC, 4, 128], F32, chan=True)
            cw = self.sb(ph, "cw", [128, 4, FC], F32)
            S.dma_group("sp", [(cwr.t[:, jj, :], self.ffn_conv_w[layer, jj, :].rearrange("(c p) -> c p", p=128)) for jj in range(3)]
                        + [(cwr.t[:, 3, :], self.ffn_conv_b[layer, :].rearrange("(c p) -> c p", p=128))], cwr.c, writes=[cwr.b])
            pb = self.ps[7]
            for jj in range(4):
                S.op("pe", lambda e, jj=jj: e.transpose(pb.t[:, jj * FC:(jj + 1) * FC], cwr.t[:, jj, :], self.identf.t[0:FC, 0:FC]),
                     [cwr.b, self.identf.b], [pb.b], inc=(jj == 3))
            S.op("dve", lambda e: e.tensor_copy(cw.t[:], pb.t[:, 0:4 * FC].rearrange("p (j c) -> p j c", c=FC)), [pb.b], [cw.b])
            halo = self.sb(ph, "halo", [128, FC, 2], F32)
            S.op("dve", lambda e: e.memset(halo.t[:], 0.0), [], [halo.b])
            big = self.sb(ph, "big", [128, FC * TG], BF16, chan=True)
            actT_v = big.t[:].rearrange("p (c t) -> p c t", t=TG)
            bigf = big.t[:].bitcast(F32)
            x1v = [bigf[:, tt * D:(tt + 1) * D] for tt in range(4)]
            hs = self.sb(ph, "hT", [128, KC, TG], BF16, chan=True)
            xi_ = [self.sb(ph, "xi", [128, 512], F32, chan=True) for _ in range(2)]
            xo_ = [self.sb(ph, "xo", [128, 512], F32, chan=True) for _ in range(2)]
            xn = self.sb(ph, "xn", [128, D], BF16)
            st_ = [self.sb(ph, "ss", [128, 1], F32), self.sb(ph, "rs", [128, 1], F32)]
            wu = [self.sb(ph, "wu", [128, 16, 256], BF16, chan=True) for _ in range(2)]
            wg = [self.sb(ph, "wg", [128, 16, 256], BF16, chan=True) for _ in range(2)]
            wd = [self.sb(ph, "wd", [128, 11, 512], BF16, chan=True) for _ in range(2)]
            gsb = [self.sb(ph, "gsb", [128, TG + 2], F32) for _ in range(2)]
            cacc = [self.sb(ph, "cacc", [128, TG], F32) for _ in range(2)]
            sg = [self.sb(ph, "sg", [128, TG], F32) for _ in range(2)]
            nwo = nwu = nwd = nxi = nxo = 0
            for g in range(NG):
                tok = slice(g * TG, (g + 1) * TG)
                S.dma("sp", hs.t[:], oTv[:, :, tok], hs.c, writes=[hs.b])
                for nb_ in range(D // 256):
                    w = (wu + wg)[nwo % 4]
                    nwo += 1
                    S.dma("pool", w.t[:], Wo[:, :, nb_ * 256:(nb_ + 1) * 256], w.c, writes=[w.b])
                    for tt in range(4):
                        tile = g * 4 + tt
                        pbk = self.ps[(nb_ * 4 + tt) % 4]
                        xi = xi_[nxi % 2]
                        nxi += 1
                        S.dma("sp", xi.t[:, 0:256], xsrc[tile * 128:(tile + 1) * 128, nb_ * 256:(nb_ + 1) * 256], xi.c, reads=[xrb[tile]], writes=[xi.b])
                        self.mm(pbk.t[:, 0:256], pbk.b, [(hs.t[:, f, tt * 128:(tt + 1) * 128], w.t[:, f, :]) for f in range(16)], [hs.b, w.b])
                        S.op("dve", lambda e, tt=tt, pbk=pbk, xi=xi, nb_=nb_: e.tensor_tensor(x1v[tt][:, nb_ * 256:(nb_ + 1) * 256], pbk.t[:, 0:256], xi.t[:, 0:256], ALU.add),
                             [pbk.b, xi.b], [big.b])
                for tt in range(4):
                    tile = g * 4 + tt
                    S.dma("sp", self.xr[tile * 128:(tile + 1) * 128, :], x1v[tt], big.c, reads=[big.b], writes=[xrb[tile]])
                self.norm_x1(x1v, big.b, gain, xn, st_, hs)
                for ub in range(D_FF // 256):
                    wu_, wg_ = wu[nwu % 2], wg[nwu % 2]
                    nwu += 1
                    S.dma("pool", wu_.t[:], Wu[:, :, ub * 256:(ub + 1) * 256], wu_.c, writes=[wu_.b])
                    S.dma("pool", wg_.t[:], Wu[:, :, D_FF + ub * 256:D_FF + (ub + 1) * 256], wg_.c, writes=[wg_.b])
                    for cc in range(2):
                        c = ub * 2 + cc
                        pu = self.ps[(c % 2) * 2]
                        pg = self.ps[(c % 2) * 2 + 1]
                        self.mm(pg.t[:, :], pg.b, [(wg_.t[:, k, cc * 128:(cc + 1) * 128], hs.t[:, k, :]) for k in range(KC)], [wg_.b, hs.b])
                        self.mm(pu.t[:, :], pu.b, [(wu_.t[:, k, cc * 128:(cc + 1) * 128], hs.t[:, k, :]) for k in range(KC)], [wu_.b, hs.b])
                        gs, ca, sg_ = gsb[c % 2], cacc[c % 2], sg[c % 2]
                        S.op("act", lambda e, gs=gs, pg=pg: e.activation(gs.t[:, 2:TG + 2], pg.t[:, :], AF.Copy), [pg.b], [gs.b])
                        S.op("dve", lambda e, gs=gs, c=c: e.tensor_copy(gs.t[:, 0:2], halo.t[:, c, :]), [halo.b, gs.b], [gs.b])
                        S.op("dve", lambda e, gs=gs, ca=ca, c=c: e.tensor_scalar(ca.t[:], gs.t[:, 2:TG + 2], cw.t[:, 2, c:c + 1], cw.t[:, 3, c:c + 1], ALU.mult, ALU.add),
                             [gs.b, cw.b], [ca.b])
                        S.op("dve", lambda e, gs=gs, ca=ca, c=c: e.scalar_tensor_tensor(out=ca.t[:], in0=gs.t[:, 1:TG + 1], scalar=cw.t[:, 1, c:c + 1], in1=ca.t[:], op0=ALU.mult, op1=ALU.add),
                             [gs.b, cw.b, ca.b], [ca.b])
                        S.op("dve", lambda e, gs=gs, ca=ca, c=c: e.scalar_tensor_tensor(out=ca.t[:], in0=gs.t[:, 0:TG], scalar=cw.t[:, 0, c:c + 1], in1=ca.t[:], op0=ALU.mult, op1=ALU.add),
                             [gs.b, cw.b, ca.b], [ca.b])
                        S.op("dve", lambda e, gs=gs, c=c: e.tensor_copy(halo.t[:, c, :], gs.t[:, TG:TG + 2]), [gs.b, halo.b], [halo.b])
                        S.op("act", lambda e, sg_=sg_, ca=ca: e.activation(sg_.t[:], ca.t[:], AF.Silu), [ca.b], [sg_.b])
                        S.op("dve", lambda e, sg_=sg_, pu=pu, c=c: e.tensor_tensor(actT_v[:, c, :], sg_.t[:], pu.t[:, :], ALU.mult), [sg_.b, pu.b], [big.b])
                for nb_ in range(D // 512):
                    accs = [self.ps[4 + tt] for tt in range(4)]
                    for qd in range(4):
                        w = wd[nwd % 2]
                        nwd += 1
                        S.dma("pool", w.t[:], Wd[:, qd * 11:(qd + 1) * 11, nb_ * 512:(nb_ + 1) * 512], w.c, writes=[w.b])
                        for tt in range(4):
                            for ci in range(11):
                                c = qd * 11 + ci
                                S.op("pe", lambda e, tt=tt, ci=ci, c=c, w=w, acc=accs[tt]: e.matmul(
                                    acc.t[:, :], actT_v[:, c, tt * 128:(tt + 1) * 128], w.t[:, ci, :], start=(c == 0), stop=(c == FC - 1)),
                                    [big.b, w.b], [accs[tt].b], inc=(ci == 10))
                    for tt in range(4):
                        tile = g * 4 + tt
                        xi = xi_[nxi % 2]
                        nxi += 1
                        xo = xo_[nxo % 2]
                        nxo += 1
                        S.dma("sp", xi.t[:], self.xr[tile * 128:(tile + 1) * 128, nb_ * 512:(nb_ + 1) * 512], xi.c, reads=[xrb[tile]], writes=[xi.b])
                        S.op("dve", lambda e, tt=tt, xo=xo, xi=xi: e.tensor_tensor(xo.t[:], accs[tt].t[:, :], xi.t[:], ALU.add),
                             [accs[tt].b, xi.b], [xo.b])
                        S.dma("sp", self.xr[tile * 128:(tile + 1) * 128, nb_ * 512:(nb_ + 1) * 512], xo.t[:], xo.c, reads=[xo.b], writes=[xrb[tile]])

    def norm_x1(self, x1v, xb, gain, n, st_, hs):
        S = self.S
        ss, rs = st_
        for tt in range(4):
            xv = x1v[tt]
            S.op("act", lambda e, xv=xv: e.activation(n.t[:], xv, AF.Square, accum_out=ss.t[:]), [xb], [n.b, ss.b])
            S.op("dve", lambda e: e.tensor_scalar(rs.t[:], ss.t[:], 1.0 / D, EPS, ALU.mult, ALU.add), [ss.b], [rs.b])
            S.op("act", lambda e: e.activation(rs.t[:], rs.t[:], AF.Sqrt), [rs.b], [rs.b])
            S.op("dve", lambda e: e.reciprocal(rs.t[:], rs.t[:]), [rs.b], [rs.b])
            S.op("dve", lambda e, xv=xv: e.scalar_tensor_tensor(out=n.t[:], in0=xv, scalar=rs.t[:, 0:1], in1=gain.t[:],
                                                               op0=ALU.mult, op1=ALU.mult), [xb, rs.b, gain.b], [n.b])
            for half in range(2):
                pb = self.ps[2 + half]
                pT = pb.t[:, :].bitcast(BF16)
                for kk in range(8):
                    k = half * 8 + kk
                    S.op("pe", lambda e, k=k, kk=kk, pT=pT: e.transpose(pT[:, kk * 128:(kk + 1) * 128], n.t[:, k * 128:(k + 1) * 128], self.identb.t[:]),
                         [n.b, self.identb.b], [pb.b], inc=(kk == 7))
                dst = hs.t[:, half * 8:(half + 1) * 8, tt * 128:(tt + 1) * 128]
                src = pT.rearrange("p (k t) -> p k t", t=128)
                if half == 0:
                    S.op("act", lambda e, dst=dst, src=src: e.activation(dst, src, AF.Copy), [pb.b], [hs.b])
                else:
                    S.op("dve", lambda e, dst=dst, src=src: e.tensor_copy(dst, src), [pb.b], [hs.b])

    def phase_final(self, xsrc):
        S = self.S
        with ExitStack() as ph:
            gain = self.sb(ph, "gain", [128, D], F32, chan=True)
            S.dma("sp", gain.t[:], self.norm_final.partition_broadcast(128), gain.c, writes=[gain.b])
            xt = [self.sb(ph, "xt", [128, D], F32, chan=True) for _ in range(3)]
            yo = [self.sb(ph, "yo", [128, D], F32, chan=True) for _ in range(2)]
            sq = self.sb(ph, "sq", [128, D], BF16)
            ss = [self.sb(ph, "ss", [128, 1], F32) for _ in range(2)]
            rs = [self.sb(ph, "rs", [128, 1], F32) for _ in range(2)]
            for tile in range(NT):
                x = xt[tile % 3]
                s_, r_, y_ = ss[tile % 2], rs[tile % 2], yo[tile % 2]
                S.dma("sp", x.t[:], xsrc[tile * 128:(tile + 1) * 128, :], x.c, writes=[x.b])
                S.op("act", lambda e, x=x, s_=s_: e.activation(sq.t[:], x.t[:], AF.Square, accum_out=s_.t[:]), [x.b], [sq.b, s_.b])
                S.op("dve", lambda e, s_=s_, r_=r_: e.tensor_scalar(r_.t[:], s_.t[:], 1.0 / D, EPS, ALU.mult, ALU.add), [s_.b], [r_.b])
                S.op("act", lambda e, r_=r_: e.activation(r_.t[:], r_.t[:], AF.Sqrt), [r_.b], [r_.b])
                S.op("dve", lambda e, r_=r_: e.reciprocal(r_.t[:], r_.t[:]), [r_.b], [r_.b])
                S.op("dve", lambda e, x=x, r_=r_, y_=y_: e.scalar_tensor_tensor(out=y_.t[:], in0=x.t[:], scalar=r_.t[:, 0:1], in1=gain.t[:],
                                                                              op0=ALU.mult, op1=ALU.mult), [x.b, r_.b, gain.b], [y_.b])
                S.dma("sp", self.y[tile * 128:(tile + 1) * 128, :], y_.t[:], y_.c, reads=[y_.b])


_INPUT_NAMES = ["rel_bias", "norm_mix", "norm_ffn", "norm_final", "ev_w_in", "ev_b_forget", "ev_sinks", "ev_w_out",
                "od_w_in", "od_cmp_pos", "od_cmp_w1", "od_cmp_w2", "od_conv_w", "od_a_log", "od_dt_bias", "od_gdn_norm", "od_w_out",
                "ffn_w_up", "ffn_conv_w", "ffn_conv_b", "ffn_w_down"]


def kernel(**inputs):
    b = Builder()
    nc = b.build()
    consts = host_consts()
    x = np.ascontiguousarray(inputs["x"], dtype=np.float32)
    shared = {k: np.ascontiguousarray(inputs[k], dtype=np.float32) for k in _INPUT_NAMES}
    shared.update(consts)
    in_maps = []
    for c in range(N_CORES):
        m = dict(shared)
        m["x"] = x[c]
        in_maps.append(m)
    res = run_bass_kernel_spmd(nc, in_maps, core_ids=list(range(N_CORES)))
    return np.stack([np.asarray(r["y"]) for r in res.results], axis=0).astype(np.float32)
```

```python
import math
import os
import numpy as np
from contextlib import ExitStack
import concourse.bass as bass
import concourse.mybir as mybir
from concourse.bass_utils import run_bass_kernel_spmd

F32 = mybir.dt.float32
BF16 = mybir.dt.bfloat16
AF = mybir.ActivationFunctionType
ALU = mybir.AluOpType
AX = mybir.AxisListType

D = 2048
T = 4096
KC = D // 128
NT = T // 128
TG = 512
NG = T // TG
DEPTH = 4
HD = 128
D_FF = 5632
FC = D_FF // 128
EVEN_COLS = 4616
ODD_COLS = 6696
SCALE = HD ** -0.5
EPS = 1e-6
NEG = -30000.0
N_CORES = 8


class Buf:
    __slots__ = ("name", "w", "r", "excl")

    def __init__(self, name="", excl=False):
        self.name = name
        self.w = {}
        self.r = {}
        self.excl = excl


class Chan:
    __slots__ = ("sem", "cnt", "key")

    def __init__(self, sem, key):
        self.sem = sem
        self.cnt = 0
        self.key = key


class Sched:
    ENG = ("pe", "act", "dve", "pool", "sp")

    def __init__(self, nc, stack):
        self.nc = nc
        self.stack = stack
        self.cnt = {}
        self.semobj = {}
        for e in self.ENG:
            self.semobj[("e", e)] = stack.enter_context(nc.semaphore("s_" + e))
            self.cnt[e] = 0
        self.known = {e: {} for e in self.ENG}
        self.prog = {e: [] for e in self.ENG}
        self.chans = []

    def chan(self):
        key = ("c", len(self.chans))
        sem = self.stack.enter_context(self.nc.semaphore("c%d" % key[1]))
        self.semobj[key] = sem
        c = Chan(sem, key)
        self.chans.append(c)
        return c

    def _collect(self, e, reads, writes, extra=()):
        need = {}
        for b in reads:
            for k, v in b.w.items():
                if need.get(k, 0) < v:
                    need[k] = v
            if b.excl:
                for k, v in b.r.items():
                    if need.get(k, 0) < v:
                        need[k] = v
        for b in writes:
            for k, v in b.w.items():
                if need.get(k, 0) < v:
                    need[k] = v
            for k, v in b.r.items():
                if need.get(k, 0) < v:
                    need[k] = v
        for k, v in extra:
            if need.get(k, 0) < v:
                need[k] = v
        if e == "pe":
            need.pop(("e", "pe"), None)
        waits = []
        kn = self.known[e]
        for k, v in need.items():
            if kn.get(k, 0) >= v:
                continue
            kn[k] = v
            waits.append((k, v))
        return waits

    @staticmethod
    def _commit(ev, reads, writes):
        k, v = ev
        for b in reads:
            if b.r.get(k, 0) < v:
                b.r[k] = v
        for b in writes:
            if b.w.get(k, 0) < v:
                b.w[k] = v

    def op(self, e, fn, reads=(), writes=(), inc=True):
        waits = self._collect(e, reads, writes)
        ev = (("e", e), self.cnt[e] + 1)
        if inc:
            self.cnt[e] += 1
        self.prog[e].append((waits, fn, (("e", e), 1) if inc else None))
        self._commit(ev, reads, writes)

    def dma(self, q, out, in_, chan, reads=(), writes=()):
        self.dma_group(q, [(out, in_)], chan, reads, writes)

    def dma_group(self, q, pairs, chan, reads=(), writes=()):
        extra = [(chan.key, chan.cnt)] if chan.cnt > 0 else []
        waits = self._collect(q, reads, writes, extra)
        for (o, i) in pairs:
            chan.cnt += 16
            fn = lambda eng, o=o, i=i: eng.dma_start(out=o, in_=i)
            self.prog[q].append((waits, fn, (chan.key, 16)))
            waits = []
        self._commit((chan.key, chan.cnt), reads, writes)

    def barrier(self):
        evs = [(("e", o), self.cnt[o]) for o in self.ENG if self.cnt[o] > 0]
        evs += [(c.key, c.cnt) for c in self.chans if c.cnt > 0]
        for e in self.ENG:
            waits = self._collect(e, (), (), evs)
            if e == "pe":
                pass
            if waits:
                self.prog[e].append((waits, None, None))

    def emit(self):
        nc = self.nc
        with nc.Block() as block:
            def run(e):
                def body(eng):
                    for waits, fn, inc in self.prog[e]:
                        for k, v in waits:
                            eng.wait_ge(self.semobj[k], v)
                        if fn is None:
                            continue
                        ins = fn(eng)
                        if inc is not None:
                            ins.then_inc(self.semobj[inc[0]], inc[1])
                return body
            block.tensor(run("pe"))
            block.scalar(run("act"))
            block.vector(run("dve"))
            block.gpsimd(run("pool"))
            block.sync(run("sp"))


class Tl:
    __slots__ = ("t", "b", "c")

    def __init__(self, t, b, c=None):
        self.t = t
        self.b = b
        self.c = c


def _t5_bucket(dist):
    n = np.maximum(dist, 0)
    lr = np.log(np.maximum(n, 1).astype(np.float32) / np.float32(16)) / np.float32(math.log(128 / 16))
    large = np.minimum(16 + (lr * np.float32(16)).astype(np.int32), 31)
    return np.where(n < 16, n, large)


def host_consts():
    k = np.arange(128)[:, None]
    q = np.arange(128)[None, :]
    oh = np.zeros((2, 32, 128, 128), np.float32)
    for o in range(2):
        dist = q - k + 128 * o
        bk = _t5_bucket(dist)
        for b in range(32):
            oh[o, b] = ((bk == b) & (dist >= 0)).astype(np.float32)
    c = {}
    c["c_oh"] = oh.transpose(2, 0, 1, 3).reshape(128, 64 * 128).copy()
    c["c_ident"] = np.eye(128, dtype=np.float32)
    c["c_causal01"] = (q >= k).astype(np.float32)
    c["c_causalneg"] = np.where(q >= k, 0.0, NEG).astype(np.float32)
    c["c_anticausalneg"] = np.where(q < k, 0.0, NEG).astype(np.float32)
    rm = np.ones((8, T), np.float32)
    rm[:, ::64] = 0.0
    c["c_rmask"] = rm
    ii = np.arange(64)[:, None]
    jj = np.arange(64)[None, :]
    c["c_gdn_pmask"] = np.where(ii > jj, 0.0, -NEG).astype(np.float32)
    c["c_gdn_nmask"] = np.where(jj >= ii, 0.0, NEG).astype(np.float32)
    cl = np.arange(128)[:, None, None]
    ti = np.arange(32)[None, :, None]
    cm = np.zeros((128, 64, 128), np.float32)
    qq = np.arange(128)[None, :]
    for i in range(32):
        for ct in range(2):
            cc = ct * 128 + np.arange(128)[:, None]
            cm[:, i * 2 + ct, :] = ((cc <= 254) & (16 * cc + 31 <= 128 * i + qq)).astype(np.float32)
    c["c_cmask"] = cm.reshape(128, 64 * 128)
    wi = np.zeros((256, 64), np.float32)
    for cidx in range(255):
        for j in range(64):
            wi[cidx, j] = sum(1 for m in range(4 * j, 4 * j + 4) if m == cidx or m == cidx + 1)
    c["c_wimp"] = wi.reshape(2, 128, 64).transpose(1, 0, 2).reshape(128, 128).copy()
    keep = np.zeros((128, 32, 64), np.float32)
    add = np.zeros((128, 32, 64), np.float32)
    jv = np.arange(64)[None, :]
    for i in range(32):
        cur = (2 * i + (np.arange(128) >= 64).astype(np.int64))[:, None]
        forced = (jv == 0) | (jv == cur) | (jv == cur - 1)
        fut = (jv > cur) & ~forced
        keep[:, i, :] = (~forced & ~fut).astype(np.float32)
        add[:, i, :] = np.where(forced, 1e9, np.where(fut, -1e9, 0.0))
    c["c_selkeep"] = keep.reshape(128, 32 * 64)
    c["c_seladd"] = add.reshape(128, 32 * 64)
    es = np.zeros((64, 32, 128), np.float32)
    for jb in range(32):
        es[2 * jb, jb, 0:64] = 1.0
        es[2 * jb + 1, jb, 64:128] = 1.0
    c["c_esel"] = es.reshape(64, 32 * 128)
    return c


class Builder:
    def __init__(self, layers=None, debug=False):
        self.layers = list(range(DEPTH)) if layers is None else list(layers)
        self.debug = debug
        self.nc = bass.Bass("TRN2", target_bir_lowering=False)
        self.uid = 0
        self.free_chans = []
        import os
        self.skip = set(os.environ.get("KSKIP", "").split(","))
        self.feed = set(os.environ.get("KFEED", "").split(","))
        self.gdn_heads = int(os.environ.get("KGDNH", "8"))
        self.gdn_stage = float(os.environ.get("KGDNS", "99"))
        self.small = os.environ.get("KSMALL", "") == "1"

    def din(self, name, shape, dt=F32):
        if self.small and name in ("ev_w_in", "ev_w_out", "od_w_in", "od_w_out", "ffn_w_up", "ffn_w_down"):
            shape = [1, 1]
        return self.nc.dram_tensor(name, list(shape), dt, kind="ExternalInput").ap()

    def dscr(self, name, shape, dt):
        kind = "ExternalOutput" if (self.debug and name in self.debug) else "Internal"
        if name in self.feed:
            kind = "ExternalInput"
        return self.nc.dram_tensor(name, list(shape), dt, kind=kind).ap()

    def sb(self, ph, name, shape, dt, chan=False):
        self.uid += 1
        t = ph.enter_context(self.nc.sbuf_tensor("%s_%d" % (name, self.uid), list(shape), dt))
        c = None
        if chan is True:
            c = self.free_chans.pop() if self.free_chans else self.S.chan()
            ph.callback(self.free_chans.append, c)
        return Tl(t, Buf(name), c)

    def mm(self, out_ap, out_buf, pairs, reads):
        n = len(pairs)
        for i, (l, r) in enumerate(pairs):
            self.S.op("pe", lambda e, l=l, r=r, i=i: e.matmul(out_ap, l, r, start=(i == 0), stop=(i == n - 1)),
                      reads, [out_buf], inc=(i == n - 1))

    def build(self):
        nc = self.nc
        self.x = self.din("x", [T, D])
        self.rel_bias = self.din("rel_bias", [32, 8])
        self.norm_mix = self.din("norm_mix", [DEPTH, D])
        self.norm_ffn = self.din("norm_ffn", [DEPTH, D])
        self.norm_final = self.din("norm_final", [D])
        self.ev_w_in = self.din("ev_w_in", [2, D, EVEN_COLS])
        self.ev_b_forget = self.din("ev_b_forget", [2, 8])
        self.ev_sinks = self.din("ev_sinks", [2, 8])
        self.ev_w_out = self.din("ev_w_out", [2, D, D])
        self.od_w_in = self.din("od_w_in", [2, D, ODD_COLS])
        self.od_cmp_pos = self.din("od_cmp_pos", [2, 2, 32, 128])
        self.od_cmp_w1 = self.din("od_cmp_w1", [2, 2, 4096, 256])
        self.od_cmp_w2 = self.din("od_cmp_w2", [2, 2, 256, 128])
        self.od_conv_w = self.din("od_conv_w", [2, 4, 3072])
        self.od_a_log = self.din("od_a_log", [2, 8])
        self.od_dt_bias = self.din("od_dt_bias", [2, 8])
        self.od_gdn_norm = self.din("od_gdn_norm", [2, 128])
        self.od_w_out = self.din("od_w_out", [2, D, D])
        self.ffn_w_up = self.din("ffn_w_up", [DEPTH, D, 2 * D_FF])
        self.ffn_conv_w = self.din("ffn_conv_w", [DEPTH, 3, D_FF])
        self.ffn_conv_b = self.din("ffn_conv_b", [DEPTH, D_FF])
        self.ffn_w_down = self.din("ffn_w_down", [DEPTH, D_FF, D])
        self.c_oh = self.din("c_oh", [128, 64 * 128])
        self.c_ident = self.din("c_ident", [128, 128])
        self.c_causal01 = self.din("c_causal01", [128, 128])
        self.c_causalneg = self.din("c_causalneg", [128, 128])
        self.c_anticausalneg = self.din("c_anticausalneg", [128, 128])
        self.c_rmask = self.din("c_rmask", [8, T])
        self.c_gdn_pmask = self.din("c_gdn_pmask", [64, 64])
        self.c_gdn_nmask = self.din("c_gdn_nmask", [64, 64])
        self.c_cmask = self.din("c_cmask", [128, 64 * 128])
        self.c_wimp = self.din("c_wimp", [128, 128])
        self.c_selkeep = self.din("c_selkeep", [128, 32 * 64])
        self.c_seladd = self.din("c_seladd", [128, 32 * 64])
        self.c_esel = self.din("c_esel", [64, 32 * 128])
        self.y = nc.dram_tensor("y", [T, D], F32, kind="ExternalOutput").ap()
        self.xr = self.dscr("xr", [T, D], F32)
        self.qaT = self.dscr("qaT", [8, 128, T], BF16)
        self.kaT = self.dscr("kaT", [2, 128, T], BF16)
        self.va = self.dscr("va", [T, 256], BF16)
        self.qbT = self.dscr("qbT", [8, 128, T], BF16)
        self.kbT = self.dscr("kbT", [8, 128, T], BF16)
        self.vb = self.dscr("vb", [T, 1024], BF16)
        self.fT = self.dscr("fT", [8, T], F32)
        self.csd = self.dscr("csd", [8, T], F32)
        self.oT = self.dscr("oT", [16, 128, T], BF16)
        self.qcT = self.dscr("qcT", [8, 128, T], BF16)
        self.kcmpT = self.dscr("kcmpT", [2, 128, T], BF16)
        self.vcmpT = self.dscr("vcmpT", [2, 128, T], BF16)
        self.kselT = self.dscr("kselT", [2, 128, T], BF16)
        self.kwinT = self.dscr("kwinT", [2, 128, T], BF16)
        self.vsel = self.dscr("vsel", [T, 256], BF16)
        self.vwin = self.dscr("vwin", [T, 256], BF16)
        self.gates = self.dscr("gates", [T, 24], F32)
        self.qdT = self.dscr("qdT", [8, 128, T], F32)
        self.kdT = self.dscr("kdT", [8, 128, T], F32)
        self.vdT = self.dscr("vdT", [8, 128, T], F32)
        self.baT = self.dscr("baT", [16, T], F32)
        self.zg = self.dscr("zg", [T, 1024], F32)

        with ExitStack() as st:
            self.S = S = Sched(nc, st)
            self.ps = [Tl(st.enter_context(nc.psum_tensor("ps%d" % i, [128, 512], F32)), Buf("ps%d" % i, excl=True)) for i in range(8)]
            self.setup_consts(st)
            xsrc = self.x
            for layer in self.layers:
                j = layer // 2
                if layer % 2 == 0:
                    self.phase_inproj_even(layer, j, xsrc)
                    S.barrier()
                    self.phase_swa(j)
                    S.barrier()
                    self.phase_fox(j)
                    S.barrier()
                    wout = self.ev_w_out[j]
                else:
                    if "inproj" not in self.skip:
                        self.phase_inproj_odd(layer, j, xsrc)
                        S.barrier()
                    if "nsa" not in self.skip:
                        self.phase_nsa(j)
                        S.barrier()
                    if "gdn" not in self.skip:
                        self.phase_gdn(j)
                        S.barrier()
                    wout = self.od_w_out[j]
                if "ffn" not in self.skip:
                    self.phase_outproj_ffn(layer, wout, xsrc)
                    S.barrier()
                xsrc = self.xr
            if "final" not in self.skip:
                self.phase_final(xsrc)
                S.barrier()
            S.emit()
        return nc

    def setup_consts(self, st):
        nc, S = self.nc, self.S
        self.identf = self.sb(st, "identf", [128, 128], F32, chan=True)
        self.identb = self.sb(st, "identb", [128, 128], BF16)
        self.causal01 = self.sb(st, "causal01", [128, 128], BF16)
        self.onesrow = self.sb(st, "onesrow", [1, 128], BF16)
        self.bias0 = self.sb(st, "bias0", [128, 8, 128], F32)
        self.bias1w = self.sb(st, "bias1w", [128, 8, 128], F32)
        self.nb0 = self.sb(st, "nb0", [128, 8, 128], F32)
        self.nb1 = self.sb(st, "nb1", [128, 8, 128], F32)
        self.acneg = self.sb(st, "acneg", [128, 128], F32, chan=True)
        S.dma("sp", self.acneg.t[:], self.c_anticausalneg[:, :], self.acneg.c, writes=[self.acneg.b])
        S.dma("sp", self.identf.t[:], self.c_ident[:, :], self.identf.c, writes=[self.identf.b])
        S.op("dve", lambda e: e.tensor_copy(self.identb.t[:], self.identf.t[:]), [self.identf.b], [self.identb.b])
        S.op("dve", lambda e: e.memset(self.onesrow.t[:], 1.0), [], [self.onesrow.b])
        with ExitStack() as ph:
            oh = self.sb(ph, "oh", [128, 64 * 128], F32, chan=True)
            rbb = self.sb(ph, "rbb", [128, 256], F32, chan=True)
            cz = self.sb(ph, "cz", [128, 128], F32, chan=True)
            cn = self.sb(ph, "cn", [128, 128], F32, chan=True)
            an = self.sb(ph, "an", [128, 128], F32, chan=True)
            S.dma("sp", oh.t[:], self.c_oh[:, :], oh.c, writes=[oh.b])
            S.dma("sp", rbb.t[:], self.rel_bias.rearrange("b h -> (b h)").partition_broadcast(128), rbb.c, writes=[rbb.b])
            S.dma("sp", cz.t[:], self.c_causal01[:, :], cz.c, writes=[cz.b])
            S.dma("sp", cn.t[:], self.c_causalneg[:, :], cn.c, writes=[cn.b])
            S.dma("sp", an.t[:], self.c_anticausalneg[:, :], an.c, writes=[an.b])
            S.op("dve", lambda e: e.tensor_copy(self.causal01.t[:], cz.t[:]), [cz.b], [self.causal01.b])
            for h in range(8):
                for o, (dst, base) in enumerate(((self.bias0, cn), (self.bias1w, an))):
                    for b in range(32):
                        src1 = base.t[:] if b == 0 else dst.t[:, h, :]
                        col = b * 8 + h
                        S.op("dve", lambda e, o=o, b=b, h=h, dst=dst, src1=src1, col=col: e.scalar_tensor_tensor(
                            out=dst.t[:, h, :], in0=oh.t[:, (o * 32 + b) * 128:(o * 32 + b + 1) * 128],
                            scalar=rbb.t[:, col:col + 1], in1=src1, op0=ALU.mult, op1=ALU.add),
                            [oh.b, rbb.b, base.b, dst.b], [dst.b])
            for h in range(8):
                c31 = rbb.t[:, 31 * 8 + h:31 * 8 + h + 1]
                S.op("dve", lambda e, h=h: e.tensor_tensor(self.nb1.t[:, h, :], self.bias1w.t[:, h, :], an.t[:], ALU.subtract), [self.bias1w.b, an.b], [self.nb1.b])
                S.op("dve", lambda e, h=h, c31=c31: e.tensor_scalar(self.nb1.t[:, h, :], self.nb1.t[:, h, :], c31, None, ALU.subtract), [self.nb1.b, rbb.b], [self.nb1.b])
                S.op("dve", lambda e, h=h, c31=c31: e.tensor_scalar(self.nb0.t[:, h, :], self.bias0.t[:, h, :], c31, None, ALU.subtract), [self.bias0.b, rbb.b], [self.nb0.b])
            S.barrier()

    def norm_group(self, g, xsrc, gain, xt, sq, xn, st_, hs, keep_x=None):
        S = self.S
        for tt in range(4):
            tile = g * 4 + tt
            x = xt[tile % len(xt)]
            S.dma("sp", x.t[:], xsrc[tile * 128:(tile + 1) * 128, :], x.c, writes=[x.b])
            ss, rs = st_[0], st_[1]
            S.op("act", lambda e, x=x: e.activation(sq.t[:], x.t[:], AF.Square, accum_out=ss.t[:]), [x.b], [sq.b, ss.b])
            S.op("dve", lambda e: e.tensor_scalar(rs.t[:], ss.t[:], 1.0 / D, EPS, ALU.mult, ALU.add), [ss.b], [rs.b])
            S.op("act", lambda e: e.activation(rs.t[:], rs.t[:], AF.Sqrt), [rs.b], [rs.b])
            S.op("dve", lambda e: e.reciprocal(rs.t[:], rs.t[:]), [rs.b], [rs.b])
            n = xn[tile % len(xn)]
            S.op("dve", lambda e, x=x, n=n: e.scalar_tensor_tensor(out=n.t[:], in0=x.t[:], scalar=rs.t[:, 0:1], in1=gain.t[:],
                                                                  op0=ALU.mult, op1=ALU.mult), [x.b, rs.b, gain.b], [n.b])
            for half in range(2):
                pb = self.ps[6 + half]
                pT = pb.t[:, :].bitcast(BF16)
                for kk in range(8):
                    k = half * 8 + kk
                    S.op("pe", lambda e, n=n, k=k, kk=kk, pT=pT: e.transpose(pT[:, kk * 128:(kk + 1) * 128], n.t[:, k * 128:(k + 1) * 128], self.identb.t[:]),
                         [n.b, self.identb.b], [pb.b], inc=(kk == 7))
                eng = "act" if half == 0 else "dve"
                dst = hs.t[:, half * 8:(half + 1) * 8, tt * 128:(tt + 1) * 128]
                src = pT.rearrange("p (k t) -> p k t", t=128)
                if eng == "act":
                    S.op("act", lambda e, dst=dst, src=src: e.activation(dst, src, AF.Copy), [pb.b], [hs.b])
                else:
                    S.op("dve", lambda e, dst=dst, src=src: e.tensor_copy(dst, src), [pb.b], [hs.b])

    def inproj(self, gain_src, W2d, blocks, xsrc):
        S = self.S
        W = W2d.rearrange("(k p) n -> p k n", p=128)
        with ExitStack() as ph:
            gain = self.sb(ph, "gain", [128, D], F32, chan=True)
            S.dma("sp", gain.t[:], gain_src.partition_broadcast(128), gain.c, writes=[gain.b])
            xt = [self.sb(ph, "xt", [128, D], F32, chan=True) for _ in range(2)]
            sq = self.sb(ph, "sq", [128, D], BF16)
            xn = [self.sb(ph, "xn", [128, D], BF16) for _ in range(2)]
            st_ = [self.sb(ph, "ss", [128, 1], F32), self.sb(ph, "rs", [128, 1], F32)]
            hT = [self.sb(ph, "hT", [128, KC, TG], BF16) for _ in range(2)]
            wt = [self.sb(ph, "wt", [128, KC, 512], BF16, chan=True) for _ in range(2)]
            ev = [self.sb(ph, "ev", [128, 512], BF16, chan=True) for _ in range(4)]
            evf = [self.sb(ph, "evf", [128, 512], F32, chan=True) for _ in range(4)]
            nblk = 0
            nev = 0
            for g in range(NG):
                hs = hT[g % 2]
                self.norm_group(g, xsrc, gain, xt, sq, xn, st_, hs)
                tok = slice(g * TG, (g + 1) * TG)
                for (c0, wd, segs) in blocks:
                    w = wt[nblk % 2]
                    nblk += 1
                    S.dma("pool", w.t[:, :, 0:wd], W[:, :, c0:c0 + wd], w.c, writes=[w.b])
                    for sg in segs:
                        if sg[0] == "f":
                            _, loc, dst, scl, dt = sg
                            pb = self.ps[nev % 4]
                            self.mm(pb.t[:, :], pb.b, [(w.t[:, k, loc:loc + 128], hs.t[:, k, :]) for k in range(KC)], [w.b, hs.b])
                            e_ = (ev if dt == BF16 else evf)[nev % 4]
                            if nev % 2 == 0:
                                S.op("act", lambda e, e_=e_, pb=pb, scl=scl: e.activation(e_.t[:], pb.t[:, :], AF.Copy, scale=scl), [pb.b], [e_.b])
                            else:
                                S.op("dve", lambda e, e_=e_, pb=pb, scl=scl: e.tensor_scalar(e_.t[:], pb.t[:, :], scl, None, ALU.mult), [pb.b], [e_.b])
                            S.dma("sp", dst[:, tok], e_.t[:], e_.c, reads=[e_.b])
                            nev += 1
                        elif sg[0] == "t":
                            _, loc, width, dst, dcol, dt = sg
                            for tt in range(4):
                                pb = self.ps[nev % 4]
                                self.mm(pb.t[:, 0:width], pb.b, [(hs.t[:, k, tt * 128:(tt + 1) * 128], w.t[:, k, loc:loc + width]) for k in range(KC)], [w.b, hs.b])
                                e_ = (ev if dt == BF16 else evf)[nev % 4]
                                if nev % 2 == 0:
                                    S.op("act", lambda e, e_=e_, pb=pb, width=width: e.activation(e_.t[:, 0:width], pb.t[:, 0:width], AF.Copy), [pb.b], [e_.b])
                                else:
                                    S.op("dve", lambda e, e_=e_, pb=pb, width=width: e.tensor_copy(e_.t[:, 0:width], pb.t[:, 0:width]), [pb.b], [e_.b])
                                r0 = g * TG + tt * 128
                                S.dma("sp", dst[r0:r0 + 128, dcol:dcol + width], e_.t[:, 0:width], e_.c, reads=[e_.b])
                                nev += 1
                        else:
                            _, loc, width, dst = sg
                            pb = self.ps[4 + nev % 2]
                            self.mm(pb.t[0:width, :], pb.b, [(w.t[:, k, loc:loc + width], hs.t[:, k, :]) for k in range(KC)], [w.b, hs.b])
                            e_ = evf[nev % 4]
                            S.op("dve", lambda e, pb=pb, e_=e_, width=width: e.tensor_copy(e_.t[0:width, :], pb.t[0:width, :]), [pb.b], [e_.b])
                            S.dma("sp", dst[:, tok], e_.t[0:width, :], e_.c, reads=[e_.b])
                            nev += 1

    def phase_inproj_even(self, layer, j, xsrc):
        blocks = []
        for b in range(2):
            blocks.append((b * 512, 512, [("f", cc * 128, self.qaT[b * 4 + cc], SCALE, BF16) for cc in range(4)]))
        blocks.append((1024, 512, [("f", 0, self.kaT[0], 1.0, BF16), ("f", 128, self.kaT[1], 1.0, BF16), ("t", 256, 256, self.va, 0, BF16)]))
        for b in range(2):
            blocks.append((1536 + b * 512, 512, [("f", cc * 128, self.qbT[b * 4 + cc], SCALE, BF16) for cc in range(4)]))
        for b in range(2):
            blocks.append((2560 + b * 512, 512, [("f", cc * 128, self.kbT[b * 4 + cc], 1.0, BF16) for cc in range(4)]))
        for b in range(2):
            blocks.append((3584 + b * 512, 512, [("t", 0, 512, self.vb, b * 512, BF16)]))
        blocks.append((4608, 8, [("ff", 0, 8, self.fT)]))
        self.inproj(self.norm_mix[layer, :], self.ev_w_in[j], blocks, xsrc)

    def phase_inproj_odd(self, layer, j, xsrc):
        blocks = []
        for b in range(2):
            blocks.append((b * 512, 512, [("f", cc * 128, self.qcT[b * 4 + cc], SCALE, BF16) for cc in range(4)]))
        blocks.append((1024, 512, [("f", 0, self.kcmpT[0], 1.0, BF16), ("f", 128, self.kcmpT[1], 1.0, BF16),
                                   ("f", 256, self.vcmpT[0], 1.0, BF16), ("f", 384, self.vcmpT[1], 1.0, BF16)]))
        blocks.append((1536, 512, [("f", 0, self.kselT[0], 1.0, BF16), ("f", 128, self.kselT[1], 1.0, BF16), ("t", 256, 256, self.vsel, 0, BF16)]))
        blocks.append((2048, 512, [("f", 0, self.kwinT[0], 1.0, BF16), ("f", 128, self.kwinT[1], 1.0, BF16), ("t", 256, 256, self.vwin, 0, BF16)]))
        blocks.append((2560, 24, [("t", 0, 24, self.gates, 0, F32)]))
        for i, dst in enumerate((self.qdT, self.kdT, self.vdT)):
            for b in range(2):
                blocks.append((2584 + i * 1024 + b * 512, 512, [("f", cc * 128, dst[b * 4 + cc], 1.0, F32) for cc in range(4)]))
        blocks.append((5656, 16, [("ff", 0, 16, self.baT)]))
        for b in range(2):
            blocks.append((5672 + b * 512, 512, [("t", 0, 512, self.zg, b * 512, F32)]))
        self.inproj(self.norm_mix[layer, :], self.od_w_in[j], blocks, xsrc)

    def phase_swa(self, j):
        S = self.S
        with ExitStack() as ph:
            esink = self.sb(ph, "esink", [128, 8], F32, chan=True)
            S.dma("sp", esink.t[:], self.ev_sinks[j, :].partition_broadcast(128), esink.c, writes=[esink.b])
            S.op("act", lambda e: e.activation(esink.t[:], esink.t[:], AF.Exp), [esink.b], [esink.b])
            kT = [self.sb(ph, "kT", [128, T], BF16, chan=True) for _ in range(2)]
            v1 = [self.sb(ph, "v1", [128, NT, 132], BF16, chan=True) for _ in range(2)]
            qT = [self.sb(ph, "qT", [128, T], BF16, chan=True) for _ in range(2)]
            sb_ = [self.sb(ph, "sb", [128, 128], F32) for _ in range(2)]
            pT = [self.sb(ph, "pT", [128, 128], BF16) for _ in range(4)]
            rd = [self.sb(ph, "rd", [128, 1], F32) for _ in range(2)]
            on = [self.sb(ph, "on", [128, 128], BF16) for _ in range(2)]
            ost = [self.sb(ph, "ost", [128, 512], BF16, chan=True) for _ in range(2)]
            for g in range(2):
                S.dma("sp", kT[g].t[:], self.kaT[g], kT[g].c, writes=[kT[g].b])
                S.op("pool", lambda e, g=g: e.memset(v1[g].t[:, :, 128:129], 1.0), [], [v1[g].b])
                S.dma("sp", v1[g].t[:, :, 0:128], self.va.rearrange("(n p) c -> p n c", p=128)[:, :, g * 128:(g + 1) * 128], v1[g].c, writes=[v1[g].b])
            it = 0
            for h in range(8):
                g = h // 4
                q = qT[h % 2]
                S.dma("sp", q.t[:], self.qaT[h], q.c, writes=[q.b])
                for i in range(NT):
                    acc = self.ps[4 + (i % 2)]
                    js = [jb for jb in (i - 1, i) if jb >= 0]
                    for jb in js:
                        sp = self.ps[it % 4]
                        self.mm(sp.t[:, 0:128], sp.b, [(kT[g].t[:, jb * 128:(jb + 1) * 128], q.t[:, i * 128:(i + 1) * 128])], [kT[g].b, q.b])
                        bias = self.bias0 if jb == i else self.bias1w
                        s_ = sb_[it % 2]
                        S.op("dve", lambda e, s_=s_, sp=sp, bias=bias, h=h: e.tensor_tensor(s_.t[:], sp.t[:, 0:128], bias.t[:, h, :], ALU.add),
                             [sp.b, bias.b], [s_.b])
                        p = pT[it % 4]
                        S.op("act", lambda e, p=p, s_=s_: e.activation(p.t[:], s_.t[:], AF.Exp), [s_.b], [p.b])
                        self.S.op("pe", lambda e, acc=acc, p=p, jb=jb, g=g, first=(jb == js[0]), last=(jb == i): e.matmul(
                            acc.t[:, 0:129], p.t[:], v1[g].t[:, jb, 0:129], start=first, stop=last), [p.b, v1[g].b], [acc.b], inc=(jb == i))
                        it += 1
                    r = rd[i % 2]
                    S.op("dve", lambda e, r=r, acc=acc, h=h: e.tensor_tensor(r.t[:], acc.t[:, 128:129], esink.t[:, h:h + 1], ALU.add), [acc.b, esink.b], [r.b])
                    S.op("dve", lambda e, r=r: e.reciprocal(r.t[:], r.t[:]), [r.b], [r.b])
                    o = on[i % 2]
                    S.op("dve", lambda e, o=o, acc=acc, r=r: e.tensor_scalar(o.t[:], acc.t[:, 0:128], r.t[:, 0:1], None, ALU.mult), [acc.b, r.b], [o.b])
                    self.out_transpose(o, ost, i, self.oT[h])

    def out_transpose(self, o, ost, i, dst):
        S = self.S
        pb = self.ps[7]
        pTr = pb.t[:, :].bitcast(BF16)
        stg = ost[(i // 4) % 2]
        S.op("pe", lambda e, o=o, pTr=pTr: e.transpose(pTr[:, 0:128], o.t[:], self.identb.t[:]), [o.b, self.identb.b], [pb.b])
        S.op("act", lambda e, stg=stg, pTr=pTr, i=i: e.activation(stg.t[:, (i % 4) * 128:(i % 4 + 1) * 128], pTr[:, 0:128], AF.Copy), [pb.b], [stg.b])
        if i % 4 == 3:
            c = i // 4
            S.dma("sp", dst[:, c * 512:(c + 1) * 512], stg.t[:], stg.c, reads=[stg.b])

    def phase_fox(self, j):
        S = self.S
        with ExitStack() as ph:
            fr = self.sb(ph, "fr", [8, T], F32, chan=True)
            cs = self.sb(ph, "cs2", [8, T], F32, chan=True)
            ones8 = self.sb(ph, "ones8", [8, T], F32)
            nb = self.sb(ph, "nb", [8, 1], F32, chan=True)
            ck = self.sb(ph, "ck", [128, NT, 8], F32)
            S.dma("sp", fr.t[:], self.fT[:, :], fr.c, writes=[fr.b])
            S.dma("sp", nb.t[:], self.ev_b_forget[j, :].rearrange("(h o) -> h o", o=1), nb.c, writes=[nb.b])
            S.op("dve", lambda e: e.tensor_scalar(nb.t[:], nb.t[:], -1.0, None, ALU.mult), [nb.b], [nb.b])
            S.op("pool", lambda e: e.memset(ones8.t[:], 1.0), [], [ones8.b])
            S.op("act", lambda e: e.activation(fr.t[:], fr.t[:], AF.Exp, bias=nb.t[:, 0:1], scale=-1.0), [fr.b, nb.b], [fr.b])
            S.op("act", lambda e: e.activation(fr.t[:], fr.t[:], AF.Ln, bias=1.0), [fr.b], [fr.b])
            S.op("dve", lambda e: e.tensor_tensor_scan(out=cs.t[:], data0=ones8.t[:], data1=fr.t[:], initial=0.0, op0=ALU.mult, op1=ALU.add),
                 [fr.b, ones8.b], [cs.b])
            S.dma("sp", self.csd[:, :], cs.t[:], cs.c, reads=[cs.b])
            pb = self.ps[7]
            for n in range(NT):
                S.op("pe", lambda e, n=n: e.transpose(pb.t[:, n * 8:(n + 1) * 8], cs.t[0:8, n * 128:(n + 1) * 128], self.identf.t[0:8, 0:8]),
                     [cs.b, self.identf.b], [pb.b], inc=(n == NT - 1))
            S.op("dve", lambda e: e.tensor_copy(ck.t[:], pb.t[:, 0:NT * 8].rearrange("p (n h) -> p n h", h=8)), [pb.b], [ck.b])
            S.barrier()
            kT = [self.sb(ph, "kT", [128, T], BF16, chan=True) for _ in range(2)]
            qT = [self.sb(ph, "qT", [128, T], BF16, chan=True) for _ in range(2)]
            v1 = [self.sb(ph, "v1", [128, NT, 132], BF16, chan=True) for _ in range(2)]
            crow = [self.sb(ph, "crow", [1, T], F32, chan=True) for _ in range(2)]
            ncq = [self.sb(ph, "ncq", [1, T], BF16) for _ in range(2)]
            pT = [self.sb(ph, "pT", [128, 512], BF16) for _ in range(3)]
            rd = [self.sb(ph, "rd", [128, 1], F32) for _ in range(2)]
            on = [self.sb(ph, "on", [128, 128], BF16) for _ in range(2)]
            ost = [self.sb(ph, "ost", [128, 512], BF16, chan=True) for _ in range(2)]
            for s in range(2):
                S.op("pool", lambda e, s=s: e.memset(v1[s].t[:, :, 128:129], 1.0), [], [v1[s].b])
            it = 0
            for h in range(8):
                s = h % 2
                k_, q_, v_, cr, nq = kT[s], qT[s], v1[s], crow[s], ncq[s]
                S.dma("sp", k_.t[:], self.kbT[h], k_.c, writes=[k_.b])
                S.dma("sp", q_.t[:], self.qbT[h], q_.c, writes=[q_.b])
                S.dma("sp", v_.t[:, :, 0:128], self.vb.rearrange("(n p) c -> p n c", p=128)[:, :, h * 128:(h + 1) * 128], v_.c, writes=[v_.b])
                S.dma("sp", cr.t[:], self.csd[h:h + 1, :], cr.c, writes=[cr.b])
                S.op("dve", lambda e, nq=nq, cr=cr: e.tensor_scalar(nq.t[:], cr.t[:], -1.0, None, ALU.mult), [cr.b], [nq.b])
                for c in range(NG):
                    accs = [self.ps[3 + qt] for qt in range(4)]
                    njb = 4 * c + 4
                    pend = None
                    for jb in range(njb + 1):
                        if jb < njb:
                            q0 = max(c * 512, jb * 128)
                            n = (c + 1) * 512 - q0
                            sp = self.ps[it % 3]
                            self.mm(sp.t[:, 0:n], sp.b, [(k_.t[:, jb * 128:(jb + 1) * 128], q_.t[:, q0:q0 + n]),
                                                         (self.onesrow.t[0:1, :], nq.t[0:1, q0:q0 + n])], [k_.b, q_.b, nq.b, self.onesrow.b])
                            p = pT[it % 3]
                            S.op("act", lambda e, p=p, sp=sp, n=n, jb=jb, h=h: e.activation(p.t[:, 0:n], sp.t[:, 0:n], AF.Exp, bias=ck.t[:, jb, h:h + 1]),
                                 [sp.b, ck.b], [p.b])
                            if jb * 128 >= c * 512:
                                S.op("dve", lambda e, p=p: e.tensor_tensor(p.t[:, 0:128], p.t[:, 0:128], self.causal01.t[:], ALU.mult),
                                     [p.b, self.causal01.b], [p.b])
                            it += 1
                            cur = (p, jb, q0, n)
                        else:
                            cur = None
                        if pend is not None:
                            p2, jb2, q02, n2 = pend
                            for qt in range(4):
                                tile = 4 * c + qt
                                if tile < jb2:
                                    continue
                                off = tile * 128 - q02
                                acc = accs[qt]
                                S.op("pe", lambda e, acc=acc, p2=p2, off=off, jb2=jb2, tile=tile, v_=v_: e.matmul(
                                    acc.t[:, 0:129], p2.t[:, off:off + 128], v_.t[:, jb2, 0:129], start=(jb2 == 0), stop=(jb2 == tile)),
                                    [p2.b, v_.b], [acc.b], inc=(jb2 == tile))
                        pend = cur
                    for qt in range(4):
                        i = 4 * c + qt
                        acc = accs[qt]
                        r = rd[i % 2]
                        S.op("dve", lambda e, r=r, acc=acc: e.reciprocal(r.t[:], acc.t[:, 128:129]), [acc.b], [r.b])
                        o = on[i % 2]
                        S.op("dve", lambda e, o=o, acc=acc, r=r: e.tensor_scalar(o.t[:], acc.t[:, 0:128], r.t[:, 0:1], None, ALU.mult), [acc.b, r.b], [o.b])
                        self.out_transpose(o, ost, i, self.oT[8 + h])

    def attend(self, acc, tiles, q_ap, q_buf, pT, sbs):
        S = self.S
        n = len(tiles)
        LA = 2
        pendq = []
        sbank = (0, 1, 2, 4)
        for idx in range(n + LA):
            cur = None
            if idx < n:
                kT_ap, k_buf, v_ap, v_buf, extra, bias = tiles[idx]
                sp = self.ps[sbank[self.nsp % 4]]
                self.nsp += 1
                pairs = [(kT_ap, q_ap)]
                reads = [k_buf, q_buf]
                if extra is not None:
                    pairs.append((extra[0], extra[1]))
                    reads += list(extra[2])
                self.mm(sp.t[:, 0:128], sp.b, pairs, reads)
                p = pT[self.npt % len(pT)]
                self.npt += 1
                if bias is not None:
                    s_ = sbs[self.npt % len(sbs)]
                    S.op("dve", lambda e, s_=s_, sp=sp, bias=bias: e.tensor_tensor(s_.t[:], sp.t[:, 0:128], bias[0], ALU.add), [sp.b, bias[1]], [s_.b])
                    S.op("act", lambda e, p=p, s_=s_: e.activation(p.t[:], s_.t[:], AF.Exp), [s_.b], [p.b])
                else:
                    S.op("act", lambda e, p=p, sp=sp: e.activation(p.t[:], sp.t[:, 0:128], AF.Exp), [sp.b], [p.b])
                cur = (p, v_ap, v_buf, idx)
                pendq.append(cur)
            if idx >= LA and pendq:
                p2, v2, vb2, i2 = pendq.pop(0)
                S.op("pe", lambda e, p2=p2, v2=v2, i2=i2: e.matmul(acc.t[:, 0:129], p2.t[:], v2, start=(i2 == 0), stop=(i2 == n - 1)),
                     [p2.b, vb2], [acc.b], inc=(i2 == n - 1))

    def phase_nsa(self, j):
        S = self.S
        idf = self.identf
        self.nsp = 0
        self.npt = 0
        with ExitStack() as ph:
            gsig = self.sb(ph, "gsig", [128, NT, 24], F32, chan=True)
            S.dma("sp", gsig.t[:], self.gates.rearrange("(n p) c -> p n c", p=128), gsig.c, writes=[gsig.b])
            S.op("act", lambda e: e.activation(gsig.t[:], gsig.t[:], AF.Sigmoid), [gsig.b], [gsig.b])
            kcT = [self.sb(ph, "kcT", [128, 256], BF16) for _ in range(2)]
            vc1 = [self.sb(ph, "vc1", [128, 2, 132], BF16) for _ in range(2)]
            with ExitStack() as cp:
                w1 = self.sb(cp, "w1", [128, 32, 256], BF16, chan=True)
                w2 = self.sb(cp, "w2", [128, 2, 128], BF16, chan=True)
                per = self.sb(cp, "per", [32, 128], F32, chan=True)
                peT = self.sb(cp, "peT", [128, 32], BF16)
                hb = self.sb(cp, "hb", [128, 2], F32)
                xT = self.sb(cp, "xT", [128, T], BF16, chan=True)
                GT = [self.sb(cp, "GT", [128, 256], BF16) for _ in range(2)]
                xs = self.sb(cp, "xs", [128, 256], F32)
                x2 = self.sb(cp, "x2", [128, 256], F32)
                for g in range(2):
                    S.op("dve", lambda e, g=g: e.memset(kcT[g].t[:], 0.0), [], [kcT[g].b])
                    S.op("dve", lambda e, g=g: e.memset(vc1[g].t[:], 0.0), [], [vc1[g].b])
                    S.op("dve", lambda e, g=g: e.memset(vc1[g].t[:, :, 128:129], 1.0), [], [vc1[g].b])
                for hc in range(2):
                    S.op("dve", lambda e, hc=hc: e.memset(GT[hc].t[:], 0.0), [], [GT[hc].b])
                for kv in range(2):
                    S.dma("pool", w1.t[:], self.od_cmp_w1[j, kv].rearrange("(jj d) n -> d jj n", d=128), w1.c, writes=[w1.b])
                    S.dma("pool", w2.t[:], self.od_cmp_w2[j, kv].rearrange("(c p) n -> p c n", p=128), w2.c, writes=[w2.b])
                    S.dma("sp", per.t[:], self.od_cmp_pos[j, kv], per.c, writes=[per.b])
                    pb7 = self.ps[7]
                    S.op("pe", lambda e: e.transpose(pb7.t[:, 0:32], per.t[:], idf.t[0:32, 0:32]), [per.b, idf.b], [pb7.b])
                    S.op("dve", lambda e: e.tensor_copy(peT.t[:], pb7.t[:, 0:32]), [pb7.b], [peT.b])
                    for hc in range(2):
                        pbh = self.ps[6]
                        self.mm(pbh.t[:, hc:hc + 1], pbh.b, [(w1.t[:, jj, hc * 128:(hc + 1) * 128], peT.t[:, jj:jj + 1]) for jj in range(32)], [w1.b, peT.b])
                        S.op("dve", lambda e, hc=hc, pbh=pbh: e.tensor_copy(hb.t[:, hc:hc + 1], pbh.t[:, hc:hc + 1]), [pbh.b], [hb.b])
                    for g in range(2):
                        src = (self.kcmpT if kv == 0 else self.vcmpT)[g]
                        S.dma("sp", xT.t[:], src, xT.c, writes=[xT.b])
                        xv = xT.t[:].rearrange("p (n s) -> p n s", s=16)
                        for hc in range(2):
                            pbx = self.ps[hc]
                            self.mm(pbx.t[:, 0:255], pbx.b, [(w1.t[:, jj, hc * 128:(hc + 1) * 128], xv[:, jj // 16:jj // 16 + 255, jj % 16]) for jj in range(32)], [w1.b, xT.b])
                            S.op("dve", lambda e, hc=hc, pbx=pbx: e.tensor_scalar(xs.t[:, 0:255], pbx.t[:, 0:255], hb.t[:, hc:hc + 1], None, ALU.add), [pbx.b, hb.b], [xs.b])
                            S.op("dve", lambda e: e.tensor_tensor(x2.t[:, 0:255], xs.t[:, 0:255], xs.t[:, 0:255], ALU.mult), [xs.b], [x2.b])
                            S.op("dve", lambda e: e.tensor_scalar(x2.t[:, 0:255], x2.t[:, 0:255], 0.044715, 1.0, ALU.mult, ALU.add), [x2.b], [x2.b])
                            S.op("dve", lambda e: e.tensor_tensor(x2.t[:, 0:255], x2.t[:, 0:255], xs.t[:, 0:255], ALU.mult), [x2.b, xs.b], [x2.b])
                            S.op("act", lambda e: e.activation(x2.t[:, 0:255], x2.t[:, 0:255], AF.Tanh, scale=0.7978845608028654), [x2.b], [x2.b])
                            S.op("dve", lambda e: e.tensor_scalar(x2.t[:, 0:255], x2.t[:, 0:255], 0.5, 0.5, ALU.mult, ALU.add), [x2.b], [x2.b])
                            S.op("dve", lambda e, hc=hc: e.tensor_tensor(GT[hc].t[:, 0:255], x2.t[:, 0:255], xs.t[:, 0:255], ALU.mult), [x2.b, xs.b], [GT[hc].b])
                        if kv == 0:
                            pbk = self.ps[2]
                            self.mm(pbk.t[:, 0:255], pbk.b, [(w2.t[:, hc, :], GT[hc].t[:, 0:255]) for hc in range(2)], [w2.b, GT[0].b, GT[1].b])
                            S.op("act", lambda e, g=g, pbk=pbk: e.activation(kcT[g].t[:, 0:255], pbk.t[:, 0:255], AF.Copy), [pbk.b], [kcT[g].b])
                        else:
                            for ct in range(2):
                                rows = 128 if ct == 0 else 127
                                pbk = self.ps[2 + ct]
                                self.mm(pbk.t[0:rows, 0:128], pbk.b, [(GT[hc].t[:, ct * 128:ct * 128 + rows], w2.t[:, hc, :]) for hc in range(2)], [w2.b, GT[0].b, GT[1].b])
                                S.op("act", lambda e, g=g, ct=ct, rows=rows, pbk=pbk: e.activation(vc1[g].t[0:rows, ct, 0:128], pbk.t[0:rows, 0:128], AF.Copy), [pbk.b], [vc1[g].b])
                S.barrier()
            cmask = self.sb(ph, "cmask", [128, 64, 128], BF16, chan=True)
            wimp = self.sb(ph, "wimp", [128, 2, 64], BF16, chan=True)
            skeep = self.sb(ph, "skeep", [128, NT, 64], F32, chan=True)
            sadd = self.sb(ph, "sadd", [128, NT, 64], F32, chan=True)
            esel = self.sb(ph, "esel", [64, NT, 128], BF16, chan=True)
            S.dma("pool", cmask.t[:], self.c_cmask.rearrange("p (m q) -> p m q", q=128), cmask.c, writes=[cmask.b])
            S.dma("pool", wimp.t[:], self.c_wimp.rearrange("p (c j) -> p c j", j=64), wimp.c, writes=[wimp.b])
            S.dma("sp", skeep.t[:], self.c_selkeep.rearrange("p (n j) -> p n j", j=64), skeep.c, writes=[skeep.b])
            S.dma("sp", sadd.t[:], self.c_seladd.rearrange("p (n j) -> p n j", j=64), sadd.c, writes=[sadd.b])
            S.dma("pool", esel.t[:], self.c_esel.rearrange("p (n k) -> p n k", k=128), esel.c, writes=[esel.b])
            qT = [self.sb(ph, "qT", [128, T], BF16, chan=True) for _ in range(4)]
            ksT = self.sb(ph, "ksT", [128, T], BF16, chan=True)
            kwT = self.sb(ph, "kwT", [128, T], BF16, chan=True)
            vs1 = self.sb(ph, "vs1", [128, NT, 132], BF16, chan=True)
            vw1 = self.sb(ph, "vw1", [128, NT, 132], BF16, chan=True)
            S.op("dve", lambda e: e.memset(vs1.t[:, :, 128:129], 1.0), [], [vs1.b])
            S.op("dve", lambda e: e.memset(vw1.t[:, :, 128:129], 1.0), [], [vw1.b])
            pT = [self.sb(ph, "pT", [128, 128], BF16) for _ in range(8)]
            sbs = [self.sb(ph, "sbs", [128, 128], F32) for _ in range(4)]
            imp = self.sb(ph, "imp", [128, 64], F32)
            sc = self.sb(ph, "sc", [128, 64], F32)
            m8 = self.sb(ph, "m8", [128, 8], F32)
            nsel = self.sb(ph, "nsel", [128, 64], BF16)
            nsT = self.sb(ph, "nsT", [64, 128], BF16)
            oacc = [self.sb(ph, "oacc", [128, 128], F32) for _ in range(4)]
            onb = [self.sb(ph, "onb", [128, 128], BF16) for _ in range(2)]
            rd = [self.sb(ph, "rd", [128, 1], F32) for _ in range(4)]
            ostg = [[self.sb(ph, "ostg", [128, 512], BF16, chan=True) for _ in range(2)] for _ in range(4)]
            for g in range(2):
                for r in range(4):
                    S.dma("sp", qT[r].t[:], self.qcT[4 * g + r], qT[r].c, writes=[qT[r].b])
                S.dma("sp", ksT.t[:], self.kselT[g], ksT.c, writes=[ksT.b])
                S.dma("sp", kwT.t[:], self.kwinT[g], kwT.c, writes=[kwT.b])
                S.dma("sp", vs1.t[:, :, 0:128], self.vsel.rearrange("(n p) c -> p n c", p=128)[:, :, g * 128:(g + 1) * 128], vs1.c, writes=[vs1.b])
                S.dma("sp", vw1.t[:, :, 0:128], self.vwin.rearrange("(n p) c -> p n c", p=128)[:, :, g * 128:(g + 1) * 128], vw1.c, writes=[vw1.b])
                for i in range(NT):
                    qs = slice(i * 128, (i + 1) * 128)
                    cts = [0] if i < 16 else [0, 1]
                    for r in range(4):
                        h = 4 * g + r
                        accc = self.ps[3]
                        blk = Tl(self.ps[3].t[:, 256:512], self.ps[3].b)
                        pcs = []
                        for ct in cts:
                            sp = self.ps[(0, 1, 2, 4)[self.nsp % 4]]
                            self.nsp += 1
                            self.mm(sp.t[:, 0:128], sp.b, [(kcT[g].t[:, ct * 128:(ct + 1) * 128], qT[r].t[:, qs])], [kcT[g].b, qT[r].b])
                            p = pT[self.npt % len(pT)]
                            self.npt += 1
                            S.op("act", lambda e, p=p, sp=sp: e.activation(p.t[:], sp.t[:, 0:128], AF.Exp), [sp.b], [p.b])
                            S.op("dve", lambda e, p=p, i=i, ct=ct: e.tensor_tensor(p.t[:], p.t[:], cmask.t[:, i * 2 + ct, :], ALU.mult), [p.b, cmask.b], [p.b])
                            pcs.append((p, ct))
                        self.mm(accc.t[:, 0:129], accc.b, [(p.t[:], vc1[g].t[:, ct, 0:129]) for p, ct in pcs], [vc1[g].b] + [p.b for p, _ in pcs])
                        self.mm(blk.t[:, 0:64], blk.b, [(p.t[:], wimp.t[:, ct, :]) for p, ct in pcs], [wimp.b] + [p.b for p, _ in pcs])
                        r_ = rd[r]
                        S.op("dve", lambda e, r_=r_, accc=accc: e.tensor_scalar(r_.t[:], accc.t[:, 128:129], 1e-30, None, ALU.max), [accc.b], [r_.b])
                        S.op("dve", lambda e, r_=r_: e.reciprocal(r_.t[:], r_.t[:]), [r_.b], [r_.b])
                        if r == 0:
                            S.op("dve", lambda e, r_=r_, blk=blk: e.tensor_scalar(imp.t[:], blk.t[:, 0:64], r_.t[:, 0:1], None, ALU.mult), [blk.b, r_.b], [imp.b])
                        else:
                            S.op("dve", lambda e, r_=r_, blk=blk: e.scalar_tensor_tensor(out=imp.t[:], in0=blk.t[:, 0:64], scalar=r_.t[:, 0:1], in1=imp.t[:], op0=ALU.mult, op1=ALU.add),
                                 [blk.b, r_.b, imp.b], [imp.b])
                        S.op("dve", lambda e, r_=r_, i=i, h=h: e.tensor_tensor(r_.t[:], r_.t[:], gsig.t[:, i, h:h + 1], ALU.mult), [r_.b, gsig.b], [r_.b])
                        S.op("dve", lambda e, r=r, r_=r_, accc=accc: e.tensor_scalar(oacc[r].t[:], accc.t[:, 0:128], r_.t[:, 0:1], None, ALU.mult), [accc.b, r_.b], [oacc[r].b])
                    S.op("dve", lambda e, i=i: e.tensor_tensor(sc.t[:], imp.t[:], skeep.t[:, i, :], ALU.mult), [imp.b, skeep.b], [sc.b])
                    S.op("dve", lambda e, i=i: e.tensor_tensor(sc.t[:], sc.t[:], sadd.t[:, i, :], ALU.add), [sc.b, sadd.b], [sc.b])
                    S.op("dve", lambda e: e.max(out=m8.t[:], in_=sc.t[:]), [sc.b], [m8.b])
                    S.op("dve", lambda e: e.tensor_scalar(sc.t[:], sc.t[:], m8.t[:, 7:8], None, ALU.is_ge), [sc.b, m8.b], [sc.b])
                    S.op("dve", lambda e: e.tensor_scalar(nsel.t[:], sc.t[:], -NEG, NEG, ALU.mult, ALU.add), [sc.b], [nsel.b])
                    pb7 = self.ps[7]
                    pTr = pb7.t[:, :].bitcast(BF16)
                    S.op("pe", lambda e, pTr=pTr: e.transpose(pTr[0:64, 0:128], nsel.t[:], self.identb.t[:]), [nsel.b, self.identb.b], [pb7.b])
                    S.op("act", lambda e, pTr=pTr: e.activation(nsT.t[:], pTr[0:64, 0:128], AF.Copy), [pb7.b], [nsT.b])
                    for r in range(4):
                        h = 4 * g + r
                        tiles = []
                        for jb in range(i + 1):
                            bias = None
                            if jb == i:
                                bias = (self.nb0.t[:, h, :], self.nb0.b)
                            elif jb == i - 1:
                                bias = (self.nb1.t[:, h, :], self.nb1.b)
                            tiles.append((ksT.t[:, jb * 128:(jb + 1) * 128], ksT.b, vs1.t[:, jb, 0:129], vs1.b,
                                          (esel.t[:, jb, :], nsT.t[:], (esel.b, nsT.b)), bias))
                        acc = self.ps[5]
                        self.attend(acc, tiles, qT[r].t[:, qs], qT[r].b, pT, sbs)
                        self.nsa_combine(acc, rd[r], gsig, i, 8 + h, oacc[r])
                        tiles = []
                        for jb in range(max(0, i - 4), i + 1):
                            bias = None
                            if jb == i:
                                bias = (self.nb0.t[:, h, :], self.nb0.b)
                            elif jb == i - 1:
                                bias = (self.nb1.t[:, h, :], self.nb1.b)
                            elif jb == i - 4:
                                bias = (self.acneg.t[:], self.acneg.b)
                            tiles.append((kwT.t[:, jb * 128:(jb + 1) * 128], kwT.b, vw1.t[:, jb, 0:129], vw1.b, None, bias))
                        acc = self.ps[6]
                        self.attend(acc, tiles, qT[r].t[:, qs], qT[r].b, pT, sbs)
                        self.nsa_combine(acc, rd[r], gsig, i, 16 + h, oacc[r])
                        o = onb[r % 2]
                        S.op("act", lambda e, o=o, r=r: e.activation(o.t[:], oacc[r].t[:], AF.Copy), [oacc[r].b], [o.b])
                        self.out_transpose(o, ostg[r], i, self.oT[h])

    def nsa_combine(self, acc, r_, gsig, i, gcol, oacc):
        S = self.S
        S.op("dve", lambda e: e.reciprocal(r_.t[:], acc.t[:, 128:129]), [acc.b], [r_.b])
        S.op("dve", lambda e: e.tensor_tensor(r_.t[:], r_.t[:], gsig.t[:, i, gcol:gcol + 1], ALU.mult), [r_.b, gsig.b], [r_.b])
        S.op("dve", lambda e: e.scalar_tensor_tensor(out=oacc.t[:], in0=acc.t[:, 0:128], scalar=r_.t[:, 0:1], in1=oacc.t[:], op0=ALU.mult, op1=ALU.add),
             [acc.b, r_.b, oacc.b], [oacc.b])

    def phase_gdn(self, j):
        S = self.S
        B = int(os.environ.get("KGDNB", "8"))
        C = 64
        NCH = T // C
        idf = self.identf
        with ExitStack() as ph:
            psq = []
            for q in range(4):
                for b in range(6):
                    psq.append(Tl(self.ps[b].t[:, q * 128:(q + 1) * 128], self.ps[b].b))
            nq = [0]

            def slot():
                nq[0] += 1
                return psq[nq[0] % len(psq)]
            cwg = self.sb(ph, "cwg", [128, 4, 24], F32)
            gamc = self.sb(ph, "gamc", [C, NCH, 8], F32)
            betac = self.sb(ph, "betac", [C, NCH, 8], F32)
            egamc = self.sb(ph, "egamc", [C, NCH, 8], F32)
            nbetac = self.sb(ph, "nbetac", [C, NCH, 8], F32)
            begamc = self.sb(ph, "begamc", [C, NCH, 8], F32)
            dtb = self.sb(ph, "dtb", [8, 1], F32, chan=True)
            nega = self.sb(ph, "nega", [8, 1], F32, chan=True)
            prep = ExitStack()
            cwr = self.sb(prep, "cwr", [24, 4, 128], F32, chan=True)
            S.dma_group("sp", [(cwr.t[:, jj, :], self.od_conv_w[j, jj, :].rearrange("(c p) -> c p", p=128)) for jj in range(4)], cwr.c, writes=[cwr.b])
            pb7 = self.ps[7]
            for jj in range(4):
                S.op("pe", lambda e, jj=jj: e.transpose(pb7.t[:, jj * 24:(jj + 1) * 24], cwr.t[:, jj, :], idf.t[0:24, 0:24]), [cwr.b, idf.b], [pb7.b], inc=(jj == 3))
            S.op("dve", lambda e: e.tensor_copy(cwg.t[:], pb7.t[:, 0:96].rearrange("p (j c) -> p j c", c=24)), [pb7.b], [cwg.b])
            bb = self.sb(prep, "bb", [8, T], F32, chan=True)
            ba = self.sb(prep, "ba", [8, T], F32, chan=True)
            gam = self.sb(prep, "gam", [8, T], F32)
            rmask = self.sb(prep, "rmask", [8, T], F32, chan=True)
            S.dma("sp", bb.t[:], self.baT[0:8, :], bb.c, writes=[bb.b])
            S.dma("sp", ba.t[:], self.baT[8:16, :], ba.c, writes=[ba.b])
            S.dma("sp", rmask.t[:], self.c_rmask[:, :], rmask.c, writes=[rmask.b])
            S.dma("sp", dtb.t[:], self.od_dt_bias[j, :].rearrange("(h o) -> h o", o=1), dtb.c, writes=[dtb.b])
            S.dma("sp", nega.t[:], self.od_a_log[j, :].rearrange("(h o) -> h o", o=1), nega.c, writes=[nega.b])
            S.op("act", lambda e: e.activation(nega.t[:], nega.t[:], AF.Exp), [nega.b], [nega.b])
            S.op("dve", lambda e: e.tensor_scalar(nega.t[:], nega.t[:], -1.0, None, ALU.mult), [nega.b], [nega.b])
            S.op("act", lambda e: e.activation(ba.t[:], ba.t[:], AF.Exp, bias=dtb.t[:, 0:1]), [ba.b, dtb.b], [ba.b])
            S.op("act", lambda e: e.activation(ba.t[:], ba.t[:], AF.Ln, bias=1.0), [ba.b], [ba.b])
            S.op("dve", lambda e: e.tensor_scalar(ba.t[:], ba.t[:], nega.t[:, 0:1], None, ALU.mult), [ba.b, nega.b], [ba.b])
            S.op("dve", lambda e: e.tensor_tensor_scan(out=gam.t[:], data0=rmask.t[:], data1=ba.t[:], initial=0.0, op0=ALU.mult, op1=ALU.add),
                 [rmask.b, ba.b], [gam.b])
            S.op("act", lambda e: e.activation(bb.t[:], bb.t[:], AF.Sigmoid), [bb.b], [bb.b])
            for src, dst, pbk in ((gam, gamc, self.ps[6]), (bb, betac, self.ps[7])):
                for c in range(NCH):
                    S.op("pe", lambda e, c=c, src=src, pbk=pbk: e.transpose(pbk.t[0:C, c * 8:(c + 1) * 8], src.t[0:8, c * C:(c + 1) * C], idf.t[0:8, 0:8]),
                         [src.b, idf.b], [pbk.b], inc=(c == NCH - 1))
                S.op("dve", lambda e, dst=dst, pbk=pbk: e.tensor_copy(dst.t[:], pbk.t[0:C, :].rearrange("p (c h) -> p c h", h=8)), [pbk.b], [dst.b])
            S.barrier()
            prep.close()
            S.op("act", lambda e: e.activation(egamc.t[:], gamc.t[:], AF.Exp), [gamc.b], [egamc.b])
            S.op("dve", lambda e: e.tensor_scalar(nbetac.t[:], betac.t[:], -1.0, None, ALU.mult), [betac.b], [nbetac.b])
            S.op("dve", lambda e: e.tensor_tensor(begamc.t[:], betac.t[:], egamc.t[:], ALU.mult), [betac.b, egamc.b], [begamc.b])
            gnr = self.sb(ph, "gnr", [C, 128], F32, chan=True)
            S.dma("sp", gnr.t[:], self.od_gdn_norm[j, :].partition_broadcast(C), gnr.c, writes=[gnr.b])
            ones64 = self.sb(ph, "ones64", [C, 128], F32)
            onescol = self.sb(ph, "onescol", [128, 1], F32)
            S.op("dve", lambda e: e.memset(ones64.t[:], 1.0), [], [ones64.b])
            S.op("dve", lambda e: e.memset(onescol.t[:], 1.0), [], [onescol.b])
            pmask = self.sb(ph, "pmask", [C, C], F32, chan=True)
            nmask = self.sb(ph, "nmask", [C, C], F32, chan=True)
            S.dma("sp", pmask.t[:], self.c_gdn_pmask[:, :], pmask.c, writes=[pmask.b])
            S.dma("sp", nmask.t[:], self.c_gdn_nmask[:, :], nmask.c, writes=[nmask.b])
            if self.gdn_stage <= 0:
                return
            raw = self.sb(ph, "raw", [128, T + 3], F32, chan=True)
            S.op("dve", lambda e: e.memset(raw.t[:, 0:3], 0.0), [], [raw.b])
            qkv = [self.sb(ph, "qkv", [128, T], F32) for _ in range(3)]
            sqs = self.sb(ph, "sqs", [128, T], F32)
            rnc = self.sb(ph, "rnc", [C, NCH, 2], F32)
            zt = [self.sb(ph, "zt", [C, B, 128], F32, chan=True) for _ in range(2)]
            St = self.sb(ph, "St", [128, 128], F32)
            ost = [self.sb(ph, "ost", [128, 512], BF16, chan=True) for _ in range(2)]

            def mk(name, shape, n, dt=F32):
                return [self.sb(ph, name, shape, dt) for _ in range(n)]
            kn = mk("kn", [C, 128], B); qn = mk("qn", [C, 128], B); vt = mk("vt", [C, 128], B)
            knT = mk("knT", [128, C], B); qnT = mk("qnT", [128, C], 2 * B)
            dg = mk("dg", [C, C], B); t1 = mk("t1", [C, C], B); Dm = mk("Dm", [C, C], B); DT = mk("DT", [C, C], B)
            X = mk("X", [C, C], 2 * B); XT = mk("XT", [C, C], 2 * B); Y = mk("Y", [C, C], 2 * B)
            Vb = mk("Vb", [C, 128], B); Kb = mk("Kb", [C, 128], B)
            U = mk("U", [C, 128], 2 * B); WmT = mk("WmT", [128, C], 2 * B); MT = mk("MT", [C, C], 2 * B); Kd = mk("Kd", [C, 128], 2 * B)
            kdc = mk("kdc", [C, 1], 2 * B); egl = mk("egl", [128, 1], 2 * B)
            vnew = mk("vnew", [C, 128], 2); mvs = mk("mvs", [C, 128], 2); osb = mk("osb", [C, 128], 2)
            oss = mk("oss", [C, 1], 2); ors = mk("ors", [C, 1], 2); ojunk = mk("ojunk", [C, 128], 1)
            zs = mk("zs", [C, 128], 2); ofb = mk("ofb", [C, 128], 2, BF16)
            for h in range(self.gdn_heads):
                for ti, src in enumerate((self.qdT, self.kdT, self.vdT)):
                    S.dma("sp", raw.t[:, 3:T + 3], src[h], raw.c, writes=[raw.b])
                    dst = qkv[ti]
                    ci = ti * 8 + h
                    S.op("dve", lambda e, dst=dst, ci=ci: e.tensor_scalar(dst.t[:], raw.t[:, 0:T], cwg.t[:, 0, ci:ci + 1], None, ALU.mult), [raw.b, cwg.b], [dst.b])
                    for jj in range(1, 4):
                        S.op("dve", lambda e, dst=dst, ci=ci, jj=jj: e.scalar_tensor_tensor(out=dst.t[:], in0=raw.t[:, jj:T + jj], scalar=cwg.t[:, jj, ci:ci + 1], in1=dst.t[:],
                                                                                           op0=ALU.mult, op1=ALU.add), [raw.b, cwg.b, dst.b], [dst.b])
                    S.op("act", lambda e, dst=dst: e.activation(dst.t[:], dst.t[:], AF.Silu), [dst.b], [dst.b])
                if self.gdn_stage <= 1:
                    continue
                pss = self.ps[6]
                for ti in range(2):
                    S.op("act", lambda e, ti=ti: e.activation(sqs.t[:], qkv[ti].t[:], AF.Square), [qkv[ti].b], [sqs.b])
                    for c in range(NCH):
                        S.op("pe", lambda e, c=c, ti=ti: e.matmul(pss.t[0:C, c * 2 + ti:c * 2 + ti + 1], sqs.t[:, c * C:(c + 1) * C], onescol.t[:, 0:1], start=True, stop=True),
                             [sqs.b, onescol.b], [pss.b], inc=(c == NCH - 1))
                S.op("dve", lambda e: e.tensor_scalar(rnc.t[:], pss.t[0:C, 0:2 * NCH].rearrange("p (c t) -> p c t", t=2), EPS, None, ALU.add), [pss.b], [rnc.b])
                S.op("act", lambda e: e.activation(rnc.t[:], rnc.t[:], AF.Sqrt), [rnc.b], [rnc.b])
                S.op("dve", lambda e: e.reciprocal(rnc.t[:], rnc.t[:]), [rnc.b], [rnc.b])
                S.op("dve", lambda e: e.tensor_scalar(rnc.t[:, :, 0:1], rnc.t[:, :, 0:1], SCALE, None, ALU.mult), [rnc.b], [rnc.b])
                if self.gdn_stage <= 2:
                    continue
                S.op("dve", lambda e: e.memset(St.t[:], 0.0), [], [St.b])
                for bt in range(NCH // B):
                    if os.environ.get("KGDNBAR", "") == "1":
                        S.barrier()
                    par = bt % 2
                    z_ = zt[par]
                    t0 = bt * B * C
                    S.dma("sp", z_.t[:], self.zg[t0:t0 + B * C, h * 128:(h + 1) * 128].rearrange("(b p) e -> p b e", p=C), z_.c, writes=[z_.b])
                    cs = [bt * B + bi for bi in range(B)]
                    o2 = [par * B + bi for bi in range(B)]
                    for bi, c in enumerate(cs):
                        sl = slice(c * C, (c + 1) * C)
                        pq_, pk_, pv_ = slot(), slot(), slot()
                        for p_, src in ((pq_, qkv[0]), (pk_, qkv[1]), (pv_, qkv[2])):
                            S.op("pe", lambda e, p_=p_, src=src, sl=sl: e.transpose(p_.t[0:C, :], src.t[:, sl], idf.t[:]), [src.b, idf.b], [p_.b])
                        S.op("dve", lambda e, bi=bi, c=c, pq_=pq_: e.tensor_scalar(qn[bi].t[:], pq_.t[0:C, :], rnc.t[:, c, 0:1], None, ALU.mult), [pq_.b, rnc.b], [qn[bi].b])
                        S.op("dve", lambda e, bi=bi, c=c, pk_=pk_: e.tensor_scalar(kn[bi].t[:], pk_.t[0:C, :], rnc.t[:, c, 1:2], None, ALU.mult), [pk_.b, rnc.b], [kn[bi].b])
                        S.op("act", lambda e, bi=bi, pv_=pv_: e.activation(vt[bi].t[:], pv_.t[0:C, :], AF.Copy), [pv_.b], [vt[bi].b])
                    if self.gdn_stage <= 3:
                        continue
                    for bi, c in enumerate(cs):
                        o = o2[bi]
                        p1, p2, p3 = slot(), slot(), slot()
                        S.op("pe", lambda e, bi=bi, p1=p1: e.transpose(p1.t[:, 0:C], kn[bi].t[:], idf.t[0:C, 0:C]), [kn[bi].b, idf.b], [p1.b])
                        S.op("pe", lambda e, bi=bi, p2=p2: e.transpose(p2.t[:, 0:C], qn[bi].t[:], idf.t[0:C, 0:C]), [qn[bi].b, idf.b], [p2.b])
                        S.op("act", lambda e, bi=bi, p1=p1: e.activation(knT[bi].t[:], p1.t[:, 0:C], AF.Copy), [p1.b], [knT[bi].b])
                        S.op("dve", lambda e, o=o2[bi], p2=p2: e.tensor_copy(qnT[o].t[:], p2.t[:, 0:C]), [p2.b], [qnT[o].b])
                        S.op("dve", lambda e, h=h, bi=bi, c=c: e.tensor_scalar(dg[bi].t[:], idf.t[0:C, 0:C], gamc.t[:, c, h:h + 1], None, ALU.mult), [idf.b, gamc.b], [dg[bi].b])
                        S.op("pe", lambda e, bi=bi, p3=p3: e.matmul(p3.t[:, 0:C], ones64.t[:, :], dg[bi].t[:], start=True, stop=True), [ones64.b, dg[bi].b], [p3.b])
                        S.op("dve", lambda e, h=h, bi=bi, c=c, p3=p3: e.scalar_tensor_tensor(out=t1[bi].t[:], in0=p3.t[0:C, 0:C], scalar=gamc.t[:, c, h:h + 1], in1=pmask.t[:],
                                                                                       op0=ALU.subtract, op1=ALU.add), [p3.b, gamc.b, pmask.b], [t1[bi].b])
                        S.op("act", lambda e, bi=bi: e.activation(Dm[bi].t[:], t1[bi].t[:], AF.Exp, scale=-1.0), [t1[bi].b], [Dm[bi].b])
                        S.op("dve", lambda e, h=h, bi=bi, c=c, p3=p3: e.scalar_tensor_tensor(out=t1[bi].t[:], in0=p3.t[0:C, 0:C], scalar=gamc.t[:, c, h:h + 1], in1=nmask.t[:],
                                                                                       op0=ALU.subtract, op1=ALU.add), [p3.b, gamc.b, nmask.b, Dm[bi].b], [t1[bi].b])
                        S.op("act", lambda e, bi=bi: e.activation(DT[bi].t[:], t1[bi].t[:], AF.Exp), [t1[bi].b], [DT[bi].b])
                        S.op("act", lambda e, o=o, p3=p3: e.activation(egl[o].t[:], p3.t[:, C - 1:C], AF.Exp), [p3.b], [egl[o].b])
                        S.op("dve", lambda e, h=h, o=o2[bi], c=c, p3=p3: e.tensor_scalar(kdc[o].t[:], p3.t[0:C, C - 1:C], gamc.t[:, c, h:h + 1], None, ALU.subtract), [p3.b, gamc.b], [kdc[o].b])
                        S.op("act", lambda e, o=o2[bi]: e.activation(kdc[o].t[:], kdc[o].t[:], AF.Exp), [kdc[o].b], [kdc[o].b])
                    if self.gdn_stage <= 4:
                        continue
                    for bi, c in enumerate(cs):
                        o = o2[bi]
                        pg_, pm_ = slot(), slot()
                        S.op("pe", lambda e, bi=bi, pg_=pg_: e.matmul(pg_.t[0:C, 0:C], knT[bi].t[:], knT[bi].t[:], start=True, stop=True), [knT[bi].b], [pg_.b])
                        S.op("dve", lambda e, h=h, bi=bi, c=c, o=o, pg_=pg_: e.scalar_tensor_tensor(out=X[o].t[:], in0=pg_.t[0:C, 0:C], scalar=nbetac.t[:, c, h:h + 1], in1=Dm[bi].t[:],
                                                                                              op0=ALU.mult, op1=ALU.mult), [pg_.b, nbetac.b, Dm[bi].b], [X[o].b])
                        S.op("pe", lambda e, bi=bi, o=o, pm_=pm_: e.matmul(pm_.t[0:C, 0:C], knT[bi].t[:], qnT[o].t[:], start=True, stop=True), [knT[bi].b, qnT[o].b], [pm_.b])
                        S.op("dve", lambda e, bi=bi, o=o, pm_=pm_: e.tensor_tensor(MT[o].t[:], pm_.t[0:C, 0:C], DT[bi].t[:], ALU.mult), [pm_.b, DT[bi].b], [MT[o].b])
                    for bi, c in enumerate(cs):
                        o = o2[bi]
                        px = slot()
                        S.op("pe", lambda e, o=o, px=px: e.transpose(px.t[0:C, 0:C], X[o].t[:], idf.t[0:C, 0:C]), [X[o].b, idf.b], [px.b])
                        S.op("act", lambda e, o=o, px=px: e.activation(XT[o].t[:], px.t[0:C, 0:C], AF.Copy), [px.b], [XT[o].b])
                        S.op("dve", lambda e, o=o, px=px: e.tensor_tensor(Y[o].t[:], px.t[0:C, 0:C], idf.t[0:C, 0:C], ALU.add), [px.b, idf.b], [Y[o].b])
                    if self.gdn_stage <= 5:
                        continue
                    for s_ in range(5):
                        for bi, c in enumerate(cs):
                            o = o2[bi]
                            pa, pbq = slot(), slot()
                            S.op("pe", lambda e, o=o, pa=pa: e.matmul(pa.t[0:C, 0:C], XT[o].t[:], X[o].t[:], start=True, stop=True), [XT[o].b, X[o].b], [pa.b])
                            if s_ < 4:
                                S.op("pe", lambda e, o=o, pbq=pbq: e.matmul(pbq.t[0:C, 0:C], X[o].t[:], XT[o].t[:], start=True, stop=True), [XT[o].b, X[o].b], [pbq.b])
                            S.op("act", lambda e, o=o, pa=pa: e.activation(X[o].t[:], pa.t[0:C, 0:C], AF.Copy), [pa.b], [X[o].b])
                            if s_ < 4:
                                S.op("act", lambda e, o=o, pbq=pbq: e.activation(XT[o].t[:], pbq.t[0:C, 0:C], AF.Copy), [pbq.b], [XT[o].b])
                        for bi, c in enumerate(cs):
                            o = o2[bi]
                            py = slot()
                            S.op("pe", lambda e, o=o, py=py: e.matmul(py.t[0:C, 0:C], X[o].t[:], Y[o].t[:], start=True, stop=True), [X[o].b, Y[o].b], [py.b])
                            S.op("dve", lambda e, o=o, py=py: e.tensor_tensor(Y[o].t[:], Y[o].t[:], py.t[0:C, 0:C], ALU.add), [py.b, Y[o].b], [Y[o].b])
                    if self.gdn_stage <= 6:
                        continue
                    for bi, c in enumerate(cs):
                        o = o2[bi]
                        S.op("dve", lambda e, h=h, bi=bi, c=c: e.tensor_scalar(Vb[bi].t[:], vt[bi].t[:], betac.t[:, c, h:h + 1], None, ALU.mult), [vt[bi].b, betac.b], [Vb[bi].b])
                        S.op("dve", lambda e, h=h, bi=bi, c=c: e.tensor_scalar(Kb[bi].t[:], kn[bi].t[:], begamc.t[:, c, h:h + 1], None, ALU.mult), [kn[bi].b, begamc.b], [Kb[bi].b])
                        S.op("dve", lambda e, bi=bi, o=o: e.tensor_scalar(Kd[o].t[:], kn[bi].t[:], kdc[o].t[:, 0:1], None, ALU.mult), [kn[bi].b, kdc[o].b], [Kd[o].b])
                        if self.gdn_stage <= 6.3:
                            continue
                        pu_, pw_ = slot(), slot()
                        S.op("pe", lambda e, bi=bi, o=o, pu_=pu_: e.matmul(pu_.t[0:C, :], Y[o].t[:], Vb[bi].t[:], start=True, stop=True), [Y[o].b, Vb[bi].b], [pu_.b])
                        S.op("act", lambda e, o=o, pu_=pu_: e.activation(U[o].t[:], pu_.t[0:C, :], AF.Copy), [pu_.b], [U[o].b])
                        if self.gdn_stage <= 6.5:
                            continue
                        S.op("pe", lambda e, bi=bi, o=o, pw_=pw_: e.matmul(pw_.t[:, 0:C], Kb[bi].t[:], Y[o].t[:], start=True, stop=True), [Y[o].b, Kb[bi].b], [pw_.b])
                        if self.gdn_stage <= 6.7:
                            continue
                        S.op("act", lambda e, o=o, pw_=pw_: e.activation(WmT[o].t[:], pw_.t[:, 0:C], AF.Copy), [pw_.b], [WmT[o].b])
                    if self.gdn_stage <= 7:
                        continue
                    for bi, c in enumerate(cs):
                        o = o2[bi]
                        k2 = c % 2
                        pws, pqs, pmv, psu = slot(), slot(), slot(), slot()
                        S.op("pe", lambda e, o=o, pws=pws: e.matmul(pws.t[0:C, :], WmT[o].t[:], St.t[:], start=True, stop=True), [WmT[o].b, St.b], [pws.b])
                        S.op("pe", lambda e, o=o, pqs=pqs: e.matmul(pqs.t[0:C, :], qnT[o].t[:], St.t[:], start=True, stop=True), [qnT[o].b, St.b], [pqs.b])
                        S.op("dve", lambda e, o=o, k2=k2, pws=pws: e.tensor_tensor(vnew[k2].t[:], U[o].t[:], pws.t[0:C, :], ALU.subtract), [U[o].b, pws.b], [vnew[k2].b])
                        S.op("pe", lambda e, o=o, k2=k2, pmv=pmv: e.matmul(pmv.t[0:C, :], MT[o].t[:], vnew[k2].t[:], start=True, stop=True), [MT[o].b, vnew[k2].b], [pmv.b])
                        S.op("pe", lambda e, o=o, k2=k2, psu=psu: e.matmul(psu.t[:, :], Kd[o].t[:], vnew[k2].t[:], start=True, stop=True), [Kd[o].b, vnew[k2].b], [psu.b])
                        S.op("dve", lambda e, o=o, psu=psu: e.scalar_tensor_tensor(out=St.t[:], in0=St.t[:], scalar=egl[o].t[:, 0:1], in1=psu.t[:, :], op0=ALU.mult, op1=ALU.add),
                             [St.b, egl[o].b, psu.b], [St.b])
                        S.op("act", lambda e, k2=k2, pmv=pmv: e.activation(mvs[k2].t[:], pmv.t[0:C, :], AF.Copy), [pmv.b], [mvs[k2].b])
                        S.op("dve", lambda e, h=h, k2=k2, c=c, pqs=pqs: e.scalar_tensor_tensor(out=osb[k2].t[:], in0=pqs.t[0:C, :], scalar=egamc.t[:, c, h:h + 1], in1=mvs[k2].t[:],
                                                                                        op0=ALU.mult, op1=ALU.add), [pqs.b, egamc.b, mvs[k2].b], [osb[k2].b])
                        S.op("act", lambda e, k2=k2: e.activation(ojunk[0].t[:], osb[k2].t[:], AF.Square, accum_out=oss[k2].t[:]), [osb[k2].b], [ojunk[0].b, oss[k2].b])
                        S.op("dve", lambda e, k2=k2: e.tensor_scalar(ors[k2].t[:], oss[k2].t[:], 1.0 / 128, EPS, ALU.mult, ALU.add), [oss[k2].b], [ors[k2].b])
                        S.op("act", lambda e, k2=k2: e.activation(ors[k2].t[:], ors[k2].t[:], AF.Sqrt), [ors[k2].b], [ors[k2].b])
                        S.op("dve", lambda e, k2=k2: e.reciprocal(ors[k2].t[:], ors[k2].t[:]), [ors[k2].b], [ors[k2].b])
                        S.op("dve", lambda e, k2=k2: e.scalar_tensor_tensor(out=osb[k2].t[:], in0=osb[k2].t[:], scalar=ors[k2].t[:, 0:1], in1=gnr.t[:], op0=ALU.mult, op1=ALU.mult),
                             [osb[k2].b, ors[k2].b, gnr.b], [osb[k2].b])
                        S.op("act", lambda e, k2=k2, z_=z_, bi=bi: e.activation(zs[k2].t[:], z_.t[:, bi, :], AF.Silu), [z_.b], [zs[k2].b])
                        dbgm = os.environ.get("KGDNDBG", "")
                        dsel = {"vn": vnew[k2], "u": U[o], "vb": Vb[bi], "kb": Kb[bi], "kd": Kd[o], "vt": vt[bi], "kn": kn[bi], "qn": qn[bi], "mv": mvs[k2]}.get(dbgm)
                        d64 = {"y": Y[o], "x": X[o], "mt": MT[o], "dm": Dm[bi], "dt": DT[bi]}.get(dbgm)
                        if d64 is not None:
                            S.op("dve", lambda e, k2=k2, d64=d64: e.tensor_copy(ofb[k2].t[:, 0:64], d64.t[:]), [osb[k2].b, zs[k2].b, d64.b], [ofb[k2].b])
                            S.op("dve", lambda e, k2=k2, d64=d64: e.tensor_copy(ofb[k2].t[:, 64:128], d64.t[:]), [osb[k2].b, zs[k2].b, d64.b], [ofb[k2].b])
                        elif dsel is not None:
                            S.op("dve", lambda e, k2=k2, dsel=dsel: e.tensor_copy(ofb[k2].t[:], dsel.t[:]), [osb[k2].b, zs[k2].b, dsel.b], [ofb[k2].b])
                        elif os.environ.get("KGDNDBG", "") == "z":
                            S.op("dve", lambda e, k2=k2: e.tensor_copy(ofb[k2].t[:], zs[k2].t[:]), [osb[k2].b, zs[k2].b], [ofb[k2].b])
                        elif os.environ.get("KGDNDBG", "") == "o":
                            S.op("dve", lambda e, k2=k2: e.tensor_copy(ofb[k2].t[:], osb[k2].t[:]), [osb[k2].b, zs[k2].b], [ofb[k2].b])
                        else:
                            S.op("dve", lambda e, k2=k2: e.tensor_tensor(ofb[k2].t[:], osb[k2].t[:], zs[k2].t[:], ALU.mult), [osb[k2].b, zs[k2].b], [ofb[k2].b])
                        pbt = self.ps[7]
                        pTr = pbt.t[:, :].bitcast(BF16)
                        stg = ost[(c // 8) % 2]
                        S.op("pe", lambda e, k2=k2, pTr=pTr: e.transpose(pTr[:, 0:C], ofb[k2].t[:], self.identb.t[0:C, 0:C]), [ofb[k2].b, self.identb.b], [pbt.b])
                        S.op("act", lambda e, stg=stg, pTr=pTr, c=c: e.activation(stg.t[:, (c % 8) * C:(c % 8 + 1) * C], pTr[:, 0:C], AF.Copy), [pbt.b], [stg.b])
                        if c % 8 == 7:
                            cc = c // 8
                            S.dma("sp", self.oT[8 + h][:, cc * 512:(cc + 1) * 512], stg.t[:], stg.c, reads=[stg.b])

    def phase_outproj_ffn(self, layer, wout, xsrc):
        S = self.S
        Wo = wout.rearrange("(k p) n -> p k n", p=128)
        Wu = self.ffn_w_up[layer].rearrange("(k p) n -> p k n", p=128)
        Wd = self.ffn_w_down[layer].rearrange("(c p) n -> p c n", p=128)
        oTv = self.oT.rearrange("f p t -> p f t")
        xrb = [Buf("xr%d" % i) for i in range(NT)]
        with ExitStack() as ph:
            gain = self.sb(ph, "gain", [128, D], F32, chan=True)
            S.dma("sp", gain.t[:], self.norm_ffn[layer, :].partition_broadcast(128), gain.c, writes=[gain.b])
            cwr = self.sb(ph, "cwr", [FC, 4, 128], F32, chan=True)
            cw = self.sb(ph, "cw", [128, 4, FC], F32)
            S.dma_group("sp", [(cwr.t[:, jj, :], self.ffn_conv_w[layer, jj, :].rearrange("(c p) -> c p", p=128)) for jj in range(3)]
                        + [(cwr.t[:, 3, :], self.ffn_conv_b[layer, :].rearrange("(c p) -> c p", p=128))], cwr.c, writes=[cwr.b])
            pb = self.ps[7]
            for jj in range(4):
                S.op("pe", lambda e, jj=jj: e.transpose(pb.t[:, jj * FC:(jj + 1) * FC], cwr.t[:, jj, :], self.identf.t[0:FC, 0:FC]),
                     [cwr.b, self.identf.b], [pb.b], inc=(jj == 3))
            S.op("dve", lambda e: e.tensor_copy(cw.t[:], pb.t[:, 0:4 * FC].rearrange("p (j c) -> p j c", c=FC)), [pb.b], [cw.b])
            halo = self.sb(ph, "halo", [128, FC, 2], F32)
            S.op("dve", lambda e: e.memset(halo.t[:], 0.0), [], [halo.b])
            big = self.sb(ph, "big", [128, FC * TG], BF16, chan=True)
            actT_v = big.t[:].rearrange("p (c t) -> p c t", t=TG)
            bigf = big.t[:].bitcast(F32)
            x1v = [bigf[:, tt * D:(tt + 1) * D] for tt in range(4)]
            hs = self.sb(ph, "hT", [128, KC, TG], BF16, chan=True)
            xi_ = [self.sb(ph, "xi", [128, 512], F32, chan=True) for _ in range(2)]
            xo_ = [self.sb(ph, "xo", [128, 512], F32, chan=True) for _ in range(2)]
            xn = self.sb(ph, "xn", [128, D], BF16)
            st_ = [self.sb(ph, "ss", [128, 1], F32), self.sb(ph, "rs", [128, 1], F32)]
            wu = [self.sb(ph, "wu", [128, 16, 256], BF16, chan=True) for _ in range(2)]
            wg = [self.sb(ph, "wg", [128, 16, 256], BF16, chan=True) for _ in range(2)]
            wd = [self.sb(ph, "wd", [128, 11, 512], BF16, chan=True) for _ in range(2)]
            gsb = [self.sb(ph, "gsb", [128, TG + 2], F32) for _ in range(2)]
            cacc = [self.sb(ph, "cacc", [128, TG], F32) for _ in range(2)]
            sg = [self.sb(ph, "sg", [128, TG], F32) for _ in range(2)]
            nwo = nwu = nwd = nxi = nxo = 0
            for g in range(NG):
                tok = slice(g * TG, (g + 1) * TG)
                S.dma("sp", hs.t[:], oTv[:, :, tok], hs.c, writes=[hs.b])
                for nb_ in range(D // 256):
                    w = (wu + wg)[nwo % 4]
                    nwo += 1
                    S.dma("pool", w.t[:], Wo[:, :, nb_ * 256:(nb_ + 1) * 256], w.c, writes=[w.b])
                    for tt in range(4):
                        tile = g * 4 + tt
                        pbk = self.ps[(nb_ * 4 + tt) % 4]
                        xi = xi_[nxi % 2]
                        nxi += 1
                        S.dma("sp", xi.t[:, 0:256], xsrc[tile * 128:(tile + 1) * 128, nb_ * 256:(nb_ + 1) * 256], xi.c, reads=[xrb[tile]], writes=[xi.b])
                        self.mm(pbk.t[:, 0:256], pbk.b, [(hs.t[:, f, tt * 128:(tt + 1) * 128], w.t[:, f, :]) for f in range(16)], [hs.b, w.b])
                        S.op("dve", lambda e, tt=tt, pbk=pbk, xi=xi, nb_=nb_: e.tensor_tensor(x1v[tt][:, nb_ * 256:(nb_ + 1) * 256], pbk.t[:, 0:256], xi.t[:, 0:256], ALU.add),
                             [pbk.b, xi.b], [big.b])
                for tt in range(4):
                    tile = g * 4 + tt
                    S.dma("sp", self.xr[tile * 128:(tile + 1) * 128, :], x1v[tt], big.c, reads=[big.b], writes=[xrb[tile]])
                self.norm_x1(x1v, big.b, gain, xn, st_, hs)
                for ub in range(D_FF // 256):
                    wu_, wg_ = wu[nwu % 2], wg[nwu % 2]
                    nwu += 1
                    S.dma("pool", wu_.t[:], Wu[:, :, ub * 256:(ub + 1) * 256], wu_.c, writes=[wu_.b])
                    S.dma("pool", wg_.t[:], Wu[:, :, D_FF + ub * 256:D_FF + (ub + 1) * 256], wg_.c, writes=[wg_.b])
                    for cc in range(2):
                        c = ub * 2 + cc
                        pu = self.ps[(c % 2) * 2]
                        pg = self.ps[(c % 2) * 2 + 1]
                        self.mm(pg.t[:, :], pg.b, [(wg_.t[:, k, cc * 128:(cc + 1) * 128], hs.t[:, k, :]) for k in range(KC)], [wg_.b, hs.b])
                        self.mm(pu.t[:, :], pu.b, [(wu_.t[:, k, cc * 128:(cc + 1) * 128], hs.t[:, k, :]) for k in range(KC)], [wu_.b, hs.b])
                        gs, ca, sg_ = gsb[c % 2], cacc[c % 2], sg[c % 2]
                        S.op("act", lambda e, gs=gs, pg=pg: e.activation(gs.t[:, 2:TG + 2], pg.t[:, :], AF.Copy), [pg.b], [gs.b])
                        S.op("dve", lambda e, gs=gs, c=c: e.tensor_copy(gs.t[:, 0:2], halo.t[:, c, :]), [halo.b, gs.b], [gs.b])
                        S.op("dve", lambda e, gs=gs, ca=ca, c=c: e.tensor_scalar(ca.t[:], gs.t[:, 2:TG + 2], cw.t[:, 2, c:c + 1], cw.t[:, 3, c:c + 1], ALU.mult, ALU.add),
                             [gs.b, cw.b], [ca.b])
                        S.op("dve", lambda e, gs=gs, ca=ca, c=c: e.scalar_tensor_tensor(out=ca.t[:], in0=gs.t[:, 1:TG + 1], scalar=cw.t[:, 1, c:c + 1], in1=ca.t[:], op0=ALU.mult, op1=ALU.add),
                             [gs.b, cw.b, ca.b], [ca.b])
                        S.op("dve", lambda e, gs=gs, ca=ca, c=c: e.scalar_tensor_tensor(out=ca.t[:], in0=gs.t[:, 0:TG], scalar=cw.t[:, 0, c:c + 1], in1=ca.t[:], op0=ALU.mult, op1=ALU.add),
                             [gs.b, cw.b, ca.b], [ca.b])
                        S.op("dve", lambda e, gs=gs, c=c: e.tensor_copy(halo.t[:, c, :], gs.t[:, TG:TG + 2]), [gs.b, halo.b], [halo.b])
                        S.op("act", lambda e, sg_=sg_, ca=ca: e.activation(sg_.t[:], ca.t[:], AF.Silu), [ca.b], [sg_.b])
                        S.op("dve", lambda e, sg_=sg_, pu=pu, c=c: e.tensor_tensor(actT_v[:, c, :], sg_.t[:], pu.t[:, :], ALU.mult), [sg_.b, pu.b], [big.b])
                for nb_ in range(D // 512):
                    accs = [self.ps[4 + tt] for tt in range(4)]
                    for qd in range(4):
                        w = wd[nwd % 2]
                        nwd += 1
                        S.dma("pool", w.t[:], Wd[:, qd * 11:(qd + 1) * 11, nb_ * 512:(nb_ + 1) * 512], w.c, writes=[w.b])
                        for tt in range(4):
                            for ci in range(11):
                                c = qd * 11 + ci
                                S.op("pe", lambda e, tt=tt, ci=ci, c=c, w=w, acc=accs[tt]: e.matmul(
                                    acc.t[:, :], actT_v[:, c, tt * 128:(tt + 1) * 128], w.t[:, ci, :], start=(c == 0), stop=(c == FC - 1)),
                                    [big.b, w.b], [accs[tt].b], inc=(ci == 10))
                    for tt in range(4):
                        tile = g * 4 + tt
                        xi = xi_[nxi % 2]
                        nxi += 1
                        xo = xo_[nxo % 2]
                        nxo += 1
                        S.dma("sp", xi.t[:], self.xr[tile * 128:(tile + 1) * 128, nb_ * 512:(nb_ + 1) * 512], xi.c, reads=[xrb[tile]], writes=[xi.b])
                        S.op("dve", lambda e, tt=tt, xo=xo, xi=xi: e.tensor_tensor(xo.t[:], accs[tt].t[:, :], xi.t[:], ALU.add),
                             [accs[tt].b, xi.b], [xo.b])
                        S.dma("sp", self.xr[tile * 128:(tile + 1) * 128, nb_ * 512:(nb_ + 1) * 512], xo.t[:], xo.c, reads=[xo.b], writes=[xrb[tile]])

    def norm_x1(self, x1v, xb, gain, n, st_, hs):
        S = self.S
        ss, rs = st_
        for tt in range(4):
            xv = x1v[tt]
            S.op("act", lambda e, xv=xv: e.activation(n.t[:], xv, AF.Square, accum_out=ss.t[:]), [xb], [n.b, ss.b])
            S.op("dve", lambda e: e.tensor_scalar(rs.t[:], ss.t[:], 1.0 / D, EPS, ALU.mult, ALU.add), [ss.b], [rs.b])
            S.op("act", lambda e: e.activation(rs.t[:], rs.t[:], AF.Sqrt), [rs.b], [rs.b])
            S.op("dve", lambda e: e.reciprocal(rs.t[:], rs.t[:]), [rs.b], [rs.b])
            S.op("dve", lambda e, xv=xv: e.scalar_tensor_tensor(out=n.t[:], in0=xv, scalar=rs.t[:, 0:1], in1=gain.t[:],
                                                               op0=ALU.mult, op1=ALU.mult), [xb, rs.b, gain.b], [n.b])
            for half in range(2):
                pb = self.ps[2 + half]
                pT = pb.t[:, :].bitcast(BF16)
                for kk in range(8):
                    k = half * 8 + kk
                    S.op("pe", lambda e, k=k, kk=kk, pT=pT: e.transpose(pT[:, kk * 128:(kk + 1) * 128], n.t[:, k * 128:(k + 1) * 128], self.identb.t[:]),
                         [n.b, self.identb.b], [pb.b], inc=(kk == 7))
                dst = hs.t[:, half * 8:(half + 1) * 8, tt * 128:(tt + 1) * 128]
                src = pT.rearrange("p (k t) -> p k t", t=128)
                if half == 0:
                    S.op("act", lambda e, dst=dst, src=src: e.activation(dst, src, AF.Copy), [pb.b], [hs.b])
                else:
                    S.op("dve", lambda e, dst=dst, src=src: e.tensor_copy(dst, src), [pb.b], [hs.b])

    def phase_final(self, xsrc):
        S = self.S
        with ExitStack() as ph:
            gain = self.sb(ph, "gain", [128, D], F32, chan=True)
            S.dma("sp", gain.t[:], self.norm_final.partition_broadcast(128), gain.c, writes=[gain.b])
            xt = [self.sb(ph, "xt", [128, D], F32, chan=True) for _ in range(3)]
            yo = [self.sb(ph, "yo", [128, D], F32, chan=True) for _ in range(2)]
            sq = self.sb(ph, "sq", [128, D], BF16)
            ss = [self.sb(ph, "ss", [128, 1], F32) for _ in range(2)]
            rs = [self.sb(ph, "rs", [128, 1], F32) for _ in range(2)]
            for tile in range(NT):
                x = xt[tile % 3]
                s_, r_, y_ = ss[tile % 2], rs[tile % 2], yo[tile % 2]
                S.dma("sp", x.t[:], xsrc[tile * 128:(tile + 1) * 128, :], x.c, writes=[x.b])
                S.op("act", lambda e, x=x, s_=s_: e.activation(sq.t[:], x.t[:], AF.Square, accum_out=s_.t[:]), [x.b], [sq.b, s_.b])
                S.op("dve", lambda e, s_=s_, r_=r_: e.tensor_scalar(r_.t[:], s_.t[:], 1.0 / D, EPS, ALU.mult, ALU.add), [s_.b], [r_.b])
                S.op("act", lambda e, r_=r_: e.activation(r_.t[:], r_.t[:], AF.Sqrt), [r_.b], [r_.b])
                S.op("dve", lambda e, r_=r_: e.reciprocal(r_.t[:], r_.t[:]), [r_.b], [r_.b])
                S.op("dve", lambda e, x=x, r_=r_, y_=y_: e.scalar_tensor_tensor(out=y_.t[:], in0=x.t[:], scalar=r_.t[:, 0:1], in1=gain.t[:],
                                                                              op0=ALU.mult, op1=ALU.mult), [x.b, r_.b, gain.b], [y_.b])
                S.dma("sp", self.y[tile * 128:(tile + 1) * 128, :], y_.t[:], y_.c, reads=[y_.b])


_INPUT_NAMES = ["rel_bias", "norm_mix", "norm_ffn", "norm_final", "ev_w_in", "ev_b_forget", "ev_sinks", "ev_w_out",
                "od_w_in", "od_cmp_pos", "od_cmp_w1", "od_cmp_w2", "od_conv_w", "od_a_log", "od_dt_bias", "od_gdn_norm", "od_w_out",
                "ffn_w_up", "ffn_conv_w", "ffn_conv_b", "ffn_w_down"]


def kernel(**inputs):
    b = Builder()
    nc = b.build()
    consts = host_consts()
    x = np.ascontiguousarray(inputs["x"], dtype=np.float32)
    shared = {k: np.ascontiguousarray(inputs[k], dtype=np.float32) for k in _INPUT_NAMES}
    shared.update(consts)
    in_maps = []
    for c in range(N_CORES):
        m = dict(shared)
        m["x"] = x[c]
        in_maps.append(m)
    res = run_bass_kernel_spmd(nc, in_maps, core_ids=list(range(N_CORES)))
    return np.stack([np.asarray(r["y"]) for r in res.results], axis=0).astype(np.float32)
```

```python
import math
import os
import numpy as np
from contextlib import ExitStack
import concourse.bass as bass
import concourse.mybir as mybir
from concourse.bass_utils import run_bass_kernel_spmd

F32 = mybir.dt.float32
BF16 = mybir.dt.bfloat16
AF = mybir.ActivationFunctionType
ALU = mybir.AluOpType
AX = mybir.AxisListType

D = 2048
T = 4096
KC = D // 128
NT = T // 128
TG = 512
NG = T // TG
DEPTH = 4
HD = 128
D_FF = 5632
FC = D_FF // 128
EVEN_COLS = 4616
ODD_COLS = 6696
SCALE = HD ** -0.5
EPS = 1e-6
NEG = -30000.0
N_CORES = 8


class Buf:
    __slots__ = ("name", "w", "r", "excl")

    def __init__(self, name="", excl=False):
        self.name = name
        self.w = {}
        self.r = {}
        self.excl = excl


class Chan:
    __slots__ = ("sem", "cnt", "key")

    def __init__(self, sem, key):
        self.sem = sem
        self.cnt = 0
        self.key = key


class Sched:
    ENG = ("pe", "act", "dve", "pool", "sp")

    def __init__(self, nc, stack):
        self.nc = nc
        self.stack = stack
        self.cnt = {}
        self.semobj = {}
        for e in self.ENG:
            self.semobj[("e", e)] = stack.enter_context(nc.semaphore("s_" + e))
            self.cnt[e] = 0
        self.known = {e: {} for e in self.ENG}
        self.prog = {e: [] for e in self.ENG}
        self.chans = []

    def chan(self):
        key = ("c", len(self.chans))
        sem = self.stack.enter_context(self.nc.semaphore("c%d" % key[1]))
        self.semobj[key] = sem
        c = Chan(sem, key)
        self.chans.append(c)
        return c

    def _collect(self, e, reads, writes, extra=()):
        need = {}
        for b in reads:
            for k, v in b.w.items():
                if need.get(k, 0) < v:
                    need[k] = v
            if b.excl:
                for k, v in b.r.items():
                    if need.get(k, 0) < v:
                        need[k] = v
        for b in writes:
            for k, v in b.w.items():
                if need.get(k, 0) < v:
                    need[k] = v
            for k, v in b.r.items():
                if need.get(k, 0) < v:
                    need[k] = v
        for k, v in extra:
            if need.get(k, 0) < v:
                need[k] = v
        if e == "pe":
            need.pop(("e", "pe"), None)
        waits = []
        kn = self.known[e]
        for k, v in need.items():
            if kn.get(k, 0) >= v:
                continue
            kn[k] = v
            waits.append((k, v))
        return waits

    @staticmethod
    def _commit(ev, reads, writes):
        k, v = ev
        for b in reads:
            if b.r.get(k, 0) < v:
                b.r[k] = v
        for b in writes:
            if b.w.get(k, 0) < v:
                b.w[k] = v

    def op(self, e, fn, reads=(), writes=(), inc=True):
        waits = self._collect(e, reads, writes)
        ev = (("e", e), self.cnt[e] + 1)
        if inc:
            self.cnt[e] += 1
        self.prog[e].append((waits, fn, (("e", e), 1) if inc else None))
        self._commit(ev, reads, writes)

    def dma(self, q, out, in_, chan, reads=(), writes=()):
        self.dma_group(q, [(out, in_)], chan, reads, writes)

    def dma_group(self, q, pairs, chan, reads=(), writes=()):
        extra = [(chan.key, chan.cnt)] if chan.cnt > 0 else []
        waits = self._collect(q, reads, writes, extra)
        for (o, i) in pairs:
            chan.cnt += 16
            fn = lambda eng, o=o, i=i: eng.dma_start(out=o, in_=i)
            self.prog[q].append((waits, fn, (chan.key, 16)))
            waits = []
        self._commit((chan.key, chan.cnt), reads, writes)

    def barrier(self):
        evs = [(("e", o), self.cnt[o]) for o in self.ENG if self.cnt[o] > 0]
        evs += [(c.key, c.cnt) for c in self.chans if c.cnt > 0]
        for e in self.ENG:
            waits = self._collect(e, (), (), evs)
            if e == "pe":
                pass
            if waits:
                self.prog[e].append((waits, None, None))

    def emit(self):
        nc = self.nc
        with nc.Block() as block:
            def run(e):
                def body(eng):
                    for waits, fn, inc in self.prog[e]:
                        for k, v in waits:
                            eng.wait_ge(self.semobj[k], v)
                        if fn is None:
                            continue
                        ins = fn(eng)
                        if inc is not None:
                            ins.then_inc(self.semobj[inc[0]], inc[1])
                return body
            block.tensor(run("pe"))
            block.scalar(run("act"))
            block.vector(run("dve"))
            block.gpsimd(run("pool"))
            block.sync(run("sp"))


class Tl:
    __slots__ = ("t", "b", "c")

    def __init__(self, t, b, c=None):
        self.t = t
        self.b = b
        self.c = c


def _t5_bucket(dist):
    n = np.maximum(dist, 0)
    lr = np.log(np.maximum(n, 1).astype(np.float32) / np.float32(16)) / np.float32(math.log(128 / 16))
    large = np.minimum(16 + (lr * np.float32(16)).astype(np.int32), 31)
    return np.where(n < 16, n, large)


def host_consts():
    k = np.arange(128)[:, None]
    q = np.arange(128)[None, :]
    oh = np.zeros((2, 32, 128, 128), np.float32)
    for o in range(2):
        dist = q - k + 128 * o
        bk = _t5_bucket(dist)
        for b in range(32):
            oh[o, b] = ((bk == b) & (dist >= 0)).astype(np.float32)
    c = {}
    c["c_oh"] = oh.transpose(2, 0, 1, 3).reshape(128, 64 * 128).copy()
    c["c_ident"] = np.eye(128, dtype=np.float32)
    c["c_causal01"] = (q >= k).astype(np.float32)
    c["c_causalneg"] = np.where(q >= k, 0.0, NEG).astype(np.float32)
    c["c_anticausalneg"] = np.where(q < k, 0.0, NEG).astype(np.float32)
    rm = np.ones((8, T), np.float32)
    rm[:, ::64] = 0.0
    c["c_rmask"] = rm
    ii = np.arange(64)[:, None]
    jj = np.arange(64)[None, :]
    c["c_gdn_pmask"] = np.where(ii > jj, 0.0, -NEG).astype(np.float32)
    c["c_gdn_nmask"] = np.where(jj >= ii, 0.0, NEG).astype(np.float32)
    cl = np.arange(128)[:, None, None]
    ti = np.arange(32)[None, :, None]
    cm = np.zeros((128, 64, 128), np.float32)
    qq = np.arange(128)[None, :]
    for i in range(32):
        for ct in range(2):
            cc = ct * 128 + np.arange(128)[:, None]
            cm[:, i * 2 + ct, :] = ((cc <= 254) & (16 * cc + 31 <= 128 * i + qq)).astype(np.float32)
    c["c_cmask"] = cm.reshape(128, 64 * 128)
    wi = np.zeros((256, 64), np.float32)
    for cidx in range(255):
        for j in range(64):
            wi[cidx, j] = sum(1 for m in range(4 * j, 4 * j + 4) if m == cidx or m == cidx + 1)
    c["c_wimp"] = wi.reshape(2, 128, 64).transpose(1, 0, 2).reshape(128, 128).copy()
    keep = np.zeros((128, 32, 64), np.float32)
    add = np.zeros((128, 32, 64), np.float32)
    jv = np.arange(64)[None, :]
    for i in range(32):
        cur = (2 * i + (np.arange(128) >= 64).astype(np.int64))[:, None]
        forced = (jv == 0) | (jv == cur) | (jv == cur - 1)
        fut = (jv > cur) & ~forced
        keep[:, i, :] = (~forced & ~fut).astype(np.float32)
        add[:, i, :] = np.where(forced, 1e9, np.where(fut, -1e9, 0.0))
    c["c_selkeep"] = keep.reshape(128, 32 * 64)
    c["c_seladd"] = add.reshape(128, 32 * 64)
    es = np.zeros((64, 32, 128), np.float32)
    for jb in range(32):
        es[2 * jb, jb, 0:64] = 1.0
        es[2 * jb + 1, jb, 64:128] = 1.0
    c["c_esel"] = es.reshape(64, 32 * 128)
    return c


class Builder:
    def __init__(self, layers=None, debug=False):
        self.layers = list(range(DEPTH)) if layers is None else list(layers)
        self.debug = debug
        self.nc = bass.Bass("TRN2", target_bir_lowering=False)
        self.uid = 0
        self.free_chans = []
        import os
        self.skip = set(os.environ.get("KSKIP", "").split(","))
        self.feed = set(os.environ.get("KFEED", "").split(","))
        self.gdn_heads = int(os.environ.get("KGDNH", "8"))
        self.gdn_stage = float(os.environ.get("KGDNS", "99"))
        self.small = os.environ.get("KSMALL", "") == "1"

    def din(self, name, shape, dt=F32):
        if self.small and name in ("ev_w_in", "ev_w_out", "od_w_in", "od_w_out", "ffn_w_up", "ffn_w_down"):
            shape = [1, 1]
        return self.nc.dram_tensor(name, list(shape), dt, kind="ExternalInput").ap()

    def dscr(self, name, shape, dt):
        kind = "ExternalOutput" if (self.debug and name in self.debug) else "Internal"
        if name in self.feed:
            kind = "ExternalInput"
        return self.nc.dram_tensor(name, list(shape), dt, kind=kind).ap()

    def sb(self, ph, name, shape, dt, chan=False):
        self.uid += 1
        t = ph.enter_context(self.nc.sbuf_tensor("%s_%d" % (name, self.uid), list(shape), dt))
        c = None
        if chan is True:
            c = self.free_chans.pop() if self.free_chans else self.S.chan()
            ph.callback(self.free_chans.append, c)
        return Tl(t, Buf(name), c)

    def mm(self, out_ap, out_buf, pairs, reads):
        n = len(pairs)
        for i, (l, r) in enumerate(pairs):
            self.S.op("pe", lambda e, l=l, r=r, i=i: e.matmul(out_ap, l, r, start=(i == 0), stop=(i == n - 1)),
                      reads, [out_buf], inc=(i == n - 1))

    def build(self):
        nc = self.nc
        self.x = self.din("x", [T, D])
        self.rel_bias = self.din("rel_bias", [32, 8])
        self.norm_mix = self.din("norm_mix", [DEPTH, D])
        self.norm_ffn = self.din("norm_ffn", [DEPTH, D])
        self.norm_final = self.din("norm_final", [D])
        self.ev_w_in = self.din("ev_w_in", [2, D, EVEN_COLS])
        self.ev_b_forget = self.din("ev_b_forget", [2, 8])
        self.ev_sinks = self.din("ev_sinks", [2, 8])
        self.ev_w_out = self.din("ev_w_out", [2, D, D])
        self.od_w_in = self.din("od_w_in", [2, D, ODD_COLS])
        self.od_cmp_pos = self.din("od_cmp_pos", [2, 2, 32, 128])
        self.od_cmp_w1 = self.din("od_cmp_w1", [2, 2, 4096, 256])
        self.od_cmp_w2 = self.din("od_cmp_w2", [2, 2, 256, 128])
        self.od_conv_w = self.din("od_conv_w", [2, 4, 3072])
        self.od_a_log = self.din("od_a_log", [2, 8])
        self.od_dt_bias = self.din("od_dt_bias", [2, 8])
        self.od_gdn_norm = self.din("od_gdn_norm", [2, 128])
        self.od_w_out = self.din("od_w_out", [2, D, D])
        self.ffn_w_up = self.din("ffn_w_up", [DEPTH, D, 2 * D_FF])
        self.ffn_conv_w = self.din("ffn_conv_w", [DEPTH, 3, D_FF])
        self.ffn_conv_b = self.din("ffn_conv_b", [DEPTH, D_FF])
        self.ffn_w_down = self.din("ffn_w_down", [DEPTH, D_FF, D])
        self.c_oh = self.din("c_oh", [128, 64 * 128])
        self.c_ident = self.din("c_ident", [128, 128])
        self.c_causal01 = self.din("c_causal01", [128, 128])
        self.c_causalneg = self.din("c_causalneg", [128, 128])
        self.c_anticausalneg = self.din("c_anticausalneg", [128, 128])
        self.c_rmask = self.din("c_rmask", [8, T])
        self.c_gdn_pmask = self.din("c_gdn_pmask", [64, 64])
        self.c_gdn_nmask = self.din("c_gdn_nmask", [64, 64])
        self.c_cmask = self.din("c_cmask", [128, 64 * 128])
        self.c_wimp = self.din("c_wimp", [128, 128])
        self.c_selkeep = self.din("c_selkeep", [128, 32 * 64])
        self.c_seladd = self.din("c_seladd", [128, 32 * 64])
        self.c_esel = self.din("c_esel", [64, 32 * 128])
        self.y = nc.dram_tensor("y", [T, D], F32, kind="ExternalOutput").ap()
        self.xr = self.dscr("xr", [T, D], F32)
        self.qaT = self.dscr("qaT", [8, 128, T], BF16)
        self.kaT = self.dscr("kaT", [2, 128, T], BF16)
        self.va = self.dscr("va", [T, 256], BF16)
        self.qbT = self.dscr("qbT", [8, 128, T], BF16)
        self.kbT = self.dscr("kbT", [8, 128, T], BF16)
        self.vb = self.dscr("vb", [T, 1024], BF16)
        self.fT = self.dscr("fT", [8, T], F32)
        self.csd = self.dscr("csd", [8, T], F32)
        self.oT = self.dscr("oT", [16, 128, T], BF16)
        self.qcT = self.dscr("qcT", [8, 128, T], BF16)
        self.kcmpT = self.dscr("kcmpT", [2, 128, T], BF16)
        self.vcmpT = self.dscr("vcmpT", [2, 128, T], BF16)
        self.kselT = self.dscr("kselT", [2, 128, T], BF16)
        self.kwinT = self.dscr("kwinT", [2, 128, T], BF16)
        self.vsel = self.dscr("vsel", [T, 256], BF16)
        self.vwin = self.dscr("vwin", [T, 256], BF16)
        self.gates = self.dscr("gates", [T, 24], F32)
        self.qdT = self.dscr("qdT", [8, 128, T], F32)
        self.kdT = self.dscr("kdT", [8, 128, T], F32)
        self.vdT = self.dscr("vdT", [8, 128, T], F32)
        self.baT = self.dscr("baT", [16, T], F32)
        self.zg = self.dscr("zg", [T, 1024], F32)

        with ExitStack() as st:
            self.S = S = Sched(nc, st)
            self.ps = [Tl(st.enter_context(nc.psum_tensor("ps%d" % i, [128, 512], F32)), Buf("ps%d" % i, excl=True)) for i in range(8)]
            self.setup_consts(st)
            xsrc = self.x
            for layer in self.layers:
                j = layer // 2
                if layer % 2 == 0:
                    self.phase_inproj_even(layer, j, xsrc)
                    S.barrier()
                    self.phase_swa(j)
                    S.barrier()
                    self.phase_fox(j)
                    S.barrier()
                    wout = self.ev_w_out[j]
                else:
                    if "inproj" not in self.skip:
                        self.phase_inproj_odd(layer, j, xsrc)
                        S.barrier()
                    if "nsa" not in self.skip:
                        self.phase_nsa(j)
                        S.barrier()
                    if "gdn" not in self.skip:
                        self.phase_gdn(j)
                        S.barrier()
                    wout = self.od_w_out[j]
                if "ffn" not in self.skip:
                    self.phase_outproj_ffn(layer, wout, xsrc)
                    S.barrier()
                xsrc = self.xr
            if "final" not in self.skip:
                self.phase_final(xsrc)
                S.barrier()
            S.emit()
        return nc

    def setup_consts(self, st):
        nc, S = self.nc, self.S
        self.identf = self.sb(st, "identf", [128, 128], F32, chan=True)
        self.identb = self.sb(st, "identb", [128, 128], BF16)
        self.causal01 = self.sb(st, "causal01", [128, 128], BF16)
        self.onesrow = self.sb(st, "onesrow", [1, 128], BF16)
        self.bias0 = self.sb(st, "bias0", [128, 8, 128], F32)
        self.bias1w = self.sb(st, "bias1w", [128, 8, 128], F32)
        self.nb0 = self.sb(st, "nb0", [128, 8, 128], F32)
        self.nb1 = self.sb(st, "nb1", [128, 8, 128], F32)
        self.acneg = self.sb(st, "acneg", [128, 128], F32, chan=True)
        S.dma("sp", self.acneg.t[:], self.c_anticausalneg[:, :], self.acneg.c, writes=[self.acneg.b])
        S.dma("sp", self.identf.t[:], self.c_ident[:, :], self.identf.c, writes=[self.identf.b])
        S.op("dve", lambda e: e.tensor_copy(self.identb.t[:], self.identf.t[:]), [self.identf.b], [self.identb.b])
        S.op("dve", lambda e: e.memset(self.onesrow.t[:], 1.0), [], [self.onesrow.b])
        with ExitStack() as ph:
            oh = self.sb(ph, "oh", [128, 64 * 128], F32, chan=True)
            rbb = self.sb(ph, "rbb", [128, 256], F32, chan=True)
            cz = self.sb(ph, "cz", [128, 128], F32, chan=True)
            cn = self.sb(ph, "cn", [128, 128], F32, chan=True)
            an = self.sb(ph, "an", [128, 128], F32, chan=True)
            S.dma("sp", oh.t[:], self.c_oh[:, :], oh.c, writes=[oh.b])
            S.dma("sp", rbb.t[:], self.rel_bias.rearrange("b h -> (b h)").partition_broadcast(128), rbb.c, writes=[rbb.b])
            S.dma("sp", cz.t[:], self.c_causal01[:, :], cz.c, writes=[cz.b])
            S.dma("sp", cn.t[:], self.c_causalneg[:, :], cn.c, writes=[cn.b])
            S.dma("sp", an.t[:], self.c_anticausalneg[:, :], an.c, writes=[an.b])
            S.op("dve", lambda e: e.tensor_copy(self.causal01.t[:], cz.t[:]), [cz.b], [self.causal01.b])
            for h in range(8):
                for o, (dst, base) in enumerate(((self.bias0, cn), (self.bias1w, an))):
                    for b in range(32):
                        src1 = base.t[:] if b == 0 else dst.t[:, h, :]
                        col = b * 8 + h
                        S.op("dve", lambda e, o=o, b=b, h=h, dst=dst, src1=src1, col=col: e.scalar_tensor_tensor(
                            out=dst.t[:, h, :], in0=oh.t[:, (o * 32 + b) * 128:(o * 32 + b + 1) * 128],
                            scalar=rbb.t[:, col:col + 1], in1=src1, op0=ALU.mult, op1=ALU.add),
                            [oh.b, rbb.b, base.b, dst.b], [dst.b])
            for h in range(8):
                c31 = rbb.t[:, 31 * 8 + h:31 * 8 + h + 1]
                S.op("dve", lambda e, h=h: e.tensor_tensor(self.nb1.t[:, h, :], self.bias1w.t[:, h, :], an.t[:], ALU.subtract), [self.bias1w.b, an.b], [self.nb1.b])
                S.op("dve", lambda e, h=h, c31=c31: e.tensor_scalar(self.nb1.t[:, h, :], self.nb1.t[:, h, :], c31, None, ALU.subtract), [self.nb1.b, rbb.b], [self.nb1.b])
                S.op("dve", lambda e, h=h, c31=c31: e.tensor_scalar(self.nb0.t[:, h, :], self.bias0.t[:, h, :], c31, None, ALU.subtract), [self.bias0.b, rbb.b], [self.nb0.b])
            S.barrier()

    def norm_group(self, g, xsrc, gain, xt, sq, xn, st_, hs, keep_x=None):
        S = self.S
        for tt in range(4):
            tile = g * 4 + tt
            x = xt[tile % len(xt)]
            S.dma("sp", x.t[:], xsrc[tile * 128:(tile + 1) * 128, :], x.c, writes=[x.b])
            ss, rs = st_[0], st_[1]
            S.op("act", lambda e, x=x: e.activation(sq.t[:], x.t[:], AF.Square, accum_out=ss.t[:]), [x.b], [sq.b, ss.b])
            S.op("dve", lambda e: e.tensor_scalar(rs.t[:], ss.t[:], 1.0 / D, EPS, ALU.mult, ALU.add), [ss.b], [rs.b])
            S.op("act", lambda e: e.activation(rs.t[:], rs.t[:], AF.Sqrt), [rs.b], [rs.b])
            S.op("dve", lambda e: e.reciprocal(rs.t[:], rs.t[:]), [rs.b], [rs.b])
            n = xn[tile % len(xn)]
            S.op("dve", lambda e, x=x, n=n: e.scalar_tensor_tensor(out=n.t[:], in0=x.t[:], scalar=rs.t[:, 0:1], in1=gain.t[:],
                                                                  op0=ALU.mult, op1=ALU.mult), [x.b, rs.b, gain.b], [n.b])
            for half in range(2):
                pb = self.ps[6 + half]
                pT = pb.t[:, :].bitcast(BF16)
                for kk in range(8):
                    k = half * 8 + kk
                    S.op("pe", lambda e, n=n, k=k, kk=kk, pT=pT: e.transpose(pT[:, kk * 128:(kk + 1) * 128], n.t[:, k * 128:(k + 1) * 128], self.identb.t[:]),
                         [n.b, self.identb.b], [pb.b], inc=(kk == 7))
                eng = "act" if half == 0 else "dve"
                dst = hs.t[:, half * 8:(half + 1) * 8, tt * 128:(tt + 1) * 128]
                src = pT.rearrange("p (k t) -> p k t", t=128)
                if eng == "act":
                    S.op("act", lambda e, dst=dst, src=src: e.activation(dst, src, AF.Copy), [pb.b], [hs.b])
                else:
                    S.op("dve", lambda e, dst=dst, src=src: e.tensor_copy(dst, src), [pb.b], [hs.b])

    def inproj(self, gain_src, W2d, blocks, xsrc):
        S = self.S
        W = W2d.rearrange("(k p) n -> p k n", p=128)
        with ExitStack() as ph:
            gain = self.sb(ph, "gain", [128, D], F32, chan=True)
            S.dma("sp", gain.t[:], gain_src.partition_broadcast(128), gain.c, writes=[gain.b])
            xt = [self.sb(ph, "xt", [128, D], F32, chan=True) for _ in range(2)]
            sq = self.sb(ph, "sq", [128, D], BF16)
            xn = [self.sb(ph, "xn", [128, D], BF16) for _ in range(2)]
            st_ = [self.sb(ph, "ss", [128, 1], F32), self.sb(ph, "rs", [128, 1], F32)]
            hT = [self.sb(ph, "hT", [128, KC, TG], BF16) for _ in range(2)]
            wt = [self.sb(ph, "wt", [128, KC, 512], BF16, chan=True) for _ in range(3)]
            ev = [self.sb(ph, "ev", [128, 512], BF16, chan=True) for _ in range(4)]
            evf = [self.sb(ph, "evf", [128, 512], F32, chan=True) for _ in range(4)]
            nblk = 0
            nev = 0
            for g in range(NG):
                hs = hT[g % 2]
                self.norm_group(g, xsrc, gain, xt, sq, xn, st_, hs)
                tok = slice(g * TG, (g + 1) * TG)
                for (c0, wd, segs) in blocks:
                    w = wt[nblk % 3]
                    nblk += 1
                    S.dma("pool", w.t[:, :, 0:wd], W[:, :, c0:c0 + wd], w.c, writes=[w.b])
                    for sg in segs:
                        if sg[0] == "f":
                            _, loc, dst, scl, dt = sg
                            pb = self.ps[nev % 4]
                            self.mm(pb.t[:, :], pb.b, [(w.t[:, k, loc:loc + 128], hs.t[:, k, :]) for k in range(KC)], [w.b, hs.b])
                            e_ = (ev if dt == BF16 else evf)[nev % 4]
                            if nev % 2 == 0:
                                S.op("act", lambda e, e_=e_, pb=pb, scl=scl: e.activation(e_.t[:], pb.t[:, :], AF.Copy, scale=scl), [pb.b], [e_.b])
                            else:
                                S.op("dve", lambda e, e_=e_, pb=pb, scl=scl: e.tensor_scalar(e_.t[:], pb.t[:, :], scl, None, ALU.mult), [pb.b], [e_.b])
                            S.dma("sp", dst[:, tok], e_.t[:], e_.c, reads=[e_.b])
                            nev += 1
                        elif sg[0] == "t":
                            _, loc, width, dst, dcol, dt = sg
                            for tt in range(4):
                                pb = self.ps[nev % 4]
                                self.mm(pb.t[:, 0:width], pb.b, [(hs.t[:, k, tt * 128:(tt + 1) * 128], w.t[:, k, loc:loc + width]) for k in range(KC)], [w.b, hs.b])
                                e_ = (ev if dt == BF16 else evf)[nev % 4]
                                if nev % 2 == 0:
                                    S.op("act", lambda e, e_=e_, pb=pb, width=width: e.activation(e_.t[:, 0:width], pb.t[:, 0:width], AF.Copy), [pb.b], [e_.b])
                                else:
                                    S.op("dve", lambda e, e_=e_, pb=pb, width=width: e.tensor_copy(e_.t[:, 0:width], pb.t[:, 0:width]), [pb.b], [e_.b])
                                r0 = g * TG + tt * 128
                                S.dma("sp", dst[r0:r0 + 128, dcol:dcol + width], e_.t[:, 0:width], e_.c, reads=[e_.b])
                                nev += 1
                        else:
                            _, loc, width, dst = sg
                            pb = self.ps[4 + nev % 2]
                            self.mm(pb.t[0:width, :], pb.b, [(w.t[:, k, loc:loc + width], hs.t[:, k, :]) for k in range(KC)], [w.b, hs.b])
                            e_ = evf[nev % 4]
                            S.op("dve", lambda e, pb=pb, e_=e_, width=width: e.tensor_copy(e_.t[0:width, :], pb.t[0:width, :]), [pb.b], [e_.b])
                            S.dma("sp", dst[:, tok], e_.t[0:width, :], e_.c, reads=[e_.b])
                            nev += 1

    def phase_inproj_even(self, layer, j, xsrc):
        blocks = []
        for b in range(2):
            blocks.append((b * 512, 512, [("f", cc * 128, self.qaT[b * 4 + cc], SCALE, BF16) for cc in range(4)]))
        blocks.append((1024, 512, [("f", 0, self.kaT[0], 1.0, BF16), ("f", 128, self.kaT[1], 1.0, BF16), ("t", 256, 256, self.va, 0, BF16)]))
        for b in range(2):
            blocks.append((1536 + b * 512, 512, [("f", cc * 128, self.qbT[b * 4 + cc], SCALE, BF16) for cc in range(4)]))
        for b in range(2):
            blocks.append((2560 + b * 512, 512, [("f", cc * 128, self.kbT[b * 4 + cc], 1.0, BF16) for cc in range(4)]))
        for b in range(2):
            blocks.append((3584 + b * 512, 512, [("t", 0, 512, self.vb, b * 512, BF16)]))
        blocks.append((4608, 8, [("ff", 0, 8, self.fT)]))
        self.inproj(self.norm_mix[layer, :], self.ev_w_in[j], blocks, xsrc)

    def phase_inproj_odd(self, layer, j, xsrc):
        blocks = []
        for b in range(2):
            blocks.append((b * 512, 512, [("f", cc * 128, self.qcT[b * 4 + cc], SCALE, BF16) for cc in range(4)]))
        blocks.append((1024, 512, [("f", 0, self.kcmpT[0], 1.0, BF16), ("f", 128, self.kcmpT[1], 1.0, BF16),
                                   ("f", 256, self.vcmpT[0], 1.0, BF16), ("f", 384, self.vcmpT[1], 1.0, BF16)]))
        blocks.append((1536, 512, [("f", 0, self.kselT[0], 1.0, BF16), ("f", 128, self.kselT[1], 1.0, BF16), ("t", 256, 256, self.vsel, 0, BF16)]))
        blocks.append((2048, 512, [("f", 0, self.kwinT[0], 1.0, BF16), ("f", 128, self.kwinT[1], 1.0, BF16), ("t", 256, 256, self.vwin, 0, BF16)]))
        blocks.append((2560, 24, [("t", 0, 24, self.gates, 0, F32)]))
        for i, dst in enumerate((self.qdT, self.kdT, self.vdT)):
            for b in range(2):
                blocks.append((2584 + i * 1024 + b * 512, 512, [("f", cc * 128, dst[b * 4 + cc], 1.0, F32) for cc in range(4)]))
        blocks.append((5656, 16, [("ff", 0, 16, self.baT)]))
        for b in range(2):
            blocks.append((5672 + b * 512, 512, [("t", 0, 512, self.zg, b * 512, F32)]))
        self.inproj(self.norm_mix[layer, :], self.od_w_in[j], blocks, xsrc)

    def phase_swa(self, j):
        S = self.S
        with ExitStack() as ph:
            esink = self.sb(ph, "esink", [128, 8], F32, chan=True)
            S.dma("sp", esink.t[:], self.ev_sinks[j, :].partition_broadcast(128), esink.c, writes=[esink.b])
            S.op("act", lambda e: e.activation(esink.t[:], esink.t[:], AF.Exp), [esink.b], [esink.b])
            kT = [self.sb(ph, "kT", [128, T], BF16, chan=True) for _ in range(2)]
            v1 = [self.sb(ph, "v1", [128, NT, 132], BF16, chan=True) for _ in range(2)]
            qT = [self.sb(ph, "qT", [128, T], BF16, chan=True) for _ in range(2)]
            sb_ = [self.sb(ph, "sb", [128, 128], F32) for _ in range(2)]
            pT = [self.sb(ph, "pT", [128, 128], BF16) for _ in range(4)]
            rd = [self.sb(ph, "rd", [128, 1], F32) for _ in range(2)]
            on = [self.sb(ph, "on", [128, 128], BF16) for _ in range(2)]
            ost = [self.sb(ph, "ost", [128, 512], BF16, chan=True) for _ in range(2)]
            for g in range(2):
                S.dma("sp", kT[g].t[:], self.kaT[g], kT[g].c, writes=[kT[g].b])
                S.op("pool", lambda e, g=g: e.memset(v1[g].t[:, :, 128:129], 1.0), [], [v1[g].b])
                S.dma("sp", v1[g].t[:, :, 0:128], self.va.rearrange("(n p) c -> p n c", p=128)[:, :, g * 128:(g + 1) * 128], v1[g].c, writes=[v1[g].b])
            it = 0
            for h in range(8):
                g = h // 4
                q = qT[h % 2]
                S.dma("sp", q.t[:], self.qaT[h], q.c, writes=[q.b])
                for i in range(NT):
                    acc = self.ps[4 + (i % 2)]
                    js = [jb for jb in (i - 1, i) if jb >= 0]
                    for jb in js:
                        sp = self.ps[it % 4]
                        self.mm(sp.t[:, 0:128], sp.b, [(kT[g].t[:, jb * 128:(jb + 1) * 128], q.t[:, i * 128:(i + 1) * 128])], [kT[g].b, q.b])
                        bias = self.bias0 if jb == i else self.bias1w
                        s_ = sb_[it % 2]
                        S.op("dve", lambda e, s_=s_, sp=sp, bias=bias, h=h: e.tensor_tensor(s_.t[:], sp.t[:, 0:128], bias.t[:, h, :], ALU.add),
                             [sp.b, bias.b], [s_.b])
                        p = pT[it % 4]
                        S.op("act", lambda e, p=p, s_=s_: e.activation(p.t[:], s_.t[:], AF.Exp), [s_.b], [p.b])
                        self.S.op("pe", lambda e, acc=acc, p=p, jb=jb, g=g, first=(jb == js[0]), last=(jb == i): e.matmul(
                            acc.t[:, 0:129], p.t[:], v1[g].t[:, jb, 0:129], start=first, stop=last), [p.b, v1[g].b], [acc.b], inc=(jb == i))
                        it += 1
                    r = rd[i % 2]
                    S.op("dve", lambda e, r=r, acc=acc, h=h: e.tensor_tensor(r.t[:], acc.t[:, 128:129], esink.t[:, h:h + 1], ALU.add), [acc.b, esink.b], [r.b])
                    S.op("dve", lambda e, r=r: e.reciprocal(r.t[:], r.t[:]), [r.b], [r.b])
                    o = on[i % 2]
                    S.op("dve", lambda e, o=o, acc=acc, r=r: e.tensor_scalar(o.t[:], acc.t[:, 0:128], r.t[:, 0:1], None, ALU.mult), [acc.b, r.b], [o.b])
                    self.out_transpose(o, ost, i, self.oT[h])

    def out_transpose(self, o, ost, i, dst):
        S = self.S
        pb = self.ps[7]
        pTr = pb.t[:, :].bitcast(BF16)
        stg = ost[(i // 4) % 2]
        S.op("pe", lambda e, o=o, pTr=pTr: e.transpose(pTr[:, 0:128], o.t[:], self.identb.t[:]), [o.b, self.identb.b], [pb.b])
        S.op("act", lambda e, stg=stg, pTr=pTr, i=i: e.activation(stg.t[:, (i % 4) * 128:(i % 4 + 1) * 128], pTr[:, 0:128], AF.Copy), [pb.b], [stg.b])
        if i % 4 == 3:
            c = i // 4
            S.dma("sp", dst[:, c * 512:(c + 1) * 512], stg.t[:], stg.c, reads=[stg.b])

    def phase_fox(self, j):
        S = self.S
        with ExitStack() as ph:
            fr = self.sb(ph, "fr", [8, T], F32, chan=True)
            cs = self.sb(ph, "cs2", [8, T], F32, chan=True)
            ones8 = self.sb(ph, "ones8", [8, T], F32)
            nb = self.sb(ph, "nb", [8, 1], F32, chan=True)
            ck = self.sb(ph, "ck", [128, NT, 8], F32)
            S.dma("sp", fr.t[:], self.fT[:, :], fr.c, writes=[fr.b])
            S.dma("sp", nb.t[:], self.ev_b_forget[j, :].rearrange("(h o) -> h o", o=1), nb.c, writes=[nb.b])
            S.op("dve", lambda e: e.tensor_scalar(nb.t[:], nb.t[:], -1.0, None, ALU.mult), [nb.b], [nb.b])
            S.op("pool", lambda e: e.memset(ones8.t[:], 1.0), [], [ones8.b])
            S.op("act", lambda e: e.activation(fr.t[:], fr.t[:], AF.Exp, bias=nb.t[:, 0:1], scale=-1.0), [fr.b, nb.b], [fr.b])
            S.op("act", lambda e: e.activation(fr.t[:], fr.t[:], AF.Ln, bias=1.0), [fr.b], [fr.b])
            S.op("dve", lambda e: e.tensor_tensor_scan(out=cs.t[:], data0=ones8.t[:], data1=fr.t[:], initial=0.0, op0=ALU.mult, op1=ALU.add),
                 [fr.b, ones8.b], [cs.b])
            S.dma("sp", self.csd[:, :], cs.t[:], cs.c, reads=[cs.b])
            pb = self.ps[7]
            for n in range(NT):
                S.op("pe", lambda e, n=n: e.transpose(pb.t[:, n * 8:(n + 1) * 8], cs.t[0:8, n * 128:(n + 1) * 128], self.identf.t[0:8, 0:8]),
                     [cs.b, self.identf.b], [pb.b], inc=(n == NT - 1))
            S.op("dve", lambda e: e.tensor_copy(ck.t[:], pb.t[:, 0:NT * 8].rearrange("p (n h) -> p n h", h=8)), [pb.b], [ck.b])
            S.barrier()
            kT = [self.sb(ph, "kT", [128, T], BF16, chan=True) for _ in range(2)]
            qT = [self.sb(ph, "qT", [128, T], BF16, chan=True) for _ in range(2)]
            v1 = [self.sb(ph, "v1", [128, NT, 132], BF16, chan=True) for _ in range(2)]
            crow = [self.sb(ph, "crow", [1, T], F32, chan=True) for _ in range(2)]
            ncq = [self.sb(ph, "ncq", [1, T], BF16) for _ in range(2)]
            pT = [self.sb(ph, "pT", [128, 512], BF16) for _ in range(3)]
            rd = [self.sb(ph, "rd", [128, 1], F32) for _ in range(2)]
            on = [self.sb(ph, "on", [128, 128], BF16) for _ in range(2)]
            ost = [self.sb(ph, "ost", [128, 512], BF16, chan=True) for _ in range(2)]
            for s in range(2):
                S.op("pool", lambda e, s=s: e.memset(v1[s].t[:, :, 128:129], 1.0), [], [v1[s].b])
            it = 0
            for h in range(8):
                s = h % 2
                k_, q_, v_, cr, nq = kT[s], qT[s], v1[s], crow[s], ncq[s]
                S.dma("sp", k_.t[:], self.kbT[h], k_.c, writes=[k_.b])
                S.dma("sp", q_.t[:], self.qbT[h], q_.c, writes=[q_.b])
                S.dma("sp", v_.t[:, :, 0:128], self.vb.rearrange("(n p) c -> p n c", p=128)[:, :, h * 128:(h + 1) * 128], v_.c, writes=[v_.b])
                S.dma("sp", cr.t[:], self.csd[h:h + 1, :], cr.c, writes=[cr.b])
                S.op("dve", lambda e, nq=nq, cr=cr: e.tensor_scalar(nq.t[:], cr.t[:], -1.0, None, ALU.mult), [cr.b], [nq.b])
                for c in range(NG):
                    accs = [self.ps[3 + qt] for qt in range(4)]
                    njb = 4 * c + 4
                    pend = None
                    for jb in range(njb + 1):
                        if jb < njb:
                            q0 = max(c * 512, jb * 128)
                            n = (c + 1) * 512 - q0
                            sp = self.ps[it % 3]
                            self.mm(sp.t[:, 0:n], sp.b, [(k_.t[:, jb * 128:(jb + 1) * 128], q_.t[:, q0:q0 + n]),
                                                         (self.onesrow.t[0:1, :], nq.t[0:1, q0:q0 + n])], [k_.b, q_.b, nq.b, self.onesrow.b])
                            p = pT[it % 3]
                            S.op("act", lambda e, p=p, sp=sp, n=n, jb=jb, h=h: e.activation(p.t[:, 0:n], sp.t[:, 0:n], AF.Exp, bias=ck.t[:, jb, h:h + 1]),
                                 [sp.b, ck.b], [p.b])
                            if jb * 128 >= c * 512:
                                S.op("dve", lambda e, p=p: e.tensor_tensor(p.t[:, 0:128], p.t[:, 0:128], self.causal01.t[:], ALU.mult),
                                     [p.b, self.causal01.b], [p.b])
                            it += 1
                            cur = (p, jb, q0, n)
                        else:
                            cur = None
                        if pend is not None:
                            p2, jb2, q02, n2 = pend
                            for qt in range(4):
                                tile = 4 * c + qt
                                if tile < jb2:
                                    continue
                                off = tile * 128 - q02
                                acc = accs[qt]
                                S.op("pe", lambda e, acc=acc, p2=p2, off=off, jb2=jb2, tile=tile, v_=v_: e.matmul(
                                    acc.t[:, 0:129], p2.t[:, off:off + 128], v_.t[:, jb2, 0:129], start=(jb2 == 0), stop=(jb2 == tile)),
                                    [p2.b, v_.b], [acc.b], inc=(jb2 == tile))
                        pend = cur
                    for qt in range(4):
                        i = 4 * c + qt
                        acc = accs[qt]
                        r = rd[i % 2]
                        S.op("dve", lambda e, r=r, acc=acc: e.reciprocal(r.t[:], acc.t[:, 128:129]), [acc.b], [r.b])
                        o = on[i % 2]
                        S.op("dve", lambda e, o=o, acc=acc, r=r: e.tensor_scalar(o.t[:], acc.t[:, 0:128], r.t[:, 0:1], None, ALU.mult), [acc.b, r.b], [o.b])
                        self.out_transpose(o, ost, i, self.oT[8 + h])

    def attend(self, acc, tiles, q_ap, q_buf, pT, sbs):
        S = self.S
        n = len(tiles)
        LA = 2
        pendq = []
        sbank = (0, 1, 2, 4)
        for idx in range(n + LA):
            cur = None
            if idx < n:
                kT_ap, k_buf, v_ap, v_buf, extra, bias = tiles[idx]
                sp = self.ps[sbank[self.nsp % 4]]
                self.nsp += 1
                pairs = [(kT_ap, q_ap)]
                reads = [k_buf, q_buf]
                if extra is not None:
                    pairs.append((extra[0], extra[1]))
                    reads += list(extra[2])
                self.mm(sp.t[:, 0:128], sp.b, pairs, reads)
                p = pT[self.npt % len(pT)]
                self.npt += 1
                if bias is not None:
                    s_ = sbs[self.npt % len(sbs)]
                    S.op("dve", lambda e, s_=s_, sp=sp, bias=bias: e.tensor_tensor(s_.t[:], sp.t[:, 0:128], bias[0], ALU.add), [sp.b, bias[1]], [s_.b])
                    S.op("act", lambda e, p=p, s_=s_: e.activation(p.t[:], s_.t[:], AF.Exp), [s_.b], [p.b])
                else:
                    S.op("act", lambda e, p=p, sp=sp: e.activation(p.t[:], sp.t[:, 0:128], AF.Exp), [sp.b], [p.b])
                cur = (p, v_ap, v_buf, idx)
                pendq.append(cur)
            if idx >= LA and pendq:
                p2, v2, vb2, i2 = pendq.pop(0)
                S.op("pe", lambda e, p2=p2, v2=v2, i2=i2: e.matmul(acc.t[:, 0:129], p2.t[:], v2, start=(i2 == 0), stop=(i2 == n - 1)),
                     [p2.b, vb2], [acc.b], inc=(i2 == n - 1))

    def phase_nsa(self, j):
        S = self.S
        idf = self.identf
        self.nsp = 0
        self.npt = 0
        with ExitStack() as ph:
            gsig = self.sb(ph, "gsig", [128, NT, 24], F32, chan=True)
            S.dma("sp", gsig.t[:], self.gates.rearrange("(n p) c -> p n c", p=128), gsig.c, writes=[gsig.b])
            S.op("act", lambda e: e.activation(gsig.t[:], gsig.t[:], AF.Sigmoid), [gsig.b], [gsig.b])
            kcT = [self.sb(ph, "kcT", [128, 256], BF16) for _ in range(2)]
            vc1 = [self.sb(ph, "vc1", [128, 2, 132], BF16) for _ in range(2)]
            with ExitStack() as cp:
                w1 = self.sb(cp, "w1", [128, 32, 256], BF16, chan=True)
                w2 = self.sb(cp, "w2", [128, 2, 128], BF16, chan=True)
                per = self.sb(cp, "per", [32, 128], F32, chan=True)
                peT = self.sb(cp, "peT", [128, 32], BF16)
                hb = self.sb(cp, "hb", [128, 2], F32)
                xT = self.sb(cp, "xT", [128, T], BF16, chan=True)
                GT = [self.sb(cp, "GT", [128, 256], BF16) for _ in range(2)]
                xs = self.sb(cp, "xs", [128, 256], F32)
                x2 = self.sb(cp, "x2", [128, 256], F32)
                for g in range(2):
                    S.op("dve", lambda e, g=g: e.memset(kcT[g].t[:], 0.0), [], [kcT[g].b])
                    S.op("dve", lambda e, g=g: e.memset(vc1[g].t[:], 0.0), [], [vc1[g].b])
                    S.op("dve", lambda e, g=g: e.memset(vc1[g].t[:, :, 128:129], 1.0), [], [vc1[g].b])
                for hc in range(2):
                    S.op("dve", lambda e, hc=hc: e.memset(GT[hc].t[:], 0.0), [], [GT[hc].b])
                for kv in range(2):
                    S.dma("pool", w1.t[:], self.od_cmp_w1[j, kv].rearrange("(jj d) n -> d jj n", d=128), w1.c, writes=[w1.b])
                    S.dma("pool", w2.t[:], self.od_cmp_w2[j, kv].rearrange("(c p) n -> p c n", p=128), w2.c, writes=[w2.b])
                    S.dma("sp", per.t[:], self.od_cmp_pos[j, kv], per.c, writes=[per.b])
                    pb7 = self.ps[7]
                    S.op("pe", lambda e: e.transpose(pb7.t[:, 0:32], per.t[:], idf.t[0:32, 0:32]), [per.b, idf.b], [pb7.b])
                    S.op("dve", lambda e: e.tensor_copy(peT.t[:], pb7.t[:, 0:32]), [pb7.b], [peT.b])
                    for hc in range(2):
                        pbh = self.ps[6]
                        self.mm(pbh.t[:, hc:hc + 1], pbh.b, [(w1.t[:, jj, hc * 128:(hc + 1) * 128], peT.t[:, jj:jj + 1]) for jj in range(32)], [w1.b, peT.b])
                        S.op("dve", lambda e, hc=hc, pbh=pbh: e.tensor_copy(hb.t[:, hc:hc + 1], pbh.t[:, hc:hc + 1]), [pbh.b], [hb.b])
                    for g in range(2):
                        src = (self.kcmpT if kv == 0 else self.vcmpT)[g]
                        S.dma("sp", xT.t[:], src, xT.c, writes=[xT.b])
                        xv = xT.t[:].rearrange("p (n s) -> p n s", s=16)
                        for hc in range(2):
                            pbx = self.ps[hc]
                            self.mm(pbx.t[:, 0:255], pbx.b, [(w1.t[:, jj, hc * 128:(hc + 1) * 128], xv[:, jj // 16:jj // 16 + 255, jj % 16]) for jj in range(32)], [w1.b, xT.b])
                            S.op("dve", lambda e, hc=hc, pbx=pbx: e.tensor_scalar(xs.t[:, 0:255], pbx.t[:, 0:255], hb.t[:, hc:hc + 1], None, ALU.add), [pbx.b, hb.b], [xs.b])
                            S.op("dve", lambda e: e.tensor_tensor(x2.t[:, 0:255], xs.t[:, 0:255], xs.t[:, 0:255], ALU.mult), [xs.b], [x2.b])
                            S.op("dve", lambda e: e.tensor_scalar(x2.t[:, 0:255], x2.t[:, 0:255], 0.044715, 1.0, ALU.mult, ALU.add), [x2.b], [x2.b])
                            S.op("dve", lambda e: e.tensor_tensor(x2.t[:, 0:255], x2.t[:, 0:255], xs.t[:, 0:255], ALU.mult), [x2.b, xs.b], [x2.b])
                            S.op("act", lambda e: e.activation(x2.t[:, 0:255], x2.t[:, 0:255], AF.Tanh, scale=0.7978845608028654), [x2.b], [x2.b])
                            S.op("dve", lambda e: e.tensor_scalar(x2.t[:, 0:255], x2.t[:, 0:255], 0.5, 0.5, ALU.mult, ALU.add), [x2.b], [x2.b])
                            S.op("dve", lambda e, hc=hc: e.tensor_tensor(GT[hc].t[:, 0:255], x2.t[:, 0:255], xs.t[:, 0:255], ALU.mult), [x2.b, xs.b], [GT[hc].b])
                        if kv == 0:
                            pbk = self.ps[2]
                            self.mm(pbk.t[:, 0:255], pbk.b, [(w2.t[:, hc, :], GT[hc].t[:, 0:255]) for hc in range(2)], [w2.b, GT[0].b, GT[1].b])
                            S.op("act", lambda e, g=g, pbk=pbk: e.activation(kcT[g].t[:, 0:255], pbk.t[:, 0:255], AF.Copy), [pbk.b], [kcT[g].b])
                        else:
                            for ct in range(2):
                                rows = 128 if ct == 0 else 127
                                pbk = self.ps[2 + ct]
                                self.mm(pbk.t[0:rows, 0:128], pbk.b, [(GT[hc].t[:, ct * 128:ct * 128 + rows], w2.t[:, hc, :]) for hc in range(2)], [w2.b, GT[0].b, GT[1].b])
                                S.op("act", lambda e, g=g, ct=ct, rows=rows, pbk=pbk: e.activation(vc1[g].t[0:rows, ct, 0:128], pbk.t[0:rows, 0:128], AF.Copy), [pbk.b], [vc1[g].b])
                S.barrier()
            cmask = self.sb(ph, "cmask", [128, 64, 128], BF16, chan=True)
            wimp = self.sb(ph, "wimp", [128, 2, 64], BF16, chan=True)
            skeep = self.sb(ph, "skeep", [128, NT, 64], F32, chan=True)
            sadd = self.sb(ph, "sadd", [128, NT, 64], F32, chan=True)
            esel = self.sb(ph, "esel", [64, NT, 128], BF16, chan=True)
            S.dma("pool", cmask.t[:], self.c_cmask.rearrange("p (m q) -> p m q", q=128), cmask.c, writes=[cmask.b])
            S.dma("pool", wimp.t[:], self.c_wimp.rearrange("p (c j) -> p c j", j=64), wimp.c, writes=[wimp.b])
            S.dma("sp", skeep.t[:], self.c_selkeep.rearrange("p (n j) -> p n j", j=64), skeep.c, writes=[skeep.b])
            S.dma("sp", sadd.t[:], self.c_seladd.rearrange("p (n j) -> p n j", j=64), sadd.c, writes=[sadd.b])
            S.dma("pool", esel.t[:], self.c_esel.rearrange("p (n k) -> p n k", k=128), esel.c, writes=[esel.b])
            qT = [self.sb(ph, "qT", [128, T], BF16, chan=True) for _ in range(4)]
            ksT = self.sb(ph, "ksT", [128, T], BF16, chan=True)
            kwT = self.sb(ph, "kwT", [128, T], BF16, chan=True)
            vs1 = self.sb(ph, "vs1", [128, NT, 132], BF16, chan=True)
            vw1 = self.sb(ph, "vw1", [128, NT, 132], BF16, chan=True)
            S.op("dve", lambda e: e.memset(vs1.t[:, :, 128:129], 1.0), [], [vs1.b])
            S.op("dve", lambda e: e.memset(vw1.t[:, :, 128:129], 1.0), [], [vw1.b])
            pT = [self.sb(ph, "pT", [128, 128], BF16) for _ in range(8)]
            sbs = [self.sb(ph, "sbs", [128, 128], F32) for _ in range(4)]
            imp = self.sb(ph, "imp", [128, 64], F32)
            sc = self.sb(ph, "sc", [128, 64], F32)
            m8 = self.sb(ph, "m8", [128, 8], F32)
            nsel = self.sb(ph, "nsel", [128, 64], BF16)
            nsT = self.sb(ph, "nsT", [64, 128], BF16)
            oacc = [self.sb(ph, "oacc", [128, 128], F32) for _ in range(4)]
            onb = [self.sb(ph, "onb", [128, 128], BF16) for _ in range(2)]
            rd = [self.sb(ph, "rd", [128, 1], F32) for _ in range(4)]
            ostg = [[self.sb(ph, "ostg", [128, 512], BF16, chan=True) for _ in range(2)] for _ in range(4)]
            for g in range(2):
                for r in range(4):
                    S.dma("sp", qT[r].t[:], self.qcT[4 * g + r], qT[r].c, writes=[qT[r].b])
                S.dma("sp", ksT.t[:], self.kselT[g], ksT.c, writes=[ksT.b])
                S.dma("sp", kwT.t[:], self.kwinT[g], kwT.c, writes=[kwT.b])
                S.dma("sp", vs1.t[:, :, 0:128], self.vsel.rearrange("(n p) c -> p n c", p=128)[:, :, g * 128:(g + 1) * 128], vs1.c, writes=[vs1.b])
                S.dma("sp", vw1.t[:, :, 0:128], self.vwin.rearrange("(n p) c -> p n c", p=128)[:, :, g * 128:(g + 1) * 128], vw1.c, writes=[vw1.b])
                for i in range(NT):
                    qs = slice(i * 128, (i + 1) * 128)
                    cts = [0] if i < 16 else [0, 1]
                    for r in range(4):
                        h = 4 * g + r
                        accc = self.ps[3]
                        blk = Tl(self.ps[3].t[:, 256:512], self.ps[3].b)
                        pcs = []
                        for ct in cts:
                            sp = self.ps[(0, 1, 2, 4)[self.nsp % 4]]
                            self.nsp += 1
                            self.mm(sp.t[:, 0:128], sp.b, [(kcT[g].t[:, ct * 128:(ct + 1) * 128], qT[r].t[:, qs])], [kcT[g].b, qT[r].b])
                            p = pT[self.npt % len(pT)]
                            self.npt += 1
                            S.op("act", lambda e, p=p, sp=sp: e.activation(p.t[:], sp.t[:, 0:128], AF.Exp), [sp.b], [p.b])
                            S.op("dve", lambda e, p=p, i=i, ct=ct: e.tensor_tensor(p.t[:], p.t[:], cmask.t[:, i * 2 + ct, :], ALU.mult), [p.b, cmask.b], [p.b])
                            pcs.append((p, ct))
                        self.mm(accc.t[:, 0:129], accc.b, [(p.t[:], vc1[g].t[:, ct, 0:129]) for p, ct in pcs], [vc1[g].b] + [p.b for p, _ in pcs])
                        self.mm(blk.t[:, 0:64], blk.b, [(p.t[:], wimp.t[:, ct, :]) for p, ct in pcs], [wimp.b] + [p.b for p, _ in pcs])
                        r_ = rd[r]
                        S.op("dve", lambda e, r_=r_, accc=accc: e.tensor_scalar(r_.t[:], accc.t[:, 128:129], 1e-30, None, ALU.max), [accc.b], [r_.b])
                        S.op("dve", lambda e, r_=r_: e.reciprocal(r_.t[:], r_.t[:]), [r_.b], [r_.b])
                        if r == 0:
                            S.op("dve", lambda e, r_=r_, blk=blk: e.tensor_scalar(imp.t[:], blk.t[:, 0:64], r_.t[:, 0:1], None, ALU.mult), [blk.b, r_.b], [imp.b])
                        else:
                            S.op("dve", lambda e, r_=r_, blk=blk: e.scalar_tensor_tensor(out=imp.t[:], in0=blk.t[:, 0:64], scalar=r_.t[:, 0:1], in1=imp.t[:], op0=ALU.mult, op1=ALU.add),
                                 [blk.b, r_.b, imp.b], [imp.b])
                        S.op("dve", lambda e, r_=r_, i=i, h=h: e.tensor_tensor(r_.t[:], r_.t[:], gsig.t[:, i, h:h + 1], ALU.mult), [r_.b, gsig.b], [r_.b])
                        S.op("dve", lambda e, r=r, r_=r_, accc=accc: e.tensor_scalar(oacc[r].t[:], accc.t[:, 0:128], r_.t[:, 0:1], None, ALU.mult), [accc.b, r_.b], [oacc[r].b])
                    S.op("dve", lambda e, i=i: e.tensor_tensor(sc.t[:], imp.t[:], skeep.t[:, i, :], ALU.mult), [imp.b, skeep.b], [sc.b])
                    S.op("dve", lambda e, i=i: e.tensor_tensor(sc.t[:], sc.t[:], sadd.t[:, i, :], ALU.add), [sc.b, sadd.b], [sc.b])
                    S.op("dve", lambda e: e.max(out=m8.t[:], in_=sc.t[:]), [sc.b], [m8.b])
                    S.op("dve", lambda e: e.tensor_scalar(sc.t[:], sc.t[:], m8.t[:, 7:8], None, ALU.is_ge), [sc.b, m8.b], [sc.b])
                    S.op("dve", lambda e: e.tensor_scalar(nsel.t[:], sc.t[:], -NEG, NEG, ALU.mult, ALU.add), [sc.b], [nsel.b])
                    pb7 = self.ps[7]
                    pTr = pb7.t[:, :].bitcast(BF16)
                    S.op("pe", lambda e, pTr=pTr: e.transpose(pTr[0:64, 0:128], nsel.t[:], self.identb.t[:]), [nsel.b, self.identb.b], [pb7.b])
                    S.op("act", lambda e, pTr=pTr: e.activation(nsT.t[:], pTr[0:64, 0:128], AF.Copy), [pb7.b], [nsT.b])
                    for r in range(4):
                        h = 4 * g + r
                        tiles = []
                        for jb in range(i + 1):
                            bias = None
                            if jb == i:
                                bias = (self.nb0.t[:, h, :], self.nb0.b)
                            elif jb == i - 1:
                                bias = (self.nb1.t[:, h, :], self.nb1.b)
                            tiles.append((ksT.t[:, jb * 128:(jb + 1) * 128], ksT.b, vs1.t[:, jb, 0:129], vs1.b,
                                          (esel.t[:, jb, :], nsT.t[:], (esel.b, nsT.b)), bias))
                        acc = self.ps[5]
                        self.attend(acc, tiles, qT[r].t[:, qs], qT[r].b, pT, sbs)
                        self.nsa_combine(acc, rd[r], gsig, i, 8 + h, oacc[r])
                        tiles = []
                        for jb in range(max(0, i - 4), i + 1):
                            bias = None
                            if jb == i:
                                bias = (self.nb0.t[:, h, :], self.nb0.b)
                            elif jb == i - 1:
                                bias = (self.nb1.t[:, h, :], self.nb1.b)
                            elif jb == i - 4:
                                bias = (self.acneg.t[:], self.acneg.b)
                            tiles.append((kwT.t[:, jb * 128:(jb + 1) * 128], kwT.b, vw1.t[:, jb, 0:129], vw1.b, None, bias))
                        acc = self.ps[6]
                        self.attend(acc, tiles, qT[r].t[:, qs], qT[r].b, pT, sbs)
                        self.nsa_combine(acc, rd[r], gsig, i, 16 + h, oacc[r])
                        o = onb[r % 2]
                        S.op("act", lambda e, o=o, r=r: e.activation(o.t[:], oacc[r].t[:], AF.Copy), [oacc[r].b], [o.b])
                        self.out_transpose(o, ostg[r], i, self.oT[h])

    def nsa_combine(self, acc, r_, gsig, i, gcol, oacc):
        S = self.S
        S.op("dve", lambda e: e.reciprocal(r_.t[:], acc.t[:, 128:129]), [acc.b], [r_.b])
        S.op("dve", lambda e: e.tensor_tensor(r_.t[:], r_.t[:], gsig.t[:, i, gcol:gcol + 1], ALU.mult), [r_.b, gsig.b], [r_.b])
        S.op("dve", lambda e: e.scalar_tensor_tensor(out=oacc.t[:], in0=acc.t[:, 0:128], scalar=r_.t[:, 0:1], in1=oacc.t[:], op0=ALU.mult, op1=ALU.add),
             [acc.b, r_.b, oacc.b], [oacc.b])

    def phase_gdn(self, j):
        S = self.S
        B = int(os.environ.get("KGDNB", "8"))
        C = 64
        NCH = T // C
        idf = self.identf
        with ExitStack() as ph:
            psq = []
            for q in range(4):
                for b in range(6):
                    psq.append(Tl(self.ps[b].t[:, q * 128:(q + 1) * 128], self.ps[b].b))
            nq = [0]

            def slot():
                nq[0] += 1
                return psq[nq[0] % len(psq)]
            cwg = self.sb(ph, "cwg", [128, 4, 24], F32)
            gamc = self.sb(ph, "gamc", [C, NCH, 8], F32)
            betac = self.sb(ph, "betac", [C, NCH, 8], F32)
            egamc = self.sb(ph, "egamc", [C, NCH, 8], F32)
            nbetac = self.sb(ph, "nbetac", [C, NCH, 8], F32)
            begamc = self.sb(ph, "begamc", [C, NCH, 8], F32)
            dtb = self.sb(ph, "dtb", [8, 1], F32, chan=True)
            nega = self.sb(ph, "nega", [8, 1], F32, chan=True)
            prep = ExitStack()
            cwr = self.sb(prep, "cwr", [24, 4, 128], F32, chan=True)
            S.dma_group("sp", [(cwr.t[:, jj, :], self.od_conv_w[j, jj, :].rearrange("(c p) -> c p", p=128)) for jj in range(4)], cwr.c, writes=[cwr.b])
            pb7 = self.ps[7]
            for jj in range(4):
                S.op("pe", lambda e, jj=jj: e.transpose(pb7.t[:, jj * 24:(jj + 1) * 24], cwr.t[:, jj, :], idf.t[0:24, 0:24]), [cwr.b, idf.b], [pb7.b], inc=(jj == 3))
            S.op("dve", lambda e: e.tensor_copy(cwg.t[:], pb7.t[:, 0:96].rearrange("p (j c) -> p j c", c=24)), [pb7.b], [cwg.b])
            bb = self.sb(prep, "bb", [8, T], F32, chan=True)
            ba = self.sb(prep, "ba", [8, T], F32, chan=True)
            gam = self.sb(prep, "gam", [8, T], F32)
            rmask = self.sb(prep, "rmask", [8, T], F32, chan=True)
            S.dma("sp", bb.t[:], self.baT[0:8, :], bb.c, writes=[bb.b])
            S.dma("sp", ba.t[:], self.baT[8:16, :], ba.c, writes=[ba.b])
            S.dma("sp", rmask.t[:], self.c_rmask[:, :], rmask.c, writes=[rmask.b])
            S.dma("sp", dtb.t[:], self.od_dt_bias[j, :].rearrange("(h o) -> h o", o=1), dtb.c, writes=[dtb.b])
            S.dma("sp", nega.t[:], self.od_a_log[j, :].rearrange("(h o) -> h o", o=1), nega.c, writes=[nega.b])
            S.op("act", lambda e: e.activation(nega.t[:], nega.t[:], AF.Exp), [nega.b], [nega.b])
            S.op("dve", lambda e: e.tensor_scalar(nega.t[:], nega.t[:], -1.0, None, ALU.mult), [nega.b], [nega.b])
            S.op("act", lambda e: e.activation(ba.t[:], ba.t[:], AF.Exp, bias=dtb.t[:, 0:1]), [ba.b, dtb.b], [ba.b])
            S.op("act", lambda e: e.activation(ba.t[:], ba.t[:], AF.Ln, bias=1.0), [ba.b], [ba.b])
            S.op("dve", lambda e: e.tensor_scalar(ba.t[:], ba.t[:], nega.t[:, 0:1], None, ALU.mult), [ba.b, nega.b], [ba.b])
            S.op("dve", lambda e: e.tensor_tensor_scan(out=gam.t[:], data0=rmask.t[:], data1=ba.t[:], initial=0.0, op0=ALU.mult, op1=ALU.add),
                 [rmask.b, ba.b], [gam.b])
            S.op("act", lambda e: e.activation(bb.t[:], bb.t[:], AF.Sigmoid), [bb.b], [bb.b])
            for src, dst, pbk in ((gam, gamc, self.ps[6]), (bb, betac, self.ps[7])):
                for c in range(NCH):
                    S.op("pe", lambda e, c=c, src=src, pbk=pbk: e.transpose(pbk.t[0:C, c * 8:(c + 1) * 8], src.t[0:8, c * C:(c + 1) * C], idf.t[0:8, 0:8]),
                         [src.b, idf.b], [pbk.b], inc=(c == NCH - 1))
                S.op("dve", lambda e, dst=dst, pbk=pbk: e.tensor_copy(dst.t[:], pbk.t[0:C, :].rearrange("p (c h) -> p c h", h=8)), [pbk.b], [dst.b])
            S.barrier()
            prep.close()
            S.op("act", lambda e: e.activation(egamc.t[:], gamc.t[:], AF.Exp), [gamc.b], [egamc.b])
            S.op("dve", lambda e: e.tensor_scalar(nbetac.t[:], betac.t[:], -1.0, None, ALU.mult), [betac.b], [nbetac.b])
            S.op("dve", lambda e: e.tensor_tensor(begamc.t[:], betac.t[:], egamc.t[:], ALU.mult), [betac.b, egamc.b], [begamc.b])
            gnr = self.sb(ph, "gnr", [C, 128], F32, chan=True)
            S.dma("sp", gnr.t[:], self.od_gdn_norm[j, :].partition_broadcast(C), gnr.c, writes=[gnr.b])
            ones64 = self.sb(ph, "ones64", [C, 128], F32)
            onescol = self.sb(ph, "onescol", [128, 1], F32)
            S.op("dve", lambda e: e.memset(ones64.t[:], 1.0), [], [ones64.b])
            S.op("dve", lambda e: e.memset(onescol.t[:], 1.0), [], [onescol.b])
            pmask = self.sb(ph, "pmask", [C, C], F32, chan=True)
            nmask = self.sb(ph, "nmask", [C, C], F32, chan=True)
            S.dma("sp", pmask.t[:], self.c_gdn_pmask[:, :], pmask.c, writes=[pmask.b])
            S.dma("sp", nmask.t[:], self.c_gdn_nmask[:, :], nmask.c, writes=[nmask.b])
            if self.gdn_stage <= 0:
                return
            raw = self.sb(ph, "raw", [128, T + 3], F32, chan=True)
            S.op("dve", lambda e: e.memset(raw.t[:, 0:3], 0.0), [], [raw.b])
            qkv = [self.sb(ph, "qkv", [128, T], F32) for _ in range(3)]
            sqs = self.sb(ph, "sqs", [128, T], F32)
            rnc = self.sb(ph, "rnc", [C, NCH, 2], F32)
            zt = [self.sb(ph, "zt", [C, B, 128], F32, chan=True) for _ in range(2)]
            St = self.sb(ph, "St", [128, 128], F32)
            ost = [self.sb(ph, "ost", [128, 512], BF16, chan=True) for _ in range(2)]

            def mk(name, shape, n, dt=F32):
                return [self.sb(ph, name, shape, dt) for _ in range(n)]
            kn = mk("kn", [C, 128], B); qn = mk("qn", [C, 128], B); vt = mk("vt", [C, 128], B)
            knT = mk("knT", [128, C], B); qnT = mk("qnT", [128, C], 2 * B)
            dg = mk("dg", [C, C], B); t1 = mk("t1", [C, C], B); Dm = mk("Dm", [C, C], B); DT = mk("DT", [C, C], B)
            X = mk("X", [C, C], 2 * B); XT = mk("XT", [C, C], 2 * B); Y = mk("Y", [C, C], 2 * B)
            Vb = mk("Vb", [C, 128], B); Kb = mk("Kb", [C, 128], B)
            U = mk("U", [C, 128], 2 * B); WmT = mk("WmT", [128, C], 2 * B); MT = mk("MT", [C, C], 2 * B); Kd = mk("Kd", [C, 128], 2 * B)
            kdc = mk("kdc", [C, 1], 2 * B); egl = mk("egl", [128, 1], 2 * B)
            vnew = mk("vnew", [C, 128], 2); mvs = mk("mvs", [C, 128], 2); osb = mk("osb", [C, 128], 2)
            oss = mk("oss", [C, 1], 2); ors = mk("ors", [C, 1], 2); ojunk = mk("ojunk", [C, 128], 1)
            zs = mk("zs", [C, 128], 2); ofb = mk("ofb", [C, 128], 2, BF16)
            for h in range(self.gdn_heads):
                for ti, src in enumerate((self.qdT, self.kdT, self.vdT)):
                    S.dma("sp", raw.t[:, 3:T + 3], src[h], raw.c, writes=[raw.b])
                    dst = qkv[ti]
                    ci = ti * 8 + h
                    S.op("dve", lambda e, dst=dst, ci=ci: e.tensor_scalar(dst.t[:], raw.t[:, 0:T], cwg.t[:, 0, ci:ci + 1], None, ALU.mult), [raw.b, cwg.b], [dst.b])
                    for jj in range(1, 4):
                        S.op("dve", lambda e, dst=dst, ci=ci, jj=jj: e.scalar_tensor_tensor(out=dst.t[:], in0=raw.t[:, jj:T + jj], scalar=cwg.t[:, jj, ci:ci + 1], in1=dst.t[:],
                                                                                           op0=ALU.mult, op1=ALU.add), [raw.b, cwg.b, dst.b], [dst.b])
                    S.op("act", lambda e, dst=dst: e.activation(dst.t[:], dst.t[:], AF.Silu), [dst.b], [dst.b])
                if self.gdn_stage <= 1:
                    continue
                pss = self.ps[6]
                for ti in range(2):
                    S.op("act", lambda e, ti=ti: e.activation(sqs.t[:], qkv[ti].t[:], AF.Square), [qkv[ti].b], [sqs.b])
                    for c in range(NCH):
                        S.op("pe", lambda e, c=c, ti=ti: e.matmul(pss.t[0:C, c * 2 + ti:c * 2 + ti + 1], sqs.t[:, c * C:(c + 1) * C], onescol.t[:, 0:1], start=True, stop=True),
                             [sqs.b, onescol.b], [pss.b], inc=(c == NCH - 1))
                S.op("dve", lambda e: e.tensor_scalar(rnc.t[:], pss.t[0:C, 0:2 * NCH].rearrange("p (c t) -> p c t", t=2), EPS, None, ALU.add), [pss.b], [rnc.b])
                S.op("act", lambda e: e.activation(rnc.t[:], rnc.t[:], AF.Sqrt), [rnc.b], [rnc.b])
                S.op("dve", lambda e: e.reciprocal(rnc.t[:], rnc.t[:]), [rnc.b], [rnc.b])
                S.op("dve", lambda e: e.tensor_scalar(rnc.t[:, :, 0:1], rnc.t[:, :, 0:1], SCALE, None, ALU.mult), [rnc.b], [rnc.b])
                if self.gdn_stage <= 2:
                    continue
                S.op("dve", lambda e: e.memset(St.t[:], 0.0), [], [St.b])
                for bt in range(NCH // B):
                    if os.environ.get("KGDNBAR", "") == "1":
                        S.barrier()
                    par = bt % 2
                    z_ = zt[par]
                    t0 = bt * B * C
                    S.dma("sp", z_.t[:], self.zg[t0:t0 + B * C, h * 128:(h + 1) * 128].rearrange("(b p) e -> p b e", p=C), z_.c, writes=[z_.b])
                    cs = [bt * B + bi for bi in range(B)]
                    o2 = [par * B + bi for bi in range(B)]
                    for bi, c in enumerate(cs):
                        sl = slice(c * C, (c + 1) * C)
                        pq_, pk_, pv_ = slot(), slot(), slot()
                        for p_, src in ((pq_, qkv[0]), (pk_, qkv[1]), (pv_, qkv[2])):
                            S.op("pe", lambda e, p_=p_, src=src, sl=sl: e.transpose(p_.t[0:C, :], src.t[:, sl], idf.t[:]), [src.b, idf.b], [p_.b])
                        S.op("dve", lambda e, bi=bi, c=c, pq_=pq_: e.tensor_scalar(qn[bi].t[:], pq_.t[0:C, :], rnc.t[:, c, 0:1], None, ALU.mult), [pq_.b, rnc.b], [qn[bi].b])
                        S.op("dve", lambda e, bi=bi, c=c, pk_=pk_: e.tensor_scalar(kn[bi].t[:], pk_.t[0:C, :], rnc.t[:, c, 1:2], None, ALU.mult), [pk_.b, rnc.b], [kn[bi].b])
                        S.op("act", lambda e, bi=bi, pv_=pv_: e.activation(vt[bi].t[:], pv_.t[0:C, :], AF.Copy), [pv_.b], [vt[bi].b])
                    if self.gdn_stage <= 3:
                        continue
                    for bi, c in enumerate(cs):
                        o = o2[bi]
                        p1, p2, p3 = slot(), slot(), slot()
                        S.op("pe", lambda e, bi=bi, p1=p1: e.transpose(p1.t[:, 0:C], kn[bi].t[:], idf.t[0:C, 0:C]), [kn[bi].b, idf.b], [p1.b])
                        S.op("pe", lambda e, bi=bi, p2=p2: e.transpose(p2.t[:, 0:C], qn[bi].t[:], idf.t[0:C, 0:C]), [qn[bi].b, idf.b], [p2.b])
                        S.op("act", lambda e, bi=bi, p1=p1: e.activation(knT[bi].t[:], p1.t[:, 0:C], AF.Copy), [p1.b], [knT[bi].b])
                        S.op("dve", lambda e, o=o2[bi], p2=p2: e.tensor_copy(qnT[o].t[:], p2.t[:, 0:C]), [p2.b], [qnT[o].b])
                        S.op("dve", lambda e, h=h, bi=bi, c=c: e.tensor_scalar(dg[bi].t[:], idf.t[0:C, 0:C], gamc.t[:, c, h:h + 1], None, ALU.mult), [idf.b, gamc.b], [dg[bi].b])
                        S.op("pe", lambda e, bi=bi, p3=p3: e.matmul(p3.t[:, 0:C], ones64.t[:, :], dg[bi].t[:], start=True, stop=True), [ones64.b, dg[bi].b], [p3.b])
                        S.op("dve", lambda e, h=h, bi=bi, c=c, p3=p3: e.scalar_tensor_tensor(out=t1[bi].t[:], in0=p3.t[0:C, 0:C], scalar=gamc.t[:, c, h:h + 1], in1=pmask.t[:],
                                                                                       op0=ALU.subtract, op1=ALU.add), [p3.b, gamc.b, pmask.b], [t1[bi].b])
                        S.op("act", lambda e, bi=bi: e.activation(Dm[bi].t[:], t1[bi].t[:], AF.Exp, scale=-1.0), [t1[bi].b], [Dm[bi].b])
                        S.op("dve", lambda e, h=h, bi=bi, c=c, p3=p3: e.scalar_tensor_tensor(out=t1[bi].t[:], in0=p3.t[0:C, 0:C], scalar=gamc.t[:, c, h:h + 1], in1=nmask.t[:],
                                                                                       op0=ALU.subtract, op1=ALU.add), [p3.b, gamc.b, nmask.b, Dm[bi].b], [t1[bi].b])
                        S.op("act", lambda e, bi=bi: e.activation(DT[bi].t[:], t1[bi].t[:], AF.Exp), [t1[bi].b], [DT[bi].b])
                        S.op("act", lambda e, o=o, p3=p3: e.activation(egl[o].t[:], p3.t[:, C - 1:C], AF.Exp), [p3.b], [egl[o].b])
                        S.op("dve", lambda e, h=h, o=o2[bi], c=c, p3=p3: e.tensor_scalar(kdc[o].t[:], p3.t[0:C, C - 1:C], gamc.t[:, c, h:h + 1], None, ALU.subtract), [p3.b, gamc.b], [kdc[o].b])
                        S.op("act", lambda e, o=o2[bi]: e.activation(kdc[o].t[:], kdc[o].t[:], AF.Exp), [kdc[o].b], [kdc[o].b])
                    if self.gdn_stage <= 4:
                        continue
                    for bi, c in enumerate(cs):
                        o = o2[bi]
                        pg_, pm_ = slot(), slot()
                        S.op("pe", lambda e, bi=bi, pg_=pg_: e.matmul(pg_.t[0:C, 0:C], knT[bi].t[:], knT[bi].t[:], start=True, stop=True), [knT[bi].b], [pg_.b])
                        S.op("dve", lambda e, h=h, bi=bi, c=c, o=o, pg_=pg_: e.scalar_tensor_tensor(out=X[o].t[:], in0=pg_.t[0:C, 0:C], scalar=nbetac.t[:, c, h:h + 1], in1=Dm[bi].t[:],
                                                                                              op0=ALU.mult, op1=ALU.mult), [pg_.b, nbetac.b, Dm[bi].b], [X[o].b])
                        S.op("pe", lambda e, bi=bi, o=o, pm_=pm_: e.matmul(pm_.t[0:C, 0:C], knT[bi].t[:], qnT[o].t[:], start=True, stop=True), [knT[bi].b, qnT[o].b], [pm_.b])
                        S.op("dve", lambda e, bi=bi, o=o, pm_=pm_: e.tensor_tensor(MT[o].t[:], pm_.t[0:C, 0:C], DT[bi].t[:], ALU.mult), [pm_.b, DT[bi].b], [MT[o].b])
                    for bi, c in enumerate(cs):
                        o = o2[bi]
                        px = slot()
                        S.op("pe", lambda e, o=o, px=px: e.transpose(px.t[0:C, 0:C], X[o].t[:], idf.t[0:C, 0:C]), [X[o].b, idf.b], [px.b])
                        S.op("act", lambda e, o=o, px=px: e.activation(XT[o].t[:], px.t[0:C, 0:C], AF.Copy), [px.b], [XT[o].b])
                        S.op("dve", lambda e, o=o, px=px: e.tensor_tensor(Y[o].t[:], px.t[0:C, 0:C], idf.t[0:C, 0:C], ALU.add), [px.b, idf.b], [Y[o].b])
                    if self.gdn_stage <= 5:
                        continue
                    for s_ in range(5):
                        for bi, c in enumerate(cs):
                            o = o2[bi]
                            pa, pbq = slot(), slot()
                            S.op("pe", lambda e, o=o, pa=pa: e.matmul(pa.t[0:C, 0:C], XT[o].t[:], X[o].t[:], start=True, stop=True), [XT[o].b, X[o].b], [pa.b])
                            if s_ < 4:
                                S.op("pe", lambda e, o=o, pbq=pbq: e.matmul(pbq.t[0:C, 0:C], X[o].t[:], XT[o].t[:], start=True, stop=True), [XT[o].b, X[o].b], [pbq.b])
                            S.op("act", lambda e, o=o, pa=pa: e.activation(X[o].t[:], pa.t[0:C, 0:C], AF.Copy), [pa.b], [X[o].b])
                            if s_ < 4:
                                S.op("act", lambda e, o=o, pbq=pbq: e.activation(XT[o].t[:], pbq.t[0:C, 0:C], AF.Copy), [pbq.b], [XT[o].b])
                        for bi, c in enumerate(cs):
                            o = o2[bi]
                            py = slot()
                            S.op("pe", lambda e, o=o, py=py: e.matmul(py.t[0:C, 0:C], X[o].t[:], Y[o].t[:], start=True, stop=True), [X[o].b, Y[o].b], [py.b])
                            S.op("dve", lambda e, o=o, py=py: e.tensor_tensor(Y[o].t[:], Y[o].t[:], py.t[0:C, 0:C], ALU.add), [py.b, Y[o].b], [Y[o].b])
                    if self.gdn_stage <= 6:
                        continue
                    for bi, c in enumerate(cs):
                        o = o2[bi]
                        S.op("dve", lambda e, h=h, bi=bi, c=c: e.tensor_scalar(Vb[bi].t[:], vt[bi].t[:], betac.t[:, c, h:h + 1], None, ALU.mult), [vt[bi].b, betac.b], [Vb[bi].b])
                        S.op("dve", lambda e, h=h, bi=bi, c=c: e.tensor_scalar(Kb[bi].t[:], kn[bi].t[:], begamc.t[:, c, h:h + 1], None, ALU.mult), [kn[bi].b, begamc.b], [Kb[bi].b])
                        S.op("dve", lambda e, bi=bi, o=o: e.tensor_scalar(Kd[o].t[:], kn[bi].t[:], kdc[o].t[:, 0:1], None, ALU.mult), [kn[bi].b, kdc[o].b], [Kd[o].b])
                        if self.gdn_stage <= 6.3:
                            continue
                        pu_, pw_ = slot(), slot()
                        S.op("pe", lambda e, bi=bi, o=o, pu_=pu_: e.matmul(pu_.t[0:C, :], Y[o].t[:], Vb[bi].t[:], start=True, stop=True), [Y[o].b, Vb[bi].b], [pu_.b])
                        S.op("act", lambda e, o=o, pu_=pu_: e.activation(U[o].t[:], pu_.t[0:C, :], AF.Copy), [pu_.b], [U[o].b])
                        if self.gdn_stage <= 6.5:
                            continue
                        S.op("pe", lambda e, bi=bi, o=o, pw_=pw_: e.matmul(pw_.t[:, 0:C], Kb[bi].t[:], Y[o].t[:], start=True, stop=True), [Y[o].b, Kb[bi].b], [pw_.b])
                        if self.gdn_stage <= 6.7:
                            continue
                        S.op("act", lambda e, o=o, pw_=pw_: e.activation(WmT[o].t[:], pw_.t[:, 0:C], AF.Copy), [pw_.b], [WmT[o].b])
                    if self.gdn_stage <= 7:
                        continue
                    for bi, c in enumerate(cs):
                        o = o2[bi]
                        k2 = c % 2
                        pws, pqs, pmv, psu = slot(), slot(), slot(), slot()
                        S.op("pe", lambda e, o=o, pws=pws: e.matmul(pws.t[0:C, :], WmT[o].t[:], St.t[:], start=True, stop=True), [WmT[o].b, St.b], [pws.b])
                        S.op("pe", lambda e, o=o, pqs=pqs: e.matmul(pqs.t[0:C, :], qnT[o].t[:], St.t[:], start=True, stop=True), [qnT[o].b, St.b], [pqs.b])
                        S.op("dve", lambda e, o=o, k2=k2, pws=pws: e.tensor_tensor(vnew[k2].t[:], U[o].t[:], pws.t[0:C, :], ALU.subtract), [U[o].b, pws.b], [vnew[k2].b])
                        S.op("pe", lambda e, o=o, k2=k2, pmv=pmv: e.matmul(pmv.t[0:C, :], MT[o].t[:], vnew[k2].t[:], start=True, stop=True), [MT[o].b, vnew[k2].b], [pmv.b])
                        S.op("pe", lambda e, o=o, k2=k2, psu=psu: e.matmul(psu.t[:, :], Kd[o].t[:], vnew[k2].t[:], start=True, stop=True), [Kd[o].b, vnew[k2].b], [psu.b])
                        S.op("dve", lambda e, o=o, psu=psu: e.scalar_tensor_tensor(out=St.t[:], in0=St.t[:], scalar=egl[o].t[:, 0:1], in1=psu.t[:, :], op0=ALU.mult, op1=ALU.add),
                             [St.b, egl[o].b, psu.b], [St.b])
                        S.op("act", lambda e, k2=k2, pmv=pmv: e.activation(mvs[k2].t[:], pmv.t[0:C, :], AF.Copy), [pmv.b], [mvs[k2].b])
                        S.op("dve", lambda e, h=h, k2=k2, c=c, pqs=pqs: e.scalar_tensor_tensor(out=osb[k2].t[:], in0=pqs.t[0:C, :], scalar=egamc.t[:, c, h:h + 1], in1=mvs[k2].t[:],
                                                                                        op0=ALU.mult, op1=ALU.add), [pqs.b, egamc.b, mvs[k2].b], [osb[k2].b])
                        S.op("act", lambda e, k2=k2: e.activation(ojunk[0].t[:], osb[k2].t[:], AF.Square, accum_out=oss[k2].t[:]), [osb[k2].b], [ojunk[0].b, oss[k2].b])
                        S.op("dve", lambda e, k2=k2: e.tensor_scalar(ors[k2].t[:], oss[k2].t[:], 1.0 / 128, EPS, ALU.mult, ALU.add), [oss[k2].b], [ors[k2].b])
                        S.op("act", lambda e, k2=k2: e.activation(ors[k2].t[:], ors[k2].t[:], AF.Sqrt), [ors[k2].b], [ors[k2].b])
                        S.op("dve", lambda e, k2=k2: e.reciprocal(ors[k2].t[:], ors[k2].t[:]), [ors[k2].b], [ors[k2].b])
                        S.op("dve", lambda e, k2=k2: e.scalar_tensor_tensor(out=osb[k2].t[:], in0=osb[k2].t[:], scalar=ors[k2].t[:, 0:1], in1=gnr.t[:], op0=ALU.mult, op1=ALU.mult),
                             [osb[k2].b, ors[k2].b, gnr.b], [osb[k2].b])
                        S.op("act", lambda e, k2=k2, z_=z_, bi=bi: e.activation(zs[k2].t[:], z_.t[:, bi, :], AF.Silu), [z_.b], [zs[k2].b])
                        dbgm = os.environ.get("KGDNDBG", "")
                        dsel = {"vn": vnew[k2], "u": U[o], "vb": Vb[bi], "kb": Kb[bi], "kd": Kd[o], "vt": vt[bi], "kn": kn[bi], "qn": qn[bi], "mv": mvs[k2]}.get(dbgm)
                        d64 = {"y": Y[o], "x": X[o], "mt": MT[o], "dm": Dm[bi], "dt": DT[bi]}.get(dbgm)
                        if d64 is not None:
                            S.op("dve", lambda e, k2=k2, d64=d64: e.tensor_copy(ofb[k2].t[:, 0:64], d64.t[:]), [osb[k2].b, zs[k2].b, d64.b], [ofb[k2].b])
                            S.op("dve", lambda e, k2=k2, d64=d64: e.tensor_copy(ofb[k2].t[:, 64:128], d64.t[:]), [osb[k2].b, zs[k2].b, d64.b], [ofb[k2].b])
                        elif dsel is not None:
                            S.op("dve", lambda e, k2=k2, dsel=dsel: e.tensor_copy(ofb[k2].t[:], dsel.t[:]), [osb[k2].b, zs[k2].b, dsel.b], [ofb[k2].b])
                        elif os.environ.get("KGDNDBG", "") == "z":
                            S.op("dve", lambda e, k2=k2: e.tensor_copy(ofb[k2].t[:], zs[k2].t[:]), [osb[k2].b, zs[k2].b], [ofb[k2].b])
                        elif os.environ.get("KGDNDBG", "") == "o":
                            S.op("dve", lambda e, k2=k2: e.tensor_copy(ofb[k2].t[:], osb[k2].t[:]), [osb[k2].b, zs[k2].b], [ofb[k2].b])
                        else:
                            S.op("dve", lambda e, k2=k2: e.tensor_tensor(ofb[k2].t[:], osb[k2].t[:], zs[k2].t[:], ALU.mult), [osb[k2].b, zs[k2].b], [ofb[k2].b])
                        pbt = self.ps[7]
                        pTr = pbt.t[:, :].bitcast(BF16)
                        stg = ost[(c // 8) % 2]
                        S.op("pe", lambda e, k2=k2, pTr=pTr: e.transpose(pTr[:, 0:C], ofb[k2].t[:], self.identb.t[0:C, 0:C]), [ofb[k2].b, self.identb.b], [pbt.b])
                        S.op("act", lambda e, stg=stg, pTr=pTr, c=c: e.activation(stg.t[:, (c % 8) * C:(c % 8 + 1) * C], pTr[:, 0:C], AF.Copy), [pbt.b], [stg.b])
                        if c % 8 == 7:
                            cc = c // 8
                            S.dma("sp", self.oT[8 + h][:, cc * 512:(cc + 1) * 512], stg.t[:], stg.c, reads=[stg.b])

    def phase_outproj_ffn(self, layer, wout, xsrc):
        S = self.S
        Wo = wout.rearrange("(k p) n -> p k n", p=128)
        Wu = self.ffn_w_up[layer].rearrange("(k p) n -> p k n", p=128)
        Wd = self.ffn_w_down[layer].rearrange("(c p) n -> p c n", p=128)
        oTv = self.oT.rearrange("f p t -> p f t")
        xrb = [Buf("xr%d" % i) for i in range(NT)]
        with ExitStack() as ph:
            gain = self.sb(ph, "gain", [128, D], F32, chan=True)
            S.dma("sp", gain.t[:], self.norm_ffn[layer, :].partition_broadcast(128), gain.c, writes=[gain.b])
            cwr = self.sb(ph, "cwr", [FC, 4, 128], F32, chan=True)
            cw = self.sb(ph, "cw", [128, 4, FC], F32)
            S.dma_group("sp", [(cwr.t[:, jj, :], self.ffn_conv_w[layer, jj, :].rearrange("(c p) -> c p", p=128)) for jj in range(3)]
                        + [(cwr.t[:, 3, :], self.ffn_conv_b[layer, :].rearrange("(c p) -> c p", p=128))], cwr.c, writes=[cwr.b])
            pb = self.ps[7]
            for jj in range(4):
                S.op("pe", lambda e, jj=jj: e.transpose(pb.t[:, jj * FC:(jj + 1) * FC], cwr.t[:, jj, :], self.identf.t[0:FC, 0:FC]),
                     [cwr.b, self.identf.b], [pb.b], inc=(jj == 3))
            S.op("dve", lambda e: e.tensor_copy(cw.t[:], pb.t[:, 0:4 * FC].rearrange("p (j c) -> p j c", c=FC)), [pb.b], [cw.b])
            halo = self.sb(ph, "halo", [128, FC, 2], F32)
            S.op("dve", lambda e: e.memset(halo.t[:], 0.0), [], [halo.b])
            big = self.sb(ph, "big", [128, FC * TG], BF16, chan=True)
            actT_v = big.t[:].rearrange("p (c t) -> p c t", t=TG)
            bigf = big.t[:].bitcast(F32)
            x1v = [bigf[:, tt * D:(tt + 1) * D] for tt in range(4)]
            hs = self.sb(ph, "hT", [128, KC, TG], BF16, chan=True)
            xi_ = [self.sb(ph, "xi", [128, 512], F32, chan=True) for _ in range(2)]
            xo_ = [self.sb(ph, "xo", [128, 512], F32, chan=True) for _ in range(2)]
            xn = self.sb(ph, "xn", [128, D], BF16)
            st_ = [self.sb(ph, "ss", [128, 1], F32), self.sb(ph, "rs", [128, 1], F32)]
            wu = [self.sb(ph, "wu", [128, 16, 256], BF16, chan=True) for _ in range(3)]
            wg = [self.sb(ph, "wg", [128, 16, 256], BF16, chan=True) for _ in range(3)]
            wd = [self.sb(ph, "wd", [128, 11, 512], BF16, chan=True) for _ in range(3)]
            gsb = [self.sb(ph, "gsb", [128, TG + 2], F32) for _ in range(2)]
            cacc = [self.sb(ph, "cacc", [128, TG], F32) for _ in range(2)]
            sg = [self.sb(ph, "sg", [128, TG], F32) for _ in range(2)]
            nwo = nwu = nwd = nxi = nxo = 0
            for g in range(NG):
                tok = slice(g * TG, (g + 1) * TG)
                S.dma("sp", hs.t[:], oTv[:, :, tok], hs.c, writes=[hs.b])
                for nb_ in range(D // 256):
                    w = (wu + wg)[nwo % 6]
                    nwo += 1
                    S.dma("pool", w.t[:], Wo[:, :, nb_ * 256:(nb_ + 1) * 256], w.c, writes=[w.b])
                    for tt in range(4):
                        tile = g * 4 + tt
                        pbk = self.ps[(nb_ * 4 + tt) % 4]
                        xi = xi_[nxi % 2]
                        nxi += 1
                        S.dma("sp", xi.t[:, 0:256], xsrc[tile * 128:(tile + 1) * 128, nb_ * 256:(nb_ + 1) * 256], xi.c, reads=[xrb[tile]], writes=[xi.b])
                        self.mm(pbk.t[:, 0:256], pbk.b, [(hs.t[:, f, tt * 128:(tt + 1) * 128], w.t[:, f, :]) for f in range(16)], [hs.b, w.b])
                        S.op("dve", lambda e, tt=tt, pbk=pbk, xi=xi, nb_=nb_: e.tensor_tensor(x1v[tt][:, nb_ * 256:(nb_ + 1) * 256], pbk.t[:, 0:256], xi.t[:, 0:256], ALU.add),
                             [pbk.b, xi.b], [big.b])
                for tt in range(4):
                    tile = g * 4 + tt
                    S.dma("sp", self.xr[tile * 128:(tile + 1) * 128, :], x1v[tt], big.c, reads=[big.b], writes=[xrb[tile]])
                self.norm_x1(x1v, big.b, gain, xn, st_, hs)
                for ub in range(D_FF // 256):
                    wu_, wg_ = wu[nwu % 3], wg[nwu % 3]
                    nwu += 1
                    S.dma("pool", wu_.t[:], Wu[:, :, ub * 256:(ub + 1) * 256], wu_.c, writes=[wu_.b])
                    S.dma("pool", wg_.t[:], Wu[:, :, D_FF + ub * 256:D_FF + (ub + 1) * 256], wg_.c, writes=[wg_.b])
                    for cc in range(2):
                        c = ub * 2 + cc
                        pu = self.ps[(c % 2) * 2]
                        pg = self.ps[(c % 2) * 2 + 1]
                        self.mm(pg.t[:, :], pg.b, [(wg_.t[:, k, cc * 128:(cc + 1) * 128], hs.t[:, k, :]) for k in range(KC)], [wg_.b, hs.b])
                        self.mm(pu.t[:, :], pu.b, [(wu_.t[:, k, cc * 128:(cc + 1) * 128], hs.t[:, k, :]) for k in range(KC)], [wu_.b, hs.b])
                        gs, ca, sg_ = gsb[c % 2], cacc[c % 2], sg[c % 2]
                        S.op("act", lambda e, gs=gs, pg=pg: e.activation(gs.t[:, 2:TG + 2], pg.t[:, :], AF.Copy), [pg.b], [gs.b])
                        S.op("dve", lambda e, gs=gs, c=c: e.tensor_copy(gs.t[:, 0:2], halo.t[:, c, :]), [halo.b, gs.b], [gs.b])
                        S.op("dve", lambda e, gs=gs, ca=ca, c=c: e.tensor_scalar(ca.t[:], gs.t[:, 2:TG + 2], cw.t[:, 2, c:c + 1], cw.t[:, 3, c:c + 1], ALU.mult, ALU.add),
                             [gs.b, cw.b], [ca.b])
                        S.op("dve", lambda e, gs=gs, ca=ca, c=c: e.scalar_tensor_tensor(out=ca.t[:], in0=gs.t[:, 1:TG + 1], scalar=cw.t[:, 1, c:c + 1], in1=ca.t[:], op0=ALU.mult, op1=ALU.add),
                             [gs.b, cw.b, ca.b], [ca.b])
                        S.op("dve", lambda e, gs=gs, ca=ca, c=c: e.scalar_tensor_tensor(out=ca.t[:], in0=gs.t[:, 0:TG], scalar=cw.t[:, 0, c:c + 1], in1=ca.t[:], op0=ALU.mult, op1=ALU.add),
                             [gs.b, cw.b, ca.b], [ca.b])
                        S.op("dve", lambda e, gs=gs, c=c: e.tensor_copy(halo.t[:, c, :], gs.t[:, TG:TG + 2]), [gs.b, halo.b], [halo.b])
                        S.op("act", lambda e, sg_=sg_, ca=ca: e.activation(sg_.t[:], ca.t[:], AF.Silu), [ca.b], [sg_.b])
                        S.op("dve", lambda e, sg_=sg_, pu=pu, c=c: e.tensor_tensor(actT_v[:, c, :], sg_.t[:], pu.t[:, :], ALU.mult), [sg_.b, pu.b], [big.b])
                for nb_ in range(D // 512):
                    accs = [self.ps[4 + tt] for tt in range(4)]
                    for qd in range(4):
                        w = wd[nwd % 3]
                        nwd += 1
                        S.dma("pool", w.t[:], Wd[:, qd * 11:(qd + 1) * 11, nb_ * 512:(nb_ + 1) * 512], w.c, writes=[w.b])
                        for tt in range(4):
                            for ci in range(11):
                                c = qd * 11 + ci
                                S.op("pe", lambda e, tt=tt, ci=ci, c=c, w=w, acc=accs[tt]: e.matmul(
                                    acc.t[:, :], actT_v[:, c, tt * 128:(tt + 1) * 128], w.t[:, ci, :], start=(c == 0), stop=(c == FC - 1)),
                                    [big.b, w.b], [accs[tt].b], inc=(ci == 10))
                    for tt in range(4):
                        tile = g * 4 + tt
                        xi = xi_[nxi % 2]
                        nxi += 1
                        xo = xo_[nxo % 2]
                        nxo += 1
                        S.dma("sp", xi.t[:], self.xr[tile * 128:(tile + 1) * 128, nb_ * 512:(nb_ + 1) * 512], xi.c, reads=[xrb[tile]], writes=[xi.b])
                        S.op("dve", lambda e, tt=tt, xo=xo, xi=xi: e.tensor_tensor(xo.t[:], accs[tt].t[:, :], xi.t[:], ALU.add),
                             [accs[tt].b, xi.b], [xo.b])
                        S.dma("sp", self.xr[tile * 128:(tile + 1) * 128, nb_ * 512:(nb_ + 1) * 512], xo.t[:], xo.c, reads=[xo.b], writes=[xrb[tile]])

    def norm_x1(self, x1v, xb, gain, n, st_, hs):
        S = self.S
        ss, rs = st_
        for tt in range(4):
            xv = x1v[tt]
            S.op("act", lambda e, xv=xv: e.activation(n.t[:], xv, AF.Square, accum_out=ss.t[:]), [xb], [n.b, ss.b])
            S.op("dve", lambda e: e.tensor_scalar(rs.t[:], ss.t[:], 1.0 / D, EPS, ALU.mult, ALU.add), [ss.b], [rs.b])
            S.op("act", lambda e: e.activation(rs.t[:], rs.t[:], AF.Sqrt), [rs.b], [rs.b])
            S.op("dve", lambda e: e.reciprocal(rs.t[:], rs.t[:]), [rs.b], [rs.b])
            S.op("dve", lambda e, xv=xv: e.scalar_tensor_tensor(out=n.t[:], in0=xv, scalar=rs.t[:, 0:1], in1=gain.t[:],
                                                               op0=ALU.mult, op1=ALU.mult), [xb, rs.b, gain.b], [n.b])
            for half in range(2):
                pb = self.ps[2 + half]
                pT = pb.t[:, :].bitcast(BF16)
                for kk in range(8):
                    k = half * 8 + kk
                    S.op("pe", lambda e, k=k, kk=kk, pT=pT: e.transpose(pT[:, kk * 128:(kk + 1) * 128], n.t[:, k * 128:(k + 1) * 128], self.identb.t[:]),
                         [n.b, self.identb.b], [pb.b], inc=(kk == 7))
                dst = hs.t[:, half * 8:(half + 1) * 8, tt * 128:(tt + 1) * 128]
                src = pT.rearrange("p (k t) -> p k t", t=128)
                if half == 0:
                    S.op("act", lambda e, dst=dst, src=src: e.activation(dst, src, AF.Copy), [pb.b], [hs.b])
                else:
                    S.op("dve", lambda e, dst=dst, src=src: e.tensor_copy(dst, src), [pb.b], [hs.b])

    def phase_final(self, xsrc):
        S = self.S
        with ExitStack() as ph:
            gain = self.sb(ph, "gain", [128, D], F32, chan=True)
            S.dma("sp", gain.t[:], self.norm_final.partition_broadcast(128), gain.c, writes=[gain.b])
            xt = [self.sb(ph, "xt", [128, D], F32, chan=True) for _ in range(3)]
            yo = [self.sb(ph, "yo", [128, D], F32, chan=True) for _ in range(2)]
            sq = self.sb(ph, "sq", [128, D], BF16)
            ss = [self.sb(ph, "ss", [128, 1], F32) for _ in range(2)]
            rs = [self.sb(ph, "rs", [128, 1], F32) for _ in range(2)]
            for tile in range(NT):
                x = xt[tile % 3]
                s_, r_, y_ = ss[tile % 2], rs[tile % 2], yo[tile % 2]
                S.dma("sp", x.t[:], xsrc[tile * 128:(tile + 1) * 128, :], x.c, writes=[x.b])
                S.op("act", lambda e, x=x, s_=s_: e.activation(sq.t[:], x.t[:], AF.Square, accum_out=s_.t[:]), [x.b], [sq.b, s_.b])
                S.op("dve", lambda e, s_=s_, r_=r_: e.tensor_scalar(r_.t[:], s_.t[:], 1.0 / D, EPS, ALU.mult, ALU.add), [s_.b], [r_.b])
                S.op("act", lambda e, r_=r_: e.activation(r_.t[:], r_.t[:], AF.Sqrt), [r_.b], [r_.b])
                S.op("dve", lambda e, r_=r_: e.reciprocal(r_.t[:], r_.t[:]), [r_.b], [r_.b])
                S.op("dve", lambda e, x=x, r_=r_, y_=y_: e.scalar_tensor_tensor(out=y_.t[:], in0=x.t[:], scalar=r_.t[:, 0:1], in1=gain.t[:],
                                                                              op0=ALU.mult, op1=ALU.mult), [x.b, r_.b, gain.b], [y_.b])
                S.dma("sp", self.y[tile * 128:(tile + 1) * 128, :], y_.t[:], y_.c, reads=[y_.b])


_INPUT_NAMES = ["rel_bias", "norm_mix", "norm_ffn", "norm_final", "ev_w_in", "ev_b_forget", "ev_sinks", "ev_w_out",
                "od_w_in", "od_cmp_pos", "od_cmp_w1", "od_cmp_w2", "od_conv_w", "od_a_log", "od_dt_bias", "od_gdn_norm", "od_w_out",
                "ffn_w_up", "ffn_conv_w", "ffn_conv_b", "ffn_w_down"]


def kernel(**inputs):
    b = Builder()
    nc = b.build()
    consts = host_consts()
    x = np.ascontiguousarray(inputs["x"], dtype=np.float32)
    shared = {k: np.ascontiguousarray(inputs[k], dtype=np.float32) for k in _INPUT_NAMES}
    shared.update(consts)
    in_maps = []
    for c in range(N_CORES):
        m = dict(shared)
        m["x"] = x[c]
        in_maps.append(m)
    res = run_bass_kernel_spmd(nc, in_maps, core_ids=list(range(N_CORES)))
    return np.stack([np.asarray(r["y"]) for r in res.results], axis=0).astype(np.float32)
```

```python
import math
import os
import numpy as np
from contextlib import ExitStack
import concourse.bass as bass
import concourse.mybir as mybir
from concourse.bass_utils import run_bass_kernel_spmd

F32 = mybir.dt.float32
BF16 = mybir.dt.bfloat16
AF = mybir.ActivationFunctionType
ALU = mybir.AluOpType
AX = mybir.AxisListType

D = 2048
T = 4096
KC = D // 128
NT = T // 128
TG = 512
NG = T // TG
DEPTH = 4
HD = 128
D_FF = 5632
FC = D_FF // 128
EVEN_COLS = 4616
ODD_COLS = 6696
SCALE = HD ** -0.5
EPS = 1e-6
NEG = -30000.0
N_CORES = 8


class Buf:
    __slots__ = ("name", "w", "r", "excl")

    def __init__(self, name="", excl=False):
        self.name = name
        self.w = {}
        self.r = {}
        self.excl = excl


class Chan:
    __slots__ = ("sem", "cnt", "key")

    def __init__(self, sem, key):
        self.sem = sem
        self.cnt = 0
        self.key = key


class Sched:
    ENG = ("pe", "act", "dve", "pool", "sp")

    def __init__(self, nc, stack):
        self.nc = nc
        self.stack = stack
        self.cnt = {}
        self.semobj = {}
        for e in self.ENG:
            self.semobj[("e", e)] = stack.enter_context(nc.semaphore("s_" + e))
            self.cnt[e] = 0
        self.known = {e: {} for e in self.ENG}
        self.prog = {e: [] for e in self.ENG}
        self.chans = []

    def chan(self):
        key = ("c", len(self.chans))
        sem = self.stack.enter_context(self.nc.semaphore("c%d" % key[1]))
        self.semobj[key] = sem
        c = Chan(sem, key)
        self.chans.append(c)
        return c

    def _collect(self, e, reads, writes, extra=()):
        need = {}
        for b in reads:
            for k, v in b.w.items():
                if need.get(k, 0) < v:
                    need[k] = v
            if b.excl:
                for k, v in b.r.items():
                    if need.get(k, 0) < v:
                        need[k] = v
        for b in writes:
            for k, v in b.w.items():
                if need.get(k, 0) < v:
                    need[k] = v
            for k, v in b.r.items():
                if need.get(k, 0) < v:
                    need[k] = v
        for k, v in extra:
            if need.get(k, 0) < v:
                need[k] = v
        if e == "pe":
            need.pop(("e", "pe"), None)
        waits = []
        kn = self.known[e]
        for k, v in need.items():
            if kn.get(k, 0) >= v:
                continue
            kn[k] = v
            waits.append((k, v))
        return waits

    @staticmethod
    def _commit(ev, reads, writes):
        k, v = ev
        for b in reads:
            if b.r.get(k, 0) < v:
                b.r[k] = v
        for b in writes:
            if b.w.get(k, 0) < v:
                b.w[k] = v

    def op(self, e, fn, reads=(), writes=(), inc=True):
        waits = self._collect(e, reads, writes)
        ev = (("e", e), self.cnt[e] + 1)
        if inc:
            self.cnt[e] += 1
        self.prog[e].append((waits, fn, (("e", e), 1) if inc else None))
        self._commit(ev, reads, writes)

    def dma(self, q, out, in_, chan, reads=(), writes=()):
        self.dma_group(q, [(out, in_)], chan, reads, writes)

    def dma_group(self, q, pairs, chan, reads=(), writes=()):
        extra = [(chan.key, chan.cnt)] if chan.cnt > 0 else []
        waits = self._collect(q, reads, writes, extra)
        for (o, i) in pairs:
            chan.cnt += 16
            fn = lambda eng, o=o, i=i: eng.dma_start(out=o, in_=i)
            self.prog[q].append((waits, fn, (chan.key, 16)))
            waits = []
        self._commit((chan.key, chan.cnt), reads, writes)

    def barrier(self):
        evs = [(("e", o), self.cnt[o]) for o in self.ENG if self.cnt[o] > 0]
        evs += [(c.key, c.cnt) for c in self.chans if c.cnt > 0]
        for e in self.ENG:
            waits = self._collect(e, (), (), evs)
            if e == "pe":
                pass
            if waits:
                self.prog[e].append((waits, None, None))

    def emit(self):
        nc = self.nc
        with nc.Block() as block:
            def run(e):
                def body(eng):
                    for waits, fn, inc in self.prog[e]:
                        for k, v in waits:
                            eng.wait_ge(self.semobj[k], v)
                        if fn is None:
                            continue
                        ins = fn(eng)
                        if inc is not None:
                            ins.then_inc(self.semobj[inc[0]], inc[1])
                return body
            block.tensor(run("pe"))
            block.scalar(run("act"))
            block.vector(run("dve"))
            block.gpsimd(run("pool"))
            block.sync(run("sp"))


class Tl:
    __slots__ = ("t", "b", "c")

    def __init__(self, t, b, c=None):
        self.t = t
        self.b = b
        self.c = c


def _t5_bucket(dist):
    n = np.maximum(dist, 0)
    lr = np.log(np.maximum(n, 1).astype(np.float32) / np.float32(16)) / np.float32(math.log(128 / 16))
    large = np.minimum(16 + (lr * np.float32(16)).astype(np.int32), 31)
    return np.where(n < 16, n, large)


def host_consts():
    k = np.arange(128)[:, None]
    q = np.arange(128)[None, :]
    oh = np.zeros((2, 32, 128, 128), np.float32)
    for o in range(2):
        dist = q - k + 128 * o
        bk = _t5_bucket(dist)
        for b in range(32):
            oh[o, b] = ((bk == b) & (dist >= 0)).astype(np.float32)
    c = {}
    c["c_oh"] = oh.transpose(2, 0, 1, 3).reshape(128, 64 * 128).copy()
    c["c_ident"] = np.eye(128, dtype=np.float32)
    c["c_causal01"] = (q >= k).astype(np.float32)
    c["c_causalneg"] = np.where(q >= k, 0.0, NEG).astype(np.float32)
    c["c_anticausalneg"] = np.where(q < k, 0.0, NEG).astype(np.float32)
    rm = np.ones((8, T), np.float32)
    rm[:, ::64] = 0.0
    c["c_rmask"] = rm
    ii = np.arange(64)[:, None]
    jj = np.arange(64)[None, :]
    c["c_gdn_pmask"] = np.where(ii > jj, 0.0, -NEG).astype(np.float32)
    c["c_gdn_nmask"] = np.where(jj >= ii, 0.0, NEG).astype(np.float32)
    cl = np.arange(128)[:, None, None]
    ti = np.arange(32)[None, :, None]
    cm = np.zeros((128, 64, 128), np.float32)
    qq = np.arange(128)[None, :]
    for i in range(32):
        for ct in range(2):
            cc = ct * 128 + np.arange(128)[:, None]
            cm[:, i * 2 + ct, :] = ((cc <= 254) & (16 * cc + 31 <= 128 * i + qq)).astype(np.float32)
    c["c_cmask"] = cm.reshape(128, 64 * 128)
    wi = np.zeros((256, 64), np.float32)
    for cidx in range(255):
        for j in range(64):
            wi[cidx, j] = sum(1 for m in range(4 * j, 4 * j + 4) if m == cidx or m == cidx + 1)
    c["c_wimp"] = wi.reshape(2, 128, 64).transpose(1, 0, 2).reshape(128, 128).copy()
    keep = np.zeros((128, 32, 64), np.float32)
    add = np.zeros((128, 32, 64), np.float32)
    jv = np.arange(64)[None, :]
    for i in range(32):
        cur = (2 * i + (np.arange(128) >= 64).astype(np.int64))[:, None]
        forced = (jv == 0) | (jv == cur) | (jv == cur - 1)
        fut = (jv > cur) & ~forced
        keep[:, i, :] = (~forced & ~fut).astype(np.float32)
        add[:, i, :] = np.where(forced, 1e9, np.where(fut, -1e9, 0.0))
    c["c_selkeep"] = keep.reshape(128, 32 * 64)
    c["c_seladd"] = add.reshape(128, 32 * 64)
    es = np.zeros((64, 32, 128), np.float32)
    for jb in range(32):
        es[2 * jb, jb, 0:64] = 1.0
        es[2 * jb + 1, jb, 64:128] = 1.0
    c["c_esel"] = es.reshape(64, 32 * 128)
    return c


class Builder:
    def __init__(self, layers=None, debug=False):
        self.layers = list(range(DEPTH)) if layers is None else list(layers)
        self.debug = debug
        self.nc = bass.Bass("TRN2", target_bir_lowering=False)
        self.uid = 0
        self.free_chans = []
        import os
        self.skip = set(os.environ.get("KSKIP", "").split(","))
        self.feed = set(os.environ.get("KFEED", "").split(","))
        self.gdn_heads = int(os.environ.get("KGDNH", "8"))
        self.gdn_stage = float(os.environ.get("KGDNS", "99"))
        self.small = os.environ.get("KSMALL", "") == "1"

    def din(self, name, shape, dt=F32):
        if self.small and name in ("ev_w_in", "ev_w_out", "od_w_in", "od_w_out", "ffn_w_up", "ffn_w_down"):
            shape = [1, 1]
        return self.nc.dram_tensor(name, list(shape), dt, kind="ExternalInput").ap()

    def dscr(self, name, shape, dt):
        kind = "ExternalOutput" if (self.debug and name in self.debug) else "Internal"
        if name in self.feed:
            kind = "ExternalInput"
        return self.nc.dram_tensor(name, list(shape), dt, kind=kind).ap()

    def sb(self, ph, name, shape, dt, chan=False):
        self.uid += 1
        t = ph.enter_context(self.nc.sbuf_tensor("%s_%d" % (name, self.uid), list(shape), dt))
        c = None
        if chan is True:
            c = self.free_chans.pop() if self.free_chans else self.S.chan()
            ph.callback(self.free_chans.append, c)
        return Tl(t, Buf(name), c)

    def mm(self, out_ap, out_buf, pairs, reads):
        n = len(pairs)
        for i, (l, r) in enumerate(pairs):
            self.S.op("pe", lambda e, l=l, r=r, i=i: e.matmul(out_ap, l, r, start=(i == 0), stop=(i == n - 1)),
                      reads, [out_buf], inc=(i == n - 1))

    def build(self):
        nc = self.nc
        self.x = self.din("x", [T, D])
        self.rel_bias = self.din("rel_bias", [32, 8])
        self.norm_mix = self.din("norm_mix", [DEPTH, D])
        self.norm_ffn = self.din("norm_ffn", [DEPTH, D])
        self.norm_final = self.din("norm_final", [D])
        self.ev_w_in = self.din("ev_w_in", [2, D, EVEN_COLS])
        self.ev_b_forget = self.din("ev_b_forget", [2, 8])
        self.ev_sinks = self.din("ev_sinks", [2, 8])
        self.ev_w_out = self.din("ev_w_out", [2, D, D])
        self.od_w_in = self.din("od_w_in", [2, D, ODD_COLS])
        self.od_cmp_pos = self.din("od_cmp_pos", [2, 2, 32, 128])
        self.od_cmp_w1 = self.din("od_cmp_w1", [2, 2, 4096, 256])
        self.od_cmp_w2 = self.din("od_cmp_w2", [2, 2, 256, 128])
        self.od_conv_w = self.din("od_conv_w", [2, 4, 3072])
        self.od_a_log = self.din("od_a_log", [2, 8])
        self.od_dt_bias = self.din("od_dt_bias", [2, 8])
        self.od_gdn_norm = self.din("od_gdn_norm", [2, 128])
        self.od_w_out = self.din("od_w_out", [2, D, D])
        self.ffn_w_up = self.din("ffn_w_up", [DEPTH, D, 2 * D_FF])
        self.ffn_conv_w = self.din("ffn_conv_w", [DEPTH, 3, D_FF])
        self.ffn_conv_b = self.din("ffn_conv_b", [DEPTH, D_FF])
        self.ffn_w_down = self.din("ffn_w_down", [DEPTH, D_FF, D])
        self.c_oh = self.din("c_oh", [128, 64 * 128])
        self.c_ident = self.din("c_ident", [128, 128])
        self.c_causal01 = self.din("c_causal01", [128, 128])
        self.c_causalneg = self.din("c_causalneg", [128, 128])
        self.c_anticausalneg = self.din("c_anticausalneg", [128, 128])
        self.c_rmask = self.din("c_rmask", [8, T])
        self.c_gdn_pmask = self.din("c_gdn_pmask", [64, 64])
        self.c_gdn_nmask = self.din("c_gdn_nmask", [64, 64])
        self.c_cmask = self.din("c_cmask", [128, 64 * 128])
        self.c_wimp = self.din("c_wimp", [128, 128])
        self.c_selkeep = self.din("c_selkeep", [128, 32 * 64])
        self.c_seladd = self.din("c_seladd", [128, 32 * 64])
        self.c_esel = self.din("c_esel", [64, 32 * 128])
        self.y = nc.dram_tensor("y", [T, D], F32, kind="ExternalOutput").ap()
        self.xr = self.dscr("xr", [T, D], F32)
        self.qaT = self.dscr("qaT", [8, 128, T], BF16)
        self.kaT = self.dscr("kaT", [2, 128, T], BF16)
        self.va = self.dscr("va", [T, 256], BF16)
        self.qbT = self.dscr("qbT", [8, 128, T], BF16)
        self.kbT = self.dscr("kbT", [8, 128, T], BF16)
        self.vb = self.dscr("vb", [T, 1024], BF16)
        self.fT = self.dscr("fT", [8, T], F32)
        self.csd = self.dscr("csd", [8, T], F32)
        self.oT = self.dscr("oT", [16, 128, T], BF16)
        self.qcT = self.dscr("qcT", [8, 128, T], BF16)
        self.kcmpT = self.dscr("kcmpT", [2, 128, T], BF16)
        self.vcmpT = self.dscr("vcmpT", [2, 128, T], BF16)
        self.kselT = self.dscr("kselT", [2, 128, T], BF16)
        self.kwinT = self.dscr("kwinT", [2, 128, T], BF16)
        self.vsel = self.dscr("vsel", [T, 256], BF16)
        self.vwin = self.dscr("vwin", [T, 256], BF16)
        self.gates = self.dscr("gates", [T, 24], F32)
        self.qdT = self.dscr("qdT", [8, 128, T], F32)
        self.kdT = self.dscr("kdT", [8, 128, T], F32)
        self.vdT = self.dscr("vdT", [8, 128, T], F32)
        self.baT = self.dscr("baT", [16, T], F32)
        self.zg = self.dscr("zg", [T, 1024], F32)

        with ExitStack() as st:
            self.S = S = Sched(nc, st)
            self.ps = [Tl(st.enter_context(nc.psum_tensor("ps%d" % i, [128, 512], F32)), Buf("ps%d" % i, excl=True)) for i in range(8)]
            self.setup_consts(st)
            xsrc = self.x
            for layer in self.layers:
                j = layer // 2
                if layer % 2 == 0:
                    self.phase_inproj_even(layer, j, xsrc)
                    S.barrier()
                    self.phase_swa(j)
                    S.barrier()
                    self.phase_fox(j)
                    S.barrier()
                    wout = self.ev_w_out[j]
                else:
                    if "inproj" not in self.skip:
                        self.phase_inproj_odd(layer, j, xsrc)
                        S.barrier()
                    if "nsa" not in self.skip:
                        self.phase_nsa(j)
                        S.barrier()
                    if "gdn" not in self.skip:
                        self.phase_gdn(j)
                        S.barrier()
                    wout = self.od_w_out[j]
                if "ffn" not in self.skip:
                    self.phase_outproj_ffn(layer, wout, xsrc)
                    S.barrier()
                xsrc = self.xr
            if "final" not in self.skip:
                self.phase_final(xsrc)
                S.barrier()
            S.emit()
        return nc

    def setup_consts(self, st):
        nc, S = self.nc, self.S
        self.identf = self.sb(st, "identf", [128, 128], F32, chan=True)
        self.identb = self.sb(st, "identb", [128, 128], BF16)
        self.causal01 = self.sb(st, "causal01", [128, 128], BF16)
        self.onesrow = self.sb(st, "onesrow", [1, 128], BF16)
        self.bias0 = self.sb(st, "bias0", [128, 8, 128], F32)
        self.bias1w = self.sb(st, "bias1w", [128, 8, 128], F32)
        self.nb0 = self.sb(st, "nb0", [128, 8, 128], F32)
        self.nb1 = self.sb(st, "nb1", [128, 8, 128], F32)
        self.acneg = self.sb(st, "acneg", [128, 128], F32, chan=True)
        S.dma("sp", self.acneg.t[:], self.c_anticausalneg[:, :], self.acneg.c, writes=[self.acneg.b])
        S.dma("sp", self.identf.t[:], self.c_ident[:, :], self.identf.c, writes=[self.identf.b])
        S.op("dve", lambda e: e.tensor_copy(self.identb.t[:], self.identf.t[:]), [self.identf.b], [self.identb.b])
        S.op("dve", lambda e: e.memset(self.onesrow.t[:], 1.0), [], [self.onesrow.b])
        with ExitStack() as ph:
            oh = self.sb(ph, "oh", [128, 64 * 128], F32, chan=True)
            rbb = self.sb(ph, "rbb", [128, 256], F32, chan=True)
            cz = self.sb(ph, "cz", [128, 128], F32, chan=True)
            cn = self.sb(ph, "cn", [128, 128], F32, chan=True)
            an = self.sb(ph, "an", [128, 128], F32, chan=True)
            S.dma("sp", oh.t[:], self.c_oh[:, :], oh.c, writes=[oh.b])
            S.dma("sp", rbb.t[:], self.rel_bias.rearrange("b h -> (b h)").partition_broadcast(128), rbb.c, writes=[rbb.b])
            S.dma("sp", cz.t[:], self.c_causal01[:, :], cz.c, writes=[cz.b])
            S.dma("sp", cn.t[:], self.c_causalneg[:, :], cn.c, writes=[cn.b])
            S.dma("sp", an.t[:], self.c_anticausalneg[:, :], an.c, writes=[an.b])
            S.op("dve", lambda e: e.tensor_copy(self.causal01.t[:], cz.t[:]), [cz.b], [self.causal01.b])
            for h in range(8):
                for o, (dst, base) in enumerate(((self.bias0, cn), (self.bias1w, an))):
                    for b in range(32):
                        src1 = base.t[:] if b == 0 else dst.t[:, h, :]
                        col = b * 8 + h
                        S.op("dve", lambda e, o=o, b=b, h=h, dst=dst, src1=src1, col=col: e.scalar_tensor_tensor(
                            out=dst.t[:, h, :], in0=oh.t[:, (o * 32 + b) * 128:(o * 32 + b + 1) * 128],
                            scalar=rbb.t[:, col:col + 1], in1=src1, op0=ALU.mult, op1=ALU.add),
                            [oh.b, rbb.b, base.b, dst.b], [dst.b])
            for h in range(8):
                c31 = rbb.t[:, 31 * 8 + h:31 * 8 + h + 1]
                S.op("dve", lambda e, h=h: e.tensor_tensor(self.nb1.t[:, h, :], self.bias1w.t[:, h, :], an.t[:], ALU.subtract), [self.bias1w.b, an.b], [self.nb1.b])
                S.op("dve", lambda e, h=h, c31=c31: e.tensor_scalar(self.nb1.t[:, h, :], self.nb1.t[:, h, :], c31, None, ALU.subtract), [self.nb1.b, rbb.b], [self.nb1.b])
                S.op("dve", lambda e, h=h, c31=c31: e.tensor_scalar(self.nb0.t[:, h, :], self.bias0.t[:, h, :], c31, None, ALU.subtract), [self.bias0.b, rbb.b], [self.nb0.b])
            S.barrier()

    def norm_group(self, g, xsrc, gain, xt, sq, xn, st_, hs, keep_x=None):
        S = self.S
        for tt in range(4):
            tile = g * 4 + tt
            x = xt[tile % len(xt)]
            S.dma("sp", x.t[:], xsrc[tile * 128:(tile + 1) * 128, :], x.c, writes=[x.b])
            ss, rs = st_[0], st_[1]
            S.op("act", lambda e, x=x: e.activation(sq.t[:], x.t[:], AF.Square, accum_out=ss.t[:]), [x.b], [sq.b, ss.b])
            S.op("dve", lambda e: e.tensor_scalar(rs.t[:], ss.t[:], 1.0 / D, EPS, ALU.mult, ALU.add), [ss.b], [rs.b])
            S.op("act", lambda e: e.activation(rs.t[:], rs.t[:], AF.Sqrt), [rs.b], [rs.b])
            S.op("dve", lambda e: e.reciprocal(rs.t[:], rs.t[:]), [rs.b], [rs.b])
            n = xn[tile % len(xn)]
            S.op("dve", lambda e, x=x, n=n: e.scalar_tensor_tensor(out=n.t[:], in0=x.t[:], scalar=rs.t[:, 0:1], in1=gain.t[:],
                                                                  op0=ALU.mult, op1=ALU.mult), [x.b, rs.b, gain.b], [n.b])
            for half in range(2):
                pb = self.ps[6 + half]
                pT = pb.t[:, :].bitcast(BF16)
                for kk in range(8):
                    k = half * 8 + kk
                    S.op("pe", lambda e, n=n, k=k, kk=kk, pT=pT: e.transpose(pT[:, kk * 128:(kk + 1) * 128], n.t[:, k * 128:(k + 1) * 128], self.identb.t[:]),
                         [n.b, self.identb.b], [pb.b], inc=(kk == 7))
                eng = "act" if half == 0 else "dve"
                dst = hs.t[:, half * 8:(half + 1) * 8, tt * 128:(tt + 1) * 128]
                src = pT.rearrange("p (k t) -> p k t", t=128)
                if eng == "act":
                    S.op("act", lambda e, dst=dst, src=src: e.activation(dst, src, AF.Copy), [pb.b], [hs.b])
                else:
                    S.op("dve", lambda e, dst=dst, src=src: e.tensor_copy(dst, src), [pb.b], [hs.b])

    def inproj(self, gain_src, W2d, blocks, xsrc):
        S = self.S
        W = W2d.rearrange("(k p) n -> p k n", p=128)
        with ExitStack() as ph:
            gain = self.sb(ph, "gain", [128, D], F32, chan=True)
            S.dma("sp", gain.t[:], gain_src.partition_broadcast(128), gain.c, writes=[gain.b])
            xt = [self.sb(ph, "xt", [128, D], F32, chan=True) for _ in range(2)]
            sq = self.sb(ph, "sq", [128, D], BF16)
            xn = [self.sb(ph, "xn", [128, D], BF16) for _ in range(2)]
            st_ = [self.sb(ph, "ss", [128, 1], F32), self.sb(ph, "rs", [128, 1], F32)]
            hT = [self.sb(ph, "hT", [128, KC, TG], BF16) for _ in range(2)]
            wt = [self.sb(ph, "wt", [128, KC, 512], BF16, chan=True) for _ in range(3)]
            ev = [self.sb(ph, "ev", [128, 512], BF16, chan=True) for _ in range(4)]
            evf = [self.sb(ph, "evf", [128, 512], F32, chan=True) for _ in range(4)]
            nblk = 0
            nev = 0
            for g in range(NG):
                hs = hT[g % 2]
                self.norm_group(g, xsrc, gain, xt, sq, xn, st_, hs)
                tok = slice(g * TG, (g + 1) * TG)
                for (c0, wd, segs) in blocks:
                    w = wt[nblk % 3]
                    nblk += 1
                    S.dma("pool", w.t[:, :, 0:wd], W[:, :, c0:c0 + wd], w.c, writes=[w.b])
                    for sg in segs:
                        if sg[0] == "f":
                            _, loc, dst, scl, dt = sg
                            pb = self.ps[nev % 4]
                            self.mm(pb.t[:, :], pb.b, [(w.t[:, k, loc:loc + 128], hs.t[:, k, :]) for k in range(KC)], [w.b, hs.b])
                            e_ = (ev if dt == BF16 else evf)[nev % 4]
                            if nev % 2 == 0:
                                S.op("act", lambda e, e_=e_, pb=pb, scl=scl: e.activation(e_.t[:], pb.t[:, :], AF.Copy, scale=scl), [pb.b], [e_.b])
                            else:
                                S.op("dve", lambda e, e_=e_, pb=pb, scl=scl: e.tensor_scalar(e_.t[:], pb.t[:, :], scl, None, ALU.mult), [pb.b], [e_.b])
                            S.dma("sp", dst[:, tok], e_.t[:], e_.c, reads=[e_.b])
                            nev += 1
                        elif sg[0] == "t":
                            _, loc, width, dst, dcol, dt = sg
                            for tt in range(4):
                                pb = self.ps[nev % 4]
                                self.mm(pb.t[:, 0:width], pb.b, [(hs.t[:, k, tt * 128:(tt + 1) * 128], w.t[:, k, loc:loc + width]) for k in range(KC)], [w.b, hs.b])
                                e_ = (ev if dt == BF16 else evf)[nev % 4]
                                if nev % 2 == 0:
                                    S.op("act", lambda e, e_=e_, pb=pb, width=width: e.activation(e_.t[:, 0:width], pb.t[:, 0:width], AF.Copy), [pb.b], [e_.b])
                                else:
                                    S.op("dve", lambda e, e_=e_, pb=pb, width=width: e.tensor_copy(e_.t[:, 0:width], pb.t[:, 0:width]), [pb.b], [e_.b])
                                r0 = g * TG + tt * 128
                                S.dma("sp", dst[r0:r0 + 128, dcol:dcol + width], e_.t[:, 0:width], e_.c, reads=[e_.b])
                                nev += 1
                        else:
                            _, loc, width, dst = sg
                            pb = self.ps[4 + nev % 2]
                            self.mm(pb.t[0:width, :], pb.b, [(w.t[:, k, loc:loc + width], hs.t[:, k, :]) for k in range(KC)], [w.b, hs.b])
                            e_ = evf[nev % 4]
                            S.op("dve", lambda e, pb=pb, e_=e_, width=width: e.tensor_copy(e_.t[0:width, :], pb.t[0:width, :]), [pb.b], [e_.b])
                            S.dma("sp", dst[:, tok], e_.t[0:width, :], e_.c, reads=[e_.b])
                            nev += 1

    def phase_inproj_even(self, layer, j, xsrc):
        blocks = []
        for b in range(2):
            blocks.append((b * 512, 512, [("f", cc * 128, self.qaT[b * 4 + cc], SCALE, BF16) for cc in range(4)]))
        blocks.append((1024, 512, [("f", 0, self.kaT[0], 1.0, BF16), ("f", 128, self.kaT[1], 1.0, BF16), ("t", 256, 256, self.va, 0, BF16)]))
        for b in range(2):
            blocks.append((1536 + b * 512, 512, [("f", cc * 128, self.qbT[b * 4 + cc], SCALE, BF16) for cc in range(4)]))
        for b in range(2):
            blocks.append((2560 + b * 512, 512, [("f", cc * 128, self.kbT[b * 4 + cc], 1.0, BF16) for cc in range(4)]))
        for b in range(2):
            blocks.append((3584 + b * 512, 512, [("t", 0, 512, self.vb, b * 512, BF16)]))
        blocks.append((4608, 8, [("ff", 0, 8, self.fT)]))
        self.inproj(self.norm_mix[layer, :], self.ev_w_in[j], blocks, xsrc)

    def phase_inproj_odd(self, layer, j, xsrc):
        blocks = []
        for b in range(2):
            blocks.append((b * 512, 512, [("f", cc * 128, self.qcT[b * 4 + cc], SCALE, BF16) for cc in range(4)]))
        blocks.append((1024, 512, [("f", 0, self.kcmpT[0], 1.0, BF16), ("f", 128, self.kcmpT[1], 1.0, BF16),
                                   ("f", 256, self.vcmpT[0], 1.0, BF16), ("f", 384, self.vcmpT[1], 1.0, BF16)]))
        blocks.append((1536, 512, [("f", 0, self.kselT[0], 1.0, BF16), ("f", 128, self.kselT[1], 1.0, BF16), ("t", 256, 256, self.vsel, 0, BF16)]))
        blocks.append((2048, 512, [("f", 0, self.kwinT[0], 1.0, BF16), ("f", 128, self.kwinT[1], 1.0, BF16), ("t", 256, 256, self.vwin, 0, BF16)]))
        blocks.append((2560, 24, [("t", 0, 24, self.gates, 0, F32)]))
        for i, dst in enumerate((self.qdT, self.kdT, self.vdT)):
            for b in range(2):
                blocks.append((2584 + i * 1024 + b * 512, 512, [("f", cc * 128, dst[b * 4 + cc], 1.0, F32) for cc in range(4)]))
        blocks.append((5656, 16, [("ff", 0, 16, self.baT)]))
        for b in range(2):
            blocks.append((5672 + b * 512, 512, [("t", 0, 512, self.zg, b * 512, F32)]))
        self.inproj(self.norm_mix[layer, :], self.od_w_in[j], blocks, xsrc)

    def phase_swa(self, j):
        S = self.S
        with ExitStack() as ph:
            esink = self.sb(ph, "esink", [128, 8], F32, chan=True)
            S.dma("sp", esink.t[:], self.ev_sinks[j, :].partition_broadcast(128), esink.c, writes=[esink.b])
            S.op("act", lambda e: e.activation(esink.t[:], esink.t[:], AF.Exp), [esink.b], [esink.b])
            kT = [self.sb(ph, "kT", [128, T], BF16, chan=True) for _ in range(2)]
            v1 = [self.sb(ph, "v1", [128, NT, 132], BF16, chan=True) for _ in range(2)]
            qT = [self.sb(ph, "qT", [128, T], BF16, chan=True) for _ in range(2)]
            sb_ = [self.sb(ph, "sb", [128, 128], F32) for _ in range(2)]
            pT = [self.sb(ph, "pT", [128, 128], BF16) for _ in range(4)]
            rd = [self.sb(ph, "rd", [128, 1], F32) for _ in range(2)]
            on = [self.sb(ph, "on", [128, 128], BF16) for _ in range(2)]
            ost = [self.sb(ph, "ost", [128, 512], BF16, chan=True) for _ in range(2)]
            for g in range(2):
                S.dma("sp", kT[g].t[:], self.kaT[g], kT[g].c, writes=[kT[g].b])
                S.op("pool", lambda e, g=g: e.memset(v1[g].t[:, :, 128:129], 1.0), [], [v1[g].b])
                S.dma("sp", v1[g].t[:, :, 0:128], self.va.rearrange("(n p) c -> p n c", p=128)[:, :, g * 128:(g + 1) * 128], v1[g].c, writes=[v1[g].b])
            it = 0
            for h in range(8):
                g = h // 4
                q = qT[h % 2]
                S.dma("sp", q.t[:], self.qaT[h], q.c, writes=[q.b])
                for i in range(NT):
                    acc = self.ps[4 + (i % 2)]
                    js = [jb for jb in (i - 1, i) if jb >= 0]
                    for jb in js:
                        sp = self.ps[it % 4]
                        self.mm(sp.t[:, 0:128], sp.b, [(kT[g].t[:, jb * 128:(jb + 1) * 128], q.t[:, i * 128:(i + 1) * 128])], [kT[g].b, q.b])
                        bias = self.bias0 if jb == i else self.bias1w
                        s_ = sb_[it % 2]
                        S.op("dve", lambda e, s_=s_, sp=sp, bias=bias, h=h: e.tensor_tensor(s_.t[:], sp.t[:, 0:128], bias.t[:, h, :], ALU.add),
                             [sp.b, bias.b], [s_.b])
                        p = pT[it % 4]
                        S.op("act", lambda e, p=p, s_=s_: e.activation(p.t[:], s_.t[:], AF.Exp), [s_.b], [p.b])
                        self.S.op("pe", lambda e, acc=acc, p=p, jb=jb, g=g, first=(jb == js[0]), last=(jb == i): e.matmul(
                            acc.t[:, 0:129], p.t[:], v1[g].t[:, jb, 0:129], start=first, stop=last), [p.b, v1[g].b], [acc.b], inc=(jb == i))
                        it += 1
                    r = rd[i % 2]
                    S.op("dve", lambda e, r=r, acc=acc, h=h: e.tensor_tensor(r.t[:], acc.t[:, 128:129], esink.t[:, h:h + 1], ALU.add), [acc.b, esink.b], [r.b])
                    S.op("dve", lambda e, r=r: e.reciprocal(r.t[:], r.t[:]), [r.b], [r.b])
                    o = on[i % 2]
                    S.op("dve", lambda e, o=o, acc=acc, r=r: e.tensor_scalar(o.t[:], acc.t[:, 0:128], r.t[:, 0:1], None, ALU.mult), [acc.b, r.b], [o.b])
                    self.out_transpose(o, ost, i, self.oT[h])

    def out_transpose(self, o, ost, i, dst):
        S = self.S
        pb = self.ps[7]
        pTr = pb.t[:, :].bitcast(BF16)
        stg = ost[(i // 4) % 2]
        S.op("pe", lambda e, o=o, pTr=pTr: e.transpose(pTr[:, 0:128], o.t[:], self.identb.t[:]), [o.b, self.identb.b], [pb.b])
        S.op("act", lambda e, stg=stg, pTr=pTr, i=i: e.activation(stg.t[:, (i % 4) * 128:(i % 4 + 1) * 128], pTr[:, 0:128], AF.Copy), [pb.b], [stg.b])
        if i % 4 == 3:
            c = i // 4
            S.dma("sp", dst[:, c * 512:(c + 1) * 512], stg.t[:], stg.c, reads=[stg.b])

    def phase_fox(self, j):
        S = self.S
        with ExitStack() as ph:
            fr = self.sb(ph, "fr", [8, T], F32, chan=True)
            cs = self.sb(ph, "cs2", [8, T], F32, chan=True)
            ones8 = self.sb(ph, "ones8", [8, T], F32)
            nb = self.sb(ph, "nb", [8, 1], F32, chan=True)
            ck = self.sb(ph, "ck", [128, NT, 8], F32)
            S.dma("sp", fr.t[:], self.fT[:, :], fr.c, writes=[fr.b])
            S.dma("sp", nb.t[:], self.ev_b_forget[j, :].rearrange("(h o) -> h o", o=1), nb.c, writes=[nb.b])
            S.op("dve", lambda e: e.tensor_scalar(nb.t[:], nb.t[:], -1.0, None, ALU.mult), [nb.b], [nb.b])
            S.op("pool", lambda e: e.memset(ones8.t[:], 1.0), [], [ones8.b])
            S.op("act", lambda e: e.activation(fr.t[:], fr.t[:], AF.Exp, bias=nb.t[:, 0:1], scale=-1.0), [fr.b, nb.b], [fr.b])
            S.op("act", lambda e: e.activation(fr.t[:], fr.t[:], AF.Ln, bias=1.0), [fr.b], [fr.b])
            S.op("dve", lambda e: e.tensor_tensor_scan(out=cs.t[:], data0=ones8.t[:], data1=fr.t[:], initial=0.0, op0=ALU.mult, op1=ALU.add),
                 [fr.b, ones8.b], [cs.b])
            S.dma("sp", self.csd[:, :], cs.t[:], cs.c, reads=[cs.b])
            pb = self.ps[7]
            for n in range(NT):
                S.op("pe", lambda e, n=n: e.transpose(pb.t[:, n * 8:(n + 1) * 8], cs.t[0:8, n * 128:(n + 1) * 128], self.identf.t[0:8, 0:8]),
                     [cs.b, self.identf.b], [pb.b], inc=(n == NT - 1))
            S.op("dve", lambda e: e.tensor_copy(ck.t[:], pb.t[:, 0:NT * 8].rearrange("p (n h) -> p n h", h=8)), [pb.b], [ck.b])
            S.barrier()
            kT = [self.sb(ph, "kT", [128, T], BF16, chan=True) for _ in range(2)]
            qT = [self.sb(ph, "qT", [128, T], BF16, chan=True) for _ in range(2)]
            v1 = [self.sb(ph, "v1", [128, NT, 132], BF16, chan=True) for _ in range(2)]
            crow = [self.sb(ph, "crow", [1, T], F32, chan=True) for _ in range(2)]
            ncq = [self.sb(ph, "ncq", [1, T], BF16) for _ in range(2)]
            pT = [self.sb(ph, "pT", [128, 512], BF16) for _ in range(3)]
            rd = [self.sb(ph, "rd", [128, 1], F32) for _ in range(2)]
            on = [self.sb(ph, "on", [128, 128], BF16) for _ in range(2)]
            ost = [self.sb(ph, "ost", [128, 512], BF16, chan=True) for _ in range(2)]
            for s in range(2):
                S.op("pool", lambda e, s=s: e.memset(v1[s].t[:, :, 128:129], 1.0), [], [v1[s].b])
            it = 0
            for h in range(8):
                s = h % 2
                k_, q_, v_, cr, nq = kT[s], qT[s], v1[s], crow[s], ncq[s]
                S.dma("sp", k_.t[:], self.kbT[h], k_.c, writes=[k_.b])
                S.dma("sp", q_.t[:], self.qbT[h], q_.c, writes=[q_.b])
                S.dma("sp", v_.t[:, :, 0:128], self.vb.rearrange("(n p) c -> p n c", p=128)[:, :, h * 128:(h + 1) * 128], v_.c, writes=[v_.b])
                S.dma("sp", cr.t[:], self.csd[h:h + 1, :], cr.c, writes=[cr.b])
                S.op("dve", lambda e, nq=nq, cr=cr: e.tensor_scalar(nq.t[:], cr.t[:], -1.0, None, ALU.mult), [cr.b], [nq.b])
                for c in range(NG):
                    accs = [self.ps[3 + qt] for qt in range(4)]
                    njb = 4 * c + 4
                    pend = None
                    for jb in range(njb + 1):
                        if jb < njb:
                            q0 = max(c * 512, jb * 128)
                            n = (c + 1) * 512 - q0
                            sp = self.ps[it % 3]
                            self.mm(sp.t[:, 0:n], sp.b, [(k_.t[:, jb * 128:(jb + 1) * 128], q_.t[:, q0:q0 + n]),
                                                         (self.onesrow.t[0:1, :], nq.t[0:1, q0:q0 + n])], [k_.b, q_.b, nq.b, self.onesrow.b])
                            p = pT[it % 3]
                            S.op("act", lambda e, p=p, sp=sp, n=n, jb=jb, h=h: e.activation(p.t[:, 0:n], sp.t[:, 0:n], AF.Exp, bias=ck.t[:, jb, h:h + 1]),
                                 [sp.b, ck.b], [p.b])
                            if jb * 128 >= c * 512:
                                S.op("dve", lambda e, p=p: e.tensor_tensor(p.t[:, 0:128], p.t[:, 0:128], self.causal01.t[:], ALU.mult),
                                     [p.b, self.causal01.b], [p.b])
                            it += 1
                            cur = (p, jb, q0, n)
                        else:
                            cur = None
                        if pend is not None:
                            p2, jb2, q02, n2 = pend
                            for qt in range(4):
                                tile = 4 * c + qt
                                if tile < jb2:
                                    continue
                                off = tile * 128 - q02
                                acc = accs[qt]
                                S.op("pe", lambda e, acc=acc, p2=p2, off=off, jb2=jb2, tile=tile, v_=v_: e.matmul(
                                    acc.t[:, 0:129], p2.t[:, off:off + 128], v_.t[:, jb2, 0:129], start=(jb2 == 0), stop=(jb2 == tile)),
                                    [p2.b, v_.b], [acc.b], inc=(jb2 == tile))
                        pend = cur
                    for qt in range(4):
                        i = 4 * c + qt
                        acc = accs[qt]
                        r = rd[i % 2]
                        S.op("dve", lambda e, r=r, acc=acc: e.reciprocal(r.t[:], acc.t[:, 128:129]), [acc.b], [r.b])
                        o = on[i % 2]
                        S.op("dve", lambda e, o=o, acc=acc, r=r: e.tensor_scalar(o.t[:], acc.t[:, 0:128], r.t[:, 0:1], None, ALU.mult), [acc.b, r.b], [o.b])
                        self.out_transpose(o, ost, i, self.oT[8 + h])

    def attend(self, jobs, pT, sbs):
        S = self.S
        LA = 2
        sbank = (0, 1, 2, 4)
        nmax = max(len(jb[1]) for jb in jobs)
        pendq = [[] for _ in jobs]
        for idx in range(nmax + LA):
            for ji, (acc, tiles, q_ap, q_buf) in enumerate(jobs):
                n = len(tiles)
                if idx < n:
                    kT_ap, k_buf, v_ap, v_buf, extra, bias = tiles[idx]
                    sp = self.ps[sbank[self.nsp % 4]]
                    self.nsp += 1
                    pairs = [(kT_ap, q_ap)]
                    reads = [k_buf, q_buf]
                    if extra is not None:
                        pairs.append((extra[0], extra[1]))
                        reads += list(extra[2])
                    self.mm(sp.t[:, 0:128], sp.b, pairs, reads)
                    p = pT[self.npt % len(pT)]
                    self.npt += 1
                    if bias is not None:
                        s_ = sbs[self.npt % len(sbs)]
                        S.op("dve", lambda e, s_=s_, sp=sp, bias=bias: e.tensor_tensor(s_.t[:], sp.t[:, 0:128], bias[0], ALU.add), [sp.b, bias[1]], [s_.b])
                        S.op("act", lambda e, p=p, s_=s_: e.activation(p.t[:], s_.t[:], AF.Exp), [s_.b], [p.b])
                    else:
                        S.op("act", lambda e, p=p, sp=sp: e.activation(p.t[:], sp.t[:, 0:128], AF.Exp), [sp.b], [p.b])
                    pendq[ji].append((p, v_ap, v_buf, idx))
                if idx >= LA and pendq[ji]:
                    p2, v2, vb2, i2 = pendq[ji].pop(0)
                    S.op("pe", lambda e, acc=acc, p2=p2, v2=v2, i2=i2, n=n: e.matmul(acc.t[:, 0:129], p2.t[:], v2, start=(i2 == 0), stop=(i2 == n - 1)),
                         [p2.b, vb2], [acc.b], inc=(i2 == n - 1))

    def phase_nsa(self, j):
        S = self.S
        idf = self.identf
        self.nsp = 0
        self.npt = 0
        with ExitStack() as ph:
            gsig = self.sb(ph, "gsig", [128, NT, 24], F32, chan=True)
            S.dma("sp", gsig.t[:], self.gates.rearrange("(n p) c -> p n c", p=128), gsig.c, writes=[gsig.b])
            S.op("act", lambda e: e.activation(gsig.t[:], gsig.t[:], AF.Sigmoid), [gsig.b], [gsig.b])
            kcT = [self.sb(ph, "kcT", [128, 256], BF16) for _ in range(2)]
            vc1 = [self.sb(ph, "vc1", [128, 2, 132], BF16) for _ in range(2)]
            with ExitStack() as cp:
                w1 = self.sb(cp, "w1", [128, 32, 256], BF16, chan=True)
                w2 = self.sb(cp, "w2", [128, 2, 128], BF16, chan=True)
                per = self.sb(cp, "per", [32, 128], F32, chan=True)
                peT = self.sb(cp, "peT", [128, 32], BF16)
                hb = self.sb(cp, "hb", [128, 2], F32)
                xT = self.sb(cp, "xT", [128, T], BF16, chan=True)
                GT = [self.sb(cp, "GT", [128, 256], BF16) for _ in range(2)]
                xs = self.sb(cp, "xs", [128, 256], F32)
                x2 = self.sb(cp, "x2", [128, 256], F32)
                for g in range(2):
                    S.op("dve", lambda e, g=g: e.memset(kcT[g].t[:], 0.0), [], [kcT[g].b])
                    S.op("dve", lambda e, g=g: e.memset(vc1[g].t[:], 0.0), [], [vc1[g].b])
                    S.op("dve", lambda e, g=g: e.memset(vc1[g].t[:, :, 128:129], 1.0), [], [vc1[g].b])
                for hc in range(2):
                    S.op("dve", lambda e, hc=hc: e.memset(GT[hc].t[:], 0.0), [], [GT[hc].b])
                for kv in range(2):
                    S.dma("pool", w1.t[:], self.od_cmp_w1[j, kv].rearrange("(jj d) n -> d jj n", d=128), w1.c, writes=[w1.b])
                    S.dma("pool", w2.t[:], self.od_cmp_w2[j, kv].rearrange("(c p) n -> p c n", p=128), w2.c, writes=[w2.b])
                    S.dma("sp", per.t[:], self.od_cmp_pos[j, kv], per.c, writes=[per.b])
                    pb7 = self.ps[7]
                    S.op("pe", lambda e: e.transpose(pb7.t[:, 0:32], per.t[:], idf.t[0:32, 0:32]), [per.b, idf.b], [pb7.b])
                    S.op("dve", lambda e: e.tensor_copy(peT.t[:], pb7.t[:, 0:32]), [pb7.b], [peT.b])
                    for hc in range(2):
                        pbh = self.ps[6]
                        self.mm(pbh.t[:, hc:hc + 1], pbh.b, [(w1.t[:, jj, hc * 128:(hc + 1) * 128], peT.t[:, jj:jj + 1]) for jj in range(32)], [w1.b, peT.b])
                        S.op("dve", lambda e, hc=hc, pbh=pbh: e.tensor_copy(hb.t[:, hc:hc + 1], pbh.t[:, hc:hc + 1]), [pbh.b], [hb.b])
                    for g in range(2):
                        src = (self.kcmpT if kv == 0 else self.vcmpT)[g]
                        S.dma("sp", xT.t[:], src, xT.c, writes=[xT.b])
                        xv = xT.t[:].rearrange("p (n s) -> p n s", s=16)
                        for hc in range(2):
                            pbx = self.ps[hc]
                            self.mm(pbx.t[:, 0:255], pbx.b, [(w1.t[:, jj, hc * 128:(hc + 1) * 128], xv[:, jj // 16:jj // 16 + 255, jj % 16]) for jj in range(32)], [w1.b, xT.b])
                            S.op("dve", lambda e, hc=hc, pbx=pbx: e.tensor_scalar(xs.t[:, 0:255], pbx.t[:, 0:255], hb.t[:, hc:hc + 1], None, ALU.add), [pbx.b, hb.b], [xs.b])
                            S.op("dve", lambda e: e.tensor_tensor(x2.t[:, 0:255], xs.t[:, 0:255], xs.t[:, 0:255], ALU.mult), [xs.b], [x2.b])
                            S.op("dve", lambda e: e.tensor_scalar(x2.t[:, 0:255], x2.t[:, 0:255], 0.044715, 1.0, ALU.mult, ALU.add), [x2.b], [x2.b])
                            S.op("dve", lambda e: e.tensor_tensor(x2.t[:, 0:255], x2.t[:, 0:255], xs.t[:, 0:255], ALU.mult), [x2.b, xs.b], [x2.b])
                            S.op("act", lambda e: e.activation(x2.t[:, 0:255], x2.t[:, 0:255], AF.Tanh, scale=0.7978845608028654), [x2.b], [x2.b])
                            S.op("dve", lambda e: e.tensor_scalar(x2.t[:, 0:255], x2.t[:, 0:255], 0.5, 0.5, ALU.mult, ALU.add), [x2.b], [x2.b])
                            S.op("dve", lambda e, hc=hc: e.tensor_tensor(GT[hc].t[:, 0:255], x2.t[:, 0:255], xs.t[:, 0:255], ALU.mult), [x2.b, xs.b], [GT[hc].b])
                        if kv == 0:
                            pbk = self.ps[2]
                            self.mm(pbk.t[:, 0:255], pbk.b, [(w2.t[:, hc, :], GT[hc].t[:, 0:255]) for hc in range(2)], [w2.b, GT[0].b, GT[1].b])
                            S.op("act", lambda e, g=g, pbk=pbk: e.activation(kcT[g].t[:, 0:255], pbk.t[:, 0:255], AF.Copy), [pbk.b], [kcT[g].b])
                        else:
                            for ct in range(2):
                                rows = 128 if ct == 0 else 127
                                pbk = self.ps[2 + ct]
                                self.mm(pbk.t[0:rows, 0:128], pbk.b, [(GT[hc].t[:, ct * 128:ct * 128 + rows], w2.t[:, hc, :]) for hc in range(2)], [w2.b, GT[0].b, GT[1].b])
                                S.op("act", lambda e, g=g, ct=ct, rows=rows, pbk=pbk: e.activation(vc1[g].t[0:rows, ct, 0:128], pbk.t[0:rows, 0:128], AF.Copy), [pbk.b], [vc1[g].b])
                S.barrier()
            cmask = self.sb(ph, "cmask", [128, 64, 128], BF16, chan=True)
            wimp = self.sb(ph, "wimp", [128, 2, 64], BF16, chan=True)
            skeep = self.sb(ph, "skeep", [128, NT, 64], F32, chan=True)
            sadd = self.sb(ph, "sadd", [128, NT, 64], F32, chan=True)
            esel = self.sb(ph, "esel", [64, NT, 128], BF16, chan=True)
            S.dma("pool", cmask.t[:], self.c_cmask.rearrange("p (m q) -> p m q", q=128), cmask.c, writes=[cmask.b])
            S.dma("pool", wimp.t[:], self.c_wimp.rearrange("p (c j) -> p c j", j=64), wimp.c, writes=[wimp.b])
            S.dma("sp", skeep.t[:], self.c_selkeep.rearrange("p (n j) -> p n j", j=64), skeep.c, writes=[skeep.b])
            S.dma("sp", sadd.t[:], self.c_seladd.rearrange("p (n j) -> p n j", j=64), sadd.c, writes=[sadd.b])
            S.dma("pool", esel.t[:], self.c_esel.rearrange("p (n k) -> p n k", k=128), esel.c, writes=[esel.b])
            qT = [self.sb(ph, "qT", [128, T], BF16, chan=True) for _ in range(4)]
            ksT = self.sb(ph, "ksT", [128, T], BF16, chan=True)
            kwT = self.sb(ph, "kwT", [128, T], BF16, chan=True)
            vs1 = self.sb(ph, "vs1", [128, NT, 132], BF16, chan=True)
            vw1 = self.sb(ph, "vw1", [128, NT, 132], BF16, chan=True)
            S.op("dve", lambda e: e.memset(vs1.t[:, :, 128:129], 1.0), [], [vs1.b])
            S.op("dve", lambda e: e.memset(vw1.t[:, :, 128:129], 1.0), [], [vw1.b])
            pT = [self.sb(ph, "pT", [128, 128], BF16) for _ in range(8)]
            sbs = [self.sb(ph, "sbs", [128, 128], F32) for _ in range(4)]
            imp = self.sb(ph, "imp", [128, 64], F32)
            sc = self.sb(ph, "sc", [128, 64], F32)
            m8 = self.sb(ph, "m8", [128, 8], F32)
            nsel = self.sb(ph, "nsel", [128, 64], BF16)
            nsT = self.sb(ph, "nsT", [64, 128], BF16)
            oacc = [self.sb(ph, "oacc", [128, 128], F32) for _ in range(4)]
            onb = [self.sb(ph, "onb", [128, 128], BF16) for _ in range(2)]
            rd = [self.sb(ph, "rd", [128, 1], F32) for _ in range(4)]
            rdw = [self.sb(ph, "rdw", [128, 1], F32) for _ in range(4)]
            ostg = [[self.sb(ph, "ostg", [128, 512], BF16, chan=True) for _ in range(2)] for _ in range(4)]
            for g in range(2):
                for r in range(4):
                    S.dma("sp", qT[r].t[:], self.qcT[4 * g + r], qT[r].c, writes=[qT[r].b])
                S.dma("sp", ksT.t[:], self.kselT[g], ksT.c, writes=[ksT.b])
                S.dma("sp", kwT.t[:], self.kwinT[g], kwT.c, writes=[kwT.b])
                S.dma("sp", vs1.t[:, :, 0:128], self.vsel.rearrange("(n p) c -> p n c", p=128)[:, :, g * 128:(g + 1) * 128], vs1.c, writes=[vs1.b])
                S.dma("sp", vw1.t[:, :, 0:128], self.vwin.rearrange("(n p) c -> p n c", p=128)[:, :, g * 128:(g + 1) * 128], vw1.c, writes=[vw1.b])
                for i in range(NT):
                    qs = slice(i * 128, (i + 1) * 128)
                    cts = [0] if i < 16 else [0, 1]
                    for r in range(4):
                        h = 4 * g + r
                        accc = self.ps[3]
                        blk = Tl(self.ps[3].t[:, 256:512], self.ps[3].b)
                        pcs = []
                        for ct in cts:
                            sp = self.ps[(0, 1, 2, 4)[self.nsp % 4]]
                            self.nsp += 1
                            self.mm(sp.t[:, 0:128], sp.b, [(kcT[g].t[:, ct * 128:(ct + 1) * 128], qT[r].t[:, qs])], [kcT[g].b, qT[r].b])
                            p = pT[self.npt % len(pT)]
                            self.npt += 1
                            S.op("act", lambda e, p=p, sp=sp: e.activation(p.t[:], sp.t[:, 0:128], AF.Exp), [sp.b], [p.b])
                            S.op("dve", lambda e, p=p, i=i, ct=ct: e.tensor_tensor(p.t[:], p.t[:], cmask.t[:, i * 2 + ct, :], ALU.mult), [p.b, cmask.b], [p.b])
                            pcs.append((p, ct))
                        self.mm(accc.t[:, 0:129], accc.b, [(p.t[:], vc1[g].t[:, ct, 0:129]) for p, ct in pcs], [vc1[g].b] + [p.b for p, _ in pcs])
                        self.mm(blk.t[:, 0:64], blk.b, [(p.t[:], wimp.t[:, ct, :]) for p, ct in pcs], [wimp.b] + [p.b for p, _ in pcs])
                        r_ = rd[r]
                        S.op("dve", lambda e, r_=r_, accc=accc: e.tensor_scalar(r_.t[:], accc.t[:, 128:129], 1e-30, None, ALU.max), [accc.b], [r_.b])
                        S.op("dve", lambda e, r_=r_: e.reciprocal(r_.t[:], r_.t[:]), [r_.b], [r_.b])
                        if r == 0:
                            S.op("dve", lambda e, r_=r_, blk=blk: e.tensor_scalar(imp.t[:], blk.t[:, 0:64], r_.t[:, 0:1], None, ALU.mult), [blk.b, r_.b], [imp.b])
                        else:
                            S.op("dve", lambda e, r_=r_, blk=blk: e.scalar_tensor_tensor(out=imp.t[:], in0=blk.t[:, 0:64], scalar=r_.t[:, 0:1], in1=imp.t[:], op0=ALU.mult, op1=ALU.add),
                                 [blk.b, r_.b, imp.b], [imp.b])
                        S.op("dve", lambda e, r_=r_, i=i, h=h: e.tensor_tensor(r_.t[:], r_.t[:], gsig.t[:, i, h:h + 1], ALU.mult), [r_.b, gsig.b], [r_.b])
                        S.op("dve", lambda e, r=r, r_=r_, accc=accc: e.tensor_scalar(oacc[r].t[:], accc.t[:, 0:128], r_.t[:, 0:1], None, ALU.mult), [accc.b, r_.b], [oacc[r].b])
                    S.op("dve", lambda e, i=i: e.tensor_tensor(sc.t[:], imp.t[:], skeep.t[:, i, :], ALU.mult), [imp.b, skeep.b], [sc.b])
                    S.op("dve", lambda e, i=i: e.tensor_tensor(sc.t[:], sc.t[:], sadd.t[:, i, :], ALU.add), [sc.b, sadd.b], [sc.b])
                    S.op("dve", lambda e: e.max(out=m8.t[:], in_=sc.t[:]), [sc.b], [m8.b])
                    S.op("dve", lambda e: e.tensor_scalar(sc.t[:], sc.t[:], m8.t[:, 7:8], None, ALU.is_ge), [sc.b, m8.b], [sc.b])
                    S.op("dve", lambda e: e.tensor_scalar(nsel.t[:], sc.t[:], -NEG, NEG, ALU.mult, ALU.add), [sc.b], [nsel.b])
                    pb7 = self.ps[7]
                    pTr = pb7.t[:, :].bitcast(BF16)
                    S.op("pe", lambda e, pTr=pTr: e.transpose(pTr[0:64, 0:128], nsel.t[:], self.identb.t[:]), [nsel.b, self.identb.b], [pb7.b])
                    S.op("act", lambda e, pTr=pTr: e.activation(nsT.t[:], pTr[0:64, 0:128], AF.Copy), [pb7.b], [nsT.b])
                    for r in range(4):
                        h = 4 * g + r
                        tiles_s = []
                        for jb in range(i + 1):
                            bias = None
                            if jb == i:
                                bias = (self.nb0.t[:, h, :], self.nb0.b)
                            elif jb == i - 1:
                                bias = (self.nb1.t[:, h, :], self.nb1.b)
                            tiles_s.append((ksT.t[:, jb * 128:(jb + 1) * 128], ksT.b, vs1.t[:, jb, 0:129], vs1.b,
                                            (esel.t[:, jb, :], nsT.t[:], (esel.b, nsT.b)), bias))
                        tiles_w = []
                        for jb in range(max(0, i - 4), i + 1):
                            bias = None
                            if jb == i:
                                bias = (self.nb0.t[:, h, :], self.nb0.b)
                            elif jb == i - 1:
                                bias = (self.nb1.t[:, h, :], self.nb1.b)
                            elif jb == i - 4:
                                bias = (self.acneg.t[:], self.acneg.b)
                            tiles_w.append((kwT.t[:, jb * 128:(jb + 1) * 128], kwT.b, vw1.t[:, jb, 0:129], vw1.b, None, bias))
                        acc_s, acc_w = self.ps[5], self.ps[6]
                        self.attend([(acc_s, tiles_s, qT[r].t[:, qs], qT[r].b), (acc_w, tiles_w, qT[r].t[:, qs], qT[r].b)], pT, sbs)
                        self.nsa_combine(acc_w, rdw[r], gsig, i, 16 + h, oacc[r])
                        self.nsa_combine(acc_s, rd[r], gsig, i, 8 + h, oacc[r])
                        o = onb[r % 2]
                        S.op("act", lambda e, o=o, r=r: e.activation(o.t[:], oacc[r].t[:], AF.Copy), [oacc[r].b], [o.b])
                        self.out_transpose(o, ostg[r], i, self.oT[h])

    def nsa_combine(self, acc, r_, gsig, i, gcol, oacc):
        S = self.S
        S.op("dve", lambda e: e.reciprocal(r_.t[:], acc.t[:, 128:129]), [acc.b], [r_.b])
        S.op("dve", lambda e: e.tensor_tensor(r_.t[:], r_.t[:], gsig.t[:, i, gcol:gcol + 1], ALU.mult), [r_.b, gsig.b], [r_.b])
        S.op("dve", lambda e: e.scalar_tensor_tensor(out=oacc.t[:], in0=acc.t[:, 0:128], scalar=r_.t[:, 0:1], in1=oacc.t[:], op0=ALU.mult, op1=ALU.add),
             [acc.b, r_.b, oacc.b], [oacc.b])

    def phase_gdn(self, j):
        S = self.S
        B = int(os.environ.get("KGDNB", "8"))
        C = 64
        NCH = T // C
        idf = self.identf
        with ExitStack() as ph:
            psq = []
            for q in range(4):
                for b in range(6):
                    psq.append(Tl(self.ps[b].t[:, q * 128:(q + 1) * 128], self.ps[b].b))
            nq = [0]

            def slot():
                nq[0] += 1
                return psq[nq[0] % len(psq)]
            cwg = self.sb(ph, "cwg", [128, 4, 24], F32)
            gamc = self.sb(ph, "gamc", [C, NCH, 8], F32)
            betac = self.sb(ph, "betac", [C, NCH, 8], F32)
            egamc = self.sb(ph, "egamc", [C, NCH, 8], F32)
            nbetac = self.sb(ph, "nbetac", [C, NCH, 8], F32)
            begamc = self.sb(ph, "begamc", [C, NCH, 8], F32)
            dtb = self.sb(ph, "dtb", [8, 1], F32, chan=True)
            nega = self.sb(ph, "nega", [8, 1], F32, chan=True)
            prep = ExitStack()
            cwr = self.sb(prep, "cwr", [24, 4, 128], F32, chan=True)
            S.dma_group("sp", [(cwr.t[:, jj, :], self.od_conv_w[j, jj, :].rearrange("(c p) -> c p", p=128)) for jj in range(4)], cwr.c, writes=[cwr.b])
            pb7 = self.ps[7]
            for jj in range(4):
                S.op("pe", lambda e, jj=jj: e.transpose(pb7.t[:, jj * 24:(jj + 1) * 24], cwr.t[:, jj, :], idf.t[0:24, 0:24]), [cwr.b, idf.b], [pb7.b], inc=(jj == 3))
            S.op("dve", lambda e: e.tensor_copy(cwg.t[:], pb7.t[:, 0:96].rearrange("p (j c) -> p j c", c=24)), [pb7.b], [cwg.b])
            bb = self.sb(prep, "bb", [8, T], F32, chan=True)
            ba = self.sb(prep, "ba", [8, T], F32, chan=True)
            gam = self.sb(prep, "gam", [8, T], F32)
            rmask = self.sb(prep, "rmask", [8, T], F32, chan=True)
            S.dma("sp", bb.t[:], self.baT[0:8, :], bb.c, writes=[bb.b])
            S.dma("sp", ba.t[:], self.baT[8:16, :], ba.c, writes=[ba.b])
            S.dma("sp", rmask.t[:], self.c_rmask[:, :], rmask.c, writes=[rmask.b])
            S.dma("sp", dtb.t[:], self.od_dt_bias[j, :].rearrange("(h o) -> h o", o=1), dtb.c, writes=[dtb.b])
            S.dma("sp", nega.t[:], self.od_a_log[j, :].rearrange("(h o) -> h o", o=1), nega.c, writes=[nega.b])
            S.op("act", lambda e: e.activation(nega.t[:], nega.t[:], AF.Exp), [nega.b], [nega.b])
            S.op("dve", lambda e: e.tensor_scalar(nega.t[:], nega.t[:], -1.0, None, ALU.mult), [nega.b], [nega.b])
            S.op("act", lambda e: e.activation(ba.t[:], ba.t[:], AF.Exp, bias=dtb.t[:, 0:1]), [ba.b, dtb.b], [ba.b])
            S.op("act", lambda e: e.activation(ba.t[:], ba.t[:], AF.Ln, bias=1.0), [ba.b], [ba.b])
            S.op("dve", lambda e: e.tensor_scalar(ba.t[:], ba.t[:], nega.t[:, 0:1], None, ALU.mult), [ba.b, nega.b], [ba.b])
            S.op("dve", lambda e: e.tensor_tensor_scan(out=gam.t[:], data0=rmask.t[:], data1=ba.t[:], initial=0.0, op0=ALU.mult, op1=ALU.add),
                 [rmask.b, ba.b], [gam.b])
            S.op("act", lambda e: e.activation(bb.t[:], bb.t[:], AF.Sigmoid), [bb.b], [bb.b])
            for src, dst, pbk in ((gam, gamc, self.ps[6]), (bb, betac, self.ps[7])):
                for c in range(NCH):
                    S.op("pe", lambda e, c=c, src=src, pbk=pbk: e.transpose(pbk.t[0:C, c * 8:(c + 1) * 8], src.t[0:8, c * C:(c + 1) * C], idf.t[0:8, 0:8]),
                         [src.b, idf.b], [pbk.b], inc=(c == NCH - 1))
                S.op("dve", lambda e, dst=dst, pbk=pbk: e.tensor_copy(dst.t[:], pbk.t[0:C, :].rearrange("p (c h) -> p c h", h=8)), [pbk.b], [dst.b])
            S.barrier()
            prep.close()
            S.op("act", lambda e: e.activation(egamc.t[:], gamc.t[:], AF.Exp), [gamc.b], [egamc.b])
            S.op("dve", lambda e: e.tensor_scalar(nbetac.t[:], betac.t[:], -1.0, None, ALU.mult), [betac.b], [nbetac.b])
            S.op("dve", lambda e: e.tensor_tensor(begamc.t[:], betac.t[:], egamc.t[:], ALU.mult), [betac.b, egamc.b], [begamc.b])
            gnr = self.sb(ph, "gnr", [C, 128], F32, chan=True)
            S.dma("sp", gnr.t[:], self.od_gdn_norm[j, :].partition_broadcast(C), gnr.c, writes=[gnr.b])
            ones64 = self.sb(ph, "ones64", [C, 128], F32)
            onescol = self.sb(ph, "onescol", [128, 1], F32)
            S.op("dve", lambda e: e.memset(ones64.t[:], 1.0), [], [ones64.b])
            S.op("dve", lambda e: e.memset(onescol.t[:], 1.0), [], [onescol.b])
            pmask = self.sb(ph, "pmask", [C, C], F32, chan=True)
            nmask = self.sb(ph, "nmask", [C, C], F32, chan=True)
            S.dma("sp", pmask.t[:], self.c_gdn_pmask[:, :], pmask.c, writes=[pmask.b])
            S.dma("sp", nmask.t[:], self.c_gdn_nmask[:, :], nmask.c, writes=[nmask.b])
            if self.gdn_stage <= 0:
                return
            raw = self.sb(ph, "raw", [128, T + 3], F32, chan=True)
            S.op("dve", lambda e: e.memset(raw.t[:, 0:3], 0.0), [], [raw.b])
            qkv = [self.sb(ph, "qkv", [128, T], F32) for _ in range(3)]
            sqs = self.sb(ph, "sqs", [128, T], F32)
            rnc = self.sb(ph, "rnc", [C, NCH, 2], F32)
            zt = [self.sb(ph, "zt", [C, B, 128], F32, chan=True) for _ in range(2)]
            St = self.sb(ph, "St", [128, 128], F32)
            ost = [self.sb(ph, "ost", [128, 512], BF16, chan=True) for _ in range(2)]

            def mk(name, shape, n, dt=F32):
                return [self.sb(ph, name, shape, dt) for _ in range(n)]
            kn = mk("kn", [C, 128], B); qn = mk("qn", [C, 128], B); vt = mk("vt", [C, 128], B)
            knT = mk("knT", [128, C], B); qnT = mk("qnT", [128, C], 2 * B)
            dg = mk("dg", [C, C], B); t1 = mk("t1", [C, C], B); Dm = mk("Dm", [C, C], B); DT = mk("DT", [C, C], B)
            X = mk("X", [C, C], 2 * B); XT = mk("XT", [C, C], 2 * B); Y = mk("Y", [C, C], 2 * B)
            Vb = mk("Vb", [C, 128], B); Kb = mk("Kb", [C, 128], B)
            U = mk("U", [C, 128], 2 * B); WmT = mk("WmT", [128, C], 2 * B); MT = mk("MT", [C, C], 2 * B); Kd = mk("Kd", [C, 128], 2 * B)
            kdc = mk("kdc", [C, 1], 2 * B); egl = mk("egl", [128, 1], 2 * B)
            vnew = mk("vnew", [C, 128], 2); mvs = mk("mvs", [C, 128], 2); osb = mk("osb", [C, 128], 2)
            oss = mk("oss", [C, 1], 2); ors = mk("ors", [C, 1], 2); ojunk = mk("ojunk", [C, 128], 1)
            zs = mk("zs", [C, 128], 2); ofb = mk("ofb", [C, 128], 2, BF16)
            for h in range(self.gdn_heads):
                for ti, src in enumerate((self.qdT, self.kdT, self.vdT)):
                    S.dma("sp", raw.t[:, 3:T + 3], src[h], raw.c, writes=[raw.b])
                    dst = qkv[ti]
                    ci = ti * 8 + h
                    S.op("dve", lambda e, dst=dst, ci=ci: e.tensor_scalar(dst.t[:], raw.t[:, 0:T], cwg.t[:, 0, ci:ci + 1], None, ALU.mult), [raw.b, cwg.b], [dst.b])
                    for jj in range(1, 4):
                        S.op("dve", lambda e, dst=dst, ci=ci, jj=jj: e.scalar_tensor_tensor(out=dst.t[:], in0=raw.t[:, jj:T + jj], scalar=cwg.t[:, jj, ci:ci + 1], in1=dst.t[:],
                                                                                           op0=ALU.mult, op1=ALU.add), [raw.b, cwg.b, dst.b], [dst.b])
                    S.op("act", lambda e, dst=dst: e.activation(dst.t[:], dst.t[:], AF.Silu), [dst.b], [dst.b])
                if self.gdn_stage <= 1:
                    continue
                pss = self.ps[6]
                for ti in range(2):
                    S.op("act", lambda e, ti=ti: e.activation(sqs.t[:], qkv[ti].t[:], AF.Square), [qkv[ti].b], [sqs.b])
                    for c in range(NCH):
                        S.op("pe", lambda e, c=c, ti=ti: e.matmul(pss.t[0:C, c * 2 + ti:c * 2 + ti + 1], sqs.t[:, c * C:(c + 1) * C], onescol.t[:, 0:1], start=True, stop=True),
                             [sqs.b, onescol.b], [pss.b], inc=(c == NCH - 1))
                S.op("dve", lambda e: e.tensor_scalar(rnc.t[:], pss.t[0:C, 0:2 * NCH].rearrange("p (c t) -> p c t", t=2), EPS, None, ALU.add), [pss.b], [rnc.b])
                S.op("act", lambda e: e.activation(rnc.t[:], rnc.t[:], AF.Sqrt), [rnc.b], [rnc.b])
                S.op("dve", lambda e: e.reciprocal(rnc.t[:], rnc.t[:]), [rnc.b], [rnc.b])
                S.op("dve", lambda e: e.tensor_scalar(rnc.t[:, :, 0:1], rnc.t[:, :, 0:1], SCALE, None, ALU.mult), [rnc.b], [rnc.b])
                if self.gdn_stage <= 2:
                    continue
                S.op("dve", lambda e: e.memset(St.t[:], 0.0), [], [St.b])
                for bt in range(NCH // B):
                    if os.environ.get("KGDNBAR", "") == "1":
                        S.barrier()
                    par = bt % 2
                    z_ = zt[par]
                    t0 = bt * B * C
                    S.dma("sp", z_.t[:], self.zg[t0:t0 + B * C, h * 128:(h + 1) * 128].rearrange("(b p) e -> p b e", p=C), z_.c, writes=[z_.b])
                    cs = [bt * B + bi for bi in range(B)]
                    o2 = [par * B + bi for bi in range(B)]
                    for bi, c in enumerate(cs):
                        sl = slice(c * C, (c + 1) * C)
                        pq_, pk_, pv_ = slot(), slot(), slot()
                        for p_, src in ((pq_, qkv[0]), (pk_, qkv[1]), (pv_, qkv[2])):
                            S.op("pe", lambda e, p_=p_, src=src, sl=sl: e.transpose(p_.t[0:C, :], src.t[:, sl], idf.t[:]), [src.b, idf.b], [p_.b])
                        S.op("dve", lambda e, bi=bi, c=c, pq_=pq_: e.tensor_scalar(qn[bi].t[:], pq_.t[0:C, :], rnc.t[:, c, 0:1], None, ALU.mult), [pq_.b, rnc.b], [qn[bi].b])
                        S.op("dve", lambda e, bi=bi, c=c, pk_=pk_: e.tensor_scalar(kn[bi].t[:], pk_.t[0:C, :], rnc.t[:, c, 1:2], None, ALU.mult), [pk_.b, rnc.b], [kn[bi].b])
                        S.op("act", lambda e, bi=bi, pv_=pv_: e.activation(vt[bi].t[:], pv_.t[0:C, :], AF.Copy), [pv_.b], [vt[bi].b])
                    if self.gdn_stage <= 3:
                        continue
                    for bi, c in enumerate(cs):
                        o = o2[bi]
                        p1, p2, p3 = slot(), slot(), slot()
                        S.op("pe", lambda e, bi=bi, p1=p1: e.transpose(p1.t[:, 0:C], kn[bi].t[:], idf.t[0:C, 0:C]), [kn[bi].b, idf.b], [p1.b])
                        S.op("pe", lambda e, bi=bi, p2=p2: e.transpose(p2.t[:, 0:C], qn[bi].t[:], idf.t[0:C, 0:C]), [qn[bi].b, idf.b], [p2.b])
                        S.op("act", lambda e, bi=bi, p1=p1: e.activation(knT[bi].t[:], p1.t[:, 0:C], AF.Copy), [p1.b], [knT[bi].b])
                        S.op("dve", lambda e, o=o2[bi], p2=p2: e.tensor_copy(qnT[o].t[:], p2.t[:, 0:C]), [p2.b], [qnT[o].b])
                        S.op("dve", lambda e, h=h, bi=bi, c=c: e.tensor_scalar(dg[bi].t[:], idf.t[0:C, 0:C], gamc.t[:, c, h:h + 1], None, ALU.mult), [idf.b, gamc.b], [dg[bi].b])
                        S.op("pe", lambda e, bi=bi, p3=p3: e.matmul(p3.t[:, 0:C], ones64.t[:, :], dg[bi].t[:], start=True, stop=True), [ones64.b, dg[bi].b], [p3.b])
                        S.op("dve", lambda e, h=h, bi=bi, c=c, p3=p3: e.scalar_tensor_tensor(out=t1[bi].t[:], in0=p3.t[0:C, 0:C], scalar=gamc.t[:, c, h:h + 1], in1=pmask.t[:],
                                                                                       op0=ALU.subtract, op1=ALU.add), [p3.b, gamc.b, pmask.b], [t1[bi].b])
                        S.op("act", lambda e, bi=bi: e.activation(Dm[bi].t[:], t1[bi].t[:], AF.Exp, scale=-1.0), [t1[bi].b], [Dm[bi].b])
                        S.op("dve", lambda e, h=h, bi=bi, c=c, p3=p3: e.scalar_tensor_tensor(out=t1[bi].t[:], in0=p3.t[0:C, 0:C], scalar=gamc.t[:, c, h:h + 1], in1=nmask.t[:],
                                                                                       op0=ALU.subtract, op1=ALU.add), [p3.b, gamc.b, nmask.b, Dm[bi].b], [t1[bi].b])
                        S.op("act", lambda e, bi=bi: e.activation(DT[bi].t[:], t1[bi].t[:], AF.Exp), [t1[bi].b], [DT[bi].b])
                        S.op("act", lambda e, o=o, p3=p3: e.activation(egl[o].t[:], p3.t[:, C - 1:C], AF.Exp), [p3.b], [egl[o].b])
                        S.op("dve", lambda e, h=h, o=o2[bi], c=c, p3=p3: e.tensor_scalar(kdc[o].t[:], p3.t[0:C, C - 1:C], gamc.t[:, c, h:h + 1], None, ALU.subtract), [p3.b, gamc.b], [kdc[o].b])
                        S.op("act", lambda e, o=o2[bi]: e.activation(kdc[o].t[:], kdc[o].t[:], AF.Exp), [kdc[o].b], [kdc[o].b])
                    if self.gdn_stage <= 4:
                        continue
                    for bi, c in enumerate(cs):
                        o = o2[bi]
                        pg_, pm_ = slot(), slot()
                        S.op("pe", lambda e, bi=bi, pg_=pg_: e.matmul(pg_.t[0:C, 0:C], knT[bi].t[:], knT[bi].t[:], start=True, stop=True), [knT[bi].b], [pg_.b])
                        S.op("dve", lambda e, h=h, bi=bi, c=c, o=o, pg_=pg_: e.scalar_tensor_tensor(out=X[o].t[:], in0=pg_.t[0:C, 0:C], scalar=nbetac.t[:, c, h:h + 1], in1=Dm[bi].t[:],
                                                                                              op0=ALU.mult, op1=ALU.mult), [pg_.b, nbetac.b, Dm[bi].b], [X[o].b])
                        S.op("pe", lambda e, bi=bi, o=o, pm_=pm_: e.matmul(pm_.t[0:C, 0:C], knT[bi].t[:], qnT[o].t[:], start=True, stop=True), [knT[bi].b, qnT[o].b], [pm_.b])
                        S.op("dve", lambda e, bi=bi, o=o, pm_=pm_: e.tensor_tensor(MT[o].t[:], pm_.t[0:C, 0:C], DT[bi].t[:], ALU.mult), [pm_.b, DT[bi].b], [MT[o].b])
                    for bi, c in enumerate(cs):
                        o = o2[bi]
                        px = slot()
                        S.op("pe", lambda e, o=o, px=px: e.transpose(px.t[0:C, 0:C], X[o].t[:], idf.t[0:C, 0:C]), [X[o].b, idf.b], [px.b])
                        S.op("act", lambda e, o=o, px=px: e.activation(XT[o].t[:], px.t[0:C, 0:C], AF.Copy), [px.b], [XT[o].b])
                        S.op("dve", lambda e, o=o, px=px: e.tensor_tensor(Y[o].t[:], px.t[0:C, 0:C], idf.t[0:C, 0:C], ALU.add), [px.b, idf.b], [Y[o].b])
                    if self.gdn_stage <= 5:
                        continue
                    for s_ in range(5):
                        for bi, c in enumerate(cs):
                            o = o2[bi]
                            pa, pbq = slot(), slot()
                            S.op("pe", lambda e, o=o, pa=pa: e.matmul(pa.t[0:C, 0:C], XT[o].t[:], X[o].t[:], start=True, stop=True), [XT[o].b, X[o].b], [pa.b])
                            if s_ < 4:
                                S.op("pe", lambda e, o=o, pbq=pbq: e.matmul(pbq.t[0:C, 0:C], X[o].t[:], XT[o].t[:], start=True, stop=True), [XT[o].b, X[o].b], [pbq.b])
                            S.op("act", lambda e, o=o, pa=pa: e.activation(X[o].t[:], pa.t[0:C, 0:C], AF.Copy), [pa.b], [X[o].b])
                            if s_ < 4:
                                S.op("act", lambda e, o=o, pbq=pbq: e.activation(XT[o].t[:], pbq.t[0:C, 0:C], AF.Copy), [pbq.b], [XT[o].b])
                        for bi, c in enumerate(cs):
                            o = o2[bi]
                            py = slot()
                            S.op("pe", lambda e, o=o, py=py: e.matmul(py.t[0:C, 0:C], X[o].t[:], Y[o].t[:], start=True, stop=True), [X[o].b, Y[o].b], [py.b])
                            S.op("dve", lambda e, o=o, py=py: e.tensor_tensor(Y[o].t[:], Y[o].t[:], py.t[0:C, 0:C], ALU.add), [py.b, Y[o].b], [Y[o].b])
                    if self.gdn_stage <= 6:
                        continue
                    for bi, c in enumerate(cs):
                        o = o2[bi]
                        S.op("dve", lambda e, h=h, bi=bi, c=c: e.tensor_scalar(Vb[bi].t[:], vt[bi].t[:], betac.t[:, c, h:h + 1], None, ALU.mult), [vt[bi].b, betac.b], [Vb[bi].b])
                        S.op("dve", lambda e, h=h, bi=bi, c=c: e.tensor_scalar(Kb[bi].t[:], kn[bi].t[:], begamc.t[:, c, h:h + 1], None, ALU.mult), [kn[bi].b, begamc.b], [Kb[bi].b])
                        S.op("dve", lambda e, bi=bi, o=o: e.tensor_scalar(Kd[o].t[:], kn[bi].t[:], kdc[o].t[:, 0:1], None, ALU.mult), [kn[bi].b, kdc[o].b], [Kd[o].b])
                        if self.gdn_stage <= 6.3:
                            continue
                        pu_, pw_ = slot(), slot()
                        S.op("pe", lambda e, bi=bi, o=o, pu_=pu_: e.matmul(pu_.t[0:C, :], Y[o].t[:], Vb[bi].t[:], start=True, stop=True), [Y[o].b, Vb[bi].b], [pu_.b])
                        S.op("act", lambda e, o=o, pu_=pu_: e.activation(U[o].t[:], pu_.t[0:C, :], AF.Copy), [pu_.b], [U[o].b])
                        if self.gdn_stage <= 6.5:
                            continue
                        S.op("pe", lambda e, bi=bi, o=o, pw_=pw_: e.matmul(pw_.t[:, 0:C], Kb[bi].t[:], Y[o].t[:], start=True, stop=True), [Y[o].b, Kb[bi].b], [pw_.b])
                        if self.gdn_stage <= 6.7:
                            continue
                        S.op("act", lambda e, o=o, pw_=pw_: e.activation(WmT[o].t[:], pw_.t[:, 0:C], AF.Copy), [pw_.b], [WmT[o].b])
                    if self.gdn_stage <= 7:
                        continue
                    for bi, c in enumerate(cs):
                        o = o2[bi]
                        k2 = c % 2
                        pws, pqs, pmv, psu = slot(), slot(), slot(), slot()
                        S.op("pe", lambda e, o=o, pws=pws: e.matmul(pws.t[0:C, :], WmT[o].t[:], St.t[:], start=True, stop=True), [WmT[o].b, St.b], [pws.b])
                        S.op("pe", lambda e, o=o, pqs=pqs: e.matmul(pqs.t[0:C, :], qnT[o].t[:], St.t[:], start=True, stop=True), [qnT[o].b, St.b], [pqs.b])
                        S.op("dve", lambda e, o=o, k2=k2, pws=pws: e.tensor_tensor(vnew[k2].t[:], U[o].t[:], pws.t[0:C, :], ALU.subtract), [U[o].b, pws.b], [vnew[k2].b])
                        S.op("pe", lambda e, o=o, k2=k2, pmv=pmv: e.matmul(pmv.t[0:C, :], MT[o].t[:], vnew[k2].t[:], start=True, stop=True), [MT[o].b, vnew[k2].b], [pmv.b])
                        S.op("pe", lambda e, o=o, k2=k2, psu=psu: e.matmul(psu.t[:, :], Kd[o].t[:], vnew[k2].t[:], start=True, stop=True), [Kd[o].b, vnew[k2].b], [psu.b])
                        S.op("dve", lambda e, o=o, psu=psu: e.scalar_tensor_tensor(out=St.t[:], in0=St.t[:], scalar=egl[o].t[:, 0:1], in1=psu.t[:, :], op0=ALU.mult, op1=ALU.add),
                             [St.b, egl[o].b, psu.b], [St.b])
                        S.op("act", lambda e, k2=k2, pmv=pmv: e.activation(mvs[k2].t[:], pmv.t[0:C, :], AF.Copy), [pmv.b], [mvs[k2].b])
                        S.op("dve", lambda e, h=h, k2=k2, c=c, pqs=pqs: e.scalar_tensor_tensor(out=osb[k2].t[:], in0=pqs.t[0:C, :], scalar=egamc.t[:, c, h:h + 1], in1=mvs[k2].t[:],
                                                                                        op0=ALU.mult, op1=ALU.add), [pqs.b, egamc.b, mvs[k2].b], [osb[k2].b])
                        S.op("act", lambda e, k2=k2: e.activation(ojunk[0].t[:], osb[k2].t[:], AF.Square, accum_out=oss[k2].t[:]), [osb[k2].b], [ojunk[0].b, oss[k2].b])
                        S.op("dve", lambda e, k2=k2: e.tensor_scalar(ors[k2].t[:], oss[k2].t[:], 1.0 / 128, EPS, ALU.mult, ALU.add), [oss[k2].b], [ors[k2].b])
                        S.op("act", lambda e, k2=k2: e.activation(ors[k2].t[:], ors[k2].t[:], AF.Sqrt), [ors[k2].b], [ors[k2].b])
                        S.op("dve", lambda e, k2=k2: e.reciprocal(ors[k2].t[:], ors[k2].t[:]), [ors[k2].b], [ors[k2].b])
                        S.op("dve", lambda e, k2=k2: e.scalar_tensor_tensor(out=osb[k2].t[:], in0=osb[k2].t[:], scalar=ors[k2].t[:, 0:1], in1=gnr.t[:], op0=ALU.mult, op1=ALU.mult),
                             [osb[k2].b, ors[k2].b, gnr.b], [osb[k2].b])
                        S.op("act", lambda e, k2=k2, z_=z_, bi=bi: e.activation(zs[k2].t[:], z_.t[:, bi, :], AF.Silu), [z_.b], [zs[k2].b])
                        dbgm = os.environ.get("KGDNDBG", "")
                        dsel = {"vn": vnew[k2], "u": U[o], "vb": Vb[bi], "kb": Kb[bi], "kd": Kd[o], "vt": vt[bi], "kn": kn[bi], "qn": qn[bi], "mv": mvs[k2]}.get(dbgm)
                        d64 = {"y": Y[o], "x": X[o], "mt": MT[o], "dm": Dm[bi], "dt": DT[bi]}.get(dbgm)
                        if d64 is not None:
                            S.op("dve", lambda e, k2=k2, d64=d64: e.tensor_copy(ofb[k2].t[:, 0:64], d64.t[:]), [osb[k2].b, zs[k2].b, d64.b], [ofb[k2].b])
                            S.op("dve", lambda e, k2=k2, d64=d64: e.tensor_copy(ofb[k2].t[:, 64:128], d64.t[:]), [osb[k2].b, zs[k2].b, d64.b], [ofb[k2].b])
                        elif dsel is not None:
                            S.op("dve", lambda e, k2=k2, dsel=dsel: e.tensor_copy(ofb[k2].t[:], dsel.t[:]), [osb[k2].b, zs[k2].b, dsel.b], [ofb[k2].b])
                        elif os.environ.get("KGDNDBG", "") == "z":
                            S.op("dve", lambda e, k2=k2: e.tensor_copy(ofb[k2].t[:], zs[k2].t[:]), [osb[k2].b, zs[k2].b], [ofb[k2].b])
                        elif os.environ.get("KGDNDBG", "") == "o":
                            S.op("dve", lambda e, k2=k2: e.tensor_copy(ofb[k2].t[:], osb[k2].t[:]), [osb[k2].b, zs[k2].b], [ofb[k2].b])
                        else:
                            S.op("dve", lambda e, k2=k2: e.tensor_tensor(ofb[k2].t[:], osb[k2].t[:], zs[k2].t[:], ALU.mult), [osb[k2].b, zs[k2].b], [ofb[k2].b])
                        pbt = self.ps[7]
                        pTr = pbt.t[:, :].bitcast(BF16)
                        stg = ost[(c // 8) % 2]
                        S.op("pe", lambda e, k2=k2, pTr=pTr: e.transpose(pTr[:, 0:C], ofb[k2].t[:], self.identb.t[0:C, 0:C]), [ofb[k2].b, self.identb.b], [pbt.b])
                        S.op("act", lambda e, stg=stg, pTr=pTr, c=c: e.activation(stg.t[:, (c % 8) * C:(c % 8 + 1) * C], pTr[:, 0:C], AF.Copy), [pbt.b], [stg.b])
                        if c % 8 == 7:
                            cc = c // 8
                            S.dma("sp", self.oT[8 + h][:, cc * 512:(cc + 1) * 512], stg.t[:], stg.c, reads=[stg.b])

    def phase_outproj_ffn(self, layer, wout, xsrc):
        S = self.S
        Wo = wout.rearrange("(k p) n -> p k n", p=128)
        Wu = self.ffn_w_up[layer].rearrange("(k p) n -> p k n", p=128)
        Wd = self.ffn_w_down[layer].rearrange("(c p) n -> p c n", p=128)
        oTv = self.oT.rearrange("f p t -> p f t")
        xrb = [Buf("xr%d" % i) for i in range(NT)]
        with ExitStack() as ph:
            gain = self.sb(ph, "gain", [128, D], F32, chan=True)
            S.dma("sp", gain.t[:], self.norm_ffn[layer, :].partition_broadcast(128), gain.c, writes=[gain.b])
            cwr = self.sb(ph, "cwr", [FC, 4, 128], F32, chan=True)
            cw = self.sb(ph, "cw", [128, 4, FC], F32)
            S.dma_group("sp", [(cwr.t[:, jj, :], self.ffn_conv_w[layer, jj, :].rearrange("(c p) -> c p", p=128)) for jj in range(3)]
                        + [(cwr.t[:, 3, :], self.ffn_conv_b[layer, :].rearrange("(c p) -> c p", p=128))], cwr.c, writes=[cwr.b])
            pb = self.ps[7]
            for jj in range(4):
                S.op("pe", lambda e, jj=jj: e.transpose(pb.t[:, jj * FC:(jj + 1) * FC], cwr.t[:, jj, :], self.identf.t[0:FC, 0:FC]),
                     [cwr.b, self.identf.b], [pb.b], inc=(jj == 3))
            S.op("dve", lambda e: e.tensor_copy(cw.t[:], pb.t[:, 0:4 * FC].rearrange("p (j c) -> p j c", c=FC)), [pb.b], [cw.b])
            halo = self.sb(ph, "halo", [128, FC, 2], F32)
            S.op("dve", lambda e: e.memset(halo.t[:], 0.0), [], [halo.b])
            big = self.sb(ph, "big", [128, FC * TG], BF16, chan=True)
            actT_v = big.t[:].rearrange("p (c t) -> p c t", t=TG)
            bigf = big.t[:].bitcast(F32)
            x1v = [bigf[:, tt * D:(tt + 1) * D] for tt in range(4)]
            hs = self.sb(ph, "hT", [128, KC, TG], BF16, chan=True)
            xi_ = [self.sb(ph, "xi", [128, 512], F32, chan=True) for _ in range(2)]
            xo_ = [self.sb(ph, "xo", [128, 512], F32, chan=True) for _ in range(2)]
            xn = self.sb(ph, "xn", [128, D], BF16)
            st_ = [self.sb(ph, "ss", [128, 1], F32), self.sb(ph, "rs", [128, 1], F32)]
            wu = [self.sb(ph, "wu", [128, 16, 256], BF16, chan=True) for _ in range(3)]
            wg = [self.sb(ph, "wg", [128, 16, 256], BF16, chan=True) for _ in range(3)]
            wd = [self.sb(ph, "wd", [128, 11, 512], BF16, chan=True) for _ in range(3)]
            gsb = [self.sb(ph, "gsb", [128, TG + 2], F32) for _ in range(2)]
            cacc = [self.sb(ph, "cacc", [128, TG], F32) for _ in range(2)]
            sg = [self.sb(ph, "sg", [128, TG], F32) for _ in range(2)]
            nwo = nwu = nwd = nxi = nxo = 0
            for g in range(NG):
                tok = slice(g * TG, (g + 1) * TG)
                S.dma("sp", hs.t[:], oTv[:, :, tok], hs.c, writes=[hs.b])
                for nb_ in range(D // 256):
                    w = (wu + wg)[nwo % 6]
                    nwo += 1
                    S.dma("pool", w.t[:], Wo[:, :, nb_ * 256:(nb_ + 1) * 256], w.c, writes=[w.b])
                    for tt in range(4):
                        tile = g * 4 + tt
                        pbk = self.ps[(nb_ * 4 + tt) % 4]
                        xi = xi_[nxi % 2]
                        nxi += 1
                        S.dma("sp", xi.t[:, 0:256], xsrc[tile * 128:(tile + 1) * 128, nb_ * 256:(nb_ + 1) * 256], xi.c, reads=[xrb[tile]], writes=[xi.b])
                        self.mm(pbk.t[:, 0:256], pbk.b, [(hs.t[:, f, tt * 128:(tt + 1) * 128], w.t[:, f, :]) for f in range(16)], [hs.b, w.b])
                        S.op("dve", lambda e, tt=tt, pbk=pbk, xi=xi, nb_=nb_: e.tensor_tensor(x1v[tt][:, nb_ * 256:(nb_ + 1) * 256], pbk.t[:, 0:256], xi.t[:, 0:256], ALU.add),
                             [pbk.b, xi.b], [big.b])
                for tt in range(4):
                    tile = g * 4 + tt
                    S.dma("sp", self.xr[tile * 128:(tile + 1) * 128, :], x1v[tt], big.c, reads=[big.b], writes=[xrb[tile]])
                self.norm_x1(x1v, big.b, gain, xn, st_, hs)
                for ub in range(D_FF // 256):
                    wu_, wg_ = wu[nwu % 3], wg[nwu % 3]
                    nwu += 1
                    S.dma("pool", wu_.t[:], Wu[:, :, ub * 256:(ub + 1) * 256], wu_.c, writes=[wu_.b])
                    S.dma("pool", wg_.t[:], Wu[:, :, D_FF + ub * 256:D_FF + (ub + 1) * 256], wg_.c, writes=[wg_.b])
                    for cc in range(2):
                        c = ub * 2 + cc
                        pu = self.ps[(c % 2) * 2]
                        pg = self.ps[(c % 2) * 2 + 1]
                        self.mm(pg.t[:, :], pg.b, [(wg_.t[:, k, cc * 128:(cc + 1) * 128], hs.t[:, k, :]) for k in range(KC)], [wg_.b, hs.b])
                        self.mm(pu.t[:, :], pu.b, [(wu_.t[:, k, cc * 128:(cc + 1) * 128], hs.t[:, k, :]) for k in range(KC)], [wu_.b, hs.b])
                        gs, ca, sg_ = gsb[c % 2], cacc[c % 2], sg[c % 2]
                        S.op("act", lambda e, gs=gs, pg=pg: e.activation(gs.t[:, 2:TG + 2], pg.t[:, :], AF.Copy), [pg.b], [gs.b])
                        S.op("dve", lambda e, gs=gs, c=c: e.tensor_copy(gs.t[:, 0:2], halo.t[:, c, :]), [halo.b, gs.b], [gs.b])
                        S.op("dve", lambda e, gs=gs, ca=ca, c=c: e.tensor_scalar(ca.t[:], gs.t[:, 2:TG + 2], cw.t[:, 2, c:c + 1], cw.t[:, 3, c:c + 1], ALU.mult, ALU.add),
                             [gs.b, cw.b], [ca.b])
                        S.op("dve", lambda e, gs=gs, ca=ca, c=c: e.scalar_tensor_tensor(out=ca.t[:], in0=gs.t[:, 1:TG + 1], scalar=cw.t[:, 1, c:c + 1], in1=ca.t[:], op0=ALU.mult, op1=ALU.add),
                             [gs.b, cw.b, ca.b], [ca.b])
                        S.op("dve", lambda e, gs=gs, ca=ca, c=c: e.scalar_tensor_tensor(out=ca.t[:], in0=gs.t[:, 0:TG], scalar=cw.t[:, 0, c:c + 1], in1=ca.t[:], op0=ALU.mult, op1=ALU.add),
                             [gs.b, cw.b, ca.b], [ca.b])
                        S.op("dve", lambda e, gs=gs, c=c: e.tensor_copy(halo.t[:, c, :], gs.t[:, TG:TG + 2]), [gs.b, halo.b], [halo.b])
                        S.op("act", lambda e, sg_=sg_, ca=ca: e.activation(sg_.t[:], ca.t[:], AF.Silu), [ca.b], [sg_.b])
                        S.op("dve", lambda e, sg_=sg_, pu=pu, c=c: e.tensor_tensor(actT_v[:, c, :], sg_.t[:], pu.t[:, :], ALU.mult), [sg_.b, pu.b], [big.b])
                for nb_ in range(D // 512):
                    accs = [self.ps[4 + tt] for tt in range(4)]
                    for qd in range(4):
                        w = wd[nwd % 3]
                        nwd += 1
                        S.dma("pool", w.t[:], Wd[:, qd * 11:(qd + 1) * 11, nb_ * 512:(nb_ + 1) * 512], w.c, writes=[w.b])
                        for tt in range(4):
                            for ci in range(11):
                                c = qd * 11 + ci
                                S.op("pe", lambda e, tt=tt, ci=ci, c=c, w=w, acc=accs[tt]: e.matmul(
                                    acc.t[:, :], actT_v[:, c, tt * 128:(tt + 1) * 128], w.t[:, ci, :], start=(c == 0), stop=(c == FC - 1)),
                                    [big.b, w.b], [accs[tt].b], inc=(ci == 10))
                    for tt in range(4):
                        tile = g * 4 + tt
                        xi = xi_[nxi % 2]
                        nxi += 1
                        xo = xo_[nxo % 2]
                        nxo += 1
                        S.dma("sp", xi.t[:], self.xr[tile * 128:(tile + 1) * 128, nb_ * 512:(nb_ + 1) * 512], xi.c, reads=[xrb[tile]], writes=[xi.b])
                        S.op("dve", lambda e, tt=tt, xo=xo, xi=xi: e.tensor_tensor(xo.t[:], accs[tt].t[:, :], xi.t[:], ALU.add),
                             [accs[tt].b, xi.b], [xo.b])
                        S.dma("sp", self.xr[tile * 128:(tile + 1) * 128, nb_ * 512:(nb_ + 1) * 512], xo.t[:], xo.c, reads=[xo.b], writes=[xrb[tile]])

    def norm_x1(self, x1v, xb, gain, n, st_, hs):
        S = self.S
        ss, rs = st_
        for tt in range(4):
            xv = x1v[tt]
            S.op("act", lambda e, xv=xv: e.activation(n.t[:], xv, AF.Square, accum_out=ss.t[:]), [xb], [n.b, ss.b])
            S.op("dve", lambda e: e.tensor_scalar(rs.t[:], ss.t[:], 1.0 / D, EPS, ALU.mult, ALU.add), [ss.b], [rs.b])
            S.op("act", lambda e: e.activation(rs.t[:], rs.t[:], AF.Sqrt), [rs.b], [rs.b])
            S.op("dve", lambda e: e.reciprocal(rs.t[:], rs.t[:]), [rs.b], [rs.b])
            S.op("dve", lambda e, xv=xv: e.scalar_tensor_tensor(out=n.t[:], in0=xv, scalar=rs.t[:, 0:1], in1=gain.t[:],
                                                               op0=ALU.mult, op1=ALU.mult), [xb, rs.b, gain.b], [n.b])
            for half in range(2):
                pb = self.ps[2 + half]
                pT = pb.t[:, :].bitcast(BF16)
                for kk in range(8):
                    k = half * 8 + kk
                    S.op("pe", lambda e, k=k, kk=kk, pT=pT: e.transpose(pT[:, kk * 128:(kk + 1) * 128], n.t[:, k * 128:(k + 1) * 128], self.identb.t[:]),
                         [n.b, self.identb.b], [pb.b], inc=(kk == 7))
                dst = hs.t[:, half * 8:(half + 1) * 8, tt * 128:(tt + 1) * 128]
                src = pT.rearrange("p (k t) -> p k t", t=128)
                if half == 0:
                    S.op("act", lambda e, dst=dst, src=src: e.activation(dst, src, AF.Copy), [pb.b], [hs.b])
                else:
                    S.op("dve", lambda e, dst=dst, src=src: e.tensor_copy(dst, src), [pb.b], [hs.b])

    def phase_final(self, xsrc):
        S = self.S
        with ExitStack() as ph:
            gain = self.sb(ph, "gain", [128, D], F32, chan=True)
            S.dma("sp", gain.t[:], self.norm_final.partition_broadcast(128), gain.c, writes=[gain.b])
            xt = [self.sb(ph, "xt", [128, D], F32, chan=True) for _ in range(3)]
            yo = [self.sb(ph, "yo", [128, D], F32, chan=True) for _ in range(2)]
            sq = self.sb(ph, "sq", [128, D], BF16)
            ss = [self.sb(ph, "ss", [128, 1], F32) for _ in range(2)]
            rs = [self.sb(ph, "rs", [128, 1], F32) for _ in range(2)]
            for tile in range(NT):
                x = xt[tile % 3]
                s_, r_, y_ = ss[tile % 2], rs[tile % 2], yo[tile % 2]
                S.dma("sp", x.t[:], xsrc[tile * 128:(tile + 1) * 128, :], x.c, writes=[x.b])
                S.op("act", lambda e, x=x, s_=s_: e.activation(sq.t[:], x.t[:], AF.Square, accum_out=s_.t[:]), [x.b], [sq.b, s_.b])
                S.op("dve", lambda e, s_=s_, r_=r_: e.tensor_scalar(r_.t[:], s_.t[:], 1.0 / D, EPS, ALU.mult, ALU.add), [s_.b], [r_.b])
                S.op("act", lambda e, r_=r_: e.activation(r_.t[:], r_.t[:], AF.Sqrt), [r_.b], [r_.b])
                S.op("dve", lambda e, r_=r_: e.reciprocal(r_.t[:], r_.t[:]), [r_.b], [r_.b])
                S.op("dve", lambda e, x=x, r_=r_, y_=y_: e.scalar_tensor_tensor(out=y_.t[:], in0=x.t[:], scalar=r_.t[:, 0:1], in1=gain.t[:],
                                                                              op0=ALU.mult, op1=ALU.mult), [x.b, r_.b, gain.b], [y_.b])
                S.dma("sp", self.y[tile * 128:(tile + 1) * 128, :], y_.t[:], y_.c, reads=[y_.b])


_INPUT_NAMES = ["rel_bias", "norm_mix", "norm_ffn", "norm_final", "ev_w_in", "ev_b_forget", "ev_sinks", "ev_w_out",
                "od_w_in", "od_cmp_pos", "od_cmp_w1", "od_cmp_w2", "od_conv_w", "od_a_log", "od_dt_bias", "od_gdn_norm", "od_w_out",
                "ffn_w_up", "ffn_conv_w", "ffn_conv_b", "ffn_w_down"]


def kernel(**inputs):
    b = Builder()
    nc = b.build()
    consts = host_consts()
    x = np.ascontiguousarray(inputs["x"], dtype=np.float32)
    shared = {k: np.ascontiguousarray(inputs[k], dtype=np.float32) for k in _INPUT_NAMES}
    shared.update(consts)
    in_maps = []
    for c in range(N_CORES):
        m = dict(shared)
        m["x"] = x[c]
        in_maps.append(m)
    res = run_bass_kernel_spmd(nc, in_maps, core_ids=list(range(N_CORES)))
    return np.stack([np.asarray(r["y"]) for r in res.results], axis=0).astype(np.float32)
```
